# Optimizing a Trainium2 kernel written in Bass

```python
import math
import jax, jax.numpy as jnp
from jax import lax
import numpy as np

D_MODEL = 1024
BATCH = 32
SEQ = 2048
DEPTH = 1

N_META = 16
BLOCK_Q = 128
SB_HEADS = 8
SB_HEAD_DIM = 64
SB_WIDTH = SB_HEADS * SB_HEAD_DIM
DSA_HEADS = 8
DSA_HEAD_DIM = 64
DSA_WIDTH = DSA_HEADS * DSA_HEAD_DIM
IDX_HEADS = 8
IDX_DIM = 64
TOPK_MAX = 256
REL_BUCKETS = 32
REL_MAX_DIST = 128
LN_EPS = 1e-5
DEEPNORM_ALPHA = (2.0 * DEPTH) ** 0.25
DEEPNORM_BETA = (8.0 * DEPTH) ** -0.25
COLS = [SB_WIDTH, SB_WIDTH, SB_WIDTH, SB_WIDTH,
        DSA_WIDTH, DSA_HEAD_DIM, DSA_HEAD_DIM, DSA_WIDTH,
        IDX_HEADS * IDX_DIM, IDX_DIM, IDX_HEADS,
        D_MODEL, D_MODEL]
IN_COLS = sum(COLS)

kernel_name = "hybrid_stickbreak_dsa_gated"


def layer_norm(x, g, b):
    xf = x.astype(jnp.float32)
    mu = jnp.mean(xf, axis=-1, keepdims=True)
    var = jnp.mean(jnp.square(xf - mu), axis=-1, keepdims=True)
    return ((xf - mu) * lax.rsqrt(var + LN_EPS) * g.astype(jnp.float32) + b.astype(jnp.float32)).astype(x.dtype)


def rel_bucket(dist):
    max_exact = REL_BUCKETS // 2
    nf = jnp.maximum(dist, 1).astype(jnp.float32)
    large = max_exact + (jnp.log(nf / max_exact) / math.log(REL_MAX_DIST / max_exact)
                         * (REL_BUCKETS - max_exact)).astype(jnp.int32)
    large = jnp.minimum(large, REL_BUCKETS - 1)
    return jnp.where(dist < max_exact, dist, large)


def stick_breaking_block(q, k, v, q_pos):
    lk = k.shape[1]
    z = jnp.einsum('bqhd,bkhd->bhqk', q, k).astype(jnp.float32) * (SB_HEAD_DIM ** -0.5)
    mask = jnp.arange(lk)[None, :] < q_pos[:, None]
    log_beta = jax.nn.log_sigmoid(z)
    log_1m = jnp.where(mask, log_beta - z, 0.0)
    suffix = lax.cumsum(log_1m, axis=3, reverse=True) - log_1m
    a = jnp.where(mask, jnp.exp(log_beta + suffix), 0.0)
    return jnp.einsum('bhqk,bkhd->bqhd', a.astype(v.dtype), v)


def dsa_block(q, k, v, qi, ki, wi, q_pos, rel_bias, topk):
    lk = k.shape[1]
    s = jnp.einsum('bqhd,bkd->bqhk', qi, ki).astype(jnp.float32)
    score = jnp.einsum('bqhk,bqh->bqk', jax.nn.relu(s), wi.astype(jnp.float32))
    causal = jnp.arange(lk)[None, :] <= q_pos[:, None]
    score = jnp.where(causal[None], score, -jnp.inf)
    kb = min(topk, lk)
    _, idx = lax.top_k(score, kb)
    gather = jax.vmap(lambda a, i: a[i])
    k_sel = gather(k, idx)
    v_sel = gather(v, idx)
    logits = jnp.einsum('bqhd,bqkd->bqhk', q, k_sel).astype(jnp.float32) * (DSA_HEAD_DIM ** -0.5)
    dist = q_pos[None, :, None] - idx
    valid = dist >= 0
    bias = rel_bias[rel_bucket(jnp.maximum(dist, 0))].astype(jnp.float32)
    logits = logits + jnp.transpose(bias, (0, 1, 3, 2))
    logits = jnp.where(valid[:, :, None, :], logits, -1e30)
    p = jax.nn.softmax(logits, axis=-1)
    return jnp.einsum('bqhk,bqkd->bqhd', p.astype(v_sel.dtype), v_sel)


def hybrid_layer(h, w_in, b_gate, idx_kn_g, idx_kn_b, w_pa, w_pb, w_o, ln_g, ln_b, rel_bias, topk):
    bsz, t_len, _ = h.shape
    proj = h @ w_in
    offs = list(np.cumsum(COLS)[:-1])
    (q_a, k_a, v_a, z_a, q_b, k_b, v_b, z_b, qi, ki, wi, ga, gb) = jnp.split(proj, offs, axis=-1)
    q_a = q_a.reshape(bsz, t_len, SB_HEADS, SB_HEAD_DIM)
    k_a = k_a.reshape(bsz, t_len, SB_HEADS, SB_HEAD_DIM)
    v_a = v_a.reshape(bsz, t_len, SB_HEADS, SB_HEAD_DIM)
    q_b = q_b.reshape(bsz, t_len, DSA_HEADS, DSA_HEAD_DIM)
    qi = qi.reshape(bsz, t_len, IDX_HEADS, IDX_DIM)
    ki = layer_norm(ki, idx_kn_g, idx_kn_b)
    wi = wi * (IDX_HEADS ** -0.5 * IDX_DIM ** -0.5)
    gates = jax.nn.sigmoid(jnp.concatenate([ga, gb], axis=-1) + b_gate)
    g_a, g_b = gates[..., :D_MODEL], gates[..., D_MODEL:]

    pos = jnp.arange(t_len, dtype=jnp.int32)
    n_real = t_len - N_META
    bounds = [(0, N_META)] + [(N_META + i * BLOCK_Q, min(N_META + (i + 1) * BLOCK_Q, t_len))
                              for i in range((n_real + BLOCK_Q - 1) // BLOCK_Q)]
    outs_a, outs_b = [], []
    for (s0, e0) in bounds:
        qp = pos[s0:e0]
        outs_a.append(stick_breaking_block(q_a[:, s0:e0], k_a[:, :e0], v_a[:, :e0], qp))
        outs_b.append(dsa_block(q_b[:, s0:e0], k_b[:, :e0], v_b[:, :e0], qi[:, s0:e0],
                                ki[:, :e0], wi[:, s0:e0], qp, rel_bias, topk))
    y_a = jnp.concatenate(outs_a, axis=1).reshape(bsz, t_len, SB_WIDTH) * jax.nn.silu(z_a)
    y_b = jnp.concatenate(outs_b, axis=1).reshape(bsz, t_len, DSA_WIDTH) * jax.nn.silu(z_b)
    merged = g_a * (y_a @ w_pa) + g_b * (y_b @ w_pb)
    out = merged @ w_o
    return layer_norm(DEEPNORM_ALPHA * h + out, ln_g, ln_b)


def setup_inputs(seed: int = 0) -> dict:
    key = jax.random.key(seed)
    ks = jax.random.split(key, 16)
    f32 = jnp.float32
    n = lambda k, shape, s: (jax.random.normal(k, shape, f32) * s).astype(f32)
    return {
        "x": n(ks[0], (BATCH, SEQ, D_MODEL), 1.0),
        "meta_tokens": n(ks[1], (N_META, D_MODEL), 1.0),
        "ln_in_g": 1.0 + n(ks[2], (D_MODEL,), 0.02),
        "ln_in_b": n(ks[3], (D_MODEL,), 0.02),
        "rel_bias": n(ks[4], (REL_BUCKETS, DSA_HEADS), 0.5),
        "w_in": n(ks[5], (DEPTH, D_MODEL, IN_COLS), D_MODEL ** -0.5),
        "b_gate": n(ks[6], (DEPTH, 2 * D_MODEL), 0.02),
        "idx_kn_g": 1.0 + n(ks[7], (DEPTH, IDX_DIM), 0.02),
        "idx_kn_b": n(ks[8], (DEPTH, IDX_DIM), 0.02),
        "w_pa": n(ks[9], (DEPTH, SB_WIDTH, D_MODEL), SB_WIDTH ** -0.5 * DEEPNORM_BETA),
        "w_pb": n(ks[10], (DEPTH, DSA_WIDTH, D_MODEL), DSA_WIDTH ** -0.5 * DEEPNORM_BETA),
        "w_o": n(ks[11], (DEPTH, D_MODEL, D_MODEL), D_MODEL ** -0.5 * DEEPNORM_BETA),
        "ln_g": 1.0 + n(ks[12], (DEPTH, D_MODEL), 0.02),
        "ln_b": n(ks[13], (DEPTH, D_MODEL), 0.02),
    }


def reference(x, meta_tokens, ln_in_g, ln_in_b, rel_bias, w_in, b_gate, idx_kn_g, idx_kn_b,
              w_pa, w_pb, w_o, ln_g, ln_b):
    bsz, seq, d = x.shape
    topk = min(TOPK_MAX, seq // 4)
    meta = jnp.broadcast_to(meta_tokens[None].astype(x.dtype), (bsz, N_META, d))
    h = layer_norm(jnp.concatenate([meta, x], axis=1), ln_in_g, ln_in_b)
    for l in range(DEPTH):
        h = hybrid_layer(h, w_in[l], b_gate[l], idx_kn_g[l], idx_kn_b[l], w_pa[l], w_pb[l],
                         w_o[l], ln_g[l], ln_b[l], rel_bias, topk)
    return h[:, N_META:]
```

```python
import contextlib
import math
import os
import numpy as np
import concourse.bass as bass
import concourse.mybir as mybir
from concourse.bass_utils import run_bass_kernel_spmd

F32 = mybir.dt.float32
BF16 = mybir.dt.bfloat16
ALU = mybir.AluOpType
AF = mybir.ActivationFunctionType
AX = mybir.AxisListType

T = 2064
NT = 17
D = 1024
SEQ = 2048
NMETA = 16
TOPK = 256
EPS = 1e-5
ALPHA = 2.0 ** 0.25
WI_SCALE = (8 ** -0.5) * (64 ** -0.5)
NIT = 22
TBS = [(0, 512), (512, 512), (1024, 512), (1536, 512), (2048, 16)]
C_QA, C_KA, C_VA, C_ZA, C_QB, C_KB, C_VB, C_ZB, C_QI, C_KI, C_WI, C_GA, C_GB = (
    0, 512, 1024, 1536, 2048, 2560, 2624, 2688, 3200, 3712, 3776, 3784, 4808)
INCOLS = 5832


def tsz(j):
    return 128 if j < 16 else 16


def tiles_of(b):
    return list(range(4 * b, min(4 * b + 4, NT)))


class Buf:
    __slots__ = ("name", "w", "r")

    def __init__(self, name=""):
        self.name = name
        self.w = None
        self.r = []


class _Rec:
    def __init__(self):
        self.call = None

    def __getattr__(self, name):
        def f(*a, **kw):
            self.call = (name, a, kw)
            return self
        return f


def _freeze(fn):
    rec = _Rec()
    fn(rec)
    name, a, kw = rec.call
    return lambda e: getattr(e, name)(*a, **kw)


class Prog:
    ENGS = ("pe", "act", "dve", "pool", "sp")

    def __init__(self):
        self.ops = {e: [] for e in self.ENGS}
        self.cnt = {e: 0 for e in self.ENGS}
        self.pending = {e: set() for e in self.ENGS}
        self.ndma = 0
        self.ndma_e = {}
        self.dma_sem_use = {}
        self.NDMASEM = 24

    def _deps(self, eng, reads, writes):
        deps = set(self.pending[eng])
        self.pending[eng] = set()
        for b in reads:
            if b.w is not None:
                deps.add(b.w)
        for b in writes:
            if b.w is not None:
                deps.add(b.w)
            deps.update(b.r)
        if eng == "pe":
            deps = {d for d in deps if d[0] != "pe"}
        return deps

    def _mark(self, me, reads, writes):
        for b in reads:
            b.r.append(me)
            if len(b.r) > 64:
                best = {}
                for (k, v) in b.r:
                    if best.get(k, 0) < v:
                        best[k] = v
                b.r = list(best.items())
        for b in writes:
            b.w = me
            b.r = []

    def op(self, eng, fn, reads=(), writes=()):
        deps = self._deps(eng, reads, writes)
        self.cnt[eng] += 1
        me = (eng, self.cnt[eng])
        self.ops[eng].append(("op", _freeze(fn), deps, None))
        self._mark(me, reads, writes)
        return me

    def dma(self, eng, fn, reads=(), writes=()):
        deps = self._deps(eng, reads, writes)
        nd = self.ndma_e.get(eng, 0)
        self.ndma_e[eng] = nd + 1
        self.ndma += 1
        k = (eng, nd % self.NDMASEM)
        prev = self.dma_sem_use.get(k, 0)
        if prev:
            deps.add((("dma", k), 16 * prev))
        self.dma_sem_use[k] = prev + 1
        me = (("dma", k), 16 * (prev + 1))
        self.ops[eng].append(("dma", _freeze(fn), deps, k))
        self._mark(me, reads, writes)
        return me

    def barrier(self):
        snap = set()
        for e in self.ENGS:
            if self.cnt[e]:
                snap.add((e, self.cnt[e]))
        for k, c in self.dma_sem_use.items():
            snap.add((("dma", k), 16 * c))
        self.pending["act"] |= snap
        scr = self.bar_scratch
        me = self.op("act", lambda e: e.activation(out=scr, in_=scr, func=AF.Identity))
        for e in self.ENGS:
            if e != "act":
                self.pending[e].add(me)

    def emit(self, nc):
        with contextlib.ExitStack() as st:
            sems = {}
            for e in self.ENGS:
                sems[e] = st.enter_context(nc.semaphore("s_" + e))
            for k in self.dma_sem_use:
                sems[("dma", k)] = st.enter_context(nc.semaphore("s_dma_%s%d" % k))
            block = st.enter_context(nc.Block())
            regs = {"pe": block.tensor, "act": block.scalar, "dve": block.vector,
                    "pool": block.gpsimd, "sp": block.sync}
            for e in self.ENGS:
                ops = self.ops[e]
                if not ops and e != "sp":
                    continue

                def body(engine, ops=ops, e=e):
                    waited = {}
                    for kind, fn, deps, k in ops:
                        best = {}
                        for (sk, val) in deps:
                            if best.get(sk, 0) < val:
                                best[sk] = val
                        for sk in sorted(best, key=str):
                            val = best[sk]
                            if waited.get(sk, 0) >= val:
                                continue
                            engine.wait_ge(sems[sk], val)
                            waited[sk] = val
                        ins = fn(engine)
                        if kind == "op":
                            ins.then_inc(sems[e], 1)
                        else:
                            ins.then_inc(sems[("dma", k)], 16)
                    if e == "sp":
                        for k2, c2 in self.dma_sem_use.items():
                            engine.wait_ge(sems[("dma", k2)], 16 * c2)
                        for e2 in self.ENGS:
                            if e2 != "sp" and self.cnt[e2]:
                                engine.wait_ge(sems[e2], self.cnt[e2])
                regs[e](body)


class _Stop(Exception):
    pass


_TICK = [0]


def tick():
    _TICK[0] += 1
    stop_at('d%d' % _TICK[0])


def stop_at(name):
    if os.environ.get("K_STOP", "") == name:
        raise _Stop()


def interleave(gens):
    st = [[g, est, 0] for g, est in gens]
    while st:
        st.sort(key=lambda x: x[2] / x[1])
        g = st[0]
        try:
            next(g[0])
            g[2] += 1
        except StopIteration:
            st.pop(0)


class Arena:
    def __init__(self, ap, nelem):
        self.ap = ap
        self.n = nelem
        self.off = 0

    def reset(self):
        self.off = 0

    def alloc(self, shape, dt):
        n = 1
        for s in shape:
            n *= s
        ne = n * (2 if dt == F32 else 1)
        ne_al = (ne + 15) // 16 * 16
        assert self.off + ne_al <= self.n, ("arena overflow", self.off, ne_al, self.n)
        v = self.ap[:, self.off:self.off + ne]
        self.off += ne_al
        if dt == F32:
            v = v.bitcast(F32)
        if len(shape) == 2:
            v = v.rearrange("p (a b) -> p a b", a=shape[0])
        elif len(shape) == 3:
            v = v.rearrange("p (a b c) -> p a b c", a=shape[0], b=shape[1])
        return v


def build_program(nseq):
    nc = bass.Bass("TRN2", target_bir_lowering=False)
    dram = lambda name, shape, kind="ExternalInput": nc.dram_tensor(name, shape, F32, kind=kind).ap()
    x_d = dram("x", [nseq, SEQ, D])
    meta_d = dram("meta", [NMETA, D])
    win_d = dram("w_in", [D, INCOLS])
    wpa_d = dram("w_pa", [512, D])
    wpb_d = dram("w_pb", [512, D])
    wo_d = dram("w_o", [D, D])
    relb_d = dram("rel_bias", [32, 8])
    lning_d = dram("ln_in_g", [D])
    lninb_d = dram("ln_in_b", [D])
    lng_d = dram("ln_g", [D])
    lnb_d = dram("ln_b", [D])
    cols_d = dram("cols", [128, 48])
    ikg_d = dram("ikn_g", [64])
    ikb_d = dram("ikn_b", [64])
    cmat_d = dram("cmat", [32, 383])
    c128_d = dram("c128", [128, 5, 128])
    pw_d = dram("pw", [32])
    out_d = dram("out", [nseq, SEQ, D], kind="ExternalOutput")
    gscr_d = nc.dram_tensor("gscr", [8, 383], F32, kind="Internal").ap()

    P = Prog()
    st = contextlib.ExitStack()
    with st:
        sbt = lambda name, shape, dt: st.enter_context(nc.sbuf_tensor("sb_" + name, shape, dt))
        hT = sbt("hT", [128, 8, T], BF16)
        yT = sbt("yT", [128, 2, 4, T], BF16)
        ident = sbt("ident", [128, 128], BF16)
        identf = sbt("identf", [128, 128], F32)
        trimask = sbt("trimask", [128, 128], BF16)
        tri8 = sbt("tri8", [128, 128], BF16)
        ones8 = sbt("ones8", [128, 128], BF16)
        ones64 = sbt("ones64", [128, 64], BF16)
        negmask = sbt("negmask", [128, 128], F32)
        Jm = sbt("Jm", [128, 128], F32)
        c128f = sbt("c128f", [128, 5, 128], F32)
        BD = sbt("BD", [128, 8, 2, 2, 128], BF16)
        cols = sbt("cols", [128, 48], F32)
        negbg = sbt("negbg", [128, 16], F32)
        b31 = sbt("b31", [128, 8], F32)
        pw = sbt("pw", [128, 32], F32)
        ikg = sbt("ikg", [128, 64], F32)
        ikb = sbt("ikb", [128, 64], F32)
        relb = sbt("relb", [32, 8], F32)
        cmat = sbt("cmat", [32, 383], F32)
        gvec = sbt("gvec", [8, 383], F32)
        ARENA_N = 63400
        arena_t = sbt("arena", [128, ARENA_N], BF16)
        A = Arena(arena_t, ARENA_N)
        ps = st.enter_context(nc.psum_tensor("ps", [128, 4096], F32))
        bank = [ps[:, 512 * k:512 * (k + 1)] for k in range(8)]
        Bps = [Buf("ps%d" % k) for k in range(8)]
        Bconst = Buf("const")
        Bh = [Buf("h%d" % b) for b in range(5)]
        By = [[[Buf() for b in range(5)] for pr in range(4)] for br in range(2)]

        def mm(out, lhsT, rhs, start, stop, reads, writes):
            return P.op("pe", lambda e: e.matmul(out, lhsT=lhsT, rhs=rhs, start=start, stop=stop),
                        reads=reads, writes=writes)

        def act(out, in_, func, reads, writes, bias=0.0, scale=1.0):
            return P.op("act", lambda e: e.activation(out=out, in_=in_, func=func, bias=bias, scale=scale),
                        reads=reads, writes=writes)

        def dve(fn, reads, writes):
            return P.op("dve", fn, reads=reads, writes=writes)

        def pool(fn, reads, writes):
            return P.op("pool", fn, reads=reads, writes=writes)

        def ld(out, in_, writes, reads=()):
            return P.dma("sp", lambda e: e.dma_start(out=out, in_=in_), reads=reads, writes=writes)

        def ldc(out, in_, writes, reads=()):
            return P.dma("pool", lambda e: e.dma_start(out=out, in_=in_), reads=reads, writes=writes)

        P.bar_scratch = cols[0:1, 40:48]
        ld(c128f[:], c128_d, [Bconst])
        ld(cols[:], cols_d, [Bconst])
        ld(relb[:], relb_d, [Bconst])
        ld(cmat[:], cmat_d, [Bconst])
        ld(b31[:], relb_d[31, :].partition_broadcast(128), [Bconst])
        ld(pw[:], pw_d.partition_broadcast(128), [Bconst])
        ld(ikg[:], ikg_d.partition_broadcast(128), [Bconst])
        ld(ikb[:], ikb_d.partition_broadcast(128), [Bconst])
        dve(lambda e: e.tensor_copy(out=ident[:], in_=c128f[:, 0, :]), [Bconst], [Bconst])
        dve(lambda e: e.tensor_copy(out=identf[:], in_=c128f[:, 0, :]), [Bconst], [Bconst])
        dve(lambda e: e.tensor_copy(out=trimask[:], in_=c128f[:, 1, :]), [Bconst], [Bconst])
        dve(lambda e: e.tensor_scalar(out=tri8[:], in0=c128f[:, 2, :], scalar1=-8.0, scalar2=None, op0=ALU.mult),
            [Bconst], [Bconst])
        dve(lambda e: e.tensor_copy(out=Jm[:], in_=c128f[:, 3, :]), [Bconst], [Bconst])
        dve(lambda e: e.tensor_copy(out=negmask[:], in_=c128f[:, 4, :]), [Bconst], [Bconst])
        dve(lambda e: e.memset(ones8[:], -8.0), [], [Bconst])
        dve(lambda e: e.memset(ones64[:], 1.0), [], [Bconst])
        dve(lambda e: e.tensor_scalar(out=negbg[:], in0=cols[:, 16:32], scalar1=-1.0, scalar2=None, op0=ALU.mult),
            [Bconst], [Bconst])
        Bg = Buf("gscr")
        mm(bank[0][0:8, 0:383], relb[:, :], cmat[:, :], True, True, [Bconst], [Bps[0]])
        dve(lambda e: e.tensor_copy(out=gvec[:], in_=bank[0][0:8, 0:383]), [Bps[0]], [Bconst])
        ld(gscr_d, gvec[:], [Bg], reads=[Bconst])
        A.reset()
        hank = A.alloc([16, 128], F32)
        Bhank = Buf("hank")
        for h in range(8):
            for off in range(2):
                src = bass.AP(tensor=gscr_d.tensor, offset=h * 383 + 128 * off, ap=[[1, 128], [1, 128]])
                ld(hank[:, h * 2 + off, :], src, [Bhank], reads=[Bg])
        for h in range(8):
            for off in range(2):
                k = (h * 2 + off) % 4
                mm(bank[k][:, 0:128], Jm[:, :], hank[:, h * 2 + off, :], True, True, [Bhank, Bconst], [Bps[k]])
                dve(lambda e, h=h, off=off, k=k: e.tensor_copy(out=BD[:, h, off, 0, :], in_=bank[k][:, 0:128]),
                    [Bps[k]], [Bconst])
                dve(lambda e, h=h, off=off, k=k: e.tensor_tensor(out=BD[:, h, off, 1, :], in0=bank[k][:, 0:128],
                                                                  in1=BD[:, h, off, 0, :], op=ALU.subtract),
                    [Bps[k], Bconst], [Bconst])

        def ln_stats(xt, rows, stats, mv, rstd, Bx, Bs):
            for c in range(2):
                dve(lambda e, c=c: e.bn_stats(out=stats[0:rows, c, :], in_=xt[0:rows, 512 * c:512 * (c + 1)]),
                    [Bx], [Bs])
            dve(lambda e: e.bn_aggr(out=mv[0:rows, :], in_=stats[0:rows, :, :]), [Bs], [Bs])
            act(rstd[0:rows, :], mv[0:rows, 1:2], AF.Ln, [Bs], [Bs], bias=EPS)
            act(rstd[0:rows, :], rstd[0:rows, :], AF.Exp, [Bs], [Bs], scale=-0.5)

        def load_x_tile(seq, j, xt, Bx):
            rows = tsz(j)
            if j == 0:
                ld(xt[0:16, :], meta_d, [Bx])
                ld(xt[16:128, :], x_d[seq, 0:112, :], [Bx])
            else:
                r0 = 128 * j - 16
                ld(xt[0:rows, :], x_d[seq, r0:r0 + rows, :], [Bx])

        def load_w(dst, src_rows_ap, ncol, Bw, krows=8):
            ldc(dst, src_rows_ap.rearrange("(c p) n -> p c n", p=128), [Bw])

        try:
          for seq in range(nseq):
              P.barrier()
              A.reset()
              xbuf = [A.alloc([1024], F32) for _ in range(2)]
              xn = [A.alloc([1024], BF16) for _ in range(2)]
              stats = A.alloc([2, 6], F32)
              mv = A.alloc([2], F32)
              rstd = A.alloc([1], F32)
              wbuf = [A.alloc([8, 512], BF16) for _ in range(2)]
              qaT = A.alloc([4, T], BF16)
              kaT = A.alloc([4, T], BF16)
              va = A.alloc([NT, 512], BF16)
              def sb_scratch(kO, kR):
                  return dict(
                      ek=[A.alloc([512], BF16) for _ in range(3)], Bek=[Buf() for _ in range(3)],
                      sp=[A.alloc([512], BF16) for _ in range(4)], Bsp=[Buf() for _ in range(4)],
                      ec=[A.alloc([512], BF16) for _ in range(2)], Bec=[Buf() for _ in range(2)],
                      a=[A.alloc([512], BF16) for _ in range(3)], Ba=[Buf() for _ in range(3)],
                      acc=A.alloc([512], F32), Rrun=A.alloc([512], F32), Esc=A.alloc([512], F32),
                      tmp=A.alloc([512], F32), Bacc=Buf(), BR=Buf(), BE=Buf(), Btmp=Buf(),
                      kO=kO, kR=kR, c1=[0], c2=[0])

              sbsc = [sb_scratch(4, 5), sb_scratch(6, 7)]
              Bxb = [Buf(), Buf()]
              Bxn = [Buf(), Buf()]
              Bst = Buf()
              Bw = [Buf(), Buf()]
              Bqa = [[Buf() for b in range(5)] for pr in range(4)]
              Bka = [[Buf() for b in range(5)] for pr in range(4)]
              Bva = [Buf() for b in range(5)]

              stats_l = [stats, A.alloc([2, 6], F32)]
              mv_l = [mv, A.alloc([2], F32)]
              rstd_l = [rstd, A.alloc([1], F32)]
              Bst_l = [Bst, Buf()]

              def ln_a(j):
                  load_x_tile(seq, j, xbuf[j % 2], Bxb[j % 2])
                  ln_stats(xbuf[j % 2], tsz(j), stats_l[j % 2], mv_l[j % 2], rstd_l[j % 2], Bxb[j % 2], Bst_l[j % 2])

              def ln_b(j):
                  rows = tsz(j)
                  xt, bx = xbuf[j % 2], Bxb[j % 2]
                  xb, bxn = xn[j % 2], Bxn[j % 2]
                  mv_, rs_ = mv_l[j % 2], rstd_l[j % 2]
                  dve(lambda e: e.tensor_scalar(
                      out=xb[0:rows, :], in0=xt[0:rows, :], scalar1=mv_[0:rows, 0:1], scalar2=rs_[0:rows, 0:1],
                      op0=ALU.subtract, op1=ALU.mult), [bx, Bst_l[j % 2]], [bxn])
                  pk = 6 + (j % 2)
                  Xb = bank[pk].bitcast(BF16)
                  for c in range(8):
                      P.op("pe", lambda e, c=c: e.transpose(
                          out=Xb[:, c * 128:c * 128 + rows], in_=xb[0:rows, c * 128:(c + 1) * 128],
                          identity=ident[0:rows, 0:rows]), reads=[bxn, Bconst], writes=[Bps[pk]])

              def ln_c(j):
                  rows = tsz(j)
                  pk = 6 + (j % 2)
                  Xb = bank[pk].bitcast(BF16)
                  for c in range(8):
                      dve(lambda e, c=c: e.tensor_scalar(
                          out=hT[:, c, 128 * j:128 * j + rows], in0=Xb[:, c * 128:c * 128 + rows],
                          scalar1=cols[:, c:c + 1], scalar2=cols[:, 8 + c:9 + c], op0=ALU.mult, op1=ALU.add),
                          [Bps[pk], Bconst], [Bh[j // 4]])

              for j in range(NT + 2):
                  if j < NT:
                      ln_a(j)
                  if 0 <= j - 1 < NT:
                      ln_b(j - 1)
                  if 0 <= j - 2 < NT:
                      ln_c(j - 2)

              stop_at('ln')
              zrot = [0]

              def fm_project(col0, dstT, Bdst, nun=1, dup64=False):
                  u = zrot[0] % 2
                  wb, bw = wbuf[u], Bw[u]
                  if dup64:
                      ldc(wb[:, :, 0:64], win_d[:, col0:col0 + 64].rearrange("(c p) n -> p c n", p=128), [bw])
                      ldc(wb[:, :, 64:128], win_d[:, col0:col0 + 64].rearrange("(c p) n -> p c n", p=128), [bw])
                      nm = 1
                  else:
                      ldc(wb[:, :, :], win_d[:, col0:col0 + 512].rearrange("(c p) n -> p c n", p=128), [bw])
                      nm = 4
                  zrot[0] += 1
                  for b, (t0, n) in enumerate(TBS):
                      for m in range(nm):
                          k = zrot[0] % 4
                          zrot[0] += 1
                          for c in range(8):
                              mm(bank[k][:, 0:n], wb[:, c, m * 128:(m + 1) * 128], hT[:, c, t0:t0 + n],
                                 c == 0, c == 7, [bw, Bh[b]], [Bps[k]])
                          if dup64:
                              act(dstT[:, t0:t0 + n], bank[k][:, 0:n], AF.Identity, [Bps[k]], [Bdst[b]])
                          else:
                              act(dstT[:, m, t0:t0 + n], bank[k][:, 0:n], AF.Identity, [Bps[k]], [Bdst[m][b]])

              fm_project(C_QA, qaT, Bqa)
              fm_project(C_KA, kaT, Bka)
              u = zrot[0] % 2
              zrot[0] += 1
              wb, bw = wbuf[u], Bw[u]
              ldc(wb[:, :, :], win_d[:, C_VA:C_VA + 512].rearrange("(c p) n -> p c n", p=128), [bw])
              for j in range(NT):
                  rows = tsz(j)
                  k = 4 + (j % 2)
                  for c in range(8):
                      mm(bank[k][0:rows, :], hT[:, c, 128 * j:128 * j + rows], wb[:, c, :], c == 0, c == 7,
                         [bw, Bh[j // 4]], [Bps[k]])
                  act(va[0:rows, j, :], bank[k][0:rows, :], AF.Identity, [Bps[k]], [Bva[j // 4]])

              stop_at('p1a')
              ZB, CB = [0, 1], [2, 3]
              zc, cc = [0], [0]

              def sb_stream(b, pr, sc):
                  t0, n = TBS[b]
                  kO, kR = sc['kO'], sc['kR']
                  acc, Rrun, Esc, tmp = sc['acc'], sc['Rrun'], sc['Esc'], sc['tmp']
                  Bacc, BR, BE, Btmp = sc['Bacc'], sc['BR'], sc['BE'], sc['Btmp']
                  pool(lambda e: e.memset(acc[:, 0:n], 0.0), [], [Bacc])
                  pool(lambda e: e.memset(Rrun[:, 0:n], 0.0), [], [BR])
                  lastS = tiles_of(b)[-1]
                  units = [(S, hh) for S in range(lastS, -1, -1) for hh in range(2)]
                  st = {}

                  def s1(S, hh):
                      pb, h = 64 * hh, 2 * pr + hh
                      ks = tsz(S)
                      c0 = max(0, 128 * (S - 4 * b))
                      k = ZB[zc[0] % 2]
                      zc[0] += 1
                      i1 = sc['c1'][0]
                      sc['c1'][0] += 1
                      ek_t, Bek = sc['ek'][i1 % 3], sc['Bek'][i1 % 3]
                      sp_t, Bsp = sc['sp'][i1 % 4], sc['Bsp'][i1 % 4]
                      mm(bank[k][0:ks, c0:n], kaT[pb:pb + 64, pr, 128 * S:128 * S + ks],
                         qaT[pb:pb + 64, pr, t0 + c0:t0 + n], True, True, [Bka[pr][S // 4], Bqa[pr][b]], [Bps[k]])
                      act(ek_t[0:ks, c0:n], bank[k][0:ks, c0:n], AF.Exp, [Bps[k]], [Bek], scale=0.125)
                      if S >= 4 * b:
                          w = min(128, n - c0)
                          pool(lambda e: e.tensor_tensor(out=ek_t[0:ks, c0:c0 + w], in0=ek_t[0:ks, c0:c0 + w],
                                                         in1=trimask[0:ks, 0:w], op=ALU.mult), [Bek, Bconst], [Bek])
                      act(sp_t[0:ks, c0:n], ek_t[0:ks, c0:n], AF.Ln, [Bek], [Bsp], bias=1.0)
                      st[(S, hh)] = (ks, c0, ek_t, Bek, sp_t, Bsp)

                  def s2(S, hh):
                      ks, c0, ek_t, Bek, sp_t, Bsp = st[(S, hh)]
                      k2 = CB[cc[0] % 2]
                      cc[0] += 1
                      i2 = sc['c2'][0]
                      sc['c2'][0] += 1
                      ec_t, Bec = sc['ec'][i2 % 2], sc['Bec'][i2 % 2]
                      a_t, Ba_t = sc['a'][i2 % 3], sc['Ba'][i2 % 3]
                      mm(bank[k2][0:ks, c0:n], tri8[0:ks, 0:ks], sp_t[0:ks, c0:n], True, True, [Bsp, Bconst], [Bps[k2]])
                      act(ec_t[0:ks, c0:n], bank[k2][0:ks, c0:n], AF.Exp, [Bps[k2]], [Bec], scale=0.125)
                      dve(lambda e: e.tensor_tensor(out=a_t[0:ks, c0:n], in0=ek_t[0:ks, c0:n], in1=ec_t[0:ks, c0:n],
                                                    op=ALU.mult), [Bek, Bec], [Ba_t])
                      st[(S, hh)] = (ks, c0, a_t, Ba_t, sp_t, Bsp)

                  def s3(S, hh):
                      ks, c0, a_t, Ba_t, sp_t, Bsp = st.pop((S, hh))
                      pb, h = 64 * hh, 2 * pr + hh
                      mm(bank[kO][pb:pb + 64, c0:n], va[0:ks, S, h * 64:(h + 1) * 64], a_t[0:ks, c0:n], True, True,
                         [Bva[S // 4], Ba_t], [Bps[kO]])
                      mm(bank[kR][pb:pb + 64, c0:n], ones64[0:ks, :], sp_t[0:ks, c0:n], True, True,
                         [Bsp, Bconst], [Bps[kR]])
                      if hh == 1:
                          act(Esc[:, c0:n], Rrun[:, c0:n], AF.Exp, [BR], [BE], scale=-1.0)
                          dve(lambda e: e.tensor_tensor(out=tmp[:, c0:n], in0=bank[kO][:, c0:n], in1=Esc[:, c0:n],
                                                        op=ALU.mult), [Bps[kO], BE], [Btmp])
                          dve(lambda e: e.tensor_tensor(out=acc[:, c0:n], in0=acc[:, c0:n], in1=tmp[:, c0:n],
                                                        op=ALU.add), [Bacc, Btmp], [Bacc])
                          if S > 0:
                              dve(lambda e: e.tensor_tensor(out=Rrun[:, c0:n], in0=Rrun[:, c0:n],
                                                            in1=bank[kR][:, c0:n], op=ALU.add), [BR, Bps[kR]], [BR])
                          else:
                              pool(lambda e: e.tensor_copy(out=yT[:, 0, pr, t0:t0 + n], in_=acc[:, 0:n]),
                                   [Bacc], [By[0][pr][b]])

                  nu = len(units)
                  for k in range(nu + 2):
                      if k < nu:
                          s1(*units[k])
                      if 0 <= k - 1 < nu:
                          s2(*units[k - 1])
                      if 0 <= k - 2 < nu:
                          s3(*units[k - 2])
                      yield

              def chain_sb(prs, sc):
                  for b in range(len(TBS)):
                      for pr in prs:
                          yield from sb_stream(b, pr, sc)

              interleave([(chain_sb((0, 2), sbsc[0]), 1.0), (chain_sb((1, 3), sbsc[1]), 1.0)])

              stop_at('sb')
              P.barrier()
              A.reset()
              wbuf = [A.alloc([8, 512], BF16) for _ in range(2)]
              Bw = [Buf(), Buf()]
              wtm = A.alloc([8, 136], BF16)
              Bwtm = Buf()
              qbT = A.alloc([4, T], BF16)
              qiT = A.alloc([4, T], BF16)
              kbT = A.alloc([T], BF16)
              kiT = A.alloc([T], BF16)
              vb = A.alloc([NT, 64], BF16)
              wi = A.alloc([NT, 8], F32)
              kin = A.alloc([128], BF16)
              kif = A.alloc([64], F32)
              ksq = A.alloc([64], F32)
              Bksq = Buf()
              stats = A.alloc([1, 6], F32)
              mv = A.alloc([2], F32)
              rstd = A.alloc([1], F32)
              score = A.alloc([2176], F32)
              junk = A.alloc([2176], BF16)
              maskb = [A.alloc([2176], BF16) for _ in range(2)]
              maskT = A.alloc([NT, 512], BF16)
              rl = [A.alloc([512], F32) for _ in range(2)]
              pbuf = [A.alloc([512], BF16) for _ in range(3)]
              pmb = [A.alloc([512], BF16) for _ in range(3)]
              rden = A.alloc([512], F32)
              bis = A.alloc([40], F32)
              Bqb = [[Buf() for b in range(5)] for pr in range(4)]
              Bqi = [[Buf() for b in range(5)] for pr in range(4)]
              Bkb = [Buf() for b in range(5)]
              Bki = [Buf() for b in range(5)]
              Bvb = [Buf() for b in range(5)]
              Bwi = [Buf() for b in range(5)]
              Bkin, Bst, Bscore, Bjunk, Bbis, Brden = Buf(), Buf(), Buf(), Buf(), Buf(), Buf()
              Bmb = [Buf(), Buf()]
              Brl = [Buf(), Buf()]
              Bp = [Buf(), Buf(), Buf()]
              Bpm = [Buf(), Buf(), Buf()]

              zrot[0] = 0
              fm_project(C_QB, qbT, Bqb)
              fm_project(C_QI, qiT, Bqi)
              stop_at('b1')
              fm_project(C_KB, kbT, Bkb, dup64=True)
              stop_at('b2')
              ldc(wtm[:, :, 0:64], win_d[:, C_VB:C_VB + 64].rearrange("(c p) n -> p c n", p=128), [Bwtm])
              ldc(wtm[:, :, 64:136], win_d[:, C_KI:C_KI + 72].rearrange("(c p) n -> p c n", p=128), [Bwtm])
              for j in range(NT):
                  rows = tsz(j)
                  k = 4 + (j % 2)
                  for c in range(8):
                      mm(bank[k][0:rows, 0:136], hT[:, c, 128 * j:128 * j + rows], wtm[:, c, :], c == 0, c == 7,
                         [Bwtm, Bh[j // 4]], [Bps[k]])
                  act(vb[0:rows, j, :], bank[k][0:rows, 0:64], AF.Identity, [Bps[k]], [Bvb[j // 4]])
                  act(wi[0:rows, j, :], bank[k][0:rows, 128:136], AF.Identity, [Bps[k]], [Bwi[j // 4]], scale=WI_SCALE)
                  stop_at('c1')
                  act(kif[0:rows, :], bank[k][0:rows, 64:128], AF.Identity, [Bps[k]], [Bkin])
                  tick()
                  dve(lambda e, rows=rows: e.tensor_reduce(out=mv[0:rows, 0:1], in_=kif[0:rows, :], axis=AX.X, op=ALU.add),
                      [Bkin], [Bst])
                  tick()
                  dve(lambda e, rows=rows: e.tensor_scalar(out=mv[0:rows, 0:1], in0=mv[0:rows, 0:1], scalar1=1.0 / 64,
                                                           scalar2=None, op0=ALU.mult), [Bst], [Bst])
                  tick()
                  dve(lambda e, rows=rows: e.tensor_scalar(out=kif[0:rows, :], in0=kif[0:rows, :], scalar1=mv[0:rows, 0:1],
                                                           scalar2=None, op0=ALU.subtract), [Bkin, Bst], [Bkin])
                  tick()
                  dve(lambda e, rows=rows: e.tensor_tensor(out=ksq[0:rows, :], in0=kif[0:rows, :], in1=kif[0:rows, :],
                                                           op=ALU.mult), [Bkin], [Bksq])
                  tick()
                  dve(lambda e, rows=rows: e.tensor_reduce(out=mv[0:rows, 1:2], in_=ksq[0:rows, :], axis=AX.X, op=ALU.add),
                      [Bksq], [Bst])
                  tick()
                  act(rstd[0:rows, :], mv[0:rows, 1:2], AF.Ln, [Bst], [Bst], bias=EPS, scale=1.0 / 64)
                  tick()
                  act(rstd[0:rows, :], rstd[0:rows, :], AF.Exp, [Bst], [Bst], scale=-0.5)
                  tick()
                  dve(lambda e, rows=rows: e.tensor_scalar(out=kif[0:rows, :], in0=kif[0:rows, :], scalar1=rstd[0:rows, 0:1],
                                                           scalar2=None, op0=ALU.mult), [Bkin, Bst], [Bkin])
                  tick()
                  dve(lambda e, rows=rows: e.tensor_tensor(out=kif[0:rows, :], in0=kif[0:rows, :], in1=ikg[0:rows, :],
                                                           op=ALU.mult), [Bkin, Bconst], [Bkin])
                  tick()
                  dve(lambda e, rows=rows: e.tensor_tensor(out=kin[0:rows, 0:64], in0=kif[0:rows, :], in1=ikb[0:rows, :],
                                                           op=ALU.add), [Bkin, Bconst], [Bkin])
                  tick()
                  dve(lambda e, rows=rows: e.tensor_copy(out=kin[0:rows, 64:128], in_=kin[0:rows, 0:64]), [Bkin], [Bkin])
                  stop_at('c2')
                  pk = 6 + (j % 2)
                  Xb = bank[pk].bitcast(BF16)
                  P.op("pe", lambda e, rows=rows, Xb=Xb: e.transpose(out=Xb[:, 0:rows], in_=kin[0:rows, :],
                                                                      identity=ident[0:rows, 0:rows]),
                       reads=[Bkin, Bconst], writes=[Bps[pk]])
                  act(kiT[:, 128 * j:128 * j + rows], Xb[:, 0:rows], AF.Identity, [Bps[pk]], [Bki[j // 4]])

              stop_at('p1b')
              maskTs = [maskT, A.ap[:, 0:NT * 512].rearrange("p (a b) -> p a b", a=NT)]
              BmT = [[Buf() for _ in range(NT)] for _ in range(2)]
              alias_w = [[], [Bw[0], Bw[1], Bwtm]]
              mzr = [0]
              MB = [2, 3]

              def mask_stream(b):
                  ms = b % 2
                  mT = maskTs[ms]
                  for i in tiles_of(b):
                      rows = tsz(i)
                      L = 128 * i + rows
                      lc = 128 * (i - 4 * b)
                      sblocks = [(s0, min(512, L - s0)) for s0 in range(0, L, 512)]
                      for (s0, sn) in sblocks:
                          for h in range(8):
                              pr, pb = h // 2, 64 * (h % 2)
                              k = MB[mzr[0] % 2]
                              r2 = mzr[0] % 2
                              mzr[0] += 1
                              mm(bank[k][0:rows, 0:sn], qiT[pb:pb + 64, pr, 128 * i:128 * i + rows],
                                 kiT[pb:pb + 64, s0:s0 + sn], True, True,
                                 [Bqi[pr][b]] + [Bki[bb] for bb in range(s0 // 512, min(4, (s0 + sn - 1) // 512) + 1)],
                                 [Bps[k]])
                              act(rl[r2][0:rows, 0:sn], bank[k][0:rows, 0:sn], AF.Relu, [Bps[k]], [Brl[r2]])
                              if h == 0:
                                  dve(lambda e, r2=r2, h=h: e.tensor_scalar(
                                      out=score[0:rows, s0:s0 + sn], in0=rl[r2][0:rows, 0:sn],
                                      scalar1=wi[0:rows, i, h:h + 1], scalar2=None, op0=ALU.mult),
                                      [Brl[r2], Bwi[b]], [Bscore])
                              else:
                                  dve(lambda e, r2=r2, h=h: e.scalar_tensor_tensor(
                                      out=score[0:rows, s0:s0 + sn], in0=rl[r2][0:rows, 0:sn],
                                      scalar=wi[0:rows, i, h:h + 1], in1=score[0:rows, s0:s0 + sn],
                                      op0=ALU.mult, op1=ALU.add), [Brl[r2], Bwi[b], Bscore], [Bscore])
                              yield
                      mb, bmb = maskb[i % 2], Bmb[i % 2]
                      if i >= 2:
                          dve(lambda e: e.reduce_max(out=bis[0:rows, 0:1], in_=score[0:rows, 0:L], axis=AX.X),
                              [Bscore], [Bbis])
                          dve(lambda e: e.tensor_reduce(out=bis[0:rows, 1:2], in_=score[0:rows, 0:L],
                                                        axis=AX.X, op=ALU.min), [Bscore], [Bbis])
                      dve(lambda e: e.tensor_tensor(out=score[0:rows, 128 * i:128 * i + rows],
                                                    in0=score[0:rows, 128 * i:128 * i + rows],
                                                    in1=negmask[0:rows, 0:rows], op=ALU.add),
                          [Bscore, Bconst], [Bscore])
                      if i < 2:
                          dve(lambda e: e.tensor_scalar(out=mb[0:rows, 0:L], in0=score[0:rows, 0:L],
                                                        scalar1=-1e29, scalar2=None, op0=ALU.is_ge),
                              [Bscore], [bmb])
                      else:
                          dve(lambda e: e.tensor_tensor(out=bis[0:rows, 2:3], in0=bis[0:rows, 0:1],
                                                        in1=bis[0:rows, 1:2], op=ALU.subtract), [Bbis], [Bbis])
                          dve(lambda e: e.tensor_scalar(out=bis[0:rows, 8:8 + NIT + 1], in0=pw[0:rows, 0:NIT + 1],
                                                        scalar1=bis[0:rows, 2:3], scalar2=None, op0=ALU.mult),
                              [Bbis, Bconst], [Bbis])
                          dve(lambda e: e.tensor_tensor(out=bis[0:rows, 3:4], in0=bis[0:rows, 1:2],
                                                        in1=bis[0:rows, 9:10], op=ALU.add), [Bbis], [Bbis])
                          for it in range(1, NIT + 1):
                              dve(lambda e: e.tensor_scalar(
                                  out=junk[0:rows, 0:L], in0=score[0:rows, 0:L], scalar1=bis[0:rows, 3:4], scalar2=0.0,
                                  op0=ALU.is_ge, op1=ALU.add, accum_out=bis[0:rows, 4:5]), [Bscore, Bbis], [Bjunk, Bbis])
                              if it < NIT:
                                  dve(lambda e: e.tensor_scalar(
                                      out=bis[0:rows, 5:6], in0=bis[0:rows, 4:5], scalar1=TOPK - 0.5, scalar2=0.5,
                                      op0=ALU.is_ge, op1=ALU.subtract), [Bbis], [Bbis])
                                  dve(lambda e, it=it: e.scalar_tensor_tensor(
                                      out=bis[0:rows, 3:4], in0=bis[0:rows, 5:6], scalar=bis[0:rows, 8 + it:9 + it],
                                      in1=bis[0:rows, 3:4], op0=ALU.mult, op1=ALU.add), [Bbis], [Bbis])
                              else:
                                  dve(lambda e: e.tensor_scalar(
                                      out=bis[0:rows, 5:6], in0=bis[0:rows, 4:5], scalar1=TOPK - 0.5, scalar2=1.0,
                                      op0=ALU.is_ge, op1=ALU.subtract), [Bbis], [Bbis])
                                  dve(lambda e, it=it: e.scalar_tensor_tensor(
                                      out=bis[0:rows, 6:7], in0=bis[0:rows, 5:6], scalar=bis[0:rows, 8 + it:9 + it],
                                      in1=bis[0:rows, 3:4], op0=ALU.mult, op1=ALU.add), [Bbis], [Bbis])
                              yield
                          dve(lambda e: e.tensor_scalar(out=mb[0:rows, 0:L], in0=score[0:rows, 0:L],
                                                        scalar1=bis[0:rows, 6:7], scalar2=None,
                                                        op0=ALU.is_ge), [Bscore, Bbis], [bmb])
                      for S in range(i + 1):
                          ks = tsz(S)
                          pk = MB[mzr[0] % 2]
                          mzr[0] += 1
                          Xb = bank[pk].bitcast(BF16)
                          P.op("pe", lambda e, ks=ks, S=S, Xb=Xb: e.transpose(
                              out=Xb[0:ks, 0:rows], in_=mb[0:rows, 128 * S:128 * S + ks], identity=ident[0:rows, 0:rows]),
                              reads=[bmb, Bconst], writes=[Bps[pk]])
                          act(mT[0:ks, S, lc:lc + rows], Xb[0:ks, 0:rows], AF.Identity, [Bps[pk]],
                              [BmT[ms][S]] + alias_w[ms])
                          if S % 2 == 1:
                              yield
                      yield

              AZ = [0, 1, 4, 5]
              azr = [0]

              def attn_stream(b):
                  t0, n = TBS[b]
                  ms = b % 2
                  mT = maskTs[ms]
                  lastS = tiles_of(b)[-1]
                  for pr in range(4):
                      for hh in range(2):
                          pb = 64 * hh
                          h = 2 * pr + hh

                          def qk(S):
                              ks = tsz(S)
                              c0 = max(0, 128 * (S - 4 * b))
                              k = AZ[azr[0] % len(AZ)]
                              azr[0] += 1
                              near = [i for i in (S, S + 1) if i in tiles_of(b)]
                              mm(bank[k][0:ks, c0:n], kbT[pb:pb + 64, 128 * S:128 * S + ks],
                                 qbT[pb:pb + 64, pr, t0 + c0:t0 + n], True, len(near) == 0,
                                 [Bkb[S // 4], Bqb[pr][b]], [Bps[k]])
                              for ni, i in enumerate(near):
                                  lc = 128 * (i - 4 * b)
                                  w = tsz(i)
                                  off = i - S
                                  for hl in range(2):
                                      mm(bank[k][0:ks, lc:lc + w], ident[0:ks, 0:ks], BD[0:ks, h, off, hl, 0:w], False,
                                         (ni == len(near) - 1) and hl == 1, [Bconst], [Bps[k]])
                              return k

                          kq = {}

                          def ensure(S):
                              if S <= lastS and S not in kq:
                                  kq[S] = qk(S)

                          ensure(0)
                          ensure(1)
                          for S in range(lastS + 1):
                              ensure(S + 2)
                              kcur = kq[S]
                              ks = tsz(S)
                              c0 = max(0, 128 * (S - 4 * b))
                              u2 = S % 3
                              act(pbuf[u2][0:ks, c0:n], bank[kcur][0:ks, c0:n], AF.Exp, [Bps[kcur]], [Bp[u2]],
                                  bias=b31[0:ks, h:h + 1], scale=0.125)
                              pool(lambda e, u2=u2, ks=ks, c0=c0, S=S: e.tensor_tensor(
                                  out=pmb[u2][0:ks, c0:n], in0=pbuf[u2][0:ks, c0:n], in1=mT[0:ks, S, c0:n],
                                  op=ALU.mult), [Bp[u2], BmT[ms][S]], [Bpm[u2]])
                              mm(bank[6][pb:pb + 64, c0:n], vb[0:ks, S, :], pmb[u2][0:ks, c0:n], S == 0, S == lastS,
                                 [Bvb[S // 4], Bpm[u2]], [Bps[6]])
                              mm(bank[7][pb:pb + 64, c0:n], ones64[0:ks, :], pmb[u2][0:ks, c0:n], S == 0, S == lastS,
                                 [Bconst, Bpm[u2]], [Bps[7]])
                              yield
                      dve(lambda e: e.reciprocal(out=rden[:, 0:n], in_=bank[7][:, 0:n]), [Bps[7]], [Brden])
                      dve(lambda e, pr=pr: e.tensor_tensor(out=yT[:, 1, pr, t0:t0 + n], in0=bank[6][:, 0:n],
                                                           in1=rden[:, 0:n], op=ALU.mult),
                          [Bps[6], Brden], [By[1][pr][b]])
                      yield

              def est_mask(b):
                  tot = 0
                  for i in tiles_of(b):
                      L = 128 * i + tsz(i)
                      tot += 8 * ((L + 511) // 512) + (NIT if i >= 2 else 0) + (i + 1) // 2 + 1
                  return float(tot)

              def est_attn(b):
                  return float(8 * (tiles_of(b)[-1] + 1) + 4)

              interleave([(mask_stream(0), 1.0)])
              for b in range(len(TBS)):
                  gl = [(attn_stream(b), est_attn(b))]
                  if b + 1 < len(TBS):
                      gl.append((mask_stream(b + 1), est_mask(b + 1)))
                  interleave(gl)

              stop_at('dsa')
              P.barrier()
              A.reset()
              wzo = A.alloc([8, 1024], BF16)
              Bwzo = Buf()
              wst = [A.alloc([24, 128], BF16) for _ in range(3)]
              Bwst = [Buf(), Buf(), Buf()]
              mT = A.alloc([8, T], BF16)
              BmTt = [[Buf() for b in range(5)] for m in range(8)]
              gb_in_g = A.alloc([1024], F32)
              gb_in_b = A.alloc([1024], F32)
              gb_g = A.alloc([1024], F32)
              gb_b = A.alloc([1024], F32)
              Bbc = Buf()
              xbuf = [A.alloc([1024], F32) for _ in range(2)]
              Bxb = [Buf(), Buf()]
              hres = A.alloc([1024], F32)
              Bhres = Buf()
              rr = [A.alloc([1024], F32) for _ in range(2)]
              Brr = [Buf(), Buf()]
              stats = A.alloc([2, 6], F32)
              mv = A.alloc([2], F32)
              rstd = A.alloc([1], F32)
              Bst = Buf()
              stats2 = A.alloc([2, 6], F32)
              mv2 = A.alloc([2], F32)
              rstd2 = A.alloc([1], F32)
              Bst2 = Buf()
              ef = [A.alloc([512], F32) for _ in range(2)]
              Bef = [Buf(), Buf()]
              sg = [A.alloc([512], F32) for _ in range(3)]
              Bsg = [Buf(), Buf(), Buf()]
              m1 = [A.alloc([512], F32) for _ in range(2)]
              Bm1 = [Buf(), Buf()]
              ld(gb_in_g, lning_d.partition_broadcast(128), [Bbc])
              ld(gb_in_b, lninb_d.partition_broadcast(128), [Bbc])
              ld(gb_g, lng_d.partition_broadcast(128), [Bbc])
              ld(gb_b, lnb_d.partition_broadcast(128), [Bbc])

              def sigmoid_from_psum(k, n, bias, si):
                  eb = zr[0] % 2
                  act(ef[eb][:, 0:n], bank[k][:, 0:n], AF.Exp, [Bps[k]], [Bef[eb]], bias=bias, scale=-1.0)
                  act(ef[eb][:, 0:n], ef[eb][:, 0:n], AF.Ln, [Bef[eb]], [Bef[eb]], bias=1.0)
                  act(sg[si][:, 0:n], ef[eb][:, 0:n], AF.Exp, [Bef[eb]], [Bsg[si]], scale=-1.0)

              def load_ws(m):
                  ws, bws = wst[m % 3], Bwst[m % 3]
                  ldc(ws[:, 0:8, :], win_d[:, C_GA + 128 * m:C_GA + 128 * (m + 1)].rearrange("(c p) n -> p c n", p=128), [bws])
                  ldc(ws[:, 8:16, :], win_d[:, C_GB + 128 * m:C_GB + 128 * (m + 1)].rearrange("(c p) n -> p c n", p=128), [bws])
                  ldc(ws[:, 16:20, :], wpa_d[:, 128 * m:128 * (m + 1)].rearrange("(c p) n -> p c n", p=128), [bws])
                  ldc(ws[:, 20:24, :], wpb_d[:, 128 * m:128 * (m + 1)].rearrange("(c p) n -> p c n", p=128), [bws])

              ldc(wzo[:, :, 0:512], win_d[:, C_ZA:C_ZA + 512].rearrange("(c p) n -> p c n", p=128), [Bwzo])
              ldc(wzo[:, :, 512:1024], win_d[:, C_ZB:C_ZB + 512].rearrange("(c p) n -> p c n", p=128), [Bwzo])
              load_ws(0)
              load_ws(1)
              zr = [0]
              for b, (t0, n) in enumerate(TBS):
                  for m in range(8):
                      br, pr = m // 4, m % 4
                      k = zr[0] % 6
                      si = zr[0] % 3
                      zr[0] += 1
                      for c in range(8):
                          mm(bank[k][:, 0:n], wzo[:, c, m * 128:(m + 1) * 128], hT[:, c, t0:t0 + n], c == 0, c == 7,
                             [Bwzo, Bh[b]], [Bps[k]])
                      sigmoid_from_psum(k, n, 0.0, si)
                      dve(lambda e, k=k, n=n, si=si: e.tensor_tensor(out=sg[si][:, 0:n], in0=bank[k][:, 0:n],
                                                                     in1=sg[si][:, 0:n], op=ALU.mult),
                          [Bps[k], Bsg[si]], [Bsg[si]])
                      pool(lambda e, n=n, si=si, br=br, pr=pr, t0=t0: e.tensor_tensor(
                          out=yT[:, br, pr, t0:t0 + n], in0=yT[:, br, pr, t0:t0 + n], in1=sg[si][:, 0:n], op=ALU.mult),
                          [By[br][pr][b], Bsg[si]], [By[br][pr][b]])
              stop_at('3i')
              for m in range(8):
                  u = m % 3
                  ws, bws = wst[u], Bwst[u]
                  for b, (t0, n) in enumerate(TBS):
                      res = []
                      for br in range(2):
                          k = zr[0] % 6
                          si = zr[0] % 3
                          zr[0] += 1
                          for c in range(8):
                              mm(bank[k][:, 0:n], ws[:, 8 * br + c, :], hT[:, c, t0:t0 + n], c == 0, c == 7,
                                 [bws, Bh[b]], [Bps[k]])
                          sigmoid_from_psum(k, n, negbg[:, 8 * br + m:8 * br + m + 1], si)
                          k2 = zr[0] % 6
                          zr[0] += 1
                          for c in range(4):
                              mm(bank[k2][:, 0:n], ws[:, 16 + 4 * br + c, :], yT[:, br, c, t0:t0 + n], c == 0, c == 3,
                                 [bws, By[br][c][b]], [Bps[k2]])
                          dve(lambda e, k2=k2, n=n, si=si, br=br: e.tensor_tensor(
                              out=m1[br][:, 0:n], in0=bank[k2][:, 0:n], in1=sg[si][:, 0:n], op=ALU.mult),
                              [Bps[k2], Bsg[si]], [Bm1[br]])
                      pool(lambda e, n=n, m=m, t0=t0: e.tensor_tensor(out=mT[:, m, t0:t0 + n], in0=m1[0][:, 0:n],
                                                                      in1=m1[1][:, 0:n], op=ALU.add),
                           [Bm1[0], Bm1[1]], [BmTt[m][b]])
                  if m == 0:
                      ldc(wzo[:, :, :], wo_d.rearrange("(c p) n -> p c n", p=128), [Bwzo])
                  if m + 2 < 8:
                      load_ws(m + 2)
              stop_at('3ii')
              for j in range(NT):
                  rows = tsz(j)
                  b = j // 4
                  xt, bx = xbuf[j % 2], Bxb[j % 2]
                  load_x_tile(seq, j, xt, bx)
                  ln_stats(xt, rows, stats, mv, rstd, bx, Bst)
                  dve(lambda e, xt=xt, rows=rows: e.tensor_scalar(
                      out=hres[0:rows, :], in0=xt[0:rows, :], scalar1=mv[0:rows, 0:1], scalar2=rstd[0:rows, 0:1],
                      op0=ALU.subtract, op1=ALU.mult), [bx, Bst], [Bhres])
                  pool(lambda e, rows=rows: e.tensor_tensor(out=hres[0:rows, :], in0=hres[0:rows, :], in1=gb_in_g[0:rows, :],
                                                            op=ALU.mult), [Bhres, Bbc], [Bhres])
                  pool(lambda e, rows=rows: e.tensor_tensor(out=hres[0:rows, :], in0=hres[0:rows, :], in1=gb_in_b[0:rows, :],
                                                            op=ALU.add), [Bhres, Bbc], [Bhres])
                  k = 6 if j % 2 == 0 else 4
                  for half in range(2):
                      for m in range(8):
                          mm(bank[k + half][0:rows, :], mT[:, m, 128 * j:128 * j + rows], wzo[:, m, 512 * half:512 * (half + 1)],
                             m == 0, m == 7, [BmTt[m][b], Bwzo], [Bps[k + half]])
                  r_, br_ = rr[j % 2], Brr[j % 2]
                  dve(lambda e, rows=rows, k=k, r_=r_: e.scalar_tensor_tensor(
                      out=r_[0:rows, :], in0=hres[0:rows, :], scalar=ALPHA, in1=ps[0:rows, 512 * k:512 * k + 1024],
                      op0=ALU.mult, op1=ALU.add), [Bhres, Bps[k], Bps[k + 1]], [br_])
                  ln_stats(r_, rows, stats2, mv2, rstd2, br_, Bst2)
                  dve(lambda e, rows=rows, r_=r_: e.tensor_scalar(
                      out=r_[0:rows, :], in0=r_[0:rows, :], scalar1=mv2[0:rows, 0:1], scalar2=rstd2[0:rows, 0:1],
                      op0=ALU.subtract, op1=ALU.mult), [br_, Bst2], [br_])
                  pool(lambda e, rows=rows, r_=r_: e.tensor_tensor(out=r_[0:rows, :], in0=r_[0:rows, :], in1=gb_g[0:rows, :],
                                                                    op=ALU.mult), [br_, Bbc], [br_])
                  pool(lambda e, rows=rows, r_=r_: e.tensor_tensor(out=r_[0:rows, :], in0=r_[0:rows, :], in1=gb_b[0:rows, :],
                                                                    op=ALU.add), [br_, Bbc], [br_])
                  if j == 0:
                      P.dma("sp", lambda e, r_=r_, seq=seq: e.dma_start(out=out_d[seq, 0:112, :], in_=r_[16:128, :]),
                            reads=[br_], writes=[])
                  else:
                      r0 = 128 * j - 16
                      P.dma("sp", lambda e, r_=r_, seq=seq, r0=r0, rows=rows: e.dma_start(
                          out=out_d[seq, r0:r0 + rows, :], in_=r_[0:rows, :]), reads=[br_], writes=[])
        except _Stop:
            pass
        if os.environ.get('K_VERBOSE'):
            print('OPCOUNTS', P.cnt, P.ndma, {e: len(P.ops[e]) for e in P.ENGS})
        P.emit(nc)
    return nc


def rel_bucket_np(d):
    d = np.asarray(d)
    nf = np.maximum(d, 1).astype(np.float32)
    large = 16 + (np.log(nf / np.float32(16)) / np.float32(math.log(128 / 16)) * np.float32(16)).astype(np.int32)
    large = np.minimum(large, 31)
    return np.where(d < 16, d, large)


def host_constants():
    cm = np.zeros((32, 383), np.float32)
    for k in range(383):
        d = k - 127
        if d >= 128:
            continue
        bkt = int(rel_bucket_np(max(d, 0)))
        cm[bkt, k] += 8.0
        cm[31, k] -= 8.0
    p = np.arange(128)
    c128 = np.zeros((128, 5, 128), np.float32)
    c128[:, 0, :] = np.eye(128, dtype=np.float32)
    c128[:, 1, :] = (p[:, None] < p[None, :]).astype(np.float32)
    c128[:, 2, :] = (p[:, None] >= p[None, :]).astype(np.float32)
    c128[:, 3, :] = (p[:, None] + p[None, :] == 127).astype(np.float32)
    c128[:, 4, :] = np.where(p[None, :] <= p[:, None], 0.0, -1e30).astype(np.float32)
    pw = (2.0 ** -np.arange(32)).astype(np.float32)
    return cm, c128, pw


_CACHE = {}


def kernel(x, meta_tokens, ln_in_g, ln_in_b, rel_bias, w_in, b_gate, idx_kn_g, idx_kn_b,
           w_pa, w_pb, w_o, ln_g, ln_b, _ncores=8, _nseq=4):
    f = lambda a: np.ascontiguousarray(np.asarray(a, dtype=np.float32))
    x = f(x)
    ncores, nseq = _ncores, _nseq
    key = (nseq,)
    if key not in _CACHE:
        _CACHE[key] = build_program(nseq)
    nc = _CACHE[key]
    cm, c128, pw = host_constants()
    cols = np.zeros((128, 48), np.float32)
    cols[:, 0:8] = f(ln_in_g).reshape(8, 128).T
    cols[:, 8:16] = f(ln_in_b).reshape(8, 128).T
    cols[:, 16:32] = f(b_gate).reshape(16, 128).T
    shared = {
        "meta": f(meta_tokens), "w_in": f(w_in)[0], "w_pa": f(w_pa)[0], "w_pb": f(w_pb)[0], "w_o": f(w_o)[0],
        "rel_bias": f(rel_bias), "ln_in_g": f(ln_in_g), "ln_in_b": f(ln_in_b), "ln_g": f(ln_g)[0], "ln_b": f(ln_b)[0],
        "cols": cols, "ikn_g": f(idx_kn_g)[0], "ikn_b": f(idx_kn_b)[0], "cmat": cm, "c128": c128, "pw": pw,
    }
    in_maps = []
    for c in range(ncores):
        d = dict(shared)
        d["x"] = np.ascontiguousarray(x[c * nseq:(c + 1) * nseq])
        in_maps.append(d)
    res = run_bass_kernel_spmd(nc, in_maps, core_ids=list(range(ncores)))
    return np.concatenate([np.asarray(r["out"]) for r in res.results], axis=0).astype(np.float32)
```

```python
import contextlib
import math
import os
import numpy as np
import concourse.bass as bass
import concourse.mybir as mybir
from concourse.bass_utils import run_bass_kernel_spmd

F32 = mybir.dt.float32
BF16 = mybir.dt.bfloat16
ALU = mybir.AluOpType
AF = mybir.ActivationFunctionType
AX = mybir.AxisListType

T = 2064
NT = 17
D = 1024
SEQ = 2048
NMETA = 16
TOPK = 256
EPS = 1e-5
ALPHA = 2.0 ** 0.25
WI_SCALE = (8 ** -0.5) * (64 ** -0.5)
NIT = 22
TBS = [(0, 512), (512, 512), (1024, 512), (1536, 512), (2048, 16)]
C_QA, C_KA, C_VA, C_ZA, C_QB, C_KB, C_VB, C_ZB, C_QI, C_KI, C_WI, C_GA, C_GB = (
    0, 512, 1024, 1536, 2048, 2560, 2624, 2688, 3200, 3712, 3776, 3784, 4808)
INCOLS = 5832


def tsz(j):
    return 128 if j < 16 else 16


def tiles_of(b):
    return list(range(4 * b, min(4 * b + 4, NT)))


class Buf:
    __slots__ = ("name", "w", "r")

    def __init__(self, name=""):
        self.name = name
        self.w = None
        self.r = []


class _Rec:
    def __init__(self):
        self.call = None

    def __getattr__(self, name):
        def f(*a, **kw):
            self.call = (name, a, kw)
            return self
        return f


def _freeze(fn):
    rec = _Rec()
    fn(rec)
    name, a, kw = rec.call
    return lambda e: getattr(e, name)(*a, **kw)


class Prog:
    ENGS = ("pe", "act", "dve", "pool", "sp")

    def __init__(self):
        self.ops = {e: [] for e in self.ENGS}
        self.cnt = {e: 0 for e in self.ENGS}
        self.pending = {e: set() for e in self.ENGS}
        self.ndma = 0
        self.ndma_e = {}
        self.dma_sem_use = {}
        self.NDMASEM = 24

    def _deps(self, eng, reads, writes):
        deps = set(self.pending[eng])
        self.pending[eng] = set()
        for b in reads:
            if b.w is not None:
                deps.add(b.w)
        for b in writes:
            if b.w is not None:
                deps.add(b.w)
            deps.update(b.r)
        if eng == "pe":
            deps = {d for d in deps if d[0] != "pe"}
        return deps

    def _mark(self, me, reads, writes):
        for b in reads:
            b.r.append(me)
            if len(b.r) > 64:
                best = {}
                for (k, v) in b.r:
                    if best.get(k, 0) < v:
                        best[k] = v
                b.r = list(best.items())
        for b in writes:
            b.w = me
            b.r = []

    def op(self, eng, fn, reads=(), writes=()):
        deps = self._deps(eng, reads, writes)
        self.cnt[eng] += 1
        me = (eng, self.cnt[eng])
        self.ops[eng].append(("op", _freeze(fn), deps, None))
        self._mark(me, reads, writes)
        return me

    def dma(self, eng, fn, reads=(), writes=()):
        deps = self._deps(eng, reads, writes)
        nd = self.ndma_e.get(eng, 0)
        self.ndma_e[eng] = nd + 1
        self.ndma += 1
        k = (eng, nd % self.NDMASEM)
        prev = self.dma_sem_use.get(k, 0)
        if prev:
            deps.add((("dma", k), 16 * prev))
        self.dma_sem_use[k] = prev + 1
        me = (("dma", k), 16 * (prev + 1))
        self.ops[eng].append(("dma", _freeze(fn), deps, k))
        self._mark(me, reads, writes)
        return me

    def barrier(self):
        snap = set()
        for e in self.ENGS:
            if self.cnt[e]:
                snap.add((e, self.cnt[e]))
        for k, c in self.dma_sem_use.items():
            snap.add((("dma", k), 16 * c))
        self.pending["act"] |= snap
        scr = self.bar_scratch
        me = self.op("act", lambda e: e.activation(out=scr, in_=scr, func=AF.Identity))
        for e in self.ENGS:
            if e != "act":
                self.pending[e].add(me)

    def emit(self, nc):
        with contextlib.ExitStack() as st:
            sems = {}
            for e in self.ENGS:
                sems[e] = st.enter_context(nc.semaphore("s_" + e))
            for k in self.dma_sem_use:
                sems[("dma", k)] = st.enter_context(nc.semaphore("s_dma_%s%d" % k))
            block = st.enter_context(nc.Block())
            regs = {"pe": block.tensor, "act": block.scalar, "dve": block.vector,
                    "pool": block.gpsimd, "sp": block.sync}
            for e in self.ENGS:
                ops = self.ops[e]
                if not ops and e != "sp":
                    continue

                def body(engine, ops=ops, e=e):
                    waited = {}
                    for kind, fn, deps, k in ops:
                        best = {}
                        for (sk, val) in deps:
                            if best.get(sk, 0) < val:
                                best[sk] = val
                        for sk in sorted(best, key=str):
                            val = best[sk]
                            if waited.get(sk, 0) >= val:
                                continue
                            engine.wait_ge(sems[sk], val)
                            waited[sk] = val
                        ins = fn(engine)
                        if kind == "op":
                            ins.then_inc(sems[e], 1)
                        else:
                            ins.then_inc(sems[("dma", k)], 16)
                    if e == "sp":
                        for k2, c2 in self.dma_sem_use.items():
                            engine.wait_ge(sems[("dma", k2)], 16 * c2)
                        for e2 in self.ENGS:
                            if e2 != "sp" and self.cnt[e2]:
                                engine.wait_ge(sems[e2], self.cnt[e2])
                regs[e](body)


class _Stop(Exception):
    pass


_TICK = [0]


def tick():
    _TICK[0] += 1
    stop_at('d%d' % _TICK[0])


def stop_at(name):
    if os.environ.get("K_STOP", "") == name:
        raise _Stop()


def interleave(gens):
    st = [[g, est, 0] for g, est in gens]
    while st:
        st.sort(key=lambda x: x[2] / x[1])
        g = st[0]
        try:
            next(g[0])
            g[2] += 1
        except StopIteration:
            st.pop(0)


class Arena:
    def __init__(self, ap, nelem):
        self.ap = ap
        self.n = nelem
        self.off = 0

    def reset(self):
        self.off = 0

    def alloc(self, shape, dt):
        n = 1
        for s in shape:
            n *= s
        ne = n * (2 if dt == F32 else 1)
        ne_al = (ne + 15) // 16 * 16
        assert self.off + ne_al <= self.n, ("arena overflow", self.off, ne_al, self.n)
        v = self.ap[:, self.off:self.off + ne]
        self.off += ne_al
        if dt == F32:
            v = v.bitcast(F32)
        if len(shape) == 2:
            v = v.rearrange("p (a b) -> p a b", a=shape[0])
        elif len(shape) == 3:
            v = v.rearrange("p (a b c) -> p a b c", a=shape[0], b=shape[1])
        return v


def build_program(nseq):
    nc = bass.Bass("TRN2", target_bir_lowering=False)
    dram = lambda name, shape, kind="ExternalInput": nc.dram_tensor(name, shape, F32, kind=kind).ap()
    x_d = dram("x", [nseq, SEQ, D])
    meta_d = dram("meta", [NMETA, D])
    win_d = dram("w_in", [D, INCOLS])
    wpa_d = dram("w_pa", [512, D])
    wpb_d = dram("w_pb", [512, D])
    wo_d = dram("w_o", [D, D])
    relb_d = dram("rel_bias", [32, 8])
    lning_d = dram("ln_in_g", [D])
    lninb_d = dram("ln_in_b", [D])
    lng_d = dram("ln_g", [D])
    lnb_d = dram("ln_b", [D])
    cols_d = dram("cols", [128, 48])
    ikg_d = dram("ikn_g", [64])
    ikb_d = dram("ikn_b", [64])
    cmat_d = dram("cmat", [32, 383])
    c128_d = dram("c128", [128, 5, 128])
    pw_d = dram("pw", [32])
    out_d = dram("out", [nseq, SEQ, D], kind="ExternalOutput")
    gscr_d = nc.dram_tensor("gscr", [8, 383], F32, kind="Internal").ap()

    P = Prog()
    st = contextlib.ExitStack()
    with st:
        sbt = lambda name, shape, dt: st.enter_context(nc.sbuf_tensor("sb_" + name, shape, dt))
        hT = sbt("hT", [128, 8, T], BF16)
        yT = sbt("yT", [128, 2, 4, T], BF16)
        ident = sbt("ident", [128, 128], BF16)
        identf = sbt("identf", [128, 128], F32)
        trimask = sbt("trimask", [128, 128], BF16)
        tri8 = sbt("tri8", [128, 128], BF16)
        ones8 = sbt("ones8", [128, 128], BF16)
        ones64 = sbt("ones64", [128, 64], BF16)
        negmask = sbt("negmask", [128, 128], F32)
        Jm = sbt("Jm", [128, 128], F32)
        c128f = sbt("c128f", [128, 5, 128], F32)
        BD = sbt("BD", [128, 8, 2, 2, 128], BF16)
        cols = sbt("cols", [128, 48], F32)
        negbg = sbt("negbg", [128, 16], F32)
        b31 = sbt("b31", [128, 8], F32)
        pw = sbt("pw", [128, 32], F32)
        ikg = sbt("ikg", [128, 64], F32)
        ikb = sbt("ikb", [128, 64], F32)
        relb = sbt("relb", [32, 8], F32)
        cmat = sbt("cmat", [32, 383], F32)
        gvec = sbt("gvec", [8, 383], F32)
        ARENA_N = 63400
        arena_t = sbt("arena", [128, ARENA_N], BF16)
        A = Arena(arena_t, ARENA_N)
        ps = st.enter_context(nc.psum_tensor("ps", [128, 4096], F32))
        bank = [ps[:, 512 * k:512 * (k + 1)] for k in range(8)]
        Bps = [Buf("ps%d" % k) for k in range(8)]
        Bconst = Buf("const")
        Bh = [Buf("h%d" % b) for b in range(5)]
        By = [[[Buf() for b in range(5)] for pr in range(4)] for br in range(2)]

        def mm(out, lhsT, rhs, start, stop, reads, writes):
            return P.op("pe", lambda e: e.matmul(out, lhsT=lhsT, rhs=rhs, start=start, stop=stop),
                        reads=reads, writes=writes)

        def act(out, in_, func, reads, writes, bias=0.0, scale=1.0):
            return P.op("act", lambda e: e.activation(out=out, in_=in_, func=func, bias=bias, scale=scale),
                        reads=reads, writes=writes)

        def dve(fn, reads, writes):
            return P.op("dve", fn, reads=reads, writes=writes)

        def pool(fn, reads, writes):
            return P.op("pool", fn, reads=reads, writes=writes)

        def ld(out, in_, writes, reads=()):
            return P.dma("sp", lambda e: e.dma_start(out=out, in_=in_), reads=reads, writes=writes)

        def ldc(out, in_, writes, reads=()):
            return P.dma("pool", lambda e: e.dma_start(out=out, in_=in_), reads=reads, writes=writes)

        P.bar_scratch = cols[0:1, 40:48]
        ld(c128f[:], c128_d, [Bconst])
        ld(cols[:], cols_d, [Bconst])
        ld(relb[:], relb_d, [Bconst])
        ld(cmat[:], cmat_d, [Bconst])
        ld(b31[:], relb_d[31, :].partition_broadcast(128), [Bconst])
        ld(pw[:], pw_d.partition_broadcast(128), [Bconst])
        ld(ikg[:], ikg_d.partition_broadcast(128), [Bconst])
        ld(ikb[:], ikb_d.partition_broadcast(128), [Bconst])
        dve(lambda e: e.tensor_copy(out=ident[:], in_=c128f[:, 0, :]), [Bconst], [Bconst])
        dve(lambda e: e.tensor_copy(out=identf[:], in_=c128f[:, 0, :]), [Bconst], [Bconst])
        dve(lambda e: e.tensor_copy(out=trimask[:], in_=c128f[:, 1, :]), [Bconst], [Bconst])
        dve(lambda e: e.tensor_scalar(out=tri8[:], in0=c128f[:, 2, :], scalar1=-8.0, scalar2=None, op0=ALU.mult),
            [Bconst], [Bconst])
        dve(lambda e: e.tensor_copy(out=Jm[:], in_=c128f[:, 3, :]), [Bconst], [Bconst])
        dve(lambda e: e.tensor_copy(out=negmask[:], in_=c128f[:, 4, :]), [Bconst], [Bconst])
        dve(lambda e: e.memset(ones8[:], -8.0), [], [Bconst])
        dve(lambda e: e.memset(ones64[:], 1.0), [], [Bconst])
        dve(lambda e: e.tensor_scalar(out=negbg[:], in0=cols[:, 16:32], scalar1=-1.0, scalar2=None, op0=ALU.mult),
            [Bconst], [Bconst])
        Bg = Buf("gscr")
        mm(bank[0][0:8, 0:383], relb[:, :], cmat[:, :], True, True, [Bconst], [Bps[0]])
        dve(lambda e: e.tensor_copy(out=gvec[:], in_=bank[0][0:8, 0:383]), [Bps[0]], [Bconst])
        ld(gscr_d, gvec[:], [Bg], reads=[Bconst])
        A.reset()
        hank = A.alloc([16, 128], F32)
        Bhank = Buf("hank")
        for h in range(8):
            for off in range(2):
                src = bass.AP(tensor=gscr_d.tensor, offset=h * 383 + 128 * off, ap=[[1, 128], [1, 128]])
                ld(hank[:, h * 2 + off, :], src, [Bhank], reads=[Bg])
        for h in range(8):
            for off in range(2):
                k = (h * 2 + off) % 4
                mm(bank[k][:, 0:128], Jm[:, :], hank[:, h * 2 + off, :], True, True, [Bhank, Bconst], [Bps[k]])
                dve(lambda e, h=h, off=off, k=k: e.tensor_copy(out=BD[:, h, off, 0, :], in_=bank[k][:, 0:128]),
                    [Bps[k]], [Bconst])
                dve(lambda e, h=h, off=off, k=k: e.tensor_tensor(out=BD[:, h, off, 1, :], in0=bank[k][:, 0:128],
                                                                  in1=BD[:, h, off, 0, :], op=ALU.subtract),
                    [Bps[k], Bconst], [Bconst])

        def ln_stats(xt, rows, stats, mv, rstd, Bx, Bs):
            for c in range(2):
                dve(lambda e, c=c: e.bn_stats(out=stats[0:rows, c, :], in_=xt[0:rows, 512 * c:512 * (c + 1)]),
                    [Bx], [Bs])
            dve(lambda e: e.bn_aggr(out=mv[0:rows, :], in_=stats[0:rows, :, :]), [Bs], [Bs])
            act(rstd[0:rows, :], mv[0:rows, 1:2], AF.Ln, [Bs], [Bs], bias=EPS)
            act(rstd[0:rows, :], rstd[0:rows, :], AF.Exp, [Bs], [Bs], scale=-0.5)

        def load_x_tile(seq, j, xt, Bx):
            rows = tsz(j)
            if j == 0:
                ld(xt[0:16, :], meta_d, [Bx])
                ld(xt[16:128, :], x_d[seq, 0:112, :], [Bx])
            else:
                r0 = 128 * j - 16
                ld(xt[0:rows, :], x_d[seq, r0:r0 + rows, :], [Bx])

        def load_w(dst, src_rows_ap, ncol, Bw, krows=8):
            ldc(dst, src_rows_ap.rearrange("(c p) n -> p c n", p=128), [Bw])

        try:
          for seq in range(nseq):
              P.barrier()
              A.reset()
              xbuf = [A.alloc([1024], F32) for _ in range(2)]
              xn = [A.alloc([1024], BF16) for _ in range(2)]
              stats = A.alloc([2, 6], F32)
              mv = A.alloc([2], F32)
              rstd = A.alloc([1], F32)
              wbuf = [A.alloc([8, 512], BF16) for _ in range(2)]
              qaT = A.alloc([4, T], BF16)
              kaT = A.alloc([4, T], BF16)
              va = A.alloc([NT, 512], BF16)
              def sb_scratch(kO, kR):
                  return dict(
                      ek=[A.alloc([512], BF16) for _ in range(3)], Bek=[Buf() for _ in range(3)],
                      sp=[A.alloc([512], BF16) for _ in range(4)], Bsp=[Buf() for _ in range(4)],
                      ec=[A.alloc([512], BF16) for _ in range(2)], Bec=[Buf() for _ in range(2)],
                      a=[A.alloc([512], BF16) for _ in range(3)], Ba=[Buf() for _ in range(3)],
                      acc=A.alloc([512], F32), Rrun=A.alloc([512], F32), Esc=A.alloc([512], F32),
                      tmp=A.alloc([512], F32), Bacc=Buf(), BR=Buf(), BE=Buf(), Btmp=Buf(),
                      kO=kO, kR=kR, c1=[0], c2=[0])

              sbsc = [sb_scratch(4, 5), sb_scratch(6, 7)]
              Bxb = [Buf(), Buf()]
              Bxn = [Buf(), Buf()]
              Bst = Buf()
              Bw = [Buf(), Buf()]
              Bqa = [[Buf() for b in range(5)] for pr in range(4)]
              Bka = [[Buf() for b in range(5)] for pr in range(4)]
              Bva = [Buf() for b in range(5)]

              stats_l = [stats, A.alloc([2, 6], F32)]
              mv_l = [mv, A.alloc([2], F32)]
              rstd_l = [rstd, A.alloc([1], F32)]
              Bst_l = [Bst, Buf()]

              def ln_a(j):
                  load_x_tile(seq, j, xbuf[j % 2], Bxb[j % 2])
                  ln_stats(xbuf[j % 2], tsz(j), stats_l[j % 2], mv_l[j % 2], rstd_l[j % 2], Bxb[j % 2], Bst_l[j % 2])

              def ln_b(j):
                  rows = tsz(j)
                  xt, bx = xbuf[j % 2], Bxb[j % 2]
                  xb, bxn = xn[j % 2], Bxn[j % 2]
                  mv_, rs_ = mv_l[j % 2], rstd_l[j % 2]
                  dve(lambda e: e.tensor_scalar(
                      out=xb[0:rows, :], in0=xt[0:rows, :], scalar1=mv_[0:rows, 0:1], scalar2=rs_[0:rows, 0:1],
                      op0=ALU.subtract, op1=ALU.mult), [bx, Bst_l[j % 2]], [bxn])
                  pk = 6 + (j % 2)
                  Xb = bank[pk].bitcast(BF16)
                  for c in range(8):
                      P.op("pe", lambda e, c=c: e.transpose(
                          out=Xb[:, c * 128:c * 128 + rows], in_=xb[0:rows, c * 128:(c + 1) * 128],
                          identity=ident[0:rows, 0:rows]), reads=[bxn, Bconst], writes=[Bps[pk]])

              def ln_c(j):
                  rows = tsz(j)
                  pk = 6 + (j % 2)
                  Xb = bank[pk].bitcast(BF16)
                  for c in range(8):
                      dve(lambda e, c=c: e.tensor_scalar(
                          out=hT[:, c, 128 * j:128 * j + rows], in0=Xb[:, c * 128:c * 128 + rows],
                          scalar1=cols[:, c:c + 1], scalar2=cols[:, 8 + c:9 + c], op0=ALU.mult, op1=ALU.add),
                          [Bps[pk], Bconst], [Bh[j // 4]])

              for j in range(NT + 2):
                  if j < NT:
                      ln_a(j)
                  if 0 <= j - 1 < NT:
                      ln_b(j - 1)
                  if 0 <= j - 2 < NT:
                      ln_c(j - 2)

              stop_at('ln')
              zrot = [0]

              def fm_project(col0, dstT, Bdst, nun=1, dup64=False):
                  u = zrot[0] % 2
                  wb, bw = wbuf[u], Bw[u]
                  if dup64:
                      ldc(wb[:, :, 0:64], win_d[:, col0:col0 + 64].rearrange("(c p) n -> p c n", p=128), [bw])
                      ldc(wb[:, :, 64:128], win_d[:, col0:col0 + 64].rearrange("(c p) n -> p c n", p=128), [bw])
                      nm = 1
                  else:
                      ldc(wb[:, :, :], win_d[:, col0:col0 + 512].rearrange("(c p) n -> p c n", p=128), [bw])
                      nm = 4
                  zrot[0] += 1
                  for b, (t0, n) in enumerate(TBS):
                      for m in range(nm):
                          k = zrot[0] % 4
                          zrot[0] += 1
                          for c in range(8):
                              mm(bank[k][:, 0:n], wb[:, c, m * 128:(m + 1) * 128], hT[:, c, t0:t0 + n],
                                 c == 0, c == 7, [bw, Bh[b]], [Bps[k]])
                          if dup64:
                              act(dstT[:, t0:t0 + n], bank[k][:, 0:n], AF.Identity, [Bps[k]], [Bdst[b]])
                          else:
                              act(dstT[:, m, t0:t0 + n], bank[k][:, 0:n], AF.Identity, [Bps[k]], [Bdst[m][b]])

              fm_project(C_QA, qaT, Bqa)
              fm_project(C_KA, kaT, Bka)
              u = zrot[0] % 2
              zrot[0] += 1
              wb, bw = wbuf[u], Bw[u]
              ldc(wb[:, :, :], win_d[:, C_VA:C_VA + 512].rearrange("(c p) n -> p c n", p=128), [bw])
              for j in range(NT):
                  rows = tsz(j)
                  k = 4 + (j % 2)
                  for c in range(8):
                      mm(bank[k][0:rows, :], hT[:, c, 128 * j:128 * j + rows], wb[:, c, :], c == 0, c == 7,
                         [bw, Bh[j // 4]], [Bps[k]])
                  act(va[0:rows, j, :], bank[k][0:rows, :], AF.Identity, [Bps[k]], [Bva[j // 4]])

              stop_at('p1a')
              ZB, CB = [0, 1], [2, 3]
              zc, cc = [0], [0]

              def sb_stream(b, pr, sc):
                  t0, n = TBS[b]
                  kO, kR = sc['kO'], sc['kR']
                  acc, Rrun, Esc, tmp = sc['acc'], sc['Rrun'], sc['Esc'], sc['tmp']
                  Bacc, BR, BE, Btmp = sc['Bacc'], sc['BR'], sc['BE'], sc['Btmp']
                  pool(lambda e: e.memset(acc[:, 0:n], 0.0), [], [Bacc])
                  pool(lambda e: e.memset(Rrun[:, 0:n], 0.0), [], [BR])
                  lastS = tiles_of(b)[-1]
                  units = [(S, hh) for S in range(lastS, -1, -1) for hh in range(2)]
                  st = {}

                  def s1(S, hh):
                      pb, h = 64 * hh, 2 * pr + hh
                      ks = tsz(S)
                      c0 = max(0, 128 * (S - 4 * b))
                      k = ZB[zc[0] % 2]
                      zc[0] += 1
                      i1 = sc['c1'][0]
                      sc['c1'][0] += 1
                      ek_t, Bek = sc['ek'][i1 % 3], sc['Bek'][i1 % 3]
                      sp_t, Bsp = sc['sp'][i1 % 4], sc['Bsp'][i1 % 4]
                      mm(bank[k][0:ks, c0:n], kaT[pb:pb + 64, pr, 128 * S:128 * S + ks],
                         qaT[pb:pb + 64, pr, t0 + c0:t0 + n], True, True, [Bka[pr][S // 4], Bqa[pr][b]], [Bps[k]])
                      act(ek_t[0:ks, c0:n], bank[k][0:ks, c0:n], AF.Exp, [Bps[k]], [Bek], scale=0.125)
                      if S >= 4 * b:
                          w = min(128, n - c0)
                          pool(lambda e: e.tensor_tensor(out=ek_t[0:ks, c0:c0 + w], in0=ek_t[0:ks, c0:c0 + w],
                                                         in1=trimask[0:ks, 0:w], op=ALU.mult), [Bek, Bconst], [Bek])
                      act(sp_t[0:ks, c0:n], ek_t[0:ks, c0:n], AF.Ln, [Bek], [Bsp], bias=1.0)
                      st[(S, hh)] = (ks, c0, ek_t, Bek, sp_t, Bsp)

                  def s2(S, hh):
                      ks, c0, ek_t, Bek, sp_t, Bsp = st[(S, hh)]
                      k2 = CB[cc[0] % 2]
                      cc[0] += 1
                      i2 = sc['c2'][0]
                      sc['c2'][0] += 1
                      ec_t, Bec = sc['ec'][i2 % 2], sc['Bec'][i2 % 2]
                      a_t, Ba_t = sc['a'][i2 % 3], sc['Ba'][i2 % 3]
                      mm(bank[k2][0:ks, c0:n], tri8[0:ks, 0:ks], sp_t[0:ks, c0:n], True, True, [Bsp, Bconst], [Bps[k2]])
                      act(ec_t[0:ks, c0:n], bank[k2][0:ks, c0:n], AF.Exp, [Bps[k2]], [Bec], scale=0.125)
                      dve(lambda e: e.tensor_tensor(out=a_t[0:ks, c0:n], in0=ek_t[0:ks, c0:n], in1=ec_t[0:ks, c0:n],
                                                    op=ALU.mult), [Bek, Bec], [Ba_t])
                      st[(S, hh)] = (ks, c0, a_t, Ba_t, sp_t, Bsp)

                  def s3(S, hh):
                      ks, c0, a_t, Ba_t, sp_t, Bsp = st.pop((S, hh))
                      pb, h = 64 * hh, 2 * pr + hh
                      mm(bank[kO][pb:pb + 64, c0:n], va[0:ks, S, h * 64:(h + 1) * 64], a_t[0:ks, c0:n], True, True,
                         [Bva[S // 4], Ba_t], [Bps[kO]])
                      mm(bank[kR][pb:pb + 64, c0:n], ones64[0:ks, :], sp_t[0:ks, c0:n], True, True,
                         [Bsp, Bconst], [Bps[kR]])
                      if hh == 1:
                          act(Esc[:, c0:n], Rrun[:, c0:n], AF.Exp, [BR], [BE], scale=-1.0)
                          dve(lambda e: e.tensor_tensor(out=tmp[:, c0:n], in0=bank[kO][:, c0:n], in1=Esc[:, c0:n],
                                                        op=ALU.mult), [Bps[kO], BE], [Btmp])
                          dve(lambda e: e.tensor_tensor(out=acc[:, c0:n], in0=acc[:, c0:n], in1=tmp[:, c0:n],
                                                        op=ALU.add), [Bacc, Btmp], [Bacc])
                          if S > 0:
                              dve(lambda e: e.tensor_tensor(out=Rrun[:, c0:n], in0=Rrun[:, c0:n],
                                                            in1=bank[kR][:, c0:n], op=ALU.add), [BR, Bps[kR]], [BR])
                          else:
                              pool(lambda e: e.tensor_copy(out=yT[:, 0, pr, t0:t0 + n], in_=acc[:, 0:n]),
                                   [Bacc], [By[0][pr][b]])

                  nu = len(units)
                  for k in range(nu + 2):
                      if k < nu:
                          s1(*units[k])
                      if 0 <= k - 1 < nu:
                          s2(*units[k - 1])
                      if 0 <= k - 2 < nu:
                          s3(*units[k - 2])
                      yield

              def chain_sb(prs, sc):
                  for b in range(len(TBS)):
                      for pr in prs:
                          yield from sb_stream(b, pr, sc)

              interleave([(chain_sb((0, 2), sbsc[0]), 1.0), (chain_sb((1, 3), sbsc[1]), 1.0)])

              stop_at('sb')
              P.barrier()
              A.reset()
              wbuf = [A.alloc([8, 512], BF16) for _ in range(2)]
              Bw = [Buf(), Buf()]
              wtm = A.alloc([8, 136], BF16)
              Bwtm = Buf()
              qbT = A.alloc([4, T], BF16)
              qiT = A.alloc([4, T], BF16)
              kbT = A.alloc([T], BF16)
              kiT = A.alloc([T], BF16)
              vb = A.alloc([NT, 64], BF16)
              wi = A.alloc([NT, 8], F32)
              kin = A.alloc([128], BF16)
              kif = A.alloc([64], F32)
              ksq = A.alloc([64], F32)
              Bksq = Buf()
              stats = A.alloc([1, 6], F32)
              mv = A.alloc([2], F32)
              rstd = A.alloc([1], F32)
              score_l = [A.alloc([2176], F32), A.alloc([2176], F32)]
              Bscore_l = [Buf(), Buf()]
              maskb = [A.alloc([2176], BF16) for _ in range(2)]
              maskT = A.alloc([NT, 512], BF16)
              rl = [A.alloc([512], F32) for _ in range(2)]
              pbuf = [A.alloc([512], BF16) for _ in range(3)]
              pmb = [A.alloc([512], BF16) for _ in range(3)]
              rden = A.alloc([512], F32)
              bis_l = [A.alloc([40], F32), A.alloc([40], F32)]
              Bbis_l = [Buf(), Buf()]
              Bqb = [[Buf() for b in range(5)] for pr in range(4)]
              Bqi = [[Buf() for b in range(5)] for pr in range(4)]
              Bkb = [Buf() for b in range(5)]
              Bki = [Buf() for b in range(5)]
              Bvb = [Buf() for b in range(5)]
              Bwi = [Buf() for b in range(5)]
              Bkin, Bst, Brden = Buf(), Buf(), Buf()
              Bmb = [Buf(), Buf()]
              Brl = [Buf(), Buf()]
              Bp = [Buf(), Buf(), Buf()]
              Bpm = [Buf(), Buf(), Buf()]

              zrot[0] = 0
              fm_project(C_QB, qbT, Bqb)
              fm_project(C_QI, qiT, Bqi)
              stop_at('b1')
              fm_project(C_KB, kbT, Bkb, dup64=True)
              stop_at('b2')
              ldc(wtm[:, :, 0:64], win_d[:, C_VB:C_VB + 64].rearrange("(c p) n -> p c n", p=128), [Bwtm])
              ldc(wtm[:, :, 64:136], win_d[:, C_KI:C_KI + 72].rearrange("(c p) n -> p c n", p=128), [Bwtm])
              for j in range(NT):
                  rows = tsz(j)
                  k = 4 + (j % 2)
                  for c in range(8):
                      mm(bank[k][0:rows, 0:136], hT[:, c, 128 * j:128 * j + rows], wtm[:, c, :], c == 0, c == 7,
                         [Bwtm, Bh[j // 4]], [Bps[k]])
                  act(vb[0:rows, j, :], bank[k][0:rows, 0:64], AF.Identity, [Bps[k]], [Bvb[j // 4]])
                  act(wi[0:rows, j, :], bank[k][0:rows, 128:136], AF.Identity, [Bps[k]], [Bwi[j // 4]], scale=WI_SCALE)
                  stop_at('c1')
                  act(kif[0:rows, :], bank[k][0:rows, 64:128], AF.Identity, [Bps[k]], [Bkin])
                  tick()
                  dve(lambda e, rows=rows: e.tensor_reduce(out=mv[0:rows, 0:1], in_=kif[0:rows, :], axis=AX.X, op=ALU.add),
                      [Bkin], [Bst])
                  tick()
                  dve(lambda e, rows=rows: e.tensor_scalar(out=mv[0:rows, 0:1], in0=mv[0:rows, 0:1], scalar1=1.0 / 64,
                                                           scalar2=None, op0=ALU.mult), [Bst], [Bst])
                  tick()
                  dve(lambda e, rows=rows: e.tensor_scalar(out=kif[0:rows, :], in0=kif[0:rows, :], scalar1=mv[0:rows, 0:1],
                                                           scalar2=None, op0=ALU.subtract), [Bkin, Bst], [Bkin])
                  tick()
                  dve(lambda e, rows=rows: e.tensor_tensor(out=ksq[0:rows, :], in0=kif[0:rows, :], in1=kif[0:rows, :],
                                                           op=ALU.mult), [Bkin], [Bksq])
                  tick()
                  dve(lambda e, rows=rows: e.tensor_reduce(out=mv[0:rows, 1:2], in_=ksq[0:rows, :], axis=AX.X, op=ALU.add),
                      [Bksq], [Bst])
                  tick()
                  act(rstd[0:rows, :], mv[0:rows, 1:2], AF.Ln, [Bst], [Bst], bias=EPS, scale=1.0 / 64)
                  tick()
                  act(rstd[0:rows, :], rstd[0:rows, :], AF.Exp, [Bst], [Bst], scale=-0.5)
                  tick()
                  dve(lambda e, rows=rows: e.tensor_scalar(out=kif[0:rows, :], in0=kif[0:rows, :], scalar1=rstd[0:rows, 0:1],
                                                           scalar2=None, op0=ALU.mult), [Bkin, Bst], [Bkin])
                  tick()
                  dve(lambda e, rows=rows: e.tensor_tensor(out=kif[0:rows, :], in0=kif[0:rows, :], in1=ikg[0:rows, :],
                                                           op=ALU.mult), [Bkin, Bconst], [Bkin])
                  tick()
                  dve(lambda e, rows=rows: e.tensor_tensor(out=kin[0:rows, 0:64], in0=kif[0:rows, :], in1=ikb[0:rows, :],
                                                           op=ALU.add), [Bkin, Bconst], [Bkin])
                  tick()
                  dve(lambda e, rows=rows: e.tensor_copy(out=kin[0:rows, 64:128], in_=kin[0:rows, 0:64]), [Bkin], [Bkin])
                  stop_at('c2')
                  pk = 6 + (j % 2)
                  Xb = bank[pk].bitcast(BF16)
                  P.op("pe", lambda e, rows=rows, Xb=Xb: e.transpose(out=Xb[:, 0:rows], in_=kin[0:rows, :],
                                                                      identity=ident[0:rows, 0:rows]),
                       reads=[Bkin, Bconst], writes=[Bps[pk]])
                  act(kiT[:, 128 * j:128 * j + rows], Xb[:, 0:rows], AF.Identity, [Bps[pk]], [Bki[j // 4]])

              stop_at('p1b')
              maskTs = [maskT, A.ap[:, 0:NT * 512].rearrange("p (a b) -> p a b", a=NT)]
              BmT = [[Buf() for _ in range(NT)] for _ in range(2)]
              alias_w = [[], [Bw[0], Bw[1], Bwtm]]
              mzr = [0]
              MB = [2, 3]

              def mask_stream(b):
                  ms = b % 2
                  mT = maskTs[ms]
                  tl = tiles_of(b)
                  for p0 in range(0, len(tl), 2):
                      pair = tl[p0:p0 + 2]
                      for i in pair:
                          rows = tsz(i)
                          L = 128 * i + rows
                          score, Bscore = score_l[i % 2], Bscore_l[i % 2]
                          bis, Bbis = bis_l[i % 2], Bbis_l[i % 2]
                          mb, bmb = maskb[i % 2], Bmb[i % 2]
                          sblocks = [(s0, min(512, L - s0)) for s0 in range(0, L, 512)]
                          for (s0, sn) in sblocks:
                              for h in range(8):
                                  pr, pb = h // 2, 64 * (h % 2)
                                  k = MB[mzr[0] % 2]
                                  r2 = mzr[0] % 2
                                  mzr[0] += 1
                                  mm(bank[k][0:rows, 0:sn], qiT[pb:pb + 64, pr, 128 * i:128 * i + rows],
                                     kiT[pb:pb + 64, s0:s0 + sn], True, True,
                                     [Bqi[pr][b]] + [Bki[bb] for bb in range(s0 // 512, min(4, (s0 + sn - 1) // 512) + 1)],
                                     [Bps[k]])
                                  act(rl[r2][0:rows, 0:sn], bank[k][0:rows, 0:sn], AF.Relu, [Bps[k]], [Brl[r2]])
                                  if h == 0:
                                      dve(lambda e: e.tensor_scalar(
                                          out=score[0:rows, s0:s0 + sn], in0=rl[r2][0:rows, 0:sn],
                                          scalar1=wi[0:rows, i, h:h + 1], scalar2=None, op0=ALU.mult),
                                          [Brl[r2], Bwi[b]], [Bscore])
                                  else:
                                      dve(lambda e: e.scalar_tensor_tensor(
                                          out=score[0:rows, s0:s0 + sn], in0=rl[r2][0:rows, 0:sn],
                                          scalar=wi[0:rows, i, h:h + 1], in1=score[0:rows, s0:s0 + sn],
                                          op0=ALU.mult, op1=ALU.add), [Brl[r2], Bwi[b], Bscore], [Bscore])
                                  yield
                          if i >= 2:
                              dve(lambda e: e.reduce_max(out=bis[0:rows, 0:1], in_=score[0:rows, 0:L], axis=AX.X),
                                  [Bscore], [Bbis])
                              dve(lambda e: e.tensor_reduce(out=bis[0:rows, 1:2], in_=score[0:rows, 0:L],
                                                            axis=AX.X, op=ALU.min), [Bscore], [Bbis])
                          dve(lambda e: e.tensor_tensor(out=score[0:rows, 128 * i:128 * i + rows],
                                                        in0=score[0:rows, 128 * i:128 * i + rows],
                                                        in1=negmask[0:rows, 0:rows], op=ALU.add),
                              [Bscore, Bconst], [Bscore])
                          if i < 2:
                              dve(lambda e: e.tensor_scalar(out=mb[0:rows, 0:L], in0=score[0:rows, 0:L],
                                                            scalar1=-1e29, scalar2=None, op0=ALU.is_ge),
                                  [Bscore], [bmb])
                          else:
                              dve(lambda e: e.tensor_tensor(out=bis[0:rows, 2:3], in0=bis[0:rows, 0:1],
                                                            in1=bis[0:rows, 1:2], op=ALU.subtract), [Bbis], [Bbis])
                              dve(lambda e: e.tensor_scalar(out=bis[0:rows, 8:8 + NIT + 1], in0=pw[0:rows, 0:NIT + 1],
                                                            scalar1=bis[0:rows, 2:3], scalar2=None, op0=ALU.mult),
                                  [Bbis, Bconst], [Bbis])
                              dve(lambda e: e.tensor_tensor(out=bis[0:rows, 3:4], in0=bis[0:rows, 1:2],
                                                            in1=bis[0:rows, 9:10], op=ALU.add), [Bbis], [Bbis])
                          yield
                      active = [i for i in pair if i >= 2]
                      for it in range(1, NIT + 1):
                          for i in active:
                              rows = tsz(i)
                              L = 128 * i + rows
                              score, Bscore = score_l[i % 2], Bscore_l[i % 2]
                              bis, Bbis = bis_l[i % 2], Bbis_l[i % 2]
                              mb, bmb = maskb[i % 2], Bmb[i % 2]
                              dve(lambda e: e.tensor_scalar(
                                  out=mb[0:rows, 0:L], in0=score[0:rows, 0:L], scalar1=bis[0:rows, 3:4], scalar2=0.0,
                                  op0=ALU.is_ge, op1=ALU.add, accum_out=bis[0:rows, 4:5]), [Bscore, Bbis], [bmb, Bbis])
                          for i in active:
                              rows = tsz(i)
                              bis, Bbis = bis_l[i % 2], Bbis_l[i % 2]
                              if it < NIT:
                                  dve(lambda e: e.tensor_scalar(
                                      out=bis[0:rows, 5:6], in0=bis[0:rows, 4:5], scalar1=TOPK - 0.5, scalar2=0.5,
                                      op0=ALU.is_ge, op1=ALU.subtract), [Bbis], [Bbis])
                                  dve(lambda e: e.scalar_tensor_tensor(
                                      out=bis[0:rows, 3:4], in0=bis[0:rows, 5:6], scalar=bis[0:rows, 8 + it:9 + it],
                                      in1=bis[0:rows, 3:4], op0=ALU.mult, op1=ALU.add), [Bbis], [Bbis])
                              else:
                                  dve(lambda e: e.tensor_scalar(
                                      out=bis[0:rows, 5:6], in0=bis[0:rows, 4:5], scalar1=TOPK - 0.5, scalar2=1.0,
                                      op0=ALU.is_ge, op1=ALU.subtract), [Bbis], [Bbis])
                                  dve(lambda e: e.scalar_tensor_tensor(
                                      out=bis[0:rows, 6:7], in0=bis[0:rows, 5:6], scalar=bis[0:rows, 8 + it:9 + it],
                                      in1=bis[0:rows, 3:4], op0=ALU.mult, op1=ALU.add), [Bbis], [Bbis])
                          if active:
                              yield
                      for i in pair:
                          rows = tsz(i)
                          L = 128 * i + rows
                          lc = 128 * (i - 4 * b)
                          score, Bscore = score_l[i % 2], Bscore_l[i % 2]
                          bis, Bbis = bis_l[i % 2], Bbis_l[i % 2]
                          mb, bmb = maskb[i % 2], Bmb[i % 2]
                          if i >= 2:
                              dve(lambda e: e.tensor_scalar(out=mb[0:rows, 0:L], in0=score[0:rows, 0:L],
                                                            scalar1=bis[0:rows, 6:7], scalar2=None,
                                                            op0=ALU.is_ge), [Bscore, Bbis], [bmb])
                          for S in range(i + 1):
                              ks = tsz(S)
                              pk = MB[mzr[0] % 2]
                              mzr[0] += 1
                              Xb = bank[pk].bitcast(BF16)
                              P.op("pe", lambda e: e.transpose(
                                  out=Xb[0:ks, 0:rows], in_=mb[0:rows, 128 * S:128 * S + ks], identity=ident[0:rows, 0:rows]),
                                  reads=[bmb, Bconst], writes=[Bps[pk]])
                              act(mT[0:ks, S, lc:lc + rows], Xb[0:ks, 0:rows], AF.Identity, [Bps[pk]],
                                  [BmT[ms][S]] + alias_w[ms])
                              if S % 2 == 1:
                                  yield
                          yield

              AZ = [0, 1, 4, 5]
              azr = [0]

              def attn_stream(b):
                  t0, n = TBS[b]
                  ms = b % 2
                  mT = maskTs[ms]
                  lastS = tiles_of(b)[-1]
                  for pr in range(4):
                      for hh in range(2):
                          pb = 64 * hh
                          h = 2 * pr + hh

                          def qk(S):
                              ks = tsz(S)
                              c0 = max(0, 128 * (S - 4 * b))
                              k = AZ[azr[0] % len(AZ)]
                              azr[0] += 1
                              near = [i for i in (S, S + 1) if i in tiles_of(b)]
                              mm(bank[k][0:ks, c0:n], kbT[pb:pb + 64, 128 * S:128 * S + ks],
                                 qbT[pb:pb + 64, pr, t0 + c0:t0 + n], True, len(near) == 0,
                                 [Bkb[S // 4], Bqb[pr][b]], [Bps[k]])
                              for ni, i in enumerate(near):
                                  lc = 128 * (i - 4 * b)
                                  w = tsz(i)
                                  off = i - S
                                  for hl in range(2):
                                      mm(bank[k][0:ks, lc:lc + w], ident[0:ks, 0:ks], BD[0:ks, h, off, hl, 0:w], False,
                                         (ni == len(near) - 1) and hl == 1, [Bconst], [Bps[k]])
                              return k

                          kq = {}

                          def ensure(S):
                              if S <= lastS and S not in kq:
                                  kq[S] = qk(S)

                          ensure(0)
                          ensure(1)
                          for S in range(lastS + 1):
                              ensure(S + 2)
                              kcur = kq[S]
                              ks = tsz(S)
                              c0 = max(0, 128 * (S - 4 * b))
                              u2 = S % 3
                              act(pbuf[u2][0:ks, c0:n], bank[kcur][0:ks, c0:n], AF.Exp, [Bps[kcur]], [Bp[u2]],
                                  bias=b31[0:ks, h:h + 1], scale=0.125)
                              pool(lambda e, u2=u2, ks=ks, c0=c0, S=S: e.tensor_tensor(
                                  out=pmb[u2][0:ks, c0:n], in0=pbuf[u2][0:ks, c0:n], in1=mT[0:ks, S, c0:n],
                                  op=ALU.mult), [Bp[u2], BmT[ms][S]], [Bpm[u2]])
                              mm(bank[6][pb:pb + 64, c0:n], vb[0:ks, S, :], pmb[u2][0:ks, c0:n], S == 0, S == lastS,
                                 [Bvb[S // 4], Bpm[u2]], [Bps[6]])
                              mm(bank[7][pb:pb + 64, c0:n], ones64[0:ks, :], pmb[u2][0:ks, c0:n], S == 0, S == lastS,
                                 [Bconst, Bpm[u2]], [Bps[7]])
                              yield
                      dve(lambda e: e.reciprocal(out=rden[:, 0:n], in_=bank[7][:, 0:n]), [Bps[7]], [Brden])
                      dve(lambda e, pr=pr: e.tensor_tensor(out=yT[:, 1, pr, t0:t0 + n], in0=bank[6][:, 0:n],
                                                           in1=rden[:, 0:n], op=ALU.mult),
                          [Bps[6], Brden], [By[1][pr][b]])
                      yield

              def est_mask(b):
                  tot = 0
                  for i in tiles_of(b):
                      L = 128 * i + tsz(i)
                      tot += 8 * ((L + 511) // 512) + (NIT if i >= 2 else 0) + (i + 1) // 2 + 1
                  return float(tot)

              def est_attn(b):
                  return float(8 * (tiles_of(b)[-1] + 1) + 4)

              interleave([(mask_stream(0), 1.0)])
              for b in range(len(TBS)):
                  gl = [(attn_stream(b), est_attn(b))]
                  if b + 1 < len(TBS):
                      gl.append((mask_stream(b + 1), est_mask(b + 1)))
                  interleave(gl)

              stop_at('dsa')
              P.barrier()
              A.reset()
              wzo = A.alloc([8, 1024], BF16)
              Bwzo = Buf()
              wst = [A.alloc([24, 128], BF16) for _ in range(3)]
              Bwst = [Buf(), Buf(), Buf()]
              mT = A.alloc([8, T], BF16)
              BmTt = [[Buf() for b in range(5)] for m in range(8)]
              gb_in_g = A.alloc([1024], F32)
              gb_in_b = A.alloc([1024], F32)
              gb_g = A.alloc([1024], F32)
              gb_b = A.alloc([1024], F32)
              Bbc = Buf()
              xbuf = [A.alloc([1024], F32) for _ in range(2)]
              Bxb = [Buf(), Buf()]
              hres = A.alloc([1024], F32)
              Bhres = Buf()
              rr = [A.alloc([1024], F32) for _ in range(2)]
              Brr = [Buf(), Buf()]
              stats = A.alloc([2, 6], F32)
              mv = A.alloc([2], F32)
              rstd = A.alloc([1], F32)
              Bst = Buf()
              stats2 = A.alloc([2, 6], F32)
              mv2 = A.alloc([2], F32)
              rstd2 = A.alloc([1], F32)
              Bst2 = Buf()
              ef = [A.alloc([512], F32) for _ in range(2)]
              Bef = [Buf(), Buf()]
              sg = [A.alloc([512], F32) for _ in range(3)]
              Bsg = [Buf(), Buf(), Buf()]
              m1 = [A.alloc([512], F32) for _ in range(2)]
              Bm1 = [Buf(), Buf()]
              ld(gb_in_g, lning_d.partition_broadcast(128), [Bbc])
              ld(gb_in_b, lninb_d.partition_broadcast(128), [Bbc])
              ld(gb_g, lng_d.partition_broadcast(128), [Bbc])
              ld(gb_b, lnb_d.partition_broadcast(128), [Bbc])

              def sigmoid_from_psum(k, n, bias, si):
                  eb = zr[0] % 2
                  act(ef[eb][:, 0:n], bank[k][:, 0:n], AF.Exp, [Bps[k]], [Bef[eb]], bias=bias, scale=-1.0)
                  act(ef[eb][:, 0:n], ef[eb][:, 0:n], AF.Ln, [Bef[eb]], [Bef[eb]], bias=1.0)
                  act(sg[si][:, 0:n], ef[eb][:, 0:n], AF.Exp, [Bef[eb]], [Bsg[si]], scale=-1.0)

              def load_ws(m):
                  ws, bws = wst[m % 3], Bwst[m % 3]
                  ldc(ws[:, 0:8, :], win_d[:, C_GA + 128 * m:C_GA + 128 * (m + 1)].rearrange("(c p) n -> p c n", p=128), [bws])
                  ldc(ws[:, 8:16, :], win_d[:, C_GB + 128 * m:C_GB + 128 * (m + 1)].rearrange("(c p) n -> p c n", p=128), [bws])
                  ldc(ws[:, 16:20, :], wpa_d[:, 128 * m:128 * (m + 1)].rearrange("(c p) n -> p c n", p=128), [bws])
                  ldc(ws[:, 20:24, :], wpb_d[:, 128 * m:128 * (m + 1)].rearrange("(c p) n -> p c n", p=128), [bws])

              ldc(wzo[:, :, 0:512], win_d[:, C_ZA:C_ZA + 512].rearrange("(c p) n -> p c n", p=128), [Bwzo])
              ldc(wzo[:, :, 512:1024], win_d[:, C_ZB:C_ZB + 512].rearrange("(c p) n -> p c n", p=128), [Bwzo])
              load_ws(0)
              load_ws(1)
              zr = [0]
              for b, (t0, n) in enumerate(TBS):
                  for m in range(8):
                      br, pr = m // 4, m % 4
                      k = zr[0] % 6
                      si = zr[0] % 3
                      zr[0] += 1
                      for c in range(8):
                          mm(bank[k][:, 0:n], wzo[:, c, m * 128:(m + 1) * 128], hT[:, c, t0:t0 + n], c == 0, c == 7,
                             [Bwzo, Bh[b]], [Bps[k]])
                      sigmoid_from_psum(k, n, 0.0, si)
                      dve(lambda e, k=k, n=n, si=si: e.tensor_tensor(out=sg[si][:, 0:n], in0=bank[k][:, 0:n],
                                                                     in1=sg[si][:, 0:n], op=ALU.mult),
                          [Bps[k], Bsg[si]], [Bsg[si]])
                      pool(lambda e, n=n, si=si, br=br, pr=pr, t0=t0: e.tensor_tensor(
                          out=yT[:, br, pr, t0:t0 + n], in0=yT[:, br, pr, t0:t0 + n], in1=sg[si][:, 0:n], op=ALU.mult),
                          [By[br][pr][b], Bsg[si]], [By[br][pr][b]])
              stop_at('3i')
              for m in range(8):
                  u = m % 3
                  ws, bws = wst[u], Bwst[u]
                  for b, (t0, n) in enumerate(TBS):
                      res = []
                      for br in range(2):
                          k = zr[0] % 6
                          si = zr[0] % 3
                          zr[0] += 1
                          for c in range(8):
                              mm(bank[k][:, 0:n], ws[:, 8 * br + c, :], hT[:, c, t0:t0 + n], c == 0, c == 7,
                                 [bws, Bh[b]], [Bps[k]])
                          sigmoid_from_psum(k, n, negbg[:, 8 * br + m:8 * br + m + 1], si)
                          k2 = zr[0] % 6
                          zr[0] += 1
                          for c in range(4):
                              mm(bank[k2][:, 0:n], ws[:, 16 + 4 * br + c, :], yT[:, br, c, t0:t0 + n], c == 0, c == 3,
                                 [bws, By[br][c][b]], [Bps[k2]])
                          dve(lambda e, k2=k2, n=n, si=si, br=br: e.tensor_tensor(
                              out=m1[br][:, 0:n], in0=bank[k2][:, 0:n], in1=sg[si][:, 0:n], op=ALU.mult),
                              [Bps[k2], Bsg[si]], [Bm1[br]])
                      pool(lambda e, n=n, m=m, t0=t0: e.tensor_tensor(out=mT[:, m, t0:t0 + n], in0=m1[0][:, 0:n],
                                                                      in1=m1[1][:, 0:n], op=ALU.add),
                           [Bm1[0], Bm1[1]], [BmTt[m][b]])
                  if m == 0:
                      ldc(wzo[:, :, :], wo_d.rearrange("(c p) n -> p c n", p=128), [Bwzo])
                  if m + 2 < 8:
                      load_ws(m + 2)
              stop_at('3ii')
              hres_l = [hres, A.alloc([1024], F32)]
              Bhres_l = [Bhres, Buf()]
              st1 = [(stats, mv, rstd, Bst), (A.alloc([2, 6], F32), A.alloc([2], F32), A.alloc([1], F32), Buf())]
              st2 = [(stats2, mv2, rstd2, Bst2), (A.alloc([2, 6], F32), A.alloc([2], F32), A.alloc([1], F32), Buf())]
              nmr1 = [A.alloc([1], F32), A.alloc([1], F32)]
              nmr2 = [A.alloc([1], F32), A.alloc([1], F32)]

              def o_a(j):
                  rows = tsz(j)
                  xt, bx = xbuf[j % 2], Bxb[j % 2]
                  sts, mv_, rs_, bst = st1[j % 2]
                  hr, bhr = hres_l[j % 2], Bhres_l[j % 2]
                  nm = nmr1[j % 2]
                  load_x_tile(seq, j, xt, bx)
                  ln_stats(xt, rows, sts, mv_, rs_, bx, bst)
                  dve(lambda e: e.tensor_scalar(out=nm[0:rows, :], in0=mv_[0:rows, 0:1], scalar1=rs_[0:rows, 0:1],
                                                scalar2=-1.0, op0=ALU.mult, op1=ALU.mult), [bst], [bst])
                  act(hr[0:rows, :], xt[0:rows, :], AF.Identity, [bx, bst], [bhr], bias=nm[0:rows, 0:1],
                      scale=rs_[0:rows, 0:1])
                  pool(lambda e: e.tensor_tensor(out=hr[0:rows, :], in0=hr[0:rows, :], in1=gb_in_g[0:rows, :],
                                                 op=ALU.mult), [bhr, Bbc], [bhr])
                  pool(lambda e: e.tensor_tensor(out=hr[0:rows, :], in0=hr[0:rows, :], in1=gb_in_b[0:rows, :],
                                                 op=ALU.add), [bhr, Bbc], [bhr])

              def o_b(j):
                  rows = tsz(j)
                  b = j // 4
                  hr, bhr = hres_l[j % 2], Bhres_l[j % 2]
                  k = 6 if j % 2 == 0 else 4
                  for half in range(2):
                      for m in range(8):
                          mm(bank[k + half][0:rows, :], mT[:, m, 128 * j:128 * j + rows], wzo[:, m, 512 * half:512 * (half + 1)],
                             m == 0, m == 7, [BmTt[m][b], Bwzo], [Bps[k + half]])
                  r_, br_ = rr[j % 2], Brr[j % 2]
                  dve(lambda e: e.scalar_tensor_tensor(
                      out=r_[0:rows, :], in0=hr[0:rows, :], scalar=ALPHA, in1=ps[0:rows, 512 * k:512 * k + 1024],
                      op0=ALU.mult, op1=ALU.add), [bhr, Bps[k], Bps[k + 1]], [br_])

              def o_c(j):
                  rows = tsz(j)
                  r_, br_ = rr[j % 2], Brr[j % 2]
                  sts, mv_, rs_, bst = st2[j % 2]
                  nm = nmr2[j % 2]
                  ln_stats(r_, rows, sts, mv_, rs_, br_, bst)
                  dve(lambda e: e.tensor_scalar(out=nm[0:rows, :], in0=mv_[0:rows, 0:1], scalar1=rs_[0:rows, 0:1],
                                                scalar2=-1.0, op0=ALU.mult, op1=ALU.mult), [bst], [bst])
                  act(r_[0:rows, :], r_[0:rows, :], AF.Identity, [br_, bst], [br_], bias=nm[0:rows, 0:1],
                      scale=rs_[0:rows, 0:1])
                  dve(lambda e: e.tensor_tensor(out=r_[0:rows, :], in0=r_[0:rows, :], in1=gb_g[0:rows, :],
                                                op=ALU.mult), [br_, Bbc], [br_])
                  pool(lambda e: e.tensor_tensor(out=r_[0:rows, :], in0=r_[0:rows, :], in1=gb_b[0:rows, :],
                                                 op=ALU.add), [br_, Bbc], [br_])
                  if j == 0:
                      P.dma("sp", lambda e: e.dma_start(out=out_d[seq, 0:112, :], in_=r_[16:128, :]),
                            reads=[br_], writes=[])
                  else:
                      r0 = 128 * j - 16
                      P.dma("sp", lambda e: e.dma_start(out=out_d[seq, r0:r0 + rows, :], in_=r_[0:rows, :]),
                            reads=[br_], writes=[])

              for k_ in range(NT + 2):
                  if k_ < NT:
                      o_a(k_)
                  if 0 <= k_ - 1 < NT:
                      o_b(k_ - 1)
                  if 0 <= k_ - 2 < NT:
                      o_c(k_ - 2)
        except _Stop:
            pass
        if os.environ.get('K_VERBOSE'):
            print('OPCOUNTS', P.cnt, P.ndma, {e: len(P.ops[e]) for e in P.ENGS})
        P.emit(nc)
    return nc


def rel_bucket_np(d):
    d = np.asarray(d)
    nf = np.maximum(d, 1).astype(np.float32)
    large = 16 + (np.log(nf / np.float32(16)) / np.float32(math.log(128 / 16)) * np.float32(16)).astype(np.int32)
    large = np.minimum(large, 31)
    return np.where(d < 16, d, large)


def host_constants():
    cm = np.zeros((32, 383), np.float32)
    for k in range(383):
        d = k - 127
        if d >= 128:
            continue
        bkt = int(rel_bucket_np(max(d, 0)))
        cm[bkt, k] += 8.0
        cm[31, k] -= 8.0
    p = np.arange(128)
    c128 = np.zeros((128, 5, 128), np.float32)
    c128[:, 0, :] = np.eye(128, dtype=np.float32)
    c128[:, 1, :] = (p[:, None] < p[None, :]).astype(np.float32)
    c128[:, 2, :] = (p[:, None] >= p[None, :]).astype(np.float32)
    c128[:, 3, :] = (p[:, None] + p[None, :] == 127).astype(np.float32)
    c128[:, 4, :] = np.where(p[None, :] <= p[:, None], 0.0, -1e30).astype(np.float32)
    pw = (2.0 ** -np.arange(32)).astype(np.float32)
    return cm, c128, pw


_CACHE = {}


def kernel(x, meta_tokens, ln_in_g, ln_in_b, rel_bias, w_in, b_gate, idx_kn_g, idx_kn_b,
           w_pa, w_pb, w_o, ln_g, ln_b, _ncores=8, _nseq=4):
    f = lambda a: np.ascontiguousarray(np.asarray(a, dtype=np.float32))
    x = f(x)
    ncores, nseq = _ncores, _nseq
    key = (nseq,)
    if key not in _CACHE:
        _CACHE[key] = build_program(nseq)
    nc = _CACHE[key]
    cm, c128, pw = host_constants()
    cols = np.zeros((128, 48), np.float32)
    cols[:, 0:8] = f(ln_in_g).reshape(8, 128).T
    cols[:, 8:16] = f(ln_in_b).reshape(8, 128).T
    cols[:, 16:32] = f(b_gate).reshape(16, 128).T
    shared = {
        "meta": f(meta_tokens), "w_in": f(w_in)[0], "w_pa": f(w_pa)[0], "w_pb": f(w_pb)[0], "w_o": f(w_o)[0],
        "rel_bias": f(rel_bias), "ln_in_g": f(ln_in_g), "ln_in_b": f(ln_in_b), "ln_g": f(ln_g)[0], "ln_b": f(ln_b)[0],
        "cols": cols, "ikn_g": f(idx_kn_g)[0], "ikn_b": f(idx_kn_b)[0], "cmat": cm, "c128": c128, "pw": pw,
    }
    in_maps = []
    for c in range(ncores):
        d = dict(shared)
        d["x"] = np.ascontiguousarray(x[c * nseq:(c + 1) * nseq])
        in_maps.append(d)
    res = run_bass_kernel_spmd(nc, in_maps, core_ids=list(range(ncores)))
    return np.concatenate([np.asarray(r["out"]) for r in res.results], axis=0).astype(np.float32)
```

```python
import contextlib
import math
import os
import numpy as np
import concourse.bass as bass
import concourse.mybir as mybir
from concourse.bass_utils import run_bass_kernel_spmd

F32 = mybir.dt.float32
BF16 = mybir.dt.bfloat16
ALU = mybir.AluOpType
AF = mybir.ActivationFunctionType
AX = mybir.AxisListType

T = 2064
NT = 17
D = 1024
SEQ = 2048
NMETA = 16
TOPK = 256
EPS = 1e-5
ALPHA = 2.0 ** 0.25
WI_SCALE = (8 ** -0.5) * (64 ** -0.5)
NIT = 22
TBS = [(0, 512), (512, 512), (1024, 512), (1536, 512), (2048, 16)]
C_QA, C_KA, C_VA, C_ZA, C_QB, C_KB, C_VB, C_ZB, C_QI, C_KI, C_WI, C_GA, C_GB = (
    0, 512, 1024, 1536, 2048, 2560, 2624, 2688, 3200, 3712, 3776, 3784, 4808)
INCOLS = 5832


def tsz(j):
    return 128 if j < 16 else 16


def tiles_of(b):
    return list(range(4 * b, min(4 * b + 4, NT)))


class Buf:
    __slots__ = ("name", "w", "r")

    def __init__(self, name=""):
        self.name = name
        self.w = None
        self.r = []


class _Rec:
    def __init__(self):
        self.call = None

    def __getattr__(self, name):
        def f(*a, **kw):
            self.call = (name, a, kw)
            return self
        return f


def _freeze(fn):
    rec = _Rec()
    fn(rec)
    name, a, kw = rec.call
    return lambda e: getattr(e, name)(*a, **kw)


class Prog:
    ENGS = ("pe", "act", "dve", "pool", "sp")

    def __init__(self):
        self.ops = {e: [] for e in self.ENGS}
        self.cnt = {e: 0 for e in self.ENGS}
        self.pending = {e: set() for e in self.ENGS}
        self.ndma = 0
        self.ndma_e = {}
        self.dma_sem_use = {}
        self.NDMASEM = 24

    def _deps(self, eng, reads, writes):
        deps = set(self.pending[eng])
        self.pending[eng] = set()
        for b in reads:
            if b.w is not None:
                deps.add(b.w)
        for b in writes:
            if b.w is not None:
                deps.add(b.w)
            deps.update(b.r)
        if eng == "pe":
            deps = {d for d in deps if d[0] != "pe"}
        return deps

    def _mark(self, me, reads, writes):
        for b in reads:
            b.r.append(me)
            if len(b.r) > 64:
                best = {}
                for (k, v) in b.r:
                    if best.get(k, 0) < v:
                        best[k] = v
                b.r = list(best.items())
        for b in writes:
            b.w = me
            b.r = []

    def op(self, eng, fn, reads=(), writes=()):
        deps = self._deps(eng, reads, writes)
        self.cnt[eng] += 1
        me = (eng, self.cnt[eng])
        self.ops[eng].append(("op", _freeze(fn), deps, None))
        self._mark(me, reads, writes)
        return me

    def dma(self, eng, fn, reads=(), writes=()):
        deps = self._deps(eng, reads, writes)
        nd = self.ndma_e.get(eng, 0)
        self.ndma_e[eng] = nd + 1
        self.ndma += 1
        k = (eng, nd % self.NDMASEM)
        prev = self.dma_sem_use.get(k, 0)
        if prev:
            deps.add((("dma", k), 16 * prev))
        self.dma_sem_use[k] = prev + 1
        me = (("dma", k), 16 * (prev + 1))
        self.ops[eng].append(("dma", _freeze(fn), deps, k))
        self._mark(me, reads, writes)
        return me

    def barrier(self):
        snap = set()
        for e in self.ENGS:
            if self.cnt[e]:
                snap.add((e, self.cnt[e]))
        for k, c in self.dma_sem_use.items():
            snap.add((("dma", k), 16 * c))
        self.pending["act"] |= snap
        scr = self.bar_scratch
        me = self.op("act", lambda e: e.activation(out=scr, in_=scr, func=AF.Identity))
        for e in self.ENGS:
            if e != "act":
                self.pending[e].add(me)

    def emit(self, nc):
        with contextlib.ExitStack() as st:
            sems = {}
            for e in self.ENGS:
                sems[e] = st.enter_context(nc.semaphore("s_" + e))
            for k in self.dma_sem_use:
                sems[("dma", k)] = st.enter_context(nc.semaphore("s_dma_%s%d" % k))
            block = st.enter_context(nc.Block())
            regs = {"pe": block.tensor, "act": block.scalar, "dve": block.vector,
                    "pool": block.gpsimd, "sp": block.sync}
            for e in self.ENGS:
                ops = self.ops[e]
                if not ops and e != "sp":
                    continue

                def body(engine, ops=ops, e=e):
                    waited = {}
                    for kind, fn, deps, k in ops:
                        best = {}
                        for (sk, val) in deps:
                            if best.get(sk, 0) < val:
                                best[sk] = val
                        for sk in sorted(best, key=str):
                            val = best[sk]
                            if waited.get(sk, 0) >= val:
                                continue
                            engine.wait_ge(sems[sk], val)
                            waited[sk] = val
                        ins = fn(engine)
                        if kind == "op":
                            ins.then_inc(sems[e], 1)
                        else:
                            ins.then_inc(sems[("dma", k)], 16)
                    if e == "sp":
                        for k2, c2 in self.dma_sem_use.items():
                            engine.wait_ge(sems[("dma", k2)], 16 * c2)
                        for e2 in self.ENGS:
                            if e2 != "sp" and self.cnt[e2]:
                                engine.wait_ge(sems[e2], self.cnt[e2])
                regs[e](body)


class _Stop(Exception):
    pass


_TICK = [0]


def tick():
    _TICK[0] += 1
    stop_at('d%d' % _TICK[0])


def stop_at(name):
    if os.environ.get("K_STOP", "") == name:
        raise _Stop()


def interleave(gens):
    st = [[g, est, 0] for g, est in gens]
    while st:
        st.sort(key=lambda x: x[2] / x[1])
        g = st[0]
        try:
            next(g[0])
            g[2] += 1
        except StopIteration:
            st.pop(0)


class Arena:
    def __init__(self, ap, nelem):
        self.ap = ap
        self.n = nelem
        self.off = 0

    def reset(self):
        self.off = 0

    def alloc(self, shape, dt):
        n = 1
        for s in shape:
            n *= s
        ne = n * (2 if dt == F32 else 1)
        ne_al = (ne + 15) // 16 * 16
        assert self.off + ne_al <= self.n, ("arena overflow", self.off, ne_al, self.n)
        v = self.ap[:, self.off:self.off + ne]
        self.off += ne_al
        if dt == F32:
            v = v.bitcast(F32)
        if len(shape) == 2:
            v = v.rearrange("p (a b) -> p a b", a=shape[0])
        elif len(shape) == 3:
            v = v.rearrange("p (a b c) -> p a b c", a=shape[0], b=shape[1])
        return v


def build_program(nseq):
    nc = bass.Bass("TRN2", target_bir_lowering=False)
    dram = lambda name, shape, kind="ExternalInput": nc.dram_tensor(name, shape, F32, kind=kind).ap()
    x_d = dram("x", [nseq, SEQ, D])
    meta_d = dram("meta", [NMETA, D])
    win_d = dram("w_in", [D, INCOLS])
    wpa_d = dram("w_pa", [512, D])
    wpb_d = dram("w_pb", [512, D])
    wo_d = dram("w_o", [D, D])
    relb_d = dram("rel_bias", [32, 8])
    lning_d = dram("ln_in_g", [D])
    lninb_d = dram("ln_in_b", [D])
    lng_d = dram("ln_g", [D])
    lnb_d = dram("ln_b", [D])
    cols_d = dram("cols", [128, 48])
    ikg_d = dram("ikn_g", [64])
    ikb_d = dram("ikn_b", [64])
    cmat_d = dram("cmat", [32, 383])
    c128_d = dram("c128", [128, 5, 128])
    pw_d = dram("pw", [32])
    out_d = dram("out", [nseq, SEQ, D], kind="ExternalOutput")
    gscr_d = nc.dram_tensor("gscr", [8, 383], F32, kind="Internal").ap()

    P = Prog()
    st = contextlib.ExitStack()
    with st:
        sbt = lambda name, shape, dt: st.enter_context(nc.sbuf_tensor("sb_" + name, shape, dt))
        hT = sbt("hT", [128, 8, T], BF16)
        yT = sbt("yT", [128, 2, 4, T], BF16)
        ident = sbt("ident", [128, 128], BF16)
        identf = sbt("identf", [128, 128], F32)
        trimask = sbt("trimask", [128, 128], BF16)
        tri8 = sbt("tri8", [128, 128], BF16)
        ones8 = sbt("ones8", [128, 128], BF16)
        ones64 = sbt("ones64", [128, 64], BF16)
        negmask = sbt("negmask", [128, 128], F32)
        Jm = sbt("Jm", [128, 128], F32)
        c128f = sbt("c128f", [128, 5, 128], F32)
        BD = sbt("BD", [128, 8, 2, 2, 128], BF16)
        cols = sbt("cols", [128, 48], F32)
        negbg = sbt("negbg", [128, 16], F32)
        b31 = sbt("b31", [128, 8], F32)
        pw = sbt("pw", [128, 32], F32)
        ikg = sbt("ikg", [128, 64], F32)
        ikb = sbt("ikb", [128, 64], F32)
        relb = sbt("relb", [32, 8], F32)
        cmat = sbt("cmat", [32, 383], F32)
        gvec = sbt("gvec", [8, 383], F32)
        ARENA_N = 63400
        arena_t = sbt("arena", [128, ARENA_N], BF16)
        A = Arena(arena_t, ARENA_N)
        ps = st.enter_context(nc.psum_tensor("ps", [128, 4096], F32))
        bank = [ps[:, 512 * k:512 * (k + 1)] for k in range(8)]
        Bps = [Buf("ps%d" % k) for k in range(8)]
        Bconst = Buf("const")
        Bh = [Buf("h%d" % b) for b in range(5)]
        By = [[[Buf() for b in range(5)] for pr in range(4)] for br in range(2)]

        def mm(out, lhsT, rhs, start, stop, reads, writes):
            return P.op("pe", lambda e: e.matmul(out, lhsT=lhsT, rhs=rhs, start=start, stop=stop),
                        reads=reads, writes=writes)

        def act(out, in_, func, reads, writes, bias=0.0, scale=1.0):
            return P.op("act", lambda e: e.activation(out=out, in_=in_, func=func, bias=bias, scale=scale),
                        reads=reads, writes=writes)

        def dve(fn, reads, writes):
            return P.op("dve", fn, reads=reads, writes=writes)

        def pool(fn, reads, writes):
            return P.op("pool", fn, reads=reads, writes=writes)

        def ld(out, in_, writes, reads=()):
            return P.dma("sp", lambda e: e.dma_start(out=out, in_=in_), reads=reads, writes=writes)

        def ldc(out, in_, writes, reads=()):
            return P.dma("pool", lambda e: e.dma_start(out=out, in_=in_), reads=reads, writes=writes)

        P.bar_scratch = cols[0:1, 40:48]
        ld(c128f[:], c128_d, [Bconst])
        ld(cols[:], cols_d, [Bconst])
        ld(relb[:], relb_d, [Bconst])
        ld(cmat[:], cmat_d, [Bconst])
        ld(b31[:], relb_d[31, :].partition_broadcast(128), [Bconst])
        ld(pw[:], pw_d.partition_broadcast(128), [Bconst])
        ld(ikg[:], ikg_d.partition_broadcast(128), [Bconst])
        ld(ikb[:], ikb_d.partition_broadcast(128), [Bconst])
        dve(lambda e: e.tensor_copy(out=ident[:], in_=c128f[:, 0, :]), [Bconst], [Bconst])
        dve(lambda e: e.tensor_copy(out=identf[:], in_=c128f[:, 0, :]), [Bconst], [Bconst])
        dve(lambda e: e.tensor_copy(out=trimask[:], in_=c128f[:, 1, :]), [Bconst], [Bconst])
        dve(lambda e: e.tensor_scalar(out=tri8[:], in0=c128f[:, 2, :], scalar1=-8.0, scalar2=None, op0=ALU.mult),
            [Bconst], [Bconst])
        dve(lambda e: e.tensor_copy(out=Jm[:], in_=c128f[:, 3, :]), [Bconst], [Bconst])
        dve(lambda e: e.tensor_copy(out=negmask[:], in_=c128f[:, 4, :]), [Bconst], [Bconst])
        dve(lambda e: e.memset(ones8[:], -8.0), [], [Bconst])
        dve(lambda e: e.memset(ones64[:], 1.0), [], [Bconst])
        dve(lambda e: e.tensor_scalar(out=negbg[:], in0=cols[:, 16:32], scalar1=-1.0, scalar2=None, op0=ALU.mult),
            [Bconst], [Bconst])
        Bg = Buf("gscr")
        mm(bank[0][0:8, 0:383], relb[:, :], cmat[:, :], True, True, [Bconst], [Bps[0]])
        dve(lambda e: e.tensor_copy(out=gvec[:], in_=bank[0][0:8, 0:383]), [Bps[0]], [Bconst])
        ld(gscr_d, gvec[:], [Bg], reads=[Bconst])
        A.reset()
        hank = A.alloc([16, 128], F32)
        Bhank = Buf("hank")
        for h in range(8):
            for off in range(2):
                src = bass.AP(tensor=gscr_d.tensor, offset=h * 383 + 128 * off, ap=[[1, 128], [1, 128]])
                ld(hank[:, h * 2 + off, :], src, [Bhank], reads=[Bg])
        for h in range(8):
            for off in range(2):
                k = (h * 2 + off) % 4
                mm(bank[k][:, 0:128], Jm[:, :], hank[:, h * 2 + off, :], True, True, [Bhank, Bconst], [Bps[k]])
                dve(lambda e, h=h, off=off, k=k: e.tensor_copy(out=BD[:, h, off, 0, :], in_=bank[k][:, 0:128]),
                    [Bps[k]], [Bconst])
                dve(lambda e, h=h, off=off, k=k: e.tensor_tensor(out=BD[:, h, off, 1, :], in0=bank[k][:, 0:128],
                                                                  in1=BD[:, h, off, 0, :], op=ALU.subtract),
                    [Bps[k], Bconst], [Bconst])

        def ln_stats(xt, rows, stats, mv, rstd, Bx, Bs):
            for c in range(2):
                dve(lambda e, c=c: e.bn_stats(out=stats[0:rows, c, :], in_=xt[0:rows, 512 * c:512 * (c + 1)]),
                    [Bx], [Bs])
            dve(lambda e: e.bn_aggr(out=mv[0:rows, :], in_=stats[0:rows, :, :]), [Bs], [Bs])
            act(rstd[0:rows, :], mv[0:rows, 1:2], AF.Ln, [Bs], [Bs], bias=EPS)
            act(rstd[0:rows, :], rstd[0:rows, :], AF.Exp, [Bs], [Bs], scale=-0.5)

        def load_x_tile(seq, j, xt, Bx):
            rows = tsz(j)
            if j == 0:
                ld(xt[0:16, :], meta_d, [Bx])
                ld(xt[16:128, :], x_d[seq, 0:112, :], [Bx])
            else:
                r0 = 128 * j - 16
                ld(xt[0:rows, :], x_d[seq, r0:r0 + rows, :], [Bx])

        def load_w(dst, src_rows_ap, ncol, Bw, krows=8):
            ldc(dst, src_rows_ap.rearrange("(c p) n -> p c n", p=128), [Bw])

        try:
          for seq in range(nseq):
              P.barrier()
              A.reset()
              xbuf = [A.alloc([1024], F32) for _ in range(2)]
              xn = [A.alloc([1024], BF16) for _ in range(2)]
              stats = A.alloc([2, 6], F32)
              mv = A.alloc([2], F32)
              rstd = A.alloc([1], F32)
              wbuf = [A.alloc([8, 512], BF16) for _ in range(2)]
              qaT = A.alloc([4, T], BF16)
              kaT = A.alloc([4, T], BF16)
              va = A.alloc([NT, 512], BF16)
              def sb_scratch(kO, kR):
                  return dict(
                      ek=[A.alloc([512], BF16) for _ in range(3)], Bek=[Buf() for _ in range(3)],
                      sp=[A.alloc([512], BF16) for _ in range(4)], Bsp=[Buf() for _ in range(4)],
                      ec=[A.alloc([512], BF16) for _ in range(2)], Bec=[Buf() for _ in range(2)],
                      a=[A.alloc([512], BF16) for _ in range(3)], Ba=[Buf() for _ in range(3)],
                      acc=A.alloc([512], F32), Rrun=A.alloc([512], F32), Esc=A.alloc([512], F32),
                      tmp=A.alloc([512], F32), Bacc=Buf(), BR=Buf(), BE=Buf(), Btmp=Buf(),
                      kO=kO, kR=kR, c1=[0], c2=[0])

              sbsc = [sb_scratch(4, 5), sb_scratch(6, 7)]
              Bxb = [Buf(), Buf()]
              Bxn = [Buf(), Buf()]
              Bst = Buf()
              Bw = [Buf(), Buf()]
              Bqa = [[Buf() for b in range(5)] for pr in range(4)]
              Bka = [[Buf() for b in range(5)] for pr in range(4)]
              Bva = [Buf() for b in range(5)]

              stats_l = [stats, A.alloc([2, 6], F32)]
              mv_l = [mv, A.alloc([2], F32)]
              rstd_l = [rstd, A.alloc([1], F32)]
              Bst_l = [Bst, Buf()]

              def ln_a(j):
                  load_x_tile(seq, j, xbuf[j % 2], Bxb[j % 2])
                  ln_stats(xbuf[j % 2], tsz(j), stats_l[j % 2], mv_l[j % 2], rstd_l[j % 2], Bxb[j % 2], Bst_l[j % 2])

              def ln_b(j):
                  rows = tsz(j)
                  xt, bx = xbuf[j % 2], Bxb[j % 2]
                  xb, bxn = xn[j % 2], Bxn[j % 2]
                  mv_, rs_ = mv_l[j % 2], rstd_l[j % 2]
                  dve(lambda e: e.tensor_scalar(
                      out=xb[0:rows, :], in0=xt[0:rows, :], scalar1=mv_[0:rows, 0:1], scalar2=rs_[0:rows, 0:1],
                      op0=ALU.subtract, op1=ALU.mult), [bx, Bst_l[j % 2]], [bxn])
                  pk = 6 + (j % 2)
                  Xb = bank[pk].bitcast(BF16)
                  for c in range(8):
                      P.op("pe", lambda e, c=c: e.transpose(
                          out=Xb[:, c * 128:c * 128 + rows], in_=xb[0:rows, c * 128:(c + 1) * 128],
                          identity=ident[0:rows, 0:rows]), reads=[bxn, Bconst], writes=[Bps[pk]])

              def ln_c(j):
                  rows = tsz(j)
                  pk = 6 + (j % 2)
                  Xb = bank[pk].bitcast(BF16)
                  for c in range(8):
                      dve(lambda e, c=c: e.tensor_scalar(
                          out=hT[:, c, 128 * j:128 * j + rows], in0=Xb[:, c * 128:c * 128 + rows],
                          scalar1=cols[:, c:c + 1], scalar2=cols[:, 8 + c:9 + c], op0=ALU.mult, op1=ALU.add),
                          [Bps[pk], Bconst], [Bh[j // 4]])

              for j in range(NT + 2):
                  if j < NT:
                      ln_a(j)
                  if 0 <= j - 1 < NT:
                      ln_b(j - 1)
                  if 0 <= j - 2 < NT:
                      ln_c(j - 2)

              stop_at('ln')
              zrot = [0]

              def fm_project(col0, dstT, Bdst, nun=1, dup64=False):
                  u = zrot[0] % 2
                  wb, bw = wbuf[u], Bw[u]
                  if dup64:
                      ldc(wb[:, :, 0:64], win_d[:, col0:col0 + 64].rearrange("(c p) n -> p c n", p=128), [bw])
                      ldc(wb[:, :, 64:128], win_d[:, col0:col0 + 64].rearrange("(c p) n -> p c n", p=128), [bw])
                      nm = 1
                  else:
                      ldc(wb[:, :, :], win_d[:, col0:col0 + 512].rearrange("(c p) n -> p c n", p=128), [bw])
                      nm = 4
                  zrot[0] += 1
                  for b, (t0, n) in enumerate(TBS):
                      for m in range(nm):
                          k = zrot[0] % 4
                          zrot[0] += 1
                          for c in range(8):
                              mm(bank[k][:, 0:n], wb[:, c, m * 128:(m + 1) * 128], hT[:, c, t0:t0 + n],
                                 c == 0, c == 7, [bw, Bh[b]], [Bps[k]])
                          if dup64:
                              act(dstT[:, t0:t0 + n], bank[k][:, 0:n], AF.Identity, [Bps[k]], [Bdst[b]])
                          else:
                              act(dstT[:, m, t0:t0 + n], bank[k][:, 0:n], AF.Identity, [Bps[k]], [Bdst[m][b]])

              fm_project(C_QA, qaT, Bqa)
              fm_project(C_KA, kaT, Bka)
              u = zrot[0] % 2
              zrot[0] += 1
              wb, bw = wbuf[u], Bw[u]
              ldc(wb[:, :, :], win_d[:, C_VA:C_VA + 512].rearrange("(c p) n -> p c n", p=128), [bw])
              for j in range(NT):
                  rows = tsz(j)
                  k = 4 + (j % 2)
                  for c in range(8):
                      mm(bank[k][0:rows, :], hT[:, c, 128 * j:128 * j + rows], wb[:, c, :], c == 0, c == 7,
                         [bw, Bh[j // 4]], [Bps[k]])
                  act(va[0:rows, j, :], bank[k][0:rows, :], AF.Identity, [Bps[k]], [Bva[j // 4]])

              stop_at('p1a')
              ZB, CB = [0, 1], [2, 3]
              zc, cc = [0], [0]

              def sb_stream(b, pr, sc):
                  t0, n = TBS[b]
                  kO, kR = sc['kO'], sc['kR']
                  acc, Rrun, Esc, tmp = sc['acc'], sc['Rrun'], sc['Esc'], sc['tmp']
                  Bacc, BR, BE, Btmp = sc['Bacc'], sc['BR'], sc['BE'], sc['Btmp']
                  pool(lambda e: e.memset(acc[:, 0:n], 0.0), [], [Bacc])
                  pool(lambda e: e.memset(Rrun[:, 0:n], 0.0), [], [BR])
                  lastS = tiles_of(b)[-1]
                  units = [(S, hh) for S in range(lastS, -1, -1) for hh in range(2)]
                  st = {}

                  def s1(S, hh):
                      pb, h = 64 * hh, 2 * pr + hh
                      ks = tsz(S)
                      c0 = max(0, 128 * (S - 4 * b))
                      k = ZB[zc[0] % 2]
                      zc[0] += 1
                      i1 = sc['c1'][0]
                      sc['c1'][0] += 1
                      ek_t, Bek = sc['ek'][i1 % 3], sc['Bek'][i1 % 3]
                      sp_t, Bsp = sc['sp'][i1 % 4], sc['Bsp'][i1 % 4]
                      mm(bank[k][0:ks, c0:n], kaT[pb:pb + 64, pr, 128 * S:128 * S + ks],
                         qaT[pb:pb + 64, pr, t0 + c0:t0 + n], True, True, [Bka[pr][S // 4], Bqa[pr][b]], [Bps[k]])
                      act(ek_t[0:ks, c0:n], bank[k][0:ks, c0:n], AF.Exp, [Bps[k]], [Bek], scale=0.125)
                      if S >= 4 * b:
                          w = min(128, n - c0)
                          pool(lambda e: e.tensor_tensor(out=ek_t[0:ks, c0:c0 + w], in0=ek_t[0:ks, c0:c0 + w],
                                                         in1=trimask[0:ks, 0:w], op=ALU.mult), [Bek, Bconst], [Bek])
                      act(sp_t[0:ks, c0:n], ek_t[0:ks, c0:n], AF.Ln, [Bek], [Bsp], bias=1.0)
                      st[(S, hh)] = (ks, c0, ek_t, Bek, sp_t, Bsp)

                  def s2(S, hh):
                      ks, c0, ek_t, Bek, sp_t, Bsp = st[(S, hh)]
                      k2 = CB[cc[0] % 2]
                      cc[0] += 1
                      i2 = sc['c2'][0]
                      sc['c2'][0] += 1
                      ec_t, Bec = sc['ec'][i2 % 2], sc['Bec'][i2 % 2]
                      a_t, Ba_t = sc['a'][i2 % 3], sc['Ba'][i2 % 3]
                      mm(bank[k2][0:ks, c0:n], tri8[0:ks, 0:ks], sp_t[0:ks, c0:n], True, True, [Bsp, Bconst], [Bps[k2]])
                      act(ec_t[0:ks, c0:n], bank[k2][0:ks, c0:n], AF.Exp, [Bps[k2]], [Bec], scale=0.125)
                      dve(lambda e: e.tensor_tensor(out=a_t[0:ks, c0:n], in0=ek_t[0:ks, c0:n], in1=ec_t[0:ks, c0:n],
                                                    op=ALU.mult), [Bek, Bec], [Ba_t])
                      st[(S, hh)] = (ks, c0, a_t, Ba_t, sp_t, Bsp)

                  def s3(S, hh):
                      ks, c0, a_t, Ba_t, sp_t, Bsp = st.pop((S, hh))
                      pb, h = 64 * hh, 2 * pr + hh
                      mm(bank[kO][pb:pb + 64, c0:n], va[0:ks, S, h * 64:(h + 1) * 64], a_t[0:ks, c0:n], True, True,
                         [Bva[S // 4], Ba_t], [Bps[kO]])
                      mm(bank[kR][pb:pb + 64, c0:n], ones64[0:ks, :], sp_t[0:ks, c0:n], True, True,
                         [Bsp, Bconst], [Bps[kR]])
                      if hh == 1:
                          act(Esc[:, c0:n], Rrun[:, c0:n], AF.Exp, [BR], [BE], scale=-1.0)
                          dve(lambda e: e.tensor_tensor(out=tmp[:, c0:n], in0=bank[kO][:, c0:n], in1=Esc[:, c0:n],
                                                        op=ALU.mult), [Bps[kO], BE], [Btmp])
                          dve(lambda e: e.tensor_tensor(out=acc[:, c0:n], in0=acc[:, c0:n], in1=tmp[:, c0:n],
                                                        op=ALU.add), [Bacc, Btmp], [Bacc])
                          if S > 0:
                              dve(lambda e: e.tensor_tensor(out=Rrun[:, c0:n], in0=Rrun[:, c0:n],
                                                            in1=bank[kR][:, c0:n], op=ALU.add), [BR, Bps[kR]], [BR])
                          else:
                              pool(lambda e: e.tensor_copy(out=yT[:, 0, pr, t0:t0 + n], in_=acc[:, 0:n]),
                                   [Bacc], [By[0][pr][b]])

                  nu = len(units)
                  for k in range(nu + 2):
                      if k < nu:
                          s1(*units[k])
                      if 0 <= k - 1 < nu:
                          s2(*units[k - 1])
                      if 0 <= k - 2 < nu:
                          s3(*units[k - 2])
                      yield

              def chain_sb(prs, sc):
                  for b in range(len(TBS)):
                      for pr in prs:
                          yield from sb_stream(b, pr, sc)

              interleave([(chain_sb((0, 2), sbsc[0]), 1.0), (chain_sb((1, 3), sbsc[1]), 1.0)])

              stop_at('sb')
              P.barrier()
              A.reset()
              wbuf = [A.alloc([8, 512], BF16) for _ in range(2)]
              Bw = [Buf(), Buf()]
              wtm = A.alloc([8, 136], BF16)
              Bwtm = Buf()
              qbT = A.alloc([4, T], BF16)
              qiT = A.alloc([4, T], BF16)
              kbT = A.alloc([T], BF16)
              kiT = A.alloc([T], BF16)
              vb = A.alloc([NT, 64], BF16)
              wi = A.alloc([NT, 8], F32)
              kin = A.alloc([128], BF16)
              kif = A.alloc([64], F32)
              ksq = A.alloc([64], F32)
              Bksq = Buf()
              stats = A.alloc([1, 6], F32)
              mv = A.alloc([2], F32)
              rstd = A.alloc([1], F32)
              score_l = [A.alloc([2176], F32), A.alloc([2176], F32)]
              Bscore_l = [Buf(), Buf()]
              maskb = [A.alloc([2176], BF16) for _ in range(2)]
              maskT = A.alloc([NT, 512], BF16)
              rl = [A.alloc([512], F32) for _ in range(2)]
              pbuf = [A.alloc([512], BF16) for _ in range(3)]
              pmb = [A.alloc([512], BF16) for _ in range(3)]
              rden = A.alloc([512], F32)
              bis_l = [A.alloc([40], F32), A.alloc([40], F32)]
              Bbis_l = [Buf(), Buf()]
              Bqb = [[Buf() for b in range(5)] for pr in range(4)]
              Bqi = [[Buf() for b in range(5)] for pr in range(4)]
              Bkb = [Buf() for b in range(5)]
              Bki = [Buf() for b in range(5)]
              Bvb = [Buf() for b in range(5)]
              Bwi = [Buf() for b in range(5)]
              Bkin, Bst, Brden = Buf(), Buf(), Buf()
              Bmb = [Buf(), Buf()]
              Brl = [Buf(), Buf()]
              Bp = [Buf(), Buf(), Buf()]
              Bpm = [Buf(), Buf(), Buf()]

              zrot[0] = 0
              fm_project(C_QB, qbT, Bqb)
              fm_project(C_QI, qiT, Bqi)
              stop_at('b1')
              fm_project(C_KB, kbT, Bkb, dup64=True)
              stop_at('b2')
              ldc(wtm[:, :, 0:64], win_d[:, C_VB:C_VB + 64].rearrange("(c p) n -> p c n", p=128), [Bwtm])
              ldc(wtm[:, :, 64:136], win_d[:, C_KI:C_KI + 72].rearrange("(c p) n -> p c n", p=128), [Bwtm])
              for j in range(NT):
                  rows = tsz(j)
                  k = 4 + (j % 2)
                  for c in range(8):
                      mm(bank[k][0:rows, 0:136], hT[:, c, 128 * j:128 * j + rows], wtm[:, c, :], c == 0, c == 7,
                         [Bwtm, Bh[j // 4]], [Bps[k]])
                  act(vb[0:rows, j, :], bank[k][0:rows, 0:64], AF.Identity, [Bps[k]], [Bvb[j // 4]])
                  act(wi[0:rows, j, :], bank[k][0:rows, 128:136], AF.Identity, [Bps[k]], [Bwi[j // 4]], scale=WI_SCALE)
                  stop_at('c1')
                  act(kif[0:rows, :], bank[k][0:rows, 64:128], AF.Identity, [Bps[k]], [Bkin])
                  tick()
                  dve(lambda e, rows=rows: e.tensor_reduce(out=mv[0:rows, 0:1], in_=kif[0:rows, :], axis=AX.X, op=ALU.add),
                      [Bkin], [Bst])
                  tick()
                  dve(lambda e, rows=rows: e.tensor_scalar(out=mv[0:rows, 0:1], in0=mv[0:rows, 0:1], scalar1=1.0 / 64,
                                                           scalar2=None, op0=ALU.mult), [Bst], [Bst])
                  tick()
                  dve(lambda e, rows=rows: e.tensor_scalar(out=kif[0:rows, :], in0=kif[0:rows, :], scalar1=mv[0:rows, 0:1],
                                                           scalar2=None, op0=ALU.subtract), [Bkin, Bst], [Bkin])
                  tick()
                  dve(lambda e, rows=rows: e.tensor_tensor(out=ksq[0:rows, :], in0=kif[0:rows, :], in1=kif[0:rows, :],
                                                           op=ALU.mult), [Bkin], [Bksq])
                  tick()
                  dve(lambda e, rows=rows: e.tensor_reduce(out=mv[0:rows, 1:2], in_=ksq[0:rows, :], axis=AX.X, op=ALU.add),
                      [Bksq], [Bst])
                  tick()
                  act(rstd[0:rows, :], mv[0:rows, 1:2], AF.Ln, [Bst], [Bst], bias=EPS, scale=1.0 / 64)
                  tick()
                  act(rstd[0:rows, :], rstd[0:rows, :], AF.Exp, [Bst], [Bst], scale=-0.5)
                  tick()
                  dve(lambda e, rows=rows: e.tensor_scalar(out=kif[0:rows, :], in0=kif[0:rows, :], scalar1=rstd[0:rows, 0:1],
                                                           scalar2=None, op0=ALU.mult), [Bkin, Bst], [Bkin])
                  tick()
                  dve(lambda e, rows=rows: e.tensor_tensor(out=kif[0:rows, :], in0=kif[0:rows, :], in1=ikg[0:rows, :],
                                                           op=ALU.mult), [Bkin, Bconst], [Bkin])
                  tick()
                  dve(lambda e, rows=rows: e.tensor_tensor(out=kin[0:rows, 0:64], in0=kif[0:rows, :], in1=ikb[0:rows, :],
                                                           op=ALU.add), [Bkin, Bconst], [Bkin])
                  tick()
                  dve(lambda e, rows=rows: e.tensor_copy(out=kin[0:rows, 64:128], in_=kin[0:rows, 0:64]), [Bkin], [Bkin])
                  stop_at('c2')
                  pk = 6 + (j % 2)
                  Xb = bank[pk].bitcast(BF16)
                  P.op("pe", lambda e, rows=rows, Xb=Xb: e.transpose(out=Xb[:, 0:rows], in_=kin[0:rows, :],
                                                                      identity=ident[0:rows, 0:rows]),
                       reads=[Bkin, Bconst], writes=[Bps[pk]])
                  act(kiT[:, 128 * j:128 * j + rows], Xb[:, 0:rows], AF.Identity, [Bps[pk]], [Bki[j // 4]])

              stop_at('p1b')
              maskTs = [maskT, A.ap[:, 0:NT * 512].rearrange("p (a b) -> p a b", a=NT)]
              BmT = [[Buf() for _ in range(NT)] for _ in range(2)]
              alias_w = [[], [Bw[0], Bw[1], Bwtm]]
              mzr = [0]
              MB = [2, 3]

              def on_act(i, pair):
                  return len(pair) == 2 and pair[0] >= 2 and i == pair[1]

              def mask_stream(b):
                  ms = b % 2
                  mT = maskTs[ms]
                  tl = tiles_of(b)
                  for p0 in range(0, len(tl), 2):
                      pair = tl[p0:p0 + 2]
                      for i in pair:
                          rows = tsz(i)
                          L = 128 * i + rows
                          score, Bscore = score_l[i % 2], Bscore_l[i % 2]
                          bis, Bbis = bis_l[i % 2], Bbis_l[i % 2]
                          mb, bmb = maskb[i % 2], Bmb[i % 2]
                          sblocks = [(s0, min(512, L - s0)) for s0 in range(0, L, 512)]
                          for (s0, sn) in sblocks:
                              for h in range(8):
                                  pr, pb = h // 2, 64 * (h % 2)
                                  k = MB[mzr[0] % 2]
                                  r2 = mzr[0] % 2
                                  mzr[0] += 1
                                  mm(bank[k][0:rows, 0:sn], qiT[pb:pb + 64, pr, 128 * i:128 * i + rows],
                                     kiT[pb:pb + 64, s0:s0 + sn], True, True,
                                     [Bqi[pr][b]] + [Bki[bb] for bb in range(s0 // 512, min(4, (s0 + sn - 1) // 512) + 1)],
                                     [Bps[k]])
                                  act(rl[r2][0:rows, 0:sn], bank[k][0:rows, 0:sn], AF.Relu, [Bps[k]], [Brl[r2]])
                                  if h == 0:
                                      dve(lambda e: e.tensor_scalar(
                                          out=score[0:rows, s0:s0 + sn], in0=rl[r2][0:rows, 0:sn],
                                          scalar1=wi[0:rows, i, h:h + 1], scalar2=None, op0=ALU.mult),
                                          [Brl[r2], Bwi[b]], [Bscore])
                                  else:
                                      dve(lambda e: e.scalar_tensor_tensor(
                                          out=score[0:rows, s0:s0 + sn], in0=rl[r2][0:rows, 0:sn],
                                          scalar=wi[0:rows, i, h:h + 1], in1=score[0:rows, s0:s0 + sn],
                                          op0=ALU.mult, op1=ALU.add), [Brl[r2], Bwi[b], Bscore], [Bscore])
                                  yield
                          if i >= 2:
                              dve(lambda e: e.reduce_max(out=bis[0:rows, 0:1], in_=score[0:rows, 0:L], axis=AX.X),
                                  [Bscore], [Bbis])
                              dve(lambda e: e.tensor_reduce(out=bis[0:rows, 1:2], in_=score[0:rows, 0:L],
                                                            axis=AX.X, op=ALU.min), [Bscore], [Bbis])
                          dve(lambda e: e.tensor_tensor(out=score[0:rows, 128 * i:128 * i + rows],
                                                        in0=score[0:rows, 128 * i:128 * i + rows],
                                                        in1=negmask[0:rows, 0:rows], op=ALU.add),
                              [Bscore, Bconst], [Bscore])
                          if i < 2:
                              dve(lambda e: e.tensor_scalar(out=mb[0:rows, 0:L], in0=score[0:rows, 0:L],
                                                            scalar1=-1e29, scalar2=None, op0=ALU.is_ge),
                                  [Bscore], [bmb])
                          else:
                              dve(lambda e: e.tensor_tensor(out=bis[0:rows, 2:3], in0=bis[0:rows, 0:1],
                                                            in1=bis[0:rows, 1:2], op=ALU.subtract), [Bbis], [Bbis])
                              dve(lambda e: e.tensor_scalar(out=bis[0:rows, 8:8 + NIT + 1], in0=pw[0:rows, 0:NIT + 1],
                                                            scalar1=bis[0:rows, 2:3], scalar2=None, op0=ALU.mult),
                                  [Bbis, Bconst], [Bbis])
                              if on_act(i, pair):
                                  dve(lambda e: e.tensor_scalar(out=bis[0:rows, 3:4], in0=bis[0:rows, 1:2], scalar1=-1.0,
                                                                scalar2=bis[0:rows, 9:10], op0=ALU.mult,
                                                                op1=ALU.subtract), [Bbis], [Bbis])
                              else:
                                  dve(lambda e: e.tensor_tensor(out=bis[0:rows, 3:4], in0=bis[0:rows, 1:2],
                                                                in1=bis[0:rows, 9:10], op=ALU.add), [Bbis], [Bbis])
                          yield
                      active = [i for i in pair if i >= 2]
                      for it in range(1, NIT + 1):
                          for i in active:
                              rows = tsz(i)
                              L = 128 * i + rows
                              score, Bscore = score_l[i % 2], Bscore_l[i % 2]
                              bis, Bbis = bis_l[i % 2], Bbis_l[i % 2]
                              mb, bmb = maskb[i % 2], Bmb[i % 2]
                              if on_act(i, pair):
                                  P.op("act", lambda e: e.activation(
                                      out=mb[0:rows, 0:L], in_=score[0:rows, 0:L], func=AF.Sign, bias=bis[0:rows, 3:4],
                                      scale=1.0, accum_out=bis[0:rows, 4:5]), reads=[Bscore, Bbis], writes=[bmb, Bbis])
                              else:
                                  dve(lambda e: e.tensor_scalar(
                                      out=mb[0:rows, 0:L], in0=score[0:rows, 0:L], scalar1=bis[0:rows, 3:4], scalar2=0.0,
                                      op0=ALU.is_ge, op1=ALU.add, accum_out=bis[0:rows, 4:5]), [Bscore, Bbis], [bmb, Bbis])
                          for i in active:
                              rows = tsz(i)
                              bis, Bbis = bis_l[i % 2], Bbis_l[i % 2]
                              if on_act(i, pair):
                                  thr = float(2 * TOPK - 1 - (128 * i + rows))
                                  last_ = (it == NIT)
                                  dve(lambda e: e.tensor_scalar(
                                      out=bis[0:rows, 5:6], in0=bis[0:rows, 4:5], scalar1=thr, scalar2=(0.0 if last_ else 0.5),
                                      op0=ALU.is_lt, op1=ALU.subtract), [Bbis], [Bbis])
                                  dve(lambda e: e.scalar_tensor_tensor(
                                      out=(bis[0:rows, 6:7] if last_ else bis[0:rows, 3:4]), in0=bis[0:rows, 5:6],
                                      scalar=bis[0:rows, 8 + it:9 + it], in1=bis[0:rows, 3:4],
                                      op0=ALU.mult, op1=ALU.add), [Bbis], [Bbis])
                                  continue
                              if it < NIT:
                                  dve(lambda e: e.tensor_scalar(
                                      out=bis[0:rows, 5:6], in0=bis[0:rows, 4:5], scalar1=TOPK - 0.5, scalar2=0.5,
                                      op0=ALU.is_ge, op1=ALU.subtract), [Bbis], [Bbis])
                                  dve(lambda e: e.scalar_tensor_tensor(
                                      out=bis[0:rows, 3:4], in0=bis[0:rows, 5:6], scalar=bis[0:rows, 8 + it:9 + it],
                                      in1=bis[0:rows, 3:4], op0=ALU.mult, op1=ALU.add), [Bbis], [Bbis])
                              else:
                                  dve(lambda e: e.tensor_scalar(
                                      out=bis[0:rows, 5:6], in0=bis[0:rows, 4:5], scalar1=TOPK - 0.5, scalar2=1.0,
                                      op0=ALU.is_ge, op1=ALU.subtract), [Bbis], [Bbis])
                                  dve(lambda e: e.scalar_tensor_tensor(
                                      out=bis[0:rows, 6:7], in0=bis[0:rows, 5:6], scalar=bis[0:rows, 8 + it:9 + it],
                                      in1=bis[0:rows, 3:4], op0=ALU.mult, op1=ALU.add), [Bbis], [Bbis])
                          if active:
                              yield
                      for i in pair:
                          rows = tsz(i)
                          L = 128 * i + rows
                          lc = 128 * (i - 4 * b)
                          score, Bscore = score_l[i % 2], Bscore_l[i % 2]
                          bis, Bbis = bis_l[i % 2], Bbis_l[i % 2]
                          mb, bmb = maskb[i % 2], Bmb[i % 2]
                          if i >= 2 and on_act(i, pair):
                              dve(lambda e: e.tensor_scalar(out=mb[0:rows, 0:L], in0=score[0:rows, 0:L],
                                                            scalar1=bis[0:rows, 6:7], scalar2=0.0,
                                                            op0=ALU.add, op1=ALU.is_ge), [Bscore, Bbis], [bmb])
                          elif i >= 2:
                              dve(lambda e: e.tensor_scalar(out=mb[0:rows, 0:L], in0=score[0:rows, 0:L],
                                                            scalar1=bis[0:rows, 6:7], scalar2=None,
                                                            op0=ALU.is_ge), [Bscore, Bbis], [bmb])
                          for S in range(i + 1):
                              ks = tsz(S)
                              pk = MB[mzr[0] % 2]
                              mzr[0] += 1
                              Xb = bank[pk].bitcast(BF16)
                              P.op("pe", lambda e: e.transpose(
                                  out=Xb[0:ks, 0:rows], in_=mb[0:rows, 128 * S:128 * S + ks], identity=ident[0:rows, 0:rows]),
                                  reads=[bmb, Bconst], writes=[Bps[pk]])
                              act(mT[0:ks, S, lc:lc + rows], Xb[0:ks, 0:rows], AF.Identity, [Bps[pk]],
                                  [BmT[ms][S]] + alias_w[ms])
                              if S % 2 == 1:
                                  yield
                          yield

              AZ = [0, 1, 4, 5]
              azr = [0]

              def attn_stream(b):
                  t0, n = TBS[b]
                  ms = b % 2
                  mT = maskTs[ms]
                  lastS = tiles_of(b)[-1]
                  for pr in range(4):
                      for hh in range(2):
                          pb = 64 * hh
                          h = 2 * pr + hh

                          def qk(S):
                              ks = tsz(S)
                              c0 = max(0, 128 * (S - 4 * b))
                              k = AZ[azr[0] % len(AZ)]
                              azr[0] += 1
                              near = [i for i in (S, S + 1) if i in tiles_of(b)]
                              mm(bank[k][0:ks, c0:n], kbT[pb:pb + 64, 128 * S:128 * S + ks],
                                 qbT[pb:pb + 64, pr, t0 + c0:t0 + n], True, len(near) == 0,
                                 [Bkb[S // 4], Bqb[pr][b]], [Bps[k]])
                              for ni, i in enumerate(near):
                                  lc = 128 * (i - 4 * b)
                                  w = tsz(i)
                                  off = i - S
                                  for hl in range(2):
                                      mm(bank[k][0:ks, lc:lc + w], ident[0:ks, 0:ks], BD[0:ks, h, off, hl, 0:w], False,
                                         (ni == len(near) - 1) and hl == 1, [Bconst], [Bps[k]])
                              return k

                          kq = {}

                          def ensure(S):
                              if S <= lastS and S not in kq:
                                  kq[S] = qk(S)

                          ensure(0)
                          ensure(1)
                          for S in range(lastS + 1):
                              ensure(S + 2)
                              kcur = kq[S]
                              ks = tsz(S)
                              c0 = max(0, 128 * (S - 4 * b))
                              u2 = S % 3
                              act(pbuf[u2][0:ks, c0:n], bank[kcur][0:ks, c0:n], AF.Exp, [Bps[kcur]], [Bp[u2]],
                                  bias=b31[0:ks, h:h + 1], scale=0.125)
                              pool(lambda e, u2=u2, ks=ks, c0=c0, S=S: e.tensor_tensor(
                                  out=pmb[u2][0:ks, c0:n], in0=pbuf[u2][0:ks, c0:n], in1=mT[0:ks, S, c0:n],
                                  op=ALU.mult), [Bp[u2], BmT[ms][S]], [Bpm[u2]])
                              mm(bank[6][pb:pb + 64, c0:n], vb[0:ks, S, :], pmb[u2][0:ks, c0:n], S == 0, S == lastS,
                                 [Bvb[S // 4], Bpm[u2]], [Bps[6]])
                              mm(bank[7][pb:pb + 64, c0:n], ones64[0:ks, :], pmb[u2][0:ks, c0:n], S == 0, S == lastS,
                                 [Bconst, Bpm[u2]], [Bps[7]])
                              yield
                      dve(lambda e: e.reciprocal(out=rden[:, 0:n], in_=bank[7][:, 0:n]), [Bps[7]], [Brden])
                      dve(lambda e, pr=pr: e.tensor_tensor(out=yT[:, 1, pr, t0:t0 + n], in0=bank[6][:, 0:n],
                                                           in1=rden[:, 0:n], op=ALU.mult),
                          [Bps[6], Brden], [By[1][pr][b]])
                      yield

              def est_mask(b):
                  tot = 0
                  for i in tiles_of(b):
                      L = 128 * i + tsz(i)
                      tot += 8 * ((L + 511) // 512) + (NIT if i >= 2 else 0) + (i + 1) // 2 + 1
                  return float(tot)

              def est_attn(b):
                  return float(8 * (tiles_of(b)[-1] + 1) + 4)

              interleave([(mask_stream(0), 1.0)])
              for b in range(len(TBS)):
                  gl = [(attn_stream(b), est_attn(b))]
                  if b + 1 < len(TBS):
                      gl.append((mask_stream(b + 1), est_mask(b + 1)))
                  interleave(gl)

              stop_at('dsa')
              P.barrier()
              A.reset()
              wzo = A.alloc([8, 1024], BF16)
              Bwzo = Buf()
              wst = [A.alloc([24, 128], BF16) for _ in range(3)]
              Bwst = [Buf(), Buf(), Buf()]
              mT = A.alloc([8, T], BF16)
              BmTt = [[Buf() for b in range(5)] for m in range(8)]
              gb_in_g = A.alloc([1024], F32)
              gb_in_b = A.alloc([1024], F32)
              gb_g = A.alloc([1024], F32)
              gb_b = A.alloc([1024], F32)
              Bbc = Buf()
              xbuf = [A.alloc([1024], F32) for _ in range(2)]
              Bxb = [Buf(), Buf()]
              hres = A.alloc([1024], F32)
              Bhres = Buf()
              rr = [A.alloc([1024], F32) for _ in range(2)]
              Brr = [Buf(), Buf()]
              stats = A.alloc([2, 6], F32)
              mv = A.alloc([2], F32)
              rstd = A.alloc([1], F32)
              Bst = Buf()
              stats2 = A.alloc([2, 6], F32)
              mv2 = A.alloc([2], F32)
              rstd2 = A.alloc([1], F32)
              Bst2 = Buf()
              ef = [A.alloc([512], F32) for _ in range(2)]
              Bef = [Buf(), Buf()]
              sg = [A.alloc([512], F32) for _ in range(3)]
              Bsg = [Buf(), Buf(), Buf()]
              m1 = [A.alloc([512], F32) for _ in range(2)]
              Bm1 = [Buf(), Buf()]
              ld(gb_in_g, lning_d.partition_broadcast(128), [Bbc])
              ld(gb_in_b, lninb_d.partition_broadcast(128), [Bbc])
              ld(gb_g, lng_d.partition_broadcast(128), [Bbc])
              ld(gb_b, lnb_d.partition_broadcast(128), [Bbc])

              def sigmoid_from_psum(k, n, bias, si):
                  eb = zr[0] % 2
                  act(ef[eb][:, 0:n], bank[k][:, 0:n], AF.Exp, [Bps[k]], [Bef[eb]], bias=bias, scale=-1.0)
                  act(ef[eb][:, 0:n], ef[eb][:, 0:n], AF.Ln, [Bef[eb]], [Bef[eb]], bias=1.0)
                  act(sg[si][:, 0:n], ef[eb][:, 0:n], AF.Exp, [Bef[eb]], [Bsg[si]], scale=-1.0)

              def load_ws(m):
                  ws, bws = wst[m % 3], Bwst[m % 3]
                  ldc(ws[:, 0:8, :], win_d[:, C_GA + 128 * m:C_GA + 128 * (m + 1)].rearrange("(c p) n -> p c n", p=128), [bws])
                  ldc(ws[:, 8:16, :], win_d[:, C_GB + 128 * m:C_GB + 128 * (m + 1)].rearrange("(c p) n -> p c n", p=128), [bws])
                  ldc(ws[:, 16:20, :], wpa_d[:, 128 * m:128 * (m + 1)].rearrange("(c p) n -> p c n", p=128), [bws])
                  ldc(ws[:, 20:24, :], wpb_d[:, 128 * m:128 * (m + 1)].rearrange("(c p) n -> p c n", p=128), [bws])

              ldc(wzo[:, :, 0:512], win_d[:, C_ZA:C_ZA + 512].rearrange("(c p) n -> p c n", p=128), [Bwzo])
              ldc(wzo[:, :, 512:1024], win_d[:, C_ZB:C_ZB + 512].rearrange("(c p) n -> p c n", p=128), [Bwzo])
              load_ws(0)
              load_ws(1)
              zr = [0]
              for b, (t0, n) in enumerate(TBS):
                  for m in range(8):
                      br, pr = m // 4, m % 4
                      k = zr[0] % 6
                      si = zr[0] % 3
                      zr[0] += 1
                      for c in range(8):
                          mm(bank[k][:, 0:n], wzo[:, c, m * 128:(m + 1) * 128], hT[:, c, t0:t0 + n], c == 0, c == 7,
                             [Bwzo, Bh[b]], [Bps[k]])
                      sigmoid_from_psum(k, n, 0.0, si)
                      dve(lambda e, k=k, n=n, si=si: e.tensor_tensor(out=sg[si][:, 0:n], in0=bank[k][:, 0:n],
                                                                     in1=sg[si][:, 0:n], op=ALU.mult),
                          [Bps[k], Bsg[si]], [Bsg[si]])
                      pool(lambda e, n=n, si=si, br=br, pr=pr, t0=t0: e.tensor_tensor(
                          out=yT[:, br, pr, t0:t0 + n], in0=yT[:, br, pr, t0:t0 + n], in1=sg[si][:, 0:n], op=ALU.mult),
                          [By[br][pr][b], Bsg[si]], [By[br][pr][b]])
              stop_at('3i')
              for m in range(8):
                  u = m % 3
                  ws, bws = wst[u], Bwst[u]
                  for b, (t0, n) in enumerate(TBS):
                      res = []
                      for br in range(2):
                          k = zr[0] % 6
                          si = zr[0] % 3
                          zr[0] += 1
                          for c in range(8):
                              mm(bank[k][:, 0:n], ws[:, 8 * br + c, :], hT[:, c, t0:t0 + n], c == 0, c == 7,
                                 [bws, Bh[b]], [Bps[k]])
                          sigmoid_from_psum(k, n, negbg[:, 8 * br + m:8 * br + m + 1], si)
                          k2 = zr[0] % 6
                          zr[0] += 1
                          for c in range(4):
                              mm(bank[k2][:, 0:n], ws[:, 16 + 4 * br + c, :], yT[:, br, c, t0:t0 + n], c == 0, c == 3,
                                 [bws, By[br][c][b]], [Bps[k2]])
                          dve(lambda e, k2=k2, n=n, si=si, br=br: e.tensor_tensor(
                              out=m1[br][:, 0:n], in0=bank[k2][:, 0:n], in1=sg[si][:, 0:n], op=ALU.mult),
                              [Bps[k2], Bsg[si]], [Bm1[br]])
                      pool(lambda e, n=n, m=m, t0=t0: e.tensor_tensor(out=mT[:, m, t0:t0 + n], in0=m1[0][:, 0:n],
                                                                      in1=m1[1][:, 0:n], op=ALU.add),
                           [Bm1[0], Bm1[1]], [BmTt[m][b]])
                  if m == 0:
                      ldc(wzo[:, :, :], wo_d.rearrange("(c p) n -> p c n", p=128), [Bwzo])
                  if m + 2 < 8:
                      load_ws(m + 2)
              stop_at('3ii')
              hres_l = [hres, A.alloc([1024], F32)]
              Bhres_l = [Bhres, Buf()]
              st1 = [(stats, mv, rstd, Bst), (A.alloc([2, 6], F32), A.alloc([2], F32), A.alloc([1], F32), Buf())]
              st2 = [(stats2, mv2, rstd2, Bst2), (A.alloc([2, 6], F32), A.alloc([2], F32), A.alloc([1], F32), Buf())]
              nmr1 = [A.alloc([1], F32), A.alloc([1], F32)]
              nmr2 = [A.alloc([1], F32), A.alloc([1], F32)]

              def o_a(j):
                  rows = tsz(j)
                  xt, bx = xbuf[j % 2], Bxb[j % 2]
                  sts, mv_, rs_, bst = st1[j % 2]
                  hr, bhr = hres_l[j % 2], Bhres_l[j % 2]
                  nm = nmr1[j % 2]
                  load_x_tile(seq, j, xt, bx)
                  ln_stats(xt, rows, sts, mv_, rs_, bx, bst)
                  dve(lambda e: e.tensor_scalar(out=nm[0:rows, :], in0=mv_[0:rows, 0:1], scalar1=rs_[0:rows, 0:1],
                                                scalar2=-1.0, op0=ALU.mult, op1=ALU.mult), [bst], [bst])
                  act(hr[0:rows, :], xt[0:rows, :], AF.Identity, [bx, bst], [bhr], bias=nm[0:rows, 0:1],
                      scale=rs_[0:rows, 0:1])
                  pool(lambda e: e.tensor_tensor(out=hr[0:rows, :], in0=hr[0:rows, :], in1=gb_in_g[0:rows, :],
                                                 op=ALU.mult), [bhr, Bbc], [bhr])
                  pool(lambda e: e.tensor_tensor(out=hr[0:rows, :], in0=hr[0:rows, :], in1=gb_in_b[0:rows, :],
                                                 op=ALU.add), [bhr, Bbc], [bhr])

              def o_b(j):
                  rows = tsz(j)
                  b = j // 4
                  hr, bhr = hres_l[j % 2], Bhres_l[j % 2]
                  k = 6 if j % 2 == 0 else 4
                  for half in range(2):
                      for m in range(8):
                          mm(bank[k + half][0:rows, :], mT[:, m, 128 * j:128 * j + rows], wzo[:, m, 512 * half:512 * (half + 1)],
                             m == 0, m == 7, [BmTt[m][b], Bwzo], [Bps[k + half]])
                  r_, br_ = rr[j % 2], Brr[j % 2]
                  dve(lambda e: e.scalar_tensor_tensor(
                      out=r_[0:rows, :], in0=hr[0:rows, :], scalar=ALPHA, in1=ps[0:rows, 512 * k:512 * k + 1024],
                      op0=ALU.mult, op1=ALU.add), [bhr, Bps[k], Bps[k + 1]], [br_])

              def o_c(j):
                  rows = tsz(j)
                  r_, br_ = rr[j % 2], Brr[j % 2]
                  sts, mv_, rs_, bst = st2[j % 2]
                  nm = nmr2[j % 2]
                  ln_stats(r_, rows, sts, mv_, rs_, br_, bst)
                  dve(lambda e: e.tensor_scalar(out=nm[0:rows, :], in0=mv_[0:rows, 0:1], scalar1=rs_[0:rows, 0:1],
                                                scalar2=-1.0, op0=ALU.mult, op1=ALU.mult), [bst], [bst])
                  act(r_[0:rows, :], r_[0:rows, :], AF.Identity, [br_, bst], [br_], bias=nm[0:rows, 0:1],
                      scale=rs_[0:rows, 0:1])
                  dve(lambda e: e.tensor_tensor(out=r_[0:rows, :], in0=r_[0:rows, :], in1=gb_g[0:rows, :],
                                                op=ALU.mult), [br_, Bbc], [br_])
                  pool(lambda e: e.tensor_tensor(out=r_[0:rows, :], in0=r_[0:rows, :], in1=gb_b[0:rows, :],
                                                 op=ALU.add), [br_, Bbc], [br_])
                  if j == 0:
                      P.dma("sp", lambda e: e.dma_start(out=out_d[seq, 0:112, :], in_=r_[16:128, :]),
                            reads=[br_], writes=[])
                  else:
                      r0 = 128 * j - 16
                      P.dma("sp", lambda e: e.dma_start(out=out_d[seq, r0:r0 + rows, :], in_=r_[0:rows, :]),
                            reads=[br_], writes=[])

              for k_ in range(NT + 2):
                  if k_ < NT:
                      o_a(k_)
                  if 0 <= k_ - 1 < NT:
                      o_b(k_ - 1)
                  if 0 <= k_ - 2 < NT:
                      o_c(k_ - 2)
        except _Stop:
            pass
        if os.environ.get('K_VERBOSE'):
            print('OPCOUNTS', P.cnt, P.ndma, {e: len(P.ops[e]) for e in P.ENGS})
        P.emit(nc)
    return nc


def rel_bucket_np(d):
    d = np.asarray(d)
    nf = np.maximum(d, 1).astype(np.float32)
    large = 16 + (np.log(nf / np.float32(16)) / np.float32(math.log(128 / 16)) * np.float32(16)).astype(np.int32)
    large = np.minimum(large, 31)
    return np.where(d < 16, d, large)


def host_constants():
    cm = np.zeros((32, 383), np.float32)
    for k in range(383):
        d = k - 127
        if d >= 128:
            continue
        bkt = int(rel_bucket_np(max(d, 0)))
        cm[bkt, k] += 8.0
        cm[31, k] -= 8.0
    p = np.arange(128)
    c128 = np.zeros((128, 5, 128), np.float32)
    c128[:, 0, :] = np.eye(128, dtype=np.float32)
    c128[:, 1, :] = (p[:, None] < p[None, :]).astype(np.float32)
    c128[:, 2, :] = (p[:, None] >= p[None, :]).astype(np.float32)
    c128[:, 3, :] = (p[:, None] + p[None, :] == 127).astype(np.float32)
    c128[:, 4, :] = np.where(p[None, :] <= p[:, None], 0.0, -1e30).astype(np.float32)
    pw = (2.0 ** -np.arange(32)).astype(np.float32)
    return cm, c128, pw


_CACHE = {}


def kernel(x, meta_tokens, ln_in_g, ln_in_b, rel_bias, w_in, b_gate, idx_kn_g, idx_kn_b,
           w_pa, w_pb, w_o, ln_g, ln_b, _ncores=8, _nseq=4):
    f = lambda a: np.ascontiguousarray(np.asarray(a, dtype=np.float32))
    x = f(x)
    ncores, nseq = _ncores, _nseq
    key = (nseq,)
    if key not in _CACHE:
        _CACHE[key] = build_program(nseq)
    nc = _CACHE[key]
    cm, c128, pw = host_constants()
    cols = np.zeros((128, 48), np.float32)
    cols[:, 0:8] = f(ln_in_g).reshape(8, 128).T
    cols[:, 8:16] = f(ln_in_b).reshape(8, 128).T
    cols[:, 16:32] = f(b_gate).reshape(16, 128).T
    shared = {
        "meta": f(meta_tokens), "w_in": f(w_in)[0], "w_pa": f(w_pa)[0], "w_pb": f(w_pb)[0], "w_o": f(w_o)[0],
        "rel_bias": f(rel_bias), "ln_in_g": f(ln_in_g), "ln_in_b": f(ln_in_b), "ln_g": f(ln_g)[0], "ln_b": f(ln_b)[0],
        "cols": cols, "ikn_g": f(idx_kn_g)[0], "ikn_b": f(idx_kn_b)[0], "cmat": cm, "c128": c128, "pw": pw,
    }
    in_maps = []
    for c in range(ncores):
        d = dict(shared)
        d["x"] = np.ascontiguousarray(x[c * nseq:(c + 1) * nseq])
        in_maps.append(d)
    res = run_bass_kernel_spmd(nc, in_maps, core_ids=list(range(ncores)))
    return np.concatenate([np.asarray(r["out"]) for r in res.results], axis=0).astype(np.float32)
```

```python
import contextlib
import math
import os
import numpy as np
import concourse.bass as bass
import concourse.mybir as mybir
from concourse.bass_utils import run_bass_kernel_spmd

F32 = mybir.dt.float32
BF16 = mybir.dt.bfloat16
ALU = mybir.AluOpType
AF = mybir.ActivationFunctionType
AX = mybir.AxisListType

T = 2064
NT = 17
D = 1024
SEQ = 2048
NMETA = 16
TOPK = 256
EPS = 1e-5
ALPHA = 2.0 ** 0.25
WI_SCALE = (8 ** -0.5) * (64 ** -0.5)
NIT = 22
TBS = [(0, 512), (512, 512), (1024, 512), (1536, 512), (2048, 16)]
C_QA, C_KA, C_VA, C_ZA, C_QB, C_KB, C_VB, C_ZB, C_QI, C_KI, C_WI, C_GA, C_GB = (
    0, 512, 1024, 1536, 2048, 2560, 2624, 2688, 3200, 3712, 3776, 3784, 4808)
INCOLS = 5832


def tsz(j):
    return 128 if j < 16 else 16


def tiles_of(b):
    return list(range(4 * b, min(4 * b + 4, NT)))


class Buf:
    __slots__ = ("name", "w", "r")

    def __init__(self, name=""):
        self.name = name
        self.w = None
        self.r = []


class _Rec:
    def __init__(self):
        self.call = None

    def __getattr__(self, name):
        def f(*a, **kw):
            self.call = (name, a, kw)
            return self
        return f


def _freeze(fn):
    rec = _Rec()
    fn(rec)
    name, a, kw = rec.call
    return lambda e: getattr(e, name)(*a, **kw)


class Prog:
    ENGS = ("pe", "act", "dve", "pool", "sp")

    def __init__(self):
        self.ops = {e: [] for e in self.ENGS}
        self.cnt = {e: 0 for e in self.ENGS}
        self.pending = {e: set() for e in self.ENGS}
        self.ndma = 0
        self.ndma_e = {}
        self.dma_sem_use = {}
        self.NDMASEM = 24

    def _deps(self, eng, reads, writes):
        deps = set(self.pending[eng])
        self.pending[eng] = set()
        for b in reads:
            if b.w is not None:
                deps.add(b.w)
        for b in writes:
            if b.w is not None:
                deps.add(b.w)
            deps.update(b.r)
        if eng == "pe":
            deps = {d for d in deps if d[0] != "pe"}
        return deps

    def _mark(self, me, reads, writes):
        for b in reads:
            b.r.append(me)
            if len(b.r) > 64:
                best = {}
                for (k, v) in b.r:
                    if best.get(k, 0) < v:
                        best[k] = v
                b.r = list(best.items())
        for b in writes:
            b.w = me
            b.r = []

    def op(self, eng, fn, reads=(), writes=()):
        deps = self._deps(eng, reads, writes)
        self.cnt[eng] += 1
        me = (eng, self.cnt[eng])
        self.ops[eng].append(("op", _freeze(fn), deps, None))
        self._mark(me, reads, writes)
        return me

    def dma(self, eng, fn, reads=(), writes=()):
        deps = self._deps(eng, reads, writes)
        nd = self.ndma_e.get(eng, 0)
        self.ndma_e[eng] = nd + 1
        self.ndma += 1
        k = (eng, nd % self.NDMASEM)
        prev = self.dma_sem_use.get(k, 0)
        if prev:
            deps.add((("dma", k), 16 * prev))
        self.dma_sem_use[k] = prev + 1
        me = (("dma", k), 16 * (prev + 1))
        self.ops[eng].append(("dma", _freeze(fn), deps, k))
        self._mark(me, reads, writes)
        return me

    def barrier(self):
        snap = set()
        for e in self.ENGS:
            if self.cnt[e]:
                snap.add((e, self.cnt[e]))
        for k, c in self.dma_sem_use.items():
            snap.add((("dma", k), 16 * c))
        self.pending["act"] |= snap
        scr = self.bar_scratch
        me = self.op("act", lambda e: e.activation(out=scr, in_=scr, func=AF.Identity))
        for e in self.ENGS:
            if e != "act":
                self.pending[e].add(me)

    def emit(self, nc):
        with contextlib.ExitStack() as st:
            sems = {}
            for e in self.ENGS:
                sems[e] = st.enter_context(nc.semaphore("s_" + e))
            for k in self.dma_sem_use:
                sems[("dma", k)] = st.enter_context(nc.semaphore("s_dma_%s%d" % k))
            block = st.enter_context(nc.Block())
            regs = {"pe": block.tensor, "act": block.scalar, "dve": block.vector,
                    "pool": block.gpsimd, "sp": block.sync}
            for e in self.ENGS:
                ops = self.ops[e]
                if not ops and e != "sp":
                    continue

                def body(engine, ops=ops, e=e):
                    waited = {}
                    for kind, fn, deps, k in ops:
                        best = {}
                        for (sk, val) in deps:
                            if best.get(sk, 0) < val:
                                best[sk] = val
                        for sk in sorted(best, key=str):
                            val = best[sk]
                            if waited.get(sk, 0) >= val:
                                continue
                            engine.wait_ge(sems[sk], val)
                            waited[sk] = val
                        ins = fn(engine)
                        if kind == "op":
                            ins.then_inc(sems[e], 1)
                        else:
                            ins.then_inc(sems[("dma", k)], 16)
                    if e == "sp":
                        for k2, c2 in self.dma_sem_use.items():
                            engine.wait_ge(sems[("dma", k2)], 16 * c2)
                        for e2 in self.ENGS:
                            if e2 != "sp" and self.cnt[e2]:
                                engine.wait_ge(sems[e2], self.cnt[e2])
                regs[e](body)


class _Stop(Exception):
    pass


_TICK = [0]


def tick():
    _TICK[0] += 1
    stop_at('d%d' % _TICK[0])


def stop_at(name):
    if os.environ.get("K_STOP", "") == name:
        raise _Stop()


def interleave(gens):
    st = [[g, est, 0] for g, est in gens]
    while st:
        st.sort(key=lambda x: x[2] / x[1])
        g = st[0]
        try:
            next(g[0])
            g[2] += 1
        except StopIteration:
            st.pop(0)


class Arena:
    def __init__(self, ap, nelem):
        self.ap = ap
        self.n = nelem
        self.off = 0

    def reset(self):
        self.off = 0

    def alloc(self, shape, dt):
        n = 1
        for s in shape:
            n *= s
        ne = n * (2 if dt == F32 else 1)
        ne_al = (ne + 15) // 16 * 16
        assert self.off + ne_al <= self.n, ("arena overflow", self.off, ne_al, self.n)
        v = self.ap[:, self.off:self.off + ne]
        self.off += ne_al
        if dt == F32:
            v = v.bitcast(F32)
        if len(shape) == 2:
            v = v.rearrange("p (a b) -> p a b", a=shape[0])
        elif len(shape) == 3:
            v = v.rearrange("p (a b c) -> p a b c", a=shape[0], b=shape[1])
        return v


def build_program(nseq):
    nc = bass.Bass("TRN2", target_bir_lowering=False)
    dram = lambda name, shape, kind="ExternalInput": nc.dram_tensor(name, shape, F32, kind=kind).ap()
    x_d = dram("x", [nseq, SEQ, D])
    meta_d = dram("meta", [NMETA, D])
    win_d = dram("w_in", [D, INCOLS])
    wpa_d = dram("w_pa", [512, D])
    wpb_d = dram("w_pb", [512, D])
    wo_d = dram("w_o", [D, D])
    relb_d = dram("rel_bias", [32, 8])
    lning_d = dram("ln_in_g", [D])
    lninb_d = dram("ln_in_b", [D])
    lng_d = dram("ln_g", [D])
    lnb_d = dram("ln_b", [D])
    cols_d = dram("cols", [128, 48])
    ikg_d = dram("ikn_g", [64])
    ikb_d = dram("ikn_b", [64])
    cmat_d = dram("cmat", [32, 383])
    c128_d = dram("c128", [128, 5, 128])
    pw_d = dram("pw", [32])
    out_d = dram("out", [nseq, SEQ, D], kind="ExternalOutput")
    gscr_d = nc.dram_tensor("gscr", [8, 383], F32, kind="Internal").ap()

    P = Prog()
    st = contextlib.ExitStack()
    with st:
        sbt = lambda name, shape, dt: st.enter_context(nc.sbuf_tensor("sb_" + name, shape, dt))
        hT = sbt("hT", [128, 8, T], BF16)
        yT = sbt("yT", [128, 2, 4, T], BF16)
        ident = sbt("ident", [128, 128], BF16)
        identf = sbt("identf", [128, 128], F32)
        trimask = sbt("trimask", [128, 128], BF16)
        tri8 = sbt("tri8", [128, 128], BF16)
        ones8 = sbt("ones8", [128, 128], BF16)
        ones64 = sbt("ones64", [128, 64], BF16)
        negmask = sbt("negmask", [128, 128], F32)
        Jm = sbt("Jm", [128, 128], F32)
        c128f = sbt("c128f", [128, 5, 128], F32)
        BD = sbt("BD", [128, 8, 2, 2, 128], BF16)
        cols = sbt("cols", [128, 48], F32)
        negbg = sbt("negbg", [128, 16], F32)
        b31 = sbt("b31", [128, 8], F32)
        pw = sbt("pw", [128, 32], F32)
        ikg = sbt("ikg", [128, 64], F32)
        ikb = sbt("ikb", [128, 64], F32)
        relb = sbt("relb", [32, 8], F32)
        cmat = sbt("cmat", [32, 383], F32)
        gvec = sbt("gvec", [8, 383], F32)
        ARENA_N = 64400
        arena_t = sbt("arena", [128, ARENA_N], BF16)
        A = Arena(arena_t, ARENA_N)
        ps = st.enter_context(nc.psum_tensor("ps", [128, 4096], F32))
        bank = [ps[:, 512 * k:512 * (k + 1)] for k in range(8)]
        Bps = [Buf("ps%d" % k) for k in range(8)]
        Bconst = Buf("const")
        Bh = [Buf("h%d" % b) for b in range(5)]
        By = [[[Buf() for b in range(5)] for pr in range(4)] for br in range(2)]

        def mm(out, lhsT, rhs, start, stop, reads, writes):
            return P.op("pe", lambda e: e.matmul(out, lhsT=lhsT, rhs=rhs, start=start, stop=stop),
                        reads=reads, writes=writes)

        def act(out, in_, func, reads, writes, bias=0.0, scale=1.0):
            return P.op("act", lambda e: e.activation(out=out, in_=in_, func=func, bias=bias, scale=scale),
                        reads=reads, writes=writes)

        def dve(fn, reads, writes):
            return P.op("dve", fn, reads=reads, writes=writes)

        def pool(fn, reads, writes):
            return P.op("pool", fn, reads=reads, writes=writes)

        def ld(out, in_, writes, reads=()):
            return P.dma("sp", lambda e: e.dma_start(out=out, in_=in_), reads=reads, writes=writes)

        def ldc(out, in_, writes, reads=()):
            return P.dma("pool", lambda e: e.dma_start(out=out, in_=in_), reads=reads, writes=writes)

        P.bar_scratch = cols[0:1, 40:48]
        ld(c128f[:], c128_d, [Bconst])
        ld(cols[:], cols_d, [Bconst])
        ld(relb[:], relb_d, [Bconst])
        ld(cmat[:], cmat_d, [Bconst])
        ld(b31[:], relb_d[31, :].partition_broadcast(128), [Bconst])
        ld(pw[:], pw_d.partition_broadcast(128), [Bconst])
        ld(ikg[:], ikg_d.partition_broadcast(128), [Bconst])
        ld(ikb[:], ikb_d.partition_broadcast(128), [Bconst])
        dve(lambda e: e.tensor_copy(out=ident[:], in_=c128f[:, 0, :]), [Bconst], [Bconst])
        dve(lambda e: e.tensor_copy(out=identf[:], in_=c128f[:, 0, :]), [Bconst], [Bconst])
        dve(lambda e: e.tensor_copy(out=trimask[:], in_=c128f[:, 1, :]), [Bconst], [Bconst])
        dve(lambda e: e.tensor_scalar(out=tri8[:], in0=c128f[:, 2, :], scalar1=-8.0, scalar2=None, op0=ALU.mult),
            [Bconst], [Bconst])
        dve(lambda e: e.tensor_copy(out=Jm[:], in_=c128f[:, 3, :]), [Bconst], [Bconst])
        dve(lambda e: e.tensor_copy(out=negmask[:], in_=c128f[:, 4, :]), [Bconst], [Bconst])
        dve(lambda e: e.memset(ones8[:], -8.0), [], [Bconst])
        dve(lambda e: e.memset(ones64[:], 1.0), [], [Bconst])
        dve(lambda e: e.tensor_scalar(out=negbg[:], in0=cols[:, 16:32], scalar1=-1.0, scalar2=None, op0=ALU.mult),
            [Bconst], [Bconst])
        Bg = Buf("gscr")
        mm(bank[0][0:8, 0:383], relb[:, :], cmat[:, :], True, True, [Bconst], [Bps[0]])
        dve(lambda e: e.tensor_copy(out=gvec[:], in_=bank[0][0:8, 0:383]), [Bps[0]], [Bconst])
        ld(gscr_d, gvec[:], [Bg], reads=[Bconst])
        A.reset()
        hank = A.alloc([16, 128], F32)
        Bhank = Buf("hank")
        for h in range(8):
            for off in range(2):
                src = bass.AP(tensor=gscr_d.tensor, offset=h * 383 + 128 * off, ap=[[1, 128], [1, 128]])
                ld(hank[:, h * 2 + off, :], src, [Bhank], reads=[Bg])
        for h in range(8):
            for off in range(2):
                k = (h * 2 + off) % 4
                mm(bank[k][:, 0:128], Jm[:, :], hank[:, h * 2 + off, :], True, True, [Bhank, Bconst], [Bps[k]])
                dve(lambda e, h=h, off=off, k=k: e.tensor_copy(out=BD[:, h, off, 0, :], in_=bank[k][:, 0:128]),
                    [Bps[k]], [Bconst])
                dve(lambda e, h=h, off=off, k=k: e.tensor_tensor(out=BD[:, h, off, 1, :], in0=bank[k][:, 0:128],
                                                                  in1=BD[:, h, off, 0, :], op=ALU.subtract),
                    [Bps[k], Bconst], [Bconst])

        def ln_stats(xt, rows, stats, mv, rstd, Bx, Bs):
            for c in range(2):
                dve(lambda e, c=c: e.bn_stats(out=stats[0:rows, c, :], in_=xt[0:rows, 512 * c:512 * (c + 1)]),
                    [Bx], [Bs])
            dve(lambda e: e.bn_aggr(out=mv[0:rows, :], in_=stats[0:rows, :, :]), [Bs], [Bs])
            act(rstd[0:rows, :], mv[0:rows, 1:2], AF.Ln, [Bs], [Bs], bias=EPS)
            act(rstd[0:rows, :], rstd[0:rows, :], AF.Exp, [Bs], [Bs], scale=-0.5)

        def load_x_tile(seq, j, xt, Bx):
            rows = tsz(j)
            if j == 0:
                ld(xt[0:16, :], meta_d, [Bx])
                ld(xt[16:128, :], x_d[seq, 0:112, :], [Bx])
            else:
                r0 = 128 * j - 16
                ld(xt[0:rows, :], x_d[seq, r0:r0 + rows, :], [Bx])

        def load_w(dst, src_rows_ap, ncol, Bw, krows=8):
            ldc(dst, src_rows_ap.rearrange("(c p) n -> p c n", p=128), [Bw])

        try:
          for seq in range(nseq):
              P.barrier()
              A.reset()
              xbuf = [A.alloc([1024], F32) for _ in range(2)]
              xn = [A.alloc([1024], BF16) for _ in range(2)]
              stats = A.alloc([2, 6], F32)
              mv = A.alloc([2], F32)
              rstd = A.alloc([1], F32)
              wbuf = [A.alloc([8, 512], BF16) for _ in range(2)]
              qaT = A.alloc([4, T], BF16)
              kaT = A.alloc([4, T], BF16)
              va = A.alloc([NT, 512], BF16)
              def sb_scratch(kO, kR):
                  return dict(
                      ek=[A.alloc([512], BF16) for _ in range(4)], Bek=[Buf() for _ in range(4)],
                      sp=[A.alloc([512], BF16) for _ in range(6)], Bsp=[Buf() for _ in range(6)],
                      ec=[A.alloc([512], BF16) for _ in range(2)], Bec=[Buf() for _ in range(2)],
                      a=[A.alloc([512], BF16) for _ in range(4)], Ba=[Buf() for _ in range(4)],
                      acc=A.alloc([512], F32), Rrun=A.alloc([512], F32), Esc=A.alloc([512], F32),
                      tmp=A.alloc([512], F32), Bacc=Buf(), BR=Buf(), BE=Buf(), Btmp=Buf(),
                      kO=kO, kR=kR, c1=[0], c2=[0])

              sbsc = [sb_scratch(4, 5), sb_scratch(6, 7)]
              Bxb = [Buf(), Buf()]
              Bxn = [Buf(), Buf()]
              Bst = Buf()
              Bw = [Buf(), Buf()]
              Bqa = [[Buf() for b in range(5)] for pr in range(4)]
              Bka = [[Buf() for b in range(5)] for pr in range(4)]
              Bva = [Buf() for b in range(5)]

              stats_l = [stats, A.alloc([2, 6], F32)]
              mv_l = [mv, A.alloc([2], F32)]
              rstd_l = [rstd, A.alloc([1], F32)]
              Bst_l = [Bst, Buf()]

              def ln_a(j):
                  load_x_tile(seq, j, xbuf[j % 2], Bxb[j % 2])
                  ln_stats(xbuf[j % 2], tsz(j), stats_l[j % 2], mv_l[j % 2], rstd_l[j % 2], Bxb[j % 2], Bst_l[j % 2])

              def ln_b(j):
                  rows = tsz(j)
                  xt, bx = xbuf[j % 2], Bxb[j % 2]
                  xb, bxn = xn[j % 2], Bxn[j % 2]
                  mv_, rs_ = mv_l[j % 2], rstd_l[j % 2]
                  dve(lambda e: e.tensor_scalar(
                      out=xb[0:rows, :], in0=xt[0:rows, :], scalar1=mv_[0:rows, 0:1], scalar2=rs_[0:rows, 0:1],
                      op0=ALU.subtract, op1=ALU.mult), [bx, Bst_l[j % 2]], [bxn])
                  pk = 6 + (j % 2)
                  Xb = bank[pk].bitcast(BF16)
                  for c in range(8):
                      P.op("pe", lambda e, c=c: e.transpose(
                          out=Xb[:, c * 128:c * 128 + rows], in_=xb[0:rows, c * 128:(c + 1) * 128],
                          identity=ident[0:rows, 0:rows]), reads=[bxn, Bconst], writes=[Bps[pk]])

              def ln_c(j):
                  rows = tsz(j)
                  pk = 6 + (j % 2)
                  Xb = bank[pk].bitcast(BF16)
                  for c in range(8):
                      dve(lambda e, c=c: e.tensor_scalar(
                          out=hT[:, c, 128 * j:128 * j + rows], in0=Xb[:, c * 128:c * 128 + rows],
                          scalar1=cols[:, c:c + 1], scalar2=cols[:, 8 + c:9 + c], op0=ALU.mult, op1=ALU.add),
                          [Bps[pk], Bconst], [Bh[j // 4]])

              for j in range(NT + 2):
                  if j < NT:
                      ln_a(j)
                  if 0 <= j - 1 < NT:
                      ln_b(j - 1)
                  if 0 <= j - 2 < NT:
                      ln_c(j - 2)

              stop_at('ln')
              zrot = [0]

              def fm_project(col0, dstT, Bdst, nun=1, dup64=False):
                  u = zrot[0] % 2
                  wb, bw = wbuf[u], Bw[u]
                  if dup64:
                      ldc(wb[:, :, 0:64], win_d[:, col0:col0 + 64].rearrange("(c p) n -> p c n", p=128), [bw])
                      ldc(wb[:, :, 64:128], win_d[:, col0:col0 + 64].rearrange("(c p) n -> p c n", p=128), [bw])
                      nm = 1
                  else:
                      ldc(wb[:, :, :], win_d[:, col0:col0 + 512].rearrange("(c p) n -> p c n", p=128), [bw])
                      nm = 4
                  zrot[0] += 1
                  for b, (t0, n) in enumerate(TBS):
                      for m in range(nm):
                          k = zrot[0] % 4
                          zrot[0] += 1
                          for c in range(8):
                              mm(bank[k][:, 0:n], wb[:, c, m * 128:(m + 1) * 128], hT[:, c, t0:t0 + n],
                                 c == 0, c == 7, [bw, Bh[b]], [Bps[k]])
                          if dup64:
                              act(dstT[:, t0:t0 + n], bank[k][:, 0:n], AF.Identity, [Bps[k]], [Bdst[b]])
                          else:
                              act(dstT[:, m, t0:t0 + n], bank[k][:, 0:n], AF.Identity, [Bps[k]], [Bdst[m][b]])

              fm_project(C_QA, qaT, Bqa)
              fm_project(C_KA, kaT, Bka)
              u = zrot[0] % 2
              zrot[0] += 1
              wb, bw = wbuf[u], Bw[u]
              ldc(wb[:, :, :], win_d[:, C_VA:C_VA + 512].rearrange("(c p) n -> p c n", p=128), [bw])
              for j in range(NT):
                  rows = tsz(j)
                  k = 4 + (j % 2)
                  for c in range(8):
                      mm(bank[k][0:rows, :], hT[:, c, 128 * j:128 * j + rows], wb[:, c, :], c == 0, c == 7,
                         [bw, Bh[j // 4]], [Bps[k]])
                  act(va[0:rows, j, :], bank[k][0:rows, :], AF.Identity, [Bps[k]], [Bva[j // 4]])

              stop_at('p1a')
              ZB, CB = [0, 1], [2, 3]
              zc, cc = [0], [0]

              def sb_stream(b, pr, sc):
                  t0, n = TBS[b]
                  kO, kR = sc['kO'], sc['kR']
                  acc, Rrun, Esc, tmp = sc['acc'], sc['Rrun'], sc['Esc'], sc['tmp']
                  Bacc, BR, BE, Btmp = sc['Bacc'], sc['BR'], sc['BE'], sc['Btmp']
                  pool(lambda e: e.memset(acc[:, 0:n], 0.0), [], [Bacc])
                  pool(lambda e: e.memset(Rrun[:, 0:n], 0.0), [], [BR])
                  lastS = tiles_of(b)[-1]
                  units = list(range(lastS, -1, -1))
                  st = {}

                  def s1(S):
                      ks = tsz(S)
                      c0 = max(0, 128 * (S - 4 * b))
                      zk = []
                      for hh in range(2):
                          pb = 64 * hh
                          k = ZB[hh]
                          zk.append(k)
                          mm(bank[k][0:ks, c0:n], kaT[pb:pb + 64, pr, 128 * S:128 * S + ks],
                             qaT[pb:pb + 64, pr, t0 + c0:t0 + n], True, True, [Bka[pr][S // 4], Bqa[pr][b]], [Bps[k]])
                      rec = []
                      for hh in range(2):
                          i1 = sc['c1'][0]
                          sc['c1'][0] += 1
                          ek_t, Bek = sc['ek'][i1 % 4], sc['Bek'][i1 % 4]
                          sp_t, Bsp = sc['sp'][i1 % 6], sc['Bsp'][i1 % 6]
                          k = zk[hh]
                          act(ek_t[0:ks, c0:n], bank[k][0:ks, c0:n], AF.Exp, [Bps[k]], [Bek], scale=0.125)
                          if S >= 4 * b:
                              w = min(128, n - c0)
                              pool(lambda e: e.tensor_tensor(out=ek_t[0:ks, c0:c0 + w], in0=ek_t[0:ks, c0:c0 + w],
                                                             in1=trimask[0:ks, 0:w], op=ALU.mult), [Bek, Bconst], [Bek])
                          rec.append([ek_t, Bek, sp_t, Bsp])
                      for hh in range(2):
                          ek_t, Bek, sp_t, Bsp = rec[hh]
                          act(sp_t[0:ks, c0:n], ek_t[0:ks, c0:n], AF.Ln, [Bek], [Bsp], bias=1.0)
                      st[S] = (ks, c0, rec)

                  def s2(S):
                      ks, c0, rec = st[S]
                      for hh in range(2):
                          ek_t, Bek, sp_t, Bsp = rec[hh]
                          k2 = CB[hh]
                          mm(bank[k2][0:ks, c0:n], tri8[0:ks, 0:ks], sp_t[0:ks, c0:n], True, True, [Bsp, Bconst], [Bps[k2]])
                      for hh in range(2):
                          ek_t, Bek, sp_t, Bsp = rec[hh]
                          k2 = CB[hh]
                          i2 = sc['c2'][0]
                          sc['c2'][0] += 1
                          ec_t, Bec = sc['ec'][i2 % 2], sc['Bec'][i2 % 2]
                          a_t, Ba_t = sc['a'][i2 % 4], sc['Ba'][i2 % 4]
                          act(ec_t[0:ks, c0:n], bank[k2][0:ks, c0:n], AF.Exp, [Bps[k2]], [Bec], scale=0.125)
                          dve(lambda e: e.tensor_tensor(out=a_t[0:ks, c0:n], in0=ek_t[0:ks, c0:n], in1=ec_t[0:ks, c0:n],
                                                        op=ALU.mult), [Bek, Bec], [Ba_t])
                          rec[hh] = [a_t, Ba_t, sp_t, Bsp]

                  def s3(S):
                      ks, c0, rec = st.pop(S)
                      for hh in range(2):
                          a_t, Ba_t, sp_t, Bsp = rec[hh]
                          pb, h = 64 * hh, 2 * pr + hh
                          mm(bank[kO][pb:pb + 64, c0:n], va[0:ks, S, h * 64:(h + 1) * 64], a_t[0:ks, c0:n], True, True,
                             [Bva[S // 4], Ba_t], [Bps[kO]])
                      for hh in range(2):
                          a_t, Ba_t, sp_t, Bsp = rec[hh]
                          pb = 64 * hh
                          mm(bank[kR][pb:pb + 64, c0:n], ones64[0:ks, :], sp_t[0:ks, c0:n], True, True,
                             [Bsp, Bconst], [Bps[kR]])
                      act(Esc[:, c0:n], Rrun[:, c0:n], AF.Exp, [BR], [BE], scale=-1.0)
                      dve(lambda e: e.tensor_tensor(out=tmp[:, c0:n], in0=bank[kO][:, c0:n], in1=Esc[:, c0:n],
                                                    op=ALU.mult), [Bps[kO], BE], [Btmp])
                      dve(lambda e: e.tensor_tensor(out=acc[:, c0:n], in0=acc[:, c0:n], in1=tmp[:, c0:n],
                                                    op=ALU.add), [Bacc, Btmp], [Bacc])
                      if S > 0:
                          dve(lambda e: e.tensor_tensor(out=Rrun[:, c0:n], in0=Rrun[:, c0:n],
                                                        in1=bank[kR][:, c0:n], op=ALU.add), [BR, Bps[kR]], [BR])
                      else:
                          pool(lambda e: e.tensor_copy(out=yT[:, 0, pr, t0:t0 + n], in_=acc[:, 0:n]),
                               [Bacc], [By[0][pr][b]])

                  nu = len(units)
                  for k in range(nu + 2):
                      if k < nu:
                          s1(units[k])
                      if 0 <= k - 1 < nu:
                          s2(units[k - 1])
                      if 0 <= k - 2 < nu:
                          s3(units[k - 2])
                      yield

              def chain_sb(prs, sc):
                  for b in range(len(TBS)):
                      for pr in prs:
                          yield from sb_stream(b, pr, sc)

              interleave([(chain_sb((0, 2), sbsc[0]), 1.0), (chain_sb((1, 3), sbsc[1]), 1.0)])

              stop_at('sb')
              P.barrier()
              A.reset()
              wbuf = [A.alloc([8, 512], BF16) for _ in range(2)]
              Bw = [Buf(), Buf()]
              wtm = A.alloc([8, 136], BF16)
              Bwtm = Buf()
              qbT = A.alloc([4, T], BF16)
              qiT = A.alloc([4, T], BF16)
              kbT = A.alloc([T], BF16)
              kiT = A.alloc([T], BF16)
              vb = A.alloc([NT, 64], BF16)
              wi = A.alloc([NT, 8], F32)
              kin = A.alloc([128], BF16)
              kif = A.alloc([64], F32)
              ksq = A.alloc([64], F32)
              Bksq = Buf()
              stats = A.alloc([1, 6], F32)
              mv = A.alloc([2], F32)
              rstd = A.alloc([1], F32)
              score_l = [A.alloc([2176], F32), A.alloc([2176], F32)]
              Bscore_l = [Buf(), Buf()]
              maskb = [A.alloc([2176], BF16) for _ in range(2)]
              maskT = A.alloc([NT, 512], BF16)
              rl = [A.alloc([512], F32) for _ in range(2)]
              pbuf = [A.alloc([512], BF16) for _ in range(3)]
              pmb = [A.alloc([512], BF16) for _ in range(3)]
              rden = A.alloc([512], F32)
              bis_l = [A.alloc([40], F32), A.alloc([40], F32)]
              Bbis_l = [Buf(), Buf()]
              Bqb = [[Buf() for b in range(5)] for pr in range(4)]
              Bqi = [[Buf() for b in range(5)] for pr in range(4)]
              Bkb = [Buf() for b in range(5)]
              Bki = [Buf() for b in range(5)]
              Bvb = [Buf() for b in range(5)]
              Bwi = [Buf() for b in range(5)]
              Bkin, Bst, Brden = Buf(), Buf(), Buf()
              Bmb = [Buf(), Buf()]
              Brl = [Buf(), Buf()]
              Bp = [Buf(), Buf(), Buf()]
              Bpm = [Buf(), Buf(), Buf()]

              zrot[0] = 0
              fm_project(C_QB, qbT, Bqb)
              fm_project(C_QI, qiT, Bqi)
              stop_at('b1')
              fm_project(C_KB, kbT, Bkb, dup64=True)
              stop_at('b2')
              ldc(wtm[:, :, 0:64], win_d[:, C_VB:C_VB + 64].rearrange("(c p) n -> p c n", p=128), [Bwtm])
              ldc(wtm[:, :, 64:136], win_d[:, C_KI:C_KI + 72].rearrange("(c p) n -> p c n", p=128), [Bwtm])
              for j in range(NT):
                  rows = tsz(j)
                  k = 4 + (j % 2)
                  for c in range(8):
                      mm(bank[k][0:rows, 0:136], hT[:, c, 128 * j:128 * j + rows], wtm[:, c, :], c == 0, c == 7,
                         [Bwtm, Bh[j // 4]], [Bps[k]])
                  act(vb[0:rows, j, :], bank[k][0:rows, 0:64], AF.Identity, [Bps[k]], [Bvb[j // 4]])
                  act(wi[0:rows, j, :], bank[k][0:rows, 128:136], AF.Identity, [Bps[k]], [Bwi[j // 4]], scale=WI_SCALE)
                  stop_at('c1')
                  act(kif[0:rows, :], bank[k][0:rows, 64:128], AF.Identity, [Bps[k]], [Bkin])
                  tick()
                  dve(lambda e, rows=rows: e.tensor_reduce(out=mv[0:rows, 0:1], in_=kif[0:rows, :], axis=AX.X, op=ALU.add),
                      [Bkin], [Bst])
                  tick()
                  dve(lambda e, rows=rows: e.tensor_scalar(out=mv[0:rows, 0:1], in0=mv[0:rows, 0:1], scalar1=1.0 / 64,
                                                           scalar2=None, op0=ALU.mult), [Bst], [Bst])
                  tick()
                  dve(lambda e, rows=rows: e.tensor_scalar(out=kif[0:rows, :], in0=kif[0:rows, :], scalar1=mv[0:rows, 0:1],
                                                           scalar2=None, op0=ALU.subtract), [Bkin, Bst], [Bkin])
                  tick()
                  dve(lambda e, rows=rows: e.tensor_tensor(out=ksq[0:rows, :], in0=kif[0:rows, :], in1=kif[0:rows, :],
                                                           op=ALU.mult), [Bkin], [Bksq])
                  tick()
                  dve(lambda e, rows=rows: e.tensor_reduce(out=mv[0:rows, 1:2], in_=ksq[0:rows, :], axis=AX.X, op=ALU.add),
                      [Bksq], [Bst])
                  tick()
                  act(rstd[0:rows, :], mv[0:rows, 1:2], AF.Ln, [Bst], [Bst], bias=EPS, scale=1.0 / 64)
                  tick()
                  act(rstd[0:rows, :], rstd[0:rows, :], AF.Exp, [Bst], [Bst], scale=-0.5)
                  tick()
                  dve(lambda e, rows=rows: e.tensor_scalar(out=kif[0:rows, :], in0=kif[0:rows, :], scalar1=rstd[0:rows, 0:1],
                                                           scalar2=None, op0=ALU.mult), [Bkin, Bst], [Bkin])
                  tick()
                  dve(lambda e, rows=rows: e.tensor_tensor(out=kif[0:rows, :], in0=kif[0:rows, :], in1=ikg[0:rows, :],
                                                           op=ALU.mult), [Bkin, Bconst], [Bkin])
                  tick()
                  dve(lambda e, rows=rows: e.tensor_tensor(out=kin[0:rows, 0:64], in0=kif[0:rows, :], in1=ikb[0:rows, :],
                                                           op=ALU.add), [Bkin, Bconst], [Bkin])
                  tick()
                  dve(lambda e, rows=rows: e.tensor_copy(out=kin[0:rows, 64:128], in_=kin[0:rows, 0:64]), [Bkin], [Bkin])
                  stop_at('c2')
                  pk = 6 + (j % 2)
                  Xb = bank[pk].bitcast(BF16)
                  P.op("pe", lambda e, rows=rows, Xb=Xb: e.transpose(out=Xb[:, 0:rows], in_=kin[0:rows, :],
                                                                      identity=ident[0:rows, 0:rows]),
                       reads=[Bkin, Bconst], writes=[Bps[pk]])
                  act(kiT[:, 128 * j:128 * j + rows], Xb[:, 0:rows], AF.Identity, [Bps[pk]], [Bki[j // 4]])

              stop_at('p1b')
              maskTs = [maskT, A.ap[:, 0:NT * 512].rearrange("p (a b) -> p a b", a=NT)]
              BmT = [[Buf() for _ in range(NT)] for _ in range(2)]
              alias_w = [[], [Bw[0], Bw[1], Bwtm]]
              mzr = [0]
              MB = [2, 3]

              def on_act(i, pair):
                  return len(pair) == 2 and pair[0] >= 2 and i == pair[1]

              def mask_stream(b):
                  ms = b % 2
                  mT = maskTs[ms]
                  tl = tiles_of(b)
                  for p0 in range(0, len(tl), 2):
                      pair = tl[p0:p0 + 2]
                      for i in pair:
                          rows = tsz(i)
                          L = 128 * i + rows
                          score, Bscore = score_l[i % 2], Bscore_l[i % 2]
                          bis, Bbis = bis_l[i % 2], Bbis_l[i % 2]
                          mb, bmb = maskb[i % 2], Bmb[i % 2]
                          sblocks = [(s0, min(512, L - s0)) for s0 in range(0, L, 512)]
                          for (s0, sn) in sblocks:
                              for h in range(8):
                                  pr, pb = h // 2, 64 * (h % 2)
                                  k = MB[mzr[0] % 2]
                                  r2 = mzr[0] % 2
                                  mzr[0] += 1
                                  mm(bank[k][0:rows, 0:sn], qiT[pb:pb + 64, pr, 128 * i:128 * i + rows],
                                     kiT[pb:pb + 64, s0:s0 + sn], True, True,
                                     [Bqi[pr][b]] + [Bki[bb] for bb in range(s0 // 512, min(4, (s0 + sn - 1) // 512) + 1)],
                                     [Bps[k]])
                                  act(rl[r2][0:rows, 0:sn], bank[k][0:rows, 0:sn], AF.Relu, [Bps[k]], [Brl[r2]])
                                  if h == 0:
                                      dve(lambda e: e.tensor_scalar(
                                          out=score[0:rows, s0:s0 + sn], in0=rl[r2][0:rows, 0:sn],
                                          scalar1=wi[0:rows, i, h:h + 1], scalar2=None, op0=ALU.mult),
                                          [Brl[r2], Bwi[b]], [Bscore])
                                  else:
                                      dve(lambda e: e.scalar_tensor_tensor(
                                          out=score[0:rows, s0:s0 + sn], in0=rl[r2][0:rows, 0:sn],
                                          scalar=wi[0:rows, i, h:h + 1], in1=score[0:rows, s0:s0 + sn],
                                          op0=ALU.mult, op1=ALU.add), [Brl[r2], Bwi[b], Bscore], [Bscore])
                                  yield
                          if i >= 2:
                              dve(lambda e: e.reduce_max(out=bis[0:rows, 0:1], in_=score[0:rows, 0:L], axis=AX.X),
                                  [Bscore], [Bbis])
                              dve(lambda e: e.tensor_reduce(out=bis[0:rows, 1:2], in_=score[0:rows, 0:L],
                                                            axis=AX.X, op=ALU.min), [Bscore], [Bbis])
                          dve(lambda e: e.tensor_tensor(out=score[0:rows, 128 * i:128 * i + rows],
                                                        in0=score[0:rows, 128 * i:128 * i + rows],
                                                        in1=negmask[0:rows, 0:rows], op=ALU.add),
                              [Bscore, Bconst], [Bscore])
                          if i < 2:
                              dve(lambda e: e.tensor_scalar(out=mb[0:rows, 0:L], in0=score[0:rows, 0:L],
                                                            scalar1=-1e29, scalar2=None, op0=ALU.is_ge),
                                  [Bscore], [bmb])
                          else:
                              dve(lambda e: e.tensor_tensor(out=bis[0:rows, 2:3], in0=bis[0:rows, 0:1],
                                                            in1=bis[0:rows, 1:2], op=ALU.subtract), [Bbis], [Bbis])
                              dve(lambda e: e.tensor_scalar(out=bis[0:rows, 8:8 + NIT + 1], in0=pw[0:rows, 0:NIT + 1],
                                                            scalar1=bis[0:rows, 2:3], scalar2=None, op0=ALU.mult),
                                  [Bbis, Bconst], [Bbis])
                              if on_act(i, pair):
                                  dve(lambda e: e.tensor_scalar(out=bis[0:rows, 3:4], in0=bis[0:rows, 1:2], scalar1=-1.0,
                                                                scalar2=bis[0:rows, 9:10], op0=ALU.mult,
                                                                op1=ALU.subtract), [Bbis], [Bbis])
                              else:
                                  dve(lambda e: e.tensor_tensor(out=bis[0:rows, 3:4], in0=bis[0:rows, 1:2],
                                                                in1=bis[0:rows, 9:10], op=ALU.add), [Bbis], [Bbis])
                          yield
                      active = [i for i in pair if i >= 2]
                      for it in range(1, NIT + 1):
                          for i in active:
                              rows = tsz(i)
                              L = 128 * i + rows
                              score, Bscore = score_l[i % 2], Bscore_l[i % 2]
                              bis, Bbis = bis_l[i % 2], Bbis_l[i % 2]
                              mb, bmb = maskb[i % 2], Bmb[i % 2]
                              if on_act(i, pair):
                                  P.op("act", lambda e: e.activation(
                                      out=mb[0:rows, 0:L], in_=score[0:rows, 0:L], func=AF.Sign, bias=bis[0:rows, 3:4],
                                      scale=1.0, accum_out=bis[0:rows, 4:5]), reads=[Bscore, Bbis], writes=[bmb, Bbis])
                              else:
                                  dve(lambda e: e.tensor_scalar(
                                      out=mb[0:rows, 0:L], in0=score[0:rows, 0:L], scalar1=bis[0:rows, 3:4], scalar2=0.0,
                                      op0=ALU.is_ge, op1=ALU.add, accum_out=bis[0:rows, 4:5]), [Bscore, Bbis], [bmb, Bbis])
                          for i in active:
                              rows = tsz(i)
                              bis, Bbis = bis_l[i % 2], Bbis_l[i % 2]
                              if on_act(i, pair):
                                  thr = float(2 * TOPK - 1 - (128 * i + rows))
                                  last_ = (it == NIT)
                                  dve(lambda e: e.tensor_scalar(
                                      out=bis[0:rows, 5:6], in0=bis[0:rows, 4:5], scalar1=thr, scalar2=(0.0 if last_ else 0.5),
                                      op0=ALU.is_lt, op1=ALU.subtract), [Bbis], [Bbis])
                                  dve(lambda e: e.scalar_tensor_tensor(
                                      out=(bis[0:rows, 6:7] if last_ else bis[0:rows, 3:4]), in0=bis[0:rows, 5:6],
                                      scalar=bis[0:rows, 8 + it:9 + it], in1=bis[0:rows, 3:4],
                                      op0=ALU.mult, op1=ALU.add), [Bbis], [Bbis])
                                  continue
                              if it < NIT:
                                  dve(lambda e: e.tensor_scalar(
                                      out=bis[0:rows, 5:6], in0=bis[0:rows, 4:5], scalar1=TOPK - 0.5, scalar2=0.5,
                                      op0=ALU.is_ge, op1=ALU.subtract), [Bbis], [Bbis])
                                  dve(lambda e: e.scalar_tensor_tensor(
                                      out=bis[0:rows, 3:4], in0=bis[0:rows, 5:6], scalar=bis[0:rows, 8 + it:9 + it],
                                      in1=bis[0:rows, 3:4], op0=ALU.mult, op1=ALU.add), [Bbis], [Bbis])
                              else:
                                  dve(lambda e: e.tensor_scalar(
                                      out=bis[0:rows, 5:6], in0=bis[0:rows, 4:5], scalar1=TOPK - 0.5, scalar2=1.0,
                                      op0=ALU.is_ge, op1=ALU.subtract), [Bbis], [Bbis])
                                  dve(lambda e: e.scalar_tensor_tensor(
                                      out=bis[0:rows, 6:7], in0=bis[0:rows, 5:6], scalar=bis[0:rows, 8 + it:9 + it],
                                      in1=bis[0:rows, 3:4], op0=ALU.mult, op1=ALU.add), [Bbis], [Bbis])
                          if active:
                              yield
                      for i in pair:
                          rows = tsz(i)
                          L = 128 * i + rows
                          lc = 128 * (i - 4 * b)
                          score, Bscore = score_l[i % 2], Bscore_l[i % 2]
                          bis, Bbis = bis_l[i % 2], Bbis_l[i % 2]
                          mb, bmb = maskb[i % 2], Bmb[i % 2]
                          if i >= 2 and on_act(i, pair):
                              dve(lambda e: e.tensor_scalar(out=mb[0:rows, 0:L], in0=score[0:rows, 0:L],
                                                            scalar1=bis[0:rows, 6:7], scalar2=0.0,
                                                            op0=ALU.add, op1=ALU.is_ge), [Bscore, Bbis], [bmb])
                          elif i >= 2:
                              dve(lambda e: e.tensor_scalar(out=mb[0:rows, 0:L], in0=score[0:rows, 0:L],
                                                            scalar1=bis[0:rows, 6:7], scalar2=None,
                                                            op0=ALU.is_ge), [Bscore, Bbis], [bmb])
                          for S in range(i + 1):
                              ks = tsz(S)
                              pk = MB[mzr[0] % 2]
                              mzr[0] += 1
                              Xb = bank[pk].bitcast(BF16)
                              P.op("pe", lambda e: e.transpose(
                                  out=Xb[0:ks, 0:rows], in_=mb[0:rows, 128 * S:128 * S + ks], identity=ident[0:rows, 0:rows]),
                                  reads=[bmb, Bconst], writes=[Bps[pk]])
                              act(mT[0:ks, S, lc:lc + rows], Xb[0:ks, 0:rows], AF.Identity, [Bps[pk]],
                                  [BmT[ms][S]] + alias_w[ms])
                              if S % 2 == 1:
                                  yield
                          yield

              AZ = [0, 1, 4, 5]
              azr = [0]

              def attn_stream(b):
                  t0, n = TBS[b]
                  ms = b % 2
                  mT = maskTs[ms]
                  lastS = tiles_of(b)[-1]
                  for pr in range(4):
                      for hh in range(2):
                          pb = 64 * hh
                          h = 2 * pr + hh

                          def qk(S):
                              ks = tsz(S)
                              c0 = max(0, 128 * (S - 4 * b))
                              k = AZ[azr[0] % len(AZ)]
                              azr[0] += 1
                              near = [i for i in (S, S + 1) if i in tiles_of(b)]
                              mm(bank[k][0:ks, c0:n], kbT[pb:pb + 64, 128 * S:128 * S + ks],
                                 qbT[pb:pb + 64, pr, t0 + c0:t0 + n], True, len(near) == 0,
                                 [Bkb[S // 4], Bqb[pr][b]], [Bps[k]])
                              for ni, i in enumerate(near):
                                  lc = 128 * (i - 4 * b)
                                  w = tsz(i)
                                  off = i - S
                                  for hl in range(2):
                                      mm(bank[k][0:ks, lc:lc + w], ident[0:ks, 0:ks], BD[0:ks, h, off, hl, 0:w], False,
                                         (ni == len(near) - 1) and hl == 1, [Bconst], [Bps[k]])
                              return k

                          kq = {}

                          def ensure(S):
                              if S <= lastS and S not in kq:
                                  kq[S] = qk(S)

                          ensure(0)
                          ensure(1)
                          for S in range(lastS + 1):
                              ensure(S + 2)
                              kcur = kq[S]
                              ks = tsz(S)
                              c0 = max(0, 128 * (S - 4 * b))
                              u2 = S % 3
                              act(pbuf[u2][0:ks, c0:n], bank[kcur][0:ks, c0:n], AF.Exp, [Bps[kcur]], [Bp[u2]],
                                  bias=b31[0:ks, h:h + 1], scale=0.125)
                              pool(lambda e, u2=u2, ks=ks, c0=c0, S=S: e.tensor_tensor(
                                  out=pmb[u2][0:ks, c0:n], in0=pbuf[u2][0:ks, c0:n], in1=mT[0:ks, S, c0:n],
                                  op=ALU.mult), [Bp[u2], BmT[ms][S]], [Bpm[u2]])
                              mm(bank[6][pb:pb + 64, c0:n], vb[0:ks, S, :], pmb[u2][0:ks, c0:n], S == 0, S == lastS,
                                 [Bvb[S // 4], Bpm[u2]], [Bps[6]])
                              mm(bank[7][pb:pb + 64, c0:n], ones64[0:ks, :], pmb[u2][0:ks, c0:n], S == 0, S == lastS,
                                 [Bconst, Bpm[u2]], [Bps[7]])
                              yield
                      dve(lambda e: e.reciprocal(out=rden[:, 0:n], in_=bank[7][:, 0:n]), [Bps[7]], [Brden])
                      dve(lambda e, pr=pr: e.tensor_tensor(out=yT[:, 1, pr, t0:t0 + n], in0=bank[6][:, 0:n],
                                                           in1=rden[:, 0:n], op=ALU.mult),
                          [Bps[6], Brden], [By[1][pr][b]])
                      yield

              def est_mask(b):
                  tot = 0
                  for i in tiles_of(b):
                      L = 128 * i + tsz(i)
                      tot += 8 * ((L + 511) // 512) + (NIT if i >= 2 else 0) + (i + 1) // 2 + 1
                  return float(tot)

              def est_attn(b):
                  return float(8 * (tiles_of(b)[-1] + 1) + 4)

              interleave([(mask_stream(0), 1.0)])
              for b in range(len(TBS)):
                  gl = [(attn_stream(b), est_attn(b))]
                  if b + 1 < len(TBS):
                      gl.append((mask_stream(b + 1), est_mask(b + 1)))
                  interleave(gl)

              stop_at('dsa')
              P.barrier()
              A.reset()
              wzo = A.alloc([8, 1024], BF16)
              Bwzo = Buf()
              wst = [A.alloc([24, 128], BF16) for _ in range(3)]
              Bwst = [Buf(), Buf(), Buf()]
              mT = A.alloc([8, T], BF16)
              BmTt = [[Buf() for b in range(5)] for m in range(8)]
              gb_in_g = A.alloc([1024], F32)
              gb_in_b = A.alloc([1024], F32)
              gb_g = A.alloc([1024], F32)
              gb_b = A.alloc([1024], F32)
              Bbc = Buf()
              xbuf = [A.alloc([1024], F32) for _ in range(2)]
              Bxb = [Buf(), Buf()]
              hres = A.alloc([1024], F32)
              Bhres = Buf()
              rr = [A.alloc([1024], F32) for _ in range(2)]
              Brr = [Buf(), Buf()]
              stats = A.alloc([2, 6], F32)
              mv = A.alloc([2], F32)
              rstd = A.alloc([1], F32)
              Bst = Buf()
              stats2 = A.alloc([2, 6], F32)
              mv2 = A.alloc([2], F32)
              rstd2 = A.alloc([1], F32)
              Bst2 = Buf()
              ef = [A.alloc([512], F32) for _ in range(2)]
              Bef = [Buf(), Buf()]
              sg = [A.alloc([512], F32) for _ in range(3)]
              Bsg = [Buf(), Buf(), Buf()]
              m1 = [A.alloc([512], F32) for _ in range(2)]
              Bm1 = [Buf(), Buf()]
              ld(gb_in_g, lning_d.partition_broadcast(128), [Bbc])
              ld(gb_in_b, lninb_d.partition_broadcast(128), [Bbc])
              ld(gb_g, lng_d.partition_broadcast(128), [Bbc])
              ld(gb_b, lnb_d.partition_broadcast(128), [Bbc])

              def sigmoid_from_psum(k, n, bias, si):
                  eb = zr[0] % 2
                  act(ef[eb][:, 0:n], bank[k][:, 0:n], AF.Exp, [Bps[k]], [Bef[eb]], bias=bias, scale=-1.0)
                  act(ef[eb][:, 0:n], ef[eb][:, 0:n], AF.Ln, [Bef[eb]], [Bef[eb]], bias=1.0)
                  act(sg[si][:, 0:n], ef[eb][:, 0:n], AF.Exp, [Bef[eb]], [Bsg[si]], scale=-1.0)

              def load_ws(m):
                  ws, bws = wst[m % 3], Bwst[m % 3]
                  ldc(ws[:, 0:8, :], win_d[:, C_GA + 128 * m:C_GA + 128 * (m + 1)].rearrange("(c p) n -> p c n", p=128), [bws])
                  ldc(ws[:, 8:16, :], win_d[:, C_GB + 128 * m:C_GB + 128 * (m + 1)].rearrange("(c p) n -> p c n", p=128), [bws])
                  ldc(ws[:, 16:20, :], wpa_d[:, 128 * m:128 * (m + 1)].rearrange("(c p) n -> p c n", p=128), [bws])
                  ldc(ws[:, 20:24, :], wpb_d[:, 128 * m:128 * (m + 1)].rearrange("(c p) n -> p c n", p=128), [bws])

              ldc(wzo[:, :, 0:512], win_d[:, C_ZA:C_ZA + 512].rearrange("(c p) n -> p c n", p=128), [Bwzo])
              ldc(wzo[:, :, 512:1024], win_d[:, C_ZB:C_ZB + 512].rearrange("(c p) n -> p c n", p=128), [Bwzo])
              load_ws(0)
              load_ws(1)
              zr = [0]
              for b, (t0, n) in enumerate(TBS):
                  for m in range(8):
                      br, pr = m // 4, m % 4
                      k = zr[0] % 6
                      si = zr[0] % 3
                      zr[0] += 1
                      for c in range(8):
                          mm(bank[k][:, 0:n], wzo[:, c, m * 128:(m + 1) * 128], hT[:, c, t0:t0 + n], c == 0, c == 7,
                             [Bwzo, Bh[b]], [Bps[k]])
                      sigmoid_from_psum(k, n, 0.0, si)
                      dve(lambda e, k=k, n=n, si=si: e.tensor_tensor(out=sg[si][:, 0:n], in0=bank[k][:, 0:n],
                                                                     in1=sg[si][:, 0:n], op=ALU.mult),
                          [Bps[k], Bsg[si]], [Bsg[si]])
                      pool(lambda e, n=n, si=si, br=br, pr=pr, t0=t0: e.tensor_tensor(
                          out=yT[:, br, pr, t0:t0 + n], in0=yT[:, br, pr, t0:t0 + n], in1=sg[si][:, 0:n], op=ALU.mult),
                          [By[br][pr][b], Bsg[si]], [By[br][pr][b]])
              stop_at('3i')
              for m in range(8):
                  u = m % 3
                  ws, bws = wst[u], Bwst[u]
                  for b, (t0, n) in enumerate(TBS):
                      res = []
                      for br in range(2):
                          k = zr[0] % 6
                          si = zr[0] % 3
                          zr[0] += 1
                          for c in range(8):
                              mm(bank[k][:, 0:n], ws[:, 8 * br + c, :], hT[:, c, t0:t0 + n], c == 0, c == 7,
                                 [bws, Bh[b]], [Bps[k]])
                          sigmoid_from_psum(k, n, negbg[:, 8 * br + m:8 * br + m + 1], si)
                          k2 = zr[0] % 6
                          zr[0] += 1
                          for c in range(4):
                              mm(bank[k2][:, 0:n], ws[:, 16 + 4 * br + c, :], yT[:, br, c, t0:t0 + n], c == 0, c == 3,
                                 [bws, By[br][c][b]], [Bps[k2]])
                          dve(lambda e, k2=k2, n=n, si=si, br=br: e.tensor_tensor(
                              out=m1[br][:, 0:n], in0=bank[k2][:, 0:n], in1=sg[si][:, 0:n], op=ALU.mult),
                              [Bps[k2], Bsg[si]], [Bm1[br]])
                      pool(lambda e, n=n, m=m, t0=t0: e.tensor_tensor(out=mT[:, m, t0:t0 + n], in0=m1[0][:, 0:n],
                                                                      in1=m1[1][:, 0:n], op=ALU.add),
                           [Bm1[0], Bm1[1]], [BmTt[m][b]])
                  if m == 0:
                      ldc(wzo[:, :, :], wo_d.rearrange("(c p) n -> p c n", p=128), [Bwzo])
                  if m + 2 < 8:
                      load_ws(m + 2)
              stop_at('3ii')
              hres_l = [hres, A.alloc([1024], F32)]
              Bhres_l = [Bhres, Buf()]
              st1 = [(stats, mv, rstd, Bst), (A.alloc([2, 6], F32), A.alloc([2], F32), A.alloc([1], F32), Buf())]
              st2 = [(stats2, mv2, rstd2, Bst2), (A.alloc([2, 6], F32), A.alloc([2], F32), A.alloc([1], F32), Buf())]
              nmr1 = [A.alloc([1], F32), A.alloc([1], F32)]
              nmr2 = [A.alloc([1], F32), A.alloc([1], F32)]

              def o_a(j):
                  rows = tsz(j)
                  xt, bx = xbuf[j % 2], Bxb[j % 2]
                  sts, mv_, rs_, bst = st1[j % 2]
                  hr, bhr = hres_l[j % 2], Bhres_l[j % 2]
                  nm = nmr1[j % 2]
                  load_x_tile(seq, j, xt, bx)
                  ln_stats(xt, rows, sts, mv_, rs_, bx, bst)
                  dve(lambda e: e.tensor_scalar(out=nm[0:rows, :], in0=mv_[0:rows, 0:1], scalar1=rs_[0:rows, 0:1],
                                                scalar2=-1.0, op0=ALU.mult, op1=ALU.mult), [bst], [bst])
                  act(hr[0:rows, :], xt[0:rows, :], AF.Identity, [bx, bst], [bhr], bias=nm[0:rows, 0:1],
                      scale=rs_[0:rows, 0:1])
                  pool(lambda e: e.tensor_tensor(out=hr[0:rows, :], in0=hr[0:rows, :], in1=gb_in_g[0:rows, :],
                                                 op=ALU.mult), [bhr, Bbc], [bhr])
                  pool(lambda e: e.tensor_tensor(out=hr[0:rows, :], in0=hr[0:rows, :], in1=gb_in_b[0:rows, :],
                                                 op=ALU.add), [bhr, Bbc], [bhr])

              def o_b(j):
                  rows = tsz(j)
                  b = j // 4
                  hr, bhr = hres_l[j % 2], Bhres_l[j % 2]
                  k = 6 if j % 2 == 0 else 4
                  for half in range(2):
                      for m in range(8):
                          mm(bank[k + half][0:rows, :], mT[:, m, 128 * j:128 * j + rows], wzo[:, m, 512 * half:512 * (half + 1)],
                             m == 0, m == 7, [BmTt[m][b], Bwzo], [Bps[k + half]])
                  r_, br_ = rr[j % 2], Brr[j % 2]
                  dve(lambda e: e.scalar_tensor_tensor(
                      out=r_[0:rows, :], in0=hr[0:rows, :], scalar=ALPHA, in1=ps[0:rows, 512 * k:512 * k + 1024],
                      op0=ALU.mult, op1=ALU.add), [bhr, Bps[k], Bps[k + 1]], [br_])

              def o_c(j):
                  rows = tsz(j)
                  r_, br_ = rr[j % 2], Brr[j % 2]
                  sts, mv_, rs_, bst = st2[j % 2]
                  nm = nmr2[j % 2]
                  ln_stats(r_, rows, sts, mv_, rs_, br_, bst)
                  dve(lambda e: e.tensor_scalar(out=nm[0:rows, :], in0=mv_[0:rows, 0:1], scalar1=rs_[0:rows, 0:1],
                                                scalar2=-1.0, op0=ALU.mult, op1=ALU.mult), [bst], [bst])
                  act(r_[0:rows, :], r_[0:rows, :], AF.Identity, [br_, bst], [br_], bias=nm[0:rows, 0:1],
                      scale=rs_[0:rows, 0:1])
                  dve(lambda e: e.tensor_tensor(out=r_[0:rows, :], in0=r_[0:rows, :], in1=gb_g[0:rows, :],
                                                op=ALU.mult), [br_, Bbc], [br_])
                  pool(lambda e: e.tensor_tensor(out=r_[0:rows, :], in0=r_[0:rows, :], in1=gb_b[0:rows, :],
                                                 op=ALU.add), [br_, Bbc], [br_])
                  if j == 0:
                      P.dma("sp", lambda e: e.dma_start(out=out_d[seq, 0:112, :], in_=r_[16:128, :]),
                            reads=[br_], writes=[])
                  else:
                      r0 = 128 * j - 16
                      P.dma("sp", lambda e: e.dma_start(out=out_d[seq, r0:r0 + rows, :], in_=r_[0:rows, :]),
                            reads=[br_], writes=[])

              for k_ in range(NT + 2):
                  if k_ < NT:
                      o_a(k_)
                  if 0 <= k_ - 1 < NT:
                      o_b(k_ - 1)
                  if 0 <= k_ - 2 < NT:
                      o_c(k_ - 2)
        except _Stop:
            pass
        if os.environ.get('K_VERBOSE'):
            print('OPCOUNTS', P.cnt, P.ndma, {e: len(P.ops[e]) for e in P.ENGS})
        P.emit(nc)
    return nc


def rel_bucket_np(d):
    d = np.asarray(d)
    nf = np.maximum(d, 1).astype(np.float32)
    large = 16 + (np.log(nf / np.float32(16)) / np.float32(math.log(128 / 16)) * np.float32(16)).astype(np.int32)
    large = np.minimum(large, 31)
    return np.where(d < 16, d, large)


def host_constants():
    cm = np.zeros((32, 383), np.float32)
    for k in range(383):
        d = k - 127
        if d >= 128:
            continue
        bkt = int(rel_bucket_np(max(d, 0)))
        cm[bkt, k] += 8.0
        cm[31, k] -= 8.0
    p = np.arange(128)
    c128 = np.zeros((128, 5, 128), np.float32)
    c128[:, 0, :] = np.eye(128, dtype=np.float32)
    c128[:, 1, :] = (p[:, None] < p[None, :]).astype(np.float32)
    c128[:, 2, :] = (p[:, None] >= p[None, :]).astype(np.float32)
    c128[:, 3, :] = (p[:, None] + p[None, :] == 127).astype(np.float32)
    c128[:, 4, :] = np.where(p[None, :] <= p[:, None], 0.0, -1e30).astype(np.float32)
    pw = (2.0 ** -np.arange(32)).astype(np.float32)
    return cm, c128, pw


_CACHE = {}


def kernel(x, meta_tokens, ln_in_g, ln_in_b, rel_bias, w_in, b_gate, idx_kn_g, idx_kn_b,
           w_pa, w_pb, w_o, ln_g, ln_b, _ncores=8, _nseq=4):
    f = lambda a: np.ascontiguousarray(np.asarray(a, dtype=np.float32))
    x = f(x)
    ncores, nseq = _ncores, _nseq
    key = (nseq,)
    if key not in _CACHE:
        _CACHE[key] = build_program(nseq)
    nc = _CACHE[key]
    cm, c128, pw = host_constants()
    cols = np.zeros((128, 48), np.float32)
    cols[:, 0:8] = f(ln_in_g).reshape(8, 128).T
    cols[:, 8:16] = f(ln_in_b).reshape(8, 128).T
    cols[:, 16:32] = f(b_gate).reshape(16, 128).T
    shared = {
        "meta": f(meta_tokens), "w_in": f(w_in)[0], "w_pa": f(w_pa)[0], "w_pb": f(w_pb)[0], "w_o": f(w_o)[0],
        "rel_bias": f(rel_bias), "ln_in_g": f(ln_in_g), "ln_in_b": f(ln_in_b), "ln_g": f(ln_g)[0], "ln_b": f(ln_b)[0],
        "cols": cols, "ikn_g": f(idx_kn_g)[0], "ikn_b": f(idx_kn_b)[0], "cmat": cm, "c128": c128, "pw": pw,
    }
    in_maps = []
    for c in range(ncores):
        d = dict(shared)
        d["x"] = np.ascontiguousarray(x[c * nseq:(c + 1) * nseq])
        in_maps.append(d)
    res = run_bass_kernel_spmd(nc, in_maps, core_ids=list(range(ncores)))
    return np.concatenate([np.asarray(r["out"]) for r in res.results], axis=0).astype(np.float32)
```

```python
import contextlib
import math
import os
import numpy as np
import concourse.bass as bass
import concourse.mybir as mybir
from concourse.bass_utils import run_bass_kernel_spmd

F32 = mybir.dt.float32
BF16 = mybir.dt.bfloat16
ALU = mybir.AluOpType
AF = mybir.ActivationFunctionType
AX = mybir.AxisListType

T = 2064
NT = 17
D = 1024
SEQ = 2048
NMETA = 16
TOPK = 256
EPS = 1e-5
ALPHA = 2.0 ** 0.25
WI_SCALE = (8 ** -0.5) * (64 ** -0.5)
NIT = 22
TBS = [(0, 512), (512, 512), (1024, 512), (1536, 512), (2048, 16)]
C_QA, C_KA, C_VA, C_ZA, C_QB, C_KB, C_VB, C_ZB, C_QI, C_KI, C_WI, C_GA, C_GB = (
    0, 512, 1024, 1536, 2048, 2560, 2624, 2688, 3200, 3712, 3776, 3784, 4808)
INCOLS = 5832


def tsz(j):
    return 128 if j < 16 else 16


def tiles_of(b):
    return list(range(4 * b, min(4 * b + 4, NT)))


class Buf:
    __slots__ = ("name", "w", "r")

    def __init__(self, name=""):
        self.name = name
        self.w = None
        self.r = []


class _Rec:
    def __init__(self):
        self.call = None

    def __getattr__(self, name):
        def f(*a, **kw):
            self.call = (name, a, kw)
            return self
        return f


def _freeze(fn):
    rec = _Rec()
    fn(rec)
    name, a, kw = rec.call
    return lambda e: getattr(e, name)(*a, **kw)


class Prog:
    ENGS = ("pe", "act", "dve", "pool", "sp")

    def __init__(self):
        self.ops = {e: [] for e in self.ENGS}
        self.cnt = {e: 0 for e in self.ENGS}
        self.pending = {e: set() for e in self.ENGS}
        self.ndma = 0
        self.ndma_e = {}
        self.dma_sem_use = {}
        self.NDMASEM = 24

    def _deps(self, eng, reads, writes):
        deps = set(self.pending[eng])
        self.pending[eng] = set()
        for b in reads:
            if b.w is not None:
                deps.add(b.w)
        for b in writes:
            if b.w is not None:
                deps.add(b.w)
            deps.update(b.r)
        if eng == "pe":
            deps = {d for d in deps if d[0] != "pe"}
        return deps

    def _mark(self, me, reads, writes):
        for b in reads:
            b.r.append(me)
            if len(b.r) > 64:
                best = {}
                for (k, v) in b.r:
                    if best.get(k, 0) < v:
                        best[k] = v
                b.r = list(best.items())
        for b in writes:
            b.w = me
            b.r = []

    def op(self, eng, fn, reads=(), writes=()):
        deps = self._deps(eng, reads, writes)
        self.cnt[eng] += 1
        me = (eng, self.cnt[eng])
        self.ops[eng].append(("op", _freeze(fn), deps, None))
        self._mark(me, reads, writes)
        return me

    def dma(self, eng, fn, reads=(), writes=()):
        deps = self._deps(eng, reads, writes)
        nd = self.ndma_e.get(eng, 0)
        self.ndma_e[eng] = nd + 1
        self.ndma += 1
        k = (eng, nd % self.NDMASEM)
        prev = self.dma_sem_use.get(k, 0)
        if prev:
            deps.add((("dma", k), 16 * prev))
        self.dma_sem_use[k] = prev + 1
        me = (("dma", k), 16 * (prev + 1))
        self.ops[eng].append(("dma", _freeze(fn), deps, k))
        self._mark(me, reads, writes)
        return me

    def barrier(self):
        snap = set()
        for e in self.ENGS:
            if self.cnt[e]:
                snap.add((e, self.cnt[e]))
        for k, c in self.dma_sem_use.items():
            snap.add((("dma", k), 16 * c))
        self.pending["act"] |= snap
        scr = self.bar_scratch
        me = self.op("act", lambda e: e.activation(out=scr, in_=scr, func=AF.Identity))
        for e in self.ENGS:
            if e != "act":
                self.pending[e].add(me)

    def emit(self, nc):
        with contextlib.ExitStack() as st:
            sems = {}
            for e in self.ENGS:
                sems[e] = st.enter_context(nc.semaphore("s_" + e))
            for k in self.dma_sem_use:
                sems[("dma", k)] = st.enter_context(nc.semaphore("s_dma_%s%d" % k))
            block = st.enter_context(nc.Block())
            regs = {"pe": block.tensor, "act": block.scalar, "dve": block.vector,
                    "pool": block.gpsimd, "sp": block.sync}
            for e in self.ENGS:
                ops = self.ops[e]
                if not ops and e != "sp":
                    continue

                def body(engine, ops=ops, e=e):
                    waited = {}
                    for kind, fn, deps, k in ops:
                        best = {}
                        for (sk, val) in deps:
                            if best.get(sk, 0) < val:
                                best[sk] = val
                        for sk in sorted(best, key=str):
                            val = best[sk]
                            if waited.get(sk, 0) >= val:
                                continue
                            engine.wait_ge(sems[sk], val)
                            waited[sk] = val
                        ins = fn(engine)
                        if kind == "op":
                            ins.then_inc(sems[e], 1)
                        else:
                            ins.then_inc(sems[("dma", k)], 16)
                    if e == "sp":
                        for k2, c2 in self.dma_sem_use.items():
                            engine.wait_ge(sems[("dma", k2)], 16 * c2)
                        for e2 in self.ENGS:
                            if e2 != "sp" and self.cnt[e2]:
                                engine.wait_ge(sems[e2], self.cnt[e2])
                regs[e](body)


class _Stop(Exception):
    pass


_TICK = [0]


def tick():
    _TICK[0] += 1
    stop_at('d%d' % _TICK[0])


def stop_at(name):
    if os.environ.get("K_STOP", "") == name:
        raise _Stop()


def interleave(gens):
    st = [[g, est, 0] for g, est in gens]
    while st:
        st.sort(key=lambda x: x[2] / x[1])
        g = st[0]
        try:
            next(g[0])
            g[2] += 1
        except StopIteration:
            st.pop(0)


class Arena:
    def __init__(self, ap, nelem):
        self.ap = ap
        self.n = nelem
        self.off = 0

    def reset(self):
        self.off = 0

    def alloc(self, shape, dt):
        n = 1
        for s in shape:
            n *= s
        ne = n * (2 if dt == F32 else 1)
        ne_al = (ne + 15) // 16 * 16
        assert self.off + ne_al <= self.n, ("arena overflow", self.off, ne_al, self.n)
        v = self.ap[:, self.off:self.off + ne]
        self.off += ne_al
        if dt == F32:
            v = v.bitcast(F32)
        if len(shape) == 2:
            v = v.rearrange("p (a b) -> p a b", a=shape[0])
        elif len(shape) == 3:
            v = v.rearrange("p (a b c) -> p a b c", a=shape[0], b=shape[1])
        return v


def build_program(nseq):
    nc = bass.Bass("TRN2", target_bir_lowering=False)
    dram = lambda name, shape, kind="ExternalInput": nc.dram_tensor(name, shape, F32, kind=kind).ap()
    x_d = dram("x", [nseq, SEQ, D])
    meta_d = dram("meta", [NMETA, D])
    win_d = dram("w_in", [D, INCOLS])
    wpa_d = dram("w_pa", [512, D])
    wpb_d = dram("w_pb", [512, D])
    wo_d = dram("w_o", [D, D])
    relb_d = dram("rel_bias", [32, 8])
    lning_d = dram("ln_in_g", [D])
    lninb_d = dram("ln_in_b", [D])
    lng_d = dram("ln_g", [D])
    lnb_d = dram("ln_b", [D])
    cols_d = dram("cols", [128, 48])
    ikg_d = dram("ikn_g", [64])
    ikb_d = dram("ikn_b", [64])
    cmat_d = dram("cmat", [32, 383])
    c128_d = dram("c128", [128, 5, 128])
    pw_d = dram("pw", [32])
    out_d = dram("out", [nseq, SEQ, D], kind="ExternalOutput")
    gscr_d = nc.dram_tensor("gscr", [8, 383], F32, kind="Internal").ap()

    P = Prog()
    st = contextlib.ExitStack()
    with st:
        sbt = lambda name, shape, dt: st.enter_context(nc.sbuf_tensor("sb_" + name, shape, dt))
        hT = sbt("hT", [128, 8, T], BF16)
        yT = sbt("yT", [128, 2, 4, T], BF16)
        ident = sbt("ident", [128, 128], BF16)
        identf = sbt("identf", [128, 128], F32)
        trimask = sbt("trimask", [128, 128], BF16)
        tri8 = sbt("tri8", [128, 128], BF16)
        ones8 = sbt("ones8", [128, 128], BF16)
        ones64 = sbt("ones64", [128, 64], BF16)
        negmask = sbt("negmask", [128, 128], F32)
        Jm = sbt("Jm", [128, 128], F32)
        c128f = sbt("c128f", [128, 5, 128], F32)
        BD = sbt("BD", [128, 8, 2, 2, 128], BF16)
        cols = sbt("cols", [128, 48], F32)
        negbg = sbt("negbg", [128, 16], F32)
        b31 = sbt("b31", [128, 8], F32)
        pw = sbt("pw", [128, 32], F32)
        ikg = sbt("ikg", [128, 64], F32)
        ikb = sbt("ikb", [128, 64], F32)
        relb = sbt("relb", [32, 8], F32)
        cmat = sbt("cmat", [32, 383], F32)
        gvec = sbt("gvec", [8, 383], F32)
        ARENA_N = 64400
        arena_t = sbt("arena", [128, ARENA_N], BF16)
        A = Arena(arena_t, ARENA_N)
        ps = st.enter_context(nc.psum_tensor("ps", [128, 4096], F32))
        bank = [ps[:, 512 * k:512 * (k + 1)] for k in range(8)]
        Bps = [Buf("ps%d" % k) for k in range(8)]
        Bconst = Buf("const")
        Bh = [Buf("h%d" % b) for b in range(5)]
        By = [[[Buf() for b in range(5)] for pr in range(4)] for br in range(2)]

        def mm(out, lhsT, rhs, start, stop, reads, writes):
            return P.op("pe", lambda e: e.matmul(out, lhsT=lhsT, rhs=rhs, start=start, stop=stop),
                        reads=reads, writes=writes)

        def act(out, in_, func, reads, writes, bias=0.0, scale=1.0):
            return P.op("act", lambda e: e.activation(out=out, in_=in_, func=func, bias=bias, scale=scale),
                        reads=reads, writes=writes)

        def dve(fn, reads, writes):
            return P.op("dve", fn, reads=reads, writes=writes)

        def pool(fn, reads, writes):
            return P.op("pool", fn, reads=reads, writes=writes)

        def ld(out, in_, writes, reads=()):
            return P.dma("sp", lambda e: e.dma_start(out=out, in_=in_), reads=reads, writes=writes)

        def ldc(out, in_, writes, reads=()):
            return P.dma("pool", lambda e: e.dma_start(out=out, in_=in_), reads=reads, writes=writes)

        P.bar_scratch = cols[0:1, 40:48]
        ld(c128f[:], c128_d, [Bconst])
        ld(cols[:], cols_d, [Bconst])
        ld(relb[:], relb_d, [Bconst])
        ld(cmat[:], cmat_d, [Bconst])
        ld(b31[:], relb_d[31, :].partition_broadcast(128), [Bconst])
        ld(pw[:], pw_d.partition_broadcast(128), [Bconst])
        ld(ikg[:], ikg_d.partition_broadcast(128), [Bconst])
        ld(ikb[:], ikb_d.partition_broadcast(128), [Bconst])
        dve(lambda e: e.tensor_copy(out=ident[:], in_=c128f[:, 0, :]), [Bconst], [Bconst])
        dve(lambda e: e.tensor_copy(out=identf[:], in_=c128f[:, 0, :]), [Bconst], [Bconst])
        dve(lambda e: e.tensor_copy(out=trimask[:], in_=c128f[:, 1, :]), [Bconst], [Bconst])
        dve(lambda e: e.tensor_scalar(out=tri8[:], in0=c128f[:, 2, :], scalar1=-8.0, scalar2=None, op0=ALU.mult),
            [Bconst], [Bconst])
        dve(lambda e: e.tensor_copy(out=Jm[:], in_=c128f[:, 3, :]), [Bconst], [Bconst])
        dve(lambda e: e.tensor_copy(out=negmask[:], in_=c128f[:, 4, :]), [Bconst], [Bconst])
        dve(lambda e: e.memset(ones8[:], -8.0), [], [Bconst])
        dve(lambda e: e.memset(ones64[:], 1.0), [], [Bconst])
        dve(lambda e: e.tensor_scalar(out=negbg[:], in0=cols[:, 16:32], scalar1=-1.0, scalar2=None, op0=ALU.mult),
            [Bconst], [Bconst])
        Bg = Buf("gscr")
        mm(bank[0][0:8, 0:383], relb[:, :], cmat[:, :], True, True, [Bconst], [Bps[0]])
        dve(lambda e: e.tensor_copy(out=gvec[:], in_=bank[0][0:8, 0:383]), [Bps[0]], [Bconst])
        ld(gscr_d, gvec[:], [Bg], reads=[Bconst])
        A.reset()
        hank = A.alloc([16, 128], F32)
        Bhank = Buf("hank")
        for h in range(8):
            for off in range(2):
                src = bass.AP(tensor=gscr_d.tensor, offset=h * 383 + 128 * off, ap=[[1, 128], [1, 128]])
                ld(hank[:, h * 2 + off, :], src, [Bhank], reads=[Bg])
        for h in range(8):
            for off in range(2):
                k = (h * 2 + off) % 4
                mm(bank[k][:, 0:128], Jm[:, :], hank[:, h * 2 + off, :], True, True, [Bhank, Bconst], [Bps[k]])
                dve(lambda e, h=h, off=off, k=k: e.tensor_copy(out=BD[:, h, off, 0, :], in_=bank[k][:, 0:128]),
                    [Bps[k]], [Bconst])
                dve(lambda e, h=h, off=off, k=k: e.tensor_tensor(out=BD[:, h, off, 1, :], in0=bank[k][:, 0:128],
                                                                  in1=BD[:, h, off, 0, :], op=ALU.subtract),
                    [Bps[k], Bconst], [Bconst])

        def ln_stats(xt, rows, stats, mv, rstd, Bx, Bs):
            for c in range(2):
                dve(lambda e, c=c: e.bn_stats(out=stats[0:rows, c, :], in_=xt[0:rows, 512 * c:512 * (c + 1)]),
                    [Bx], [Bs])
            dve(lambda e: e.bn_aggr(out=mv[0:rows, :], in_=stats[0:rows, :, :]), [Bs], [Bs])
            act(rstd[0:rows, :], mv[0:rows, 1:2], AF.Ln, [Bs], [Bs], bias=EPS)
            act(rstd[0:rows, :], rstd[0:rows, :], AF.Exp, [Bs], [Bs], scale=-0.5)

        def load_x_tile(seq, j, xt, Bx):
            rows = tsz(j)
            if j == 0:
                ld(xt[0:16, :], meta_d, [Bx])
                ld(xt[16:128, :], x_d[seq, 0:112, :], [Bx])
            else:
                r0 = 128 * j - 16
                ld(xt[0:rows, :], x_d[seq, r0:r0 + rows, :], [Bx])

        def load_w(dst, src_rows_ap, ncol, Bw, krows=8):
            ldc(dst, src_rows_ap.rearrange("(c p) n -> p c n", p=128), [Bw])

        try:
          for seq in range(nseq):
              P.barrier()
              A.reset()
              xbuf = [A.alloc([1024], F32) for _ in range(2)]
              xn = [A.alloc([1024], BF16) for _ in range(2)]
              stats = A.alloc([2, 6], F32)
              mv = A.alloc([2], F32)
              rstd = A.alloc([1], F32)
              wbuf = [A.alloc([8, 512], BF16) for _ in range(2)]
              qaT = A.alloc([4, T], BF16)
              kaT = A.alloc([4, T], BF16)
              va = A.alloc([NT, 512], BF16)
              def sb_scratch(kO, kR):
                  return dict(
                      ek=[A.alloc([512], BF16) for _ in range(4)], Bek=[Buf() for _ in range(4)],
                      sp=[A.alloc([512], BF16) for _ in range(6)], Bsp=[Buf() for _ in range(6)],
                      ec=[A.alloc([512], BF16) for _ in range(2)], Bec=[Buf() for _ in range(2)],
                      a=[A.alloc([512], BF16) for _ in range(4)], Ba=[Buf() for _ in range(4)],
                      acc=A.alloc([512], F32), Rrun=A.alloc([512], F32), Esc=A.alloc([512], F32),
                      tmp=A.alloc([512], F32), Bacc=Buf(), BR=Buf(), BE=Buf(), Btmp=Buf(),
                      kO=kO, kR=kR, c1=[0], c2=[0])

              sbsc = [sb_scratch(4, 5), sb_scratch(6, 7)]
              Bxb = [Buf(), Buf()]
              Bxn = [Buf(), Buf()]
              Bst = Buf()
              Bw = [Buf(), Buf()]
              Bqa = [[Buf() for b in range(5)] for pr in range(4)]
              Bka = [[Buf() for b in range(5)] for pr in range(4)]
              Bva = [Buf() for b in range(5)]

              stats_l = [stats, A.alloc([2, 6], F32)]
              mv_l = [mv, A.alloc([2], F32)]
              rstd_l = [rstd, A.alloc([1], F32)]
              Bst_l = [Bst, Buf()]

              def ln_a(j):
                  load_x_tile(seq, j, xbuf[j % 2], Bxb[j % 2])
                  ln_stats(xbuf[j % 2], tsz(j), stats_l[j % 2], mv_l[j % 2], rstd_l[j % 2], Bxb[j % 2], Bst_l[j % 2])

              def ln_b(j):
                  rows = tsz(j)
                  xt, bx = xbuf[j % 2], Bxb[j % 2]
                  xb, bxn = xn[j % 2], Bxn[j % 2]
                  mv_, rs_ = mv_l[j % 2], rstd_l[j % 2]
                  dve(lambda e: e.tensor_scalar(
                      out=xb[0:rows, :], in0=xt[0:rows, :], scalar1=mv_[0:rows, 0:1], scalar2=rs_[0:rows, 0:1],
                      op0=ALU.subtract, op1=ALU.mult), [bx, Bst_l[j % 2]], [bxn])
                  pk = 6 + (j % 2)
                  Xb = bank[pk].bitcast(BF16)
                  for c in range(8):
                      P.op("pe", lambda e, c=c: e.transpose(
                          out=Xb[:, c * 128:c * 128 + rows], in_=xb[0:rows, c * 128:(c + 1) * 128],
                          identity=ident[0:rows, 0:rows]), reads=[bxn, Bconst], writes=[Bps[pk]])

              def ln_c(j):
                  rows = tsz(j)
                  pk = 6 + (j % 2)
                  Xb = bank[pk].bitcast(BF16)
                  for c in range(8):
                      dve(lambda e, c=c: e.tensor_scalar(
                          out=hT[:, c, 128 * j:128 * j + rows], in0=Xb[:, c * 128:c * 128 + rows],
                          scalar1=cols[:, c:c + 1], scalar2=cols[:, 8 + c:9 + c], op0=ALU.mult, op1=ALU.add),
                          [Bps[pk], Bconst], [Bh[j // 4]])

              for j in range(NT + 2):
                  if j < NT:
                      ln_a(j)
                  if 0 <= j - 1 < NT:
                      ln_b(j - 1)
                  if 0 <= j - 2 < NT:
                      ln_c(j - 2)

              stop_at('ln')
              zrot = [0]

              def fm_project(col0, dstT, Bdst, nun=1, dup64=False):
                  u = zrot[0] % 2
                  wb, bw = wbuf[u], Bw[u]
                  if dup64:
                      ldc(wb[:, :, 0:64], win_d[:, col0:col0 + 64].rearrange("(c p) n -> p c n", p=128), [bw])
                      ldc(wb[:, :, 64:128], win_d[:, col0:col0 + 64].rearrange("(c p) n -> p c n", p=128), [bw])
                      nm = 1
                  else:
                      ldc(wb[:, :, :], win_d[:, col0:col0 + 512].rearrange("(c p) n -> p c n", p=128), [bw])
                      nm = 4
                  zrot[0] += 1
                  for b, (t0, n) in enumerate(TBS):
                      for m in range(nm):
                          k = zrot[0] % 4
                          zrot[0] += 1
                          for c in range(8):
                              mm(bank[k][:, 0:n], wb[:, c, m * 128:(m + 1) * 128], hT[:, c, t0:t0 + n],
                                 c == 0, c == 7, [bw, Bh[b]], [Bps[k]])
                          if dup64:
                              act(dstT[:, t0:t0 + n], bank[k][:, 0:n], AF.Identity, [Bps[k]], [Bdst[b]])
                          else:
                              act(dstT[:, m, t0:t0 + n], bank[k][:, 0:n], AF.Identity, [Bps[k]], [Bdst[m][b]])

              fm_project(C_QA, qaT, Bqa)
              fm_project(C_KA, kaT, Bka)
              u = zrot[0] % 2
              zrot[0] += 1
              wb, bw = wbuf[u], Bw[u]
              ldc(wb[:, :, :], win_d[:, C_VA:C_VA + 512].rearrange("(c p) n -> p c n", p=128), [bw])
              for j in range(NT):
                  rows = tsz(j)
                  k = 4 + (j % 2)
                  for c in range(8):
                      mm(bank[k][0:rows, :], hT[:, c, 128 * j:128 * j + rows], wb[:, c, :], c == 0, c == 7,
                         [bw, Bh[j // 4]], [Bps[k]])
                  act(va[0:rows, j, :], bank[k][0:rows, :], AF.Identity, [Bps[k]], [Bva[j // 4]])

              stop_at('p1a')
              ZB, CB = [0, 1], [2, 3]
              zc, cc = [0], [0]

              def sb_stream(b, pr, sc):
                  t0, n = TBS[b]
                  kO, kR = sc['kO'], sc['kR']
                  acc, Rrun, Esc, tmp = sc['acc'], sc['Rrun'], sc['Esc'], sc['tmp']
                  Bacc, BR, BE, Btmp = sc['Bacc'], sc['BR'], sc['BE'], sc['Btmp']
                  pool(lambda e: e.memset(acc[:, 0:n], 0.0), [], [Bacc])
                  pool(lambda e: e.memset(Rrun[:, 0:n], 0.0), [], [BR])
                  lastS = tiles_of(b)[-1]
                  units = list(range(lastS, -1, -1))
                  st = {}

                  def s1(S):
                      ks = tsz(S)
                      c0 = max(0, 128 * (S - 4 * b))
                      zk = []
                      for hh in range(2):
                          pb = 64 * hh
                          k = ZB[hh]
                          zk.append(k)
                          mm(bank[k][0:ks, c0:n], kaT[pb:pb + 64, pr, 128 * S:128 * S + ks],
                             qaT[pb:pb + 64, pr, t0 + c0:t0 + n], True, True, [Bka[pr][S // 4], Bqa[pr][b]], [Bps[k]])
                      rec = []
                      for hh in range(2):
                          i1 = sc['c1'][0]
                          sc['c1'][0] += 1
                          ek_t, Bek = sc['ek'][i1 % 4], sc['Bek'][i1 % 4]
                          sp_t, Bsp = sc['sp'][i1 % 6], sc['Bsp'][i1 % 6]
                          k = zk[hh]
                          act(ek_t[0:ks, c0:n], bank[k][0:ks, c0:n], AF.Exp, [Bps[k]], [Bek], scale=0.125)
                          if S >= 4 * b:
                              w = min(128, n - c0)
                              pool(lambda e: e.tensor_tensor(out=ek_t[0:ks, c0:c0 + w], in0=ek_t[0:ks, c0:c0 + w],
                                                             in1=trimask[0:ks, 0:w], op=ALU.mult), [Bek, Bconst], [Bek])
                          rec.append([ek_t, Bek, sp_t, Bsp])
                      for hh in range(2):
                          ek_t, Bek, sp_t, Bsp = rec[hh]
                          act(sp_t[0:ks, c0:n], ek_t[0:ks, c0:n], AF.Ln, [Bek], [Bsp], bias=1.0)
                      st[S] = (ks, c0, rec)

                  def s2(S):
                      ks, c0, rec = st[S]
                      for hh in range(2):
                          ek_t, Bek, sp_t, Bsp = rec[hh]
                          k2 = CB[hh]
                          mm(bank[k2][0:ks, c0:n], tri8[0:ks, 0:ks], sp_t[0:ks, c0:n], True, True, [Bsp, Bconst], [Bps[k2]])
                      for hh in range(2):
                          ek_t, Bek, sp_t, Bsp = rec[hh]
                          k2 = CB[hh]
                          i2 = sc['c2'][0]
                          sc['c2'][0] += 1
                          ec_t, Bec = sc['ec'][i2 % 2], sc['Bec'][i2 % 2]
                          a_t, Ba_t = sc['a'][i2 % 4], sc['Ba'][i2 % 4]
                          act(ec_t[0:ks, c0:n], bank[k2][0:ks, c0:n], AF.Exp, [Bps[k2]], [Bec], scale=0.125)
                          dve(lambda e: e.tensor_tensor(out=a_t[0:ks, c0:n], in0=ek_t[0:ks, c0:n], in1=ec_t[0:ks, c0:n],
                                                        op=ALU.mult), [Bek, Bec], [Ba_t])
                          rec[hh] = [a_t, Ba_t, sp_t, Bsp]

                  def s3(S):
                      ks, c0, rec = st.pop(S)
                      for hh in range(2):
                          a_t, Ba_t, sp_t, Bsp = rec[hh]
                          pb, h = 64 * hh, 2 * pr + hh
                          mm(bank[kO][pb:pb + 64, c0:n], va[0:ks, S, h * 64:(h + 1) * 64], a_t[0:ks, c0:n], True, True,
                             [Bva[S // 4], Ba_t], [Bps[kO]])
                      for hh in range(2):
                          a_t, Ba_t, sp_t, Bsp = rec[hh]
                          pb = 64 * hh
                          mm(bank[kR][pb:pb + 64, c0:n], ones64[0:ks, :], sp_t[0:ks, c0:n], True, True,
                             [Bsp, Bconst], [Bps[kR]])
                      act(Esc[:, c0:n], Rrun[:, c0:n], AF.Exp, [BR], [BE], scale=-1.0)
                      dve(lambda e: e.tensor_tensor(out=tmp[:, c0:n], in0=bank[kO][:, c0:n], in1=Esc[:, c0:n],
                                                    op=ALU.mult), [Bps[kO], BE], [Btmp])
                      dve(lambda e: e.tensor_tensor(out=acc[:, c0:n], in0=acc[:, c0:n], in1=tmp[:, c0:n],
                                                    op=ALU.add), [Bacc, Btmp], [Bacc])
                      if S > 0:
                          dve(lambda e: e.tensor_tensor(out=Rrun[:, c0:n], in0=Rrun[:, c0:n],
                                                        in1=bank[kR][:, c0:n], op=ALU.add), [BR, Bps[kR]], [BR])
                      else:
                          pool(lambda e: e.tensor_copy(out=yT[:, 0, pr, t0:t0 + n], in_=acc[:, 0:n]),
                               [Bacc], [By[0][pr][b]])

                  nu = len(units)
                  for k in range(nu + 2):
                      if k < nu:
                          s1(units[k])
                      if 0 <= k - 1 < nu:
                          s2(units[k - 1])
                      if 0 <= k - 2 < nu:
                          s3(units[k - 2])
                      yield

              def chain_sb(prs, sc):
                  for b in range(len(TBS)):
                      for pr in prs:
                          yield from sb_stream(b, pr, sc)

              interleave([(chain_sb((0, 2), sbsc[0]), 1.0), (chain_sb((1, 3), sbsc[1]), 1.0)])

              stop_at('sb')
              P.barrier()
              A.reset()
              wbuf = [A.alloc([8, 512], BF16) for _ in range(2)]
              Bw = [Buf(), Buf()]
              wtm = A.alloc([8, 136], BF16)
              Bwtm = Buf()
              qbT = A.alloc([4, T], BF16)
              qiT = A.alloc([4, T], BF16)
              kbT = A.alloc([T], BF16)
              kiT = A.alloc([T], BF16)
              vb = A.alloc([NT, 64], BF16)
              wi = A.alloc([NT, 8], F32)
              kin = A.alloc([128], BF16)
              kif = A.alloc([64], F32)
              ksq = A.alloc([64], F32)
              Bksq = Buf()
              stats = A.alloc([1, 6], F32)
              mv = A.alloc([2], F32)
              rstd = A.alloc([1], F32)
              score_l = [A.alloc([2176], F32), A.alloc([2176], F32)]
              Bscore_l = [Buf(), Buf()]
              maskb = [A.alloc([2176], BF16) for _ in range(2)]
              maskT = A.alloc([NT, 512], BF16)
              rl = [A.alloc([512], F32) for _ in range(2)]
              pbuf = [A.alloc([512], BF16) for _ in range(4)]
              pmb = [A.alloc([512], BF16) for _ in range(4)]
              rden = A.alloc([512], F32)
              bis_l = [A.alloc([40], F32), A.alloc([40], F32)]
              Bbis_l = [Buf(), Buf()]
              Bqb = [[Buf() for b in range(5)] for pr in range(4)]
              Bqi = [[Buf() for b in range(5)] for pr in range(4)]
              Bkb = [Buf() for b in range(5)]
              Bki = [Buf() for b in range(5)]
              Bvb = [Buf() for b in range(5)]
              Bwi = [Buf() for b in range(5)]
              Bkin, Bst, Brden = Buf(), Buf(), Buf()
              Bmb = [Buf(), Buf()]
              Brl = [Buf(), Buf()]
              Bp = [Buf() for _ in range(4)]
              Bpm = [Buf() for _ in range(4)]

              zrot[0] = 0
              fm_project(C_QB, qbT, Bqb)
              fm_project(C_QI, qiT, Bqi)
              stop_at('b1')
              fm_project(C_KB, kbT, Bkb, dup64=True)
              stop_at('b2')
              ldc(wtm[:, :, 0:64], win_d[:, C_VB:C_VB + 64].rearrange("(c p) n -> p c n", p=128), [Bwtm])
              ldc(wtm[:, :, 64:136], win_d[:, C_KI:C_KI + 72].rearrange("(c p) n -> p c n", p=128), [Bwtm])
              for j in range(NT):
                  rows = tsz(j)
                  k = 4 + (j % 2)
                  for c in range(8):
                      mm(bank[k][0:rows, 0:136], hT[:, c, 128 * j:128 * j + rows], wtm[:, c, :], c == 0, c == 7,
                         [Bwtm, Bh[j // 4]], [Bps[k]])
                  act(vb[0:rows, j, :], bank[k][0:rows, 0:64], AF.Identity, [Bps[k]], [Bvb[j // 4]])
                  act(wi[0:rows, j, :], bank[k][0:rows, 128:136], AF.Identity, [Bps[k]], [Bwi[j // 4]], scale=WI_SCALE)
                  stop_at('c1')
                  act(kif[0:rows, :], bank[k][0:rows, 64:128], AF.Identity, [Bps[k]], [Bkin])
                  tick()
                  dve(lambda e, rows=rows: e.tensor_reduce(out=mv[0:rows, 0:1], in_=kif[0:rows, :], axis=AX.X, op=ALU.add),
                      [Bkin], [Bst])
                  tick()
                  dve(lambda e, rows=rows: e.tensor_scalar(out=mv[0:rows, 0:1], in0=mv[0:rows, 0:1], scalar1=1.0 / 64,
                                                           scalar2=None, op0=ALU.mult), [Bst], [Bst])
                  tick()
                  dve(lambda e, rows=rows: e.tensor_scalar(out=kif[0:rows, :], in0=kif[0:rows, :], scalar1=mv[0:rows, 0:1],
                                                           scalar2=None, op0=ALU.subtract), [Bkin, Bst], [Bkin])
                  tick()
                  dve(lambda e, rows=rows: e.tensor_tensor(out=ksq[0:rows, :], in0=kif[0:rows, :], in1=kif[0:rows, :],
                                                           op=ALU.mult), [Bkin], [Bksq])
                  tick()
                  dve(lambda e, rows=rows: e.tensor_reduce(out=mv[0:rows, 1:2], in_=ksq[0:rows, :], axis=AX.X, op=ALU.add),
                      [Bksq], [Bst])
                  tick()
                  act(rstd[0:rows, :], mv[0:rows, 1:2], AF.Ln, [Bst], [Bst], bias=EPS, scale=1.0 / 64)
                  tick()
                  act(rstd[0:rows, :], rstd[0:rows, :], AF.Exp, [Bst], [Bst], scale=-0.5)
                  tick()
                  dve(lambda e, rows=rows: e.tensor_scalar(out=kif[0:rows, :], in0=kif[0:rows, :], scalar1=rstd[0:rows, 0:1],
                                                           scalar2=None, op0=ALU.mult), [Bkin, Bst], [Bkin])
                  tick()
                  dve(lambda e, rows=rows: e.tensor_tensor(out=kif[0:rows, :], in0=kif[0:rows, :], in1=ikg[0:rows, :],
                                                           op=ALU.mult), [Bkin, Bconst], [Bkin])
                  tick()
                  dve(lambda e, rows=rows: e.tensor_tensor(out=kin[0:rows, 0:64], in0=kif[0:rows, :], in1=ikb[0:rows, :],
                                                           op=ALU.add), [Bkin, Bconst], [Bkin])
                  tick()
                  dve(lambda e, rows=rows: e.tensor_copy(out=kin[0:rows, 64:128], in_=kin[0:rows, 0:64]), [Bkin], [Bkin])
                  stop_at('c2')
                  pk = 6 + (j % 2)
                  Xb = bank[pk].bitcast(BF16)
                  P.op("pe", lambda e, rows=rows, Xb=Xb: e.transpose(out=Xb[:, 0:rows], in_=kin[0:rows, :],
                                                                      identity=ident[0:rows, 0:rows]),
                       reads=[Bkin, Bconst], writes=[Bps[pk]])
                  act(kiT[:, 128 * j:128 * j + rows], Xb[:, 0:rows], AF.Identity, [Bps[pk]], [Bki[j // 4]])

              stop_at('p1b')
              maskTs = [maskT, A.ap[:, 0:NT * 512].rearrange("p (a b) -> p a b", a=NT)]
              BmT = [[Buf() for _ in range(NT)] for _ in range(2)]
              alias_w = [[], [Bw[0], Bw[1], Bwtm]]
              mzr = [0]
              MB = [2, 3]

              def on_act(i, pair):
                  return len(pair) == 2 and pair[0] >= 2 and i == pair[1]

              def mask_stream(b):
                  ms = b % 2
                  mT = maskTs[ms]
                  tl = tiles_of(b)
                  for p0 in range(0, len(tl), 2):
                      pair = tl[p0:p0 + 2]
                      for i in pair:
                          rows = tsz(i)
                          L = 128 * i + rows
                          score, Bscore = score_l[i % 2], Bscore_l[i % 2]
                          bis, Bbis = bis_l[i % 2], Bbis_l[i % 2]
                          mb, bmb = maskb[i % 2], Bmb[i % 2]
                          sblocks = [(s0, min(512, L - s0)) for s0 in range(0, L, 512)]
                          for (s0, sn) in sblocks:
                              for h in range(8):
                                  pr, pb = h // 2, 64 * (h % 2)
                                  k = MB[mzr[0] % 2]
                                  r2 = mzr[0] % 2
                                  mzr[0] += 1
                                  mm(bank[k][0:rows, 0:sn], qiT[pb:pb + 64, pr, 128 * i:128 * i + rows],
                                     kiT[pb:pb + 64, s0:s0 + sn], True, True,
                                     [Bqi[pr][b]] + [Bki[bb] for bb in range(s0 // 512, min(4, (s0 + sn - 1) // 512) + 1)],
                                     [Bps[k]])
                                  act(rl[r2][0:rows, 0:sn], bank[k][0:rows, 0:sn], AF.Relu, [Bps[k]], [Brl[r2]])
                                  if h == 0:
                                      dve(lambda e: e.tensor_scalar(
                                          out=score[0:rows, s0:s0 + sn], in0=rl[r2][0:rows, 0:sn],
                                          scalar1=wi[0:rows, i, h:h + 1], scalar2=None, op0=ALU.mult),
                                          [Brl[r2], Bwi[b]], [Bscore])
                                  else:
                                      dve(lambda e: e.scalar_tensor_tensor(
                                          out=score[0:rows, s0:s0 + sn], in0=rl[r2][0:rows, 0:sn],
                                          scalar=wi[0:rows, i, h:h + 1], in1=score[0:rows, s0:s0 + sn],
                                          op0=ALU.mult, op1=ALU.add), [Brl[r2], Bwi[b], Bscore], [Bscore])
                                  yield
                          if i >= 2:
                              dve(lambda e: e.reduce_max(out=bis[0:rows, 0:1], in_=score[0:rows, 0:L], axis=AX.X),
                                  [Bscore], [Bbis])
                              dve(lambda e: e.tensor_reduce(out=bis[0:rows, 1:2], in_=score[0:rows, 0:L],
                                                            axis=AX.X, op=ALU.min), [Bscore], [Bbis])
                          dve(lambda e: e.tensor_tensor(out=score[0:rows, 128 * i:128 * i + rows],
                                                        in0=score[0:rows, 128 * i:128 * i + rows],
                                                        in1=negmask[0:rows, 0:rows], op=ALU.add),
                              [Bscore, Bconst], [Bscore])
                          if i < 2:
                              dve(lambda e: e.tensor_scalar(out=mb[0:rows, 0:L], in0=score[0:rows, 0:L],
                                                            scalar1=-1e29, scalar2=None, op0=ALU.is_ge),
                                  [Bscore], [bmb])
                          else:
                              dve(lambda e: e.tensor_tensor(out=bis[0:rows, 2:3], in0=bis[0:rows, 0:1],
                                                            in1=bis[0:rows, 1:2], op=ALU.subtract), [Bbis], [Bbis])
                              dve(lambda e: e.tensor_scalar(out=bis[0:rows, 8:8 + NIT + 1], in0=pw[0:rows, 0:NIT + 1],
                                                            scalar1=bis[0:rows, 2:3], scalar2=None, op0=ALU.mult),
                                  [Bbis, Bconst], [Bbis])
                              if on_act(i, pair):
                                  dve(lambda e: e.tensor_scalar(out=bis[0:rows, 3:4], in0=bis[0:rows, 1:2], scalar1=-1.0,
                                                                scalar2=bis[0:rows, 9:10], op0=ALU.mult,
                                                                op1=ALU.subtract), [Bbis], [Bbis])
                              else:
                                  dve(lambda e: e.tensor_tensor(out=bis[0:rows, 3:4], in0=bis[0:rows, 1:2],
                                                                in1=bis[0:rows, 9:10], op=ALU.add), [Bbis], [Bbis])
                          yield
                      active = [i for i in pair if i >= 2]
                      for it in range(1, NIT + 1):
                          for i in active:
                              rows = tsz(i)
                              L = 128 * i + rows
                              score, Bscore = score_l[i % 2], Bscore_l[i % 2]
                              bis, Bbis = bis_l[i % 2], Bbis_l[i % 2]
                              mb, bmb = maskb[i % 2], Bmb[i % 2]
                              if on_act(i, pair):
                                  P.op("act", lambda e: e.activation(
                                      out=mb[0:rows, 0:L], in_=score[0:rows, 0:L], func=AF.Sign, bias=bis[0:rows, 3:4],
                                      scale=1.0, accum_out=bis[0:rows, 4:5]), reads=[Bscore, Bbis], writes=[bmb, Bbis])
                              else:
                                  dve(lambda e: e.tensor_scalar(
                                      out=mb[0:rows, 0:L], in0=score[0:rows, 0:L], scalar1=bis[0:rows, 3:4], scalar2=0.0,
                                      op0=ALU.is_ge, op1=ALU.add, accum_out=bis[0:rows, 4:5]), [Bscore, Bbis], [bmb, Bbis])
                          for i in active:
                              rows = tsz(i)
                              bis, Bbis = bis_l[i % 2], Bbis_l[i % 2]
                              if on_act(i, pair):
                                  thr = float(2 * TOPK - 1 - (128 * i + rows))
                                  last_ = (it == NIT)
                                  dve(lambda e: e.tensor_scalar(
                                      out=bis[0:rows, 5:6], in0=bis[0:rows, 4:5], scalar1=thr, scalar2=(0.0 if last_ else 0.5),
                                      op0=ALU.is_lt, op1=ALU.subtract), [Bbis], [Bbis])
                                  dve(lambda e: e.scalar_tensor_tensor(
                                      out=(bis[0:rows, 6:7] if last_ else bis[0:rows, 3:4]), in0=bis[0:rows, 5:6],
                                      scalar=bis[0:rows, 8 + it:9 + it], in1=bis[0:rows, 3:4],
                                      op0=ALU.mult, op1=ALU.add), [Bbis], [Bbis])
                                  continue
                              if it < NIT:
                                  dve(lambda e: e.tensor_scalar(
                                      out=bis[0:rows, 5:6], in0=bis[0:rows, 4:5], scalar1=TOPK - 0.5, scalar2=0.5,
                                      op0=ALU.is_ge, op1=ALU.subtract), [Bbis], [Bbis])
                                  dve(lambda e: e.scalar_tensor_tensor(
                                      out=bis[0:rows, 3:4], in0=bis[0:rows, 5:6], scalar=bis[0:rows, 8 + it:9 + it],
                                      in1=bis[0:rows, 3:4], op0=ALU.mult, op1=ALU.add), [Bbis], [Bbis])
                              else:
                                  dve(lambda e: e.tensor_scalar(
                                      out=bis[0:rows, 5:6], in0=bis[0:rows, 4:5], scalar1=TOPK - 0.5, scalar2=1.0,
                                      op0=ALU.is_ge, op1=ALU.subtract), [Bbis], [Bbis])
                                  dve(lambda e: e.scalar_tensor_tensor(
                                      out=bis[0:rows, 6:7], in0=bis[0:rows, 5:6], scalar=bis[0:rows, 8 + it:9 + it],
                                      in1=bis[0:rows, 3:4], op0=ALU.mult, op1=ALU.add), [Bbis], [Bbis])
                          if active:
                              yield
                      for i in pair:
                          rows = tsz(i)
                          L = 128 * i + rows
                          lc = 128 * (i - 4 * b)
                          score, Bscore = score_l[i % 2], Bscore_l[i % 2]
                          bis, Bbis = bis_l[i % 2], Bbis_l[i % 2]
                          mb, bmb = maskb[i % 2], Bmb[i % 2]
                          if i >= 2 and on_act(i, pair):
                              dve(lambda e: e.tensor_scalar(out=mb[0:rows, 0:L], in0=score[0:rows, 0:L],
                                                            scalar1=bis[0:rows, 6:7], scalar2=0.0,
                                                            op0=ALU.add, op1=ALU.is_ge), [Bscore, Bbis], [bmb])
                          elif i >= 2:
                              dve(lambda e: e.tensor_scalar(out=mb[0:rows, 0:L], in0=score[0:rows, 0:L],
                                                            scalar1=bis[0:rows, 6:7], scalar2=None,
                                                            op0=ALU.is_ge), [Bscore, Bbis], [bmb])
                          for S in range(i + 1):
                              ks = tsz(S)
                              pk = MB[mzr[0] % 2]
                              mzr[0] += 1
                              Xb = bank[pk].bitcast(BF16)
                              P.op("pe", lambda e: e.transpose(
                                  out=Xb[0:ks, 0:rows], in_=mb[0:rows, 128 * S:128 * S + ks], identity=ident[0:rows, 0:rows]),
                                  reads=[bmb, Bconst], writes=[Bps[pk]])
                              act(mT[0:ks, S, lc:lc + rows], Xb[0:ks, 0:rows], AF.Identity, [Bps[pk]],
                                  [BmT[ms][S]] + alias_w[ms])
                              if S % 2 == 1:
                                  yield
                          yield

              AZ = [0, 1, 4, 5]
              azr = [0]
              pcnt = [0]

              def attn_stream(b):
                  t0, n = TBS[b]
                  ms = b % 2
                  mT = maskTs[ms]
                  lastS = tiles_of(b)[-1]
                  for pr in range(4):
                      kq = {}

                      def qk(S):
                          ks = tsz(S)
                          c0 = max(0, 128 * (S - 4 * b))
                          near = [i for i in (S, S + 1) if i in tiles_of(b)]
                          kk = []
                          for hh in range(2):
                              pb = 64 * hh
                              k = AZ[azr[0] % len(AZ)]
                              azr[0] += 1
                              kk.append(k)
                              mm(bank[k][0:ks, c0:n], kbT[pb:pb + 64, 128 * S:128 * S + ks],
                                 qbT[pb:pb + 64, pr, t0 + c0:t0 + n], True, len(near) == 0,
                                 [Bkb[S // 4], Bqb[pr][b]], [Bps[k]])
                          for hh in range(2):
                              h = 2 * pr + hh
                              k = kk[hh]
                              for ni, i in enumerate(near):
                                  lc = 128 * (i - 4 * b)
                                  w = tsz(i)
                                  off = i - S
                                  for hl in range(2):
                                      mm(bank[k][0:ks, lc:lc + w], ident[0:ks, 0:ks], BD[0:ks, h, off, hl, 0:w], False,
                                         (ni == len(near) - 1) and hl == 1, [Bconst], [Bps[k]])
                          return kk

                      def ensure(S):
                          if S <= lastS and S not in kq:
                              kq[S] = qk(S)

                      ensure(0)
                      for S in range(lastS + 1):
                          ensure(S + 1)
                          kk = kq.pop(S)
                          ks = tsz(S)
                          c0 = max(0, 128 * (S - 4 * b))
                          us = []
                          for hh in range(2):
                              h = 2 * pr + hh
                              u2 = pcnt[0] % 4
                              pcnt[0] += 1
                              us.append(u2)
                              act(pbuf[u2][0:ks, c0:n], bank[kk[hh]][0:ks, c0:n], AF.Exp, [Bps[kk[hh]]], [Bp[u2]],
                                  bias=b31[0:ks, h:h + 1], scale=0.125)
                              pool(lambda e, u2=u2: e.tensor_tensor(
                                  out=pmb[u2][0:ks, c0:n], in0=pbuf[u2][0:ks, c0:n], in1=mT[0:ks, S, c0:n],
                                  op=ALU.mult), [Bp[u2], BmT[ms][S]], [Bpm[u2]])
                          for hh in range(2):
                              pb, u2 = 64 * hh, us[hh]
                              mm(bank[6][pb:pb + 64, c0:n], vb[0:ks, S, :], pmb[u2][0:ks, c0:n], S == 0, S == lastS,
                                 [Bvb[S // 4], Bpm[u2]], [Bps[6]])
                          for hh in range(2):
                              pb, u2 = 64 * hh, us[hh]
                              mm(bank[7][pb:pb + 64, c0:n], ones64[0:ks, :], pmb[u2][0:ks, c0:n], S == 0, S == lastS,
                                 [Bconst, Bpm[u2]], [Bps[7]])
                          yield
                      dve(lambda e: e.reciprocal(out=rden[:, 0:n], in_=bank[7][:, 0:n]), [Bps[7]], [Brden])
                      dve(lambda e, pr=pr: e.tensor_tensor(out=yT[:, 1, pr, t0:t0 + n], in0=bank[6][:, 0:n],
                                                           in1=rden[:, 0:n], op=ALU.mult),
                          [Bps[6], Brden], [By[1][pr][b]])
                      yield

              def est_mask(b):
                  tot = 0
                  for i in tiles_of(b):
                      L = 128 * i + tsz(i)
                      tot += 8 * ((L + 511) // 512) + (NIT if i >= 2 else 0) + (i + 1) // 2 + 1
                  return float(tot)

              def est_attn(b):
                  return float(4 * (tiles_of(b)[-1] + 2))

              interleave([(mask_stream(0), 1.0)])
              for b in range(len(TBS)):
                  gl = [(attn_stream(b), est_attn(b))]
                  if b + 1 < len(TBS):
                      gl.append((mask_stream(b + 1), est_mask(b + 1)))
                  interleave(gl)

              stop_at('dsa')
              P.barrier()
              A.reset()
              wzo = A.alloc([8, 1024], BF16)
              Bwzo = Buf()
              wst = [A.alloc([24, 128], BF16) for _ in range(3)]
              Bwst = [Buf(), Buf(), Buf()]
              mT = A.alloc([8, T], BF16)
              BmTt = [[Buf() for b in range(5)] for m in range(8)]
              gb_in_g = A.alloc([1024], F32)
              gb_in_b = A.alloc([1024], F32)
              gb_g = A.alloc([1024], F32)
              gb_b = A.alloc([1024], F32)
              Bbc = Buf()
              xbuf = [A.alloc([1024], F32) for _ in range(2)]
              Bxb = [Buf(), Buf()]
              hres = A.alloc([1024], F32)
              Bhres = Buf()
              rr = [A.alloc([1024], F32) for _ in range(2)]
              Brr = [Buf(), Buf()]
              stats = A.alloc([2, 6], F32)
              mv = A.alloc([2], F32)
              rstd = A.alloc([1], F32)
              Bst = Buf()
              stats2 = A.alloc([2, 6], F32)
              mv2 = A.alloc([2], F32)
              rstd2 = A.alloc([1], F32)
              Bst2 = Buf()
              ef = [A.alloc([512], F32) for _ in range(2)]
              Bef = [Buf(), Buf()]
              sg = [A.alloc([512], F32) for _ in range(3)]
              Bsg = [Buf(), Buf(), Buf()]
              m1 = [A.alloc([512], F32) for _ in range(2)]
              Bm1 = [Buf(), Buf()]
              ld(gb_in_g, lning_d.partition_broadcast(128), [Bbc])
              ld(gb_in_b, lninb_d.partition_broadcast(128), [Bbc])
              ld(gb_g, lng_d.partition_broadcast(128), [Bbc])
              ld(gb_b, lnb_d.partition_broadcast(128), [Bbc])

              def sigmoid_from_psum(k, n, bias, si):
                  eb = zr[0] % 2
                  act(ef[eb][:, 0:n], bank[k][:, 0:n], AF.Exp, [Bps[k]], [Bef[eb]], bias=bias, scale=-1.0)
                  act(ef[eb][:, 0:n], ef[eb][:, 0:n], AF.Ln, [Bef[eb]], [Bef[eb]], bias=1.0)
                  act(sg[si][:, 0:n], ef[eb][:, 0:n], AF.Exp, [Bef[eb]], [Bsg[si]], scale=-1.0)

              def load_ws(m):
                  ws, bws = wst[m % 3], Bwst[m % 3]
                  ldc(ws[:, 0:8, :], win_d[:, C_GA + 128 * m:C_GA + 128 * (m + 1)].rearrange("(c p) n -> p c n", p=128), [bws])
                  ldc(ws[:, 8:16, :], win_d[:, C_GB + 128 * m:C_GB + 128 * (m + 1)].rearrange("(c p) n -> p c n", p=128), [bws])
                  ldc(ws[:, 16:20, :], wpa_d[:, 128 * m:128 * (m + 1)].rearrange("(c p) n -> p c n", p=128), [bws])
                  ldc(ws[:, 20:24, :], wpb_d[:, 128 * m:128 * (m + 1)].rearrange("(c p) n -> p c n", p=128), [bws])

              ldc(wzo[:, :, 0:512], win_d[:, C_ZA:C_ZA + 512].rearrange("(c p) n -> p c n", p=128), [Bwzo])
              ldc(wzo[:, :, 512:1024], win_d[:, C_ZB:C_ZB + 512].rearrange("(c p) n -> p c n", p=128), [Bwzo])
              load_ws(0)
              load_ws(1)
              zr = [0]
              for b, (t0, n) in enumerate(TBS):
                  for m in range(8):
                      br, pr = m // 4, m % 4
                      k = zr[0] % 6
                      si = zr[0] % 3
                      zr[0] += 1
                      for c in range(8):
                          mm(bank[k][:, 0:n], wzo[:, c, m * 128:(m + 1) * 128], hT[:, c, t0:t0 + n], c == 0, c == 7,
                             [Bwzo, Bh[b]], [Bps[k]])
                      sigmoid_from_psum(k, n, 0.0, si)
                      dve(lambda e, k=k, n=n, si=si: e.tensor_tensor(out=sg[si][:, 0:n], in0=bank[k][:, 0:n],
                                                                     in1=sg[si][:, 0:n], op=ALU.mult),
                          [Bps[k], Bsg[si]], [Bsg[si]])
                      pool(lambda e, n=n, si=si, br=br, pr=pr, t0=t0: e.tensor_tensor(
                          out=yT[:, br, pr, t0:t0 + n], in0=yT[:, br, pr, t0:t0 + n], in1=sg[si][:, 0:n], op=ALU.mult),
                          [By[br][pr][b], Bsg[si]], [By[br][pr][b]])
              stop_at('3i')
              for m in range(8):
                  u = m % 3
                  ws, bws = wst[u], Bwst[u]
                  for b, (t0, n) in enumerate(TBS):
                      res = []
                      for br in range(2):
                          k = zr[0] % 6
                          si = zr[0] % 3
                          zr[0] += 1
                          for c in range(8):
                              mm(bank[k][:, 0:n], ws[:, 8 * br + c, :], hT[:, c, t0:t0 + n], c == 0, c == 7,
                                 [bws, Bh[b]], [Bps[k]])
                          sigmoid_from_psum(k, n, negbg[:, 8 * br + m:8 * br + m + 1], si)
                          k2 = zr[0] % 6
                          zr[0] += 1
                          for c in range(4):
                              mm(bank[k2][:, 0:n], ws[:, 16 + 4 * br + c, :], yT[:, br, c, t0:t0 + n], c == 0, c == 3,
                                 [bws, By[br][c][b]], [Bps[k2]])
                          dve(lambda e, k2=k2, n=n, si=si, br=br: e.tensor_tensor(
                              out=m1[br][:, 0:n], in0=bank[k2][:, 0:n], in1=sg[si][:, 0:n], op=ALU.mult),
                              [Bps[k2], Bsg[si]], [Bm1[br]])
                      pool(lambda e, n=n, m=m, t0=t0: e.tensor_tensor(out=mT[:, m, t0:t0 + n], in0=m1[0][:, 0:n],
                                                                      in1=m1[1][:, 0:n], op=ALU.add),
                           [Bm1[0], Bm1[1]], [BmTt[m][b]])
                  if m == 0:
                      ldc(wzo[:, :, :], wo_d.rearrange("(c p) n -> p c n", p=128), [Bwzo])
                  if m + 2 < 8:
                      load_ws(m + 2)
              stop_at('3ii')
              hres_l = [hres, A.alloc([1024], F32)]
              Bhres_l = [Bhres, Buf()]
              st1 = [(stats, mv, rstd, Bst), (A.alloc([2, 6], F32), A.alloc([2], F32), A.alloc([1], F32), Buf())]
              st2 = [(stats2, mv2, rstd2, Bst2), (A.alloc([2, 6], F32), A.alloc([2], F32), A.alloc([1], F32), Buf())]
              nmr1 = [A.alloc([1], F32), A.alloc([1], F32)]
              nmr2 = [A.alloc([1], F32), A.alloc([1], F32)]

              def o_a(j):
                  rows = tsz(j)
                  xt, bx = xbuf[j % 2], Bxb[j % 2]
                  sts, mv_, rs_, bst = st1[j % 2]
                  hr, bhr = hres_l[j % 2], Bhres_l[j % 2]
                  nm = nmr1[j % 2]
                  load_x_tile(seq, j, xt, bx)
                  ln_stats(xt, rows, sts, mv_, rs_, bx, bst)
                  dve(lambda e: e.tensor_scalar(out=nm[0:rows, :], in0=mv_[0:rows, 0:1], scalar1=rs_[0:rows, 0:1],
                                                scalar2=-1.0, op0=ALU.mult, op1=ALU.mult), [bst], [bst])
                  act(hr[0:rows, :], xt[0:rows, :], AF.Identity, [bx, bst], [bhr], bias=nm[0:rows, 0:1],
                      scale=rs_[0:rows, 0:1])
                  pool(lambda e: e.tensor_tensor(out=hr[0:rows, :], in0=hr[0:rows, :], in1=gb_in_g[0:rows, :],
                                                 op=ALU.mult), [bhr, Bbc], [bhr])
                  pool(lambda e: e.tensor_tensor(out=hr[0:rows, :], in0=hr[0:rows, :], in1=gb_in_b[0:rows, :],
                                                 op=ALU.add), [bhr, Bbc], [bhr])

              def o_b(j):
                  rows = tsz(j)
                  b = j // 4
                  hr, bhr = hres_l[j % 2], Bhres_l[j % 2]
                  k = 6 if j % 2 == 0 else 4
                  for half in range(2):
                      for m in range(8):
                          mm(bank[k + half][0:rows, :], mT[:, m, 128 * j:128 * j + rows], wzo[:, m, 512 * half:512 * (half + 1)],
                             m == 0, m == 7, [BmTt[m][b], Bwzo], [Bps[k + half]])
                  r_, br_ = rr[j % 2], Brr[j % 2]
                  dve(lambda e: e.scalar_tensor_tensor(
                      out=r_[0:rows, :], in0=hr[0:rows, :], scalar=ALPHA, in1=ps[0:rows, 512 * k:512 * k + 1024],
                      op0=ALU.mult, op1=ALU.add), [bhr, Bps[k], Bps[k + 1]], [br_])

              def o_c(j):
                  rows = tsz(j)
                  r_, br_ = rr[j % 2], Brr[j % 2]
                  sts, mv_, rs_, bst = st2[j % 2]
                  nm = nmr2[j % 2]
                  ln_stats(r_, rows, sts, mv_, rs_, br_, bst)
                  dve(lambda e: e.tensor_scalar(out=nm[0:rows, :], in0=mv_[0:rows, 0:1], scalar1=rs_[0:rows, 0:1],
                                                scalar2=-1.0, op0=ALU.mult, op1=ALU.mult), [bst], [bst])
                  act(r_[0:rows, :], r_[0:rows, :], AF.Identity, [br_, bst], [br_], bias=nm[0:rows, 0:1],
                      scale=rs_[0:rows, 0:1])
                  dve(lambda e: e.tensor_tensor(out=r_[0:rows, :], in0=r_[0:rows, :], in1=gb_g[0:rows, :],
                                                op=ALU.mult), [br_, Bbc], [br_])
                  pool(lambda e: e.tensor_tensor(out=r_[0:rows, :], in0=r_[0:rows, :], in1=gb_b[0:rows, :],
                                                 op=ALU.add), [br_, Bbc], [br_])
                  if j == 0:
                      P.dma("sp", lambda e: e.dma_start(out=out_d[seq, 0:112, :], in_=r_[16:128, :]),
                            reads=[br_], writes=[])
                  else:
                      r0 = 128 * j - 16
                      P.dma("sp", lambda e: e.dma_start(out=out_d[seq, r0:r0 + rows, :], in_=r_[0:rows, :]),
                            reads=[br_], writes=[])

              for k_ in range(NT + 2):
                  if k_ < NT:
                      o_a(k_)
                  if 0 <= k_ - 1 < NT:
                      o_b(k_ - 1)
                  if 0 <= k_ - 2 < NT:
                      o_c(k_ - 2)
        except _Stop:
            pass
        if os.environ.get('K_VERBOSE'):
            print('OPCOUNTS', P.cnt, P.ndma, {e: len(P.ops[e]) for e in P.ENGS})
        P.emit(nc)
    return nc


def rel_bucket_np(d):
    d = np.asarray(d)
    nf = np.maximum(d, 1).astype(np.float32)
    large = 16 + (np.log(nf / np.float32(16)) / np.float32(math.log(128 / 16)) * np.float32(16)).astype(np.int32)
    large = np.minimum(large, 31)
    return np.where(d < 16, d, large)


def host_constants():
    cm = np.zeros((32, 383), np.float32)
    for k in range(383):
        d = k - 127
        if d >= 128:
            continue
        bkt = int(rel_bucket_np(max(d, 0)))
        cm[bkt, k] += 8.0
        cm[31, k] -= 8.0
    p = np.arange(128)
    c128 = np.zeros((128, 5, 128), np.float32)
    c128[:, 0, :] = np.eye(128, dtype=np.float32)
    c128[:, 1, :] = (p[:, None] < p[None, :]).astype(np.float32)
    c128[:, 2, :] = (p[:, None] >= p[None, :]).astype(np.float32)
    c128[:, 3, :] = (p[:, None] + p[None, :] == 127).astype(np.float32)
    c128[:, 4, :] = np.where(p[None, :] <= p[:, None], 0.0, -1e30).astype(np.float32)
    pw = (2.0 ** -np.arange(32)).astype(np.float32)
    return cm, c128, pw


_CACHE = {}


def kernel(x, meta_tokens, ln_in_g, ln_in_b, rel_bias, w_in, b_gate, idx_kn_g, idx_kn_b,
           w_pa, w_pb, w_o, ln_g, ln_b, _ncores=8, _nseq=4):
    f = lambda a: np.ascontiguousarray(np.asarray(a, dtype=np.float32))
    x = f(x)
    ncores, nseq = _ncores, _nseq
    key = (nseq,)
    if key not in _CACHE:
        _CACHE[key] = build_program(nseq)
    nc = _CACHE[key]
    cm, c128, pw = host_constants()
    cols = np.zeros((128, 48), np.float32)
    cols[:, 0:8] = f(ln_in_g).reshape(8, 128).T
    cols[:, 8:16] = f(ln_in_b).reshape(8, 128).T
    cols[:, 16:32] = f(b_gate).reshape(16, 128).T
    shared = {
        "meta": f(meta_tokens), "w_in": f(w_in)[0], "w_pa": f(w_pa)[0], "w_pb": f(w_pb)[0], "w_o": f(w_o)[0],
        "rel_bias": f(rel_bias), "ln_in_g": f(ln_in_g), "ln_in_b": f(ln_in_b), "ln_g": f(ln_g)[0], "ln_b": f(ln_b)[0],
        "cols": cols, "ikn_g": f(idx_kn_g)[0], "ikn_b": f(idx_kn_b)[0], "cmat": cm, "c128": c128, "pw": pw,
    }
    in_maps = []
    for c in range(ncores):
        d = dict(shared)
        d["x"] = np.ascontiguousarray(x[c * nseq:(c + 1) * nseq])
        in_maps.append(d)
    res = run_bass_kernel_spmd(nc, in_maps, core_ids=list(range(ncores)))
    return np.concatenate([np.asarray(r["out"]) for r in res.results], axis=0).astype(np.float32)
```

```python
import contextlib
import math
import os
import numpy as np
import concourse.bass as bass
import concourse.mybir as mybir
from concourse.bass_utils import run_bass_kernel_spmd

F32 = mybir.dt.float32
BF16 = mybir.dt.bfloat16
ALU = mybir.AluOpType
AF = mybir.ActivationFunctionType
AX = mybir.AxisListType

T = 2064
NT = 17
D = 1024
SEQ = 2048
NMETA = 16
TOPK = 256
EPS = 1e-5
ALPHA = 2.0 ** 0.25
WI_SCALE = (8 ** -0.5) * (64 ** -0.5)
NIT = 22
TBS = [(0, 512), (512, 512), (1024, 512), (1536, 512), (2048, 16)]
C_QA, C_KA, C_VA, C_ZA, C_QB, C_KB, C_VB, C_ZB, C_QI, C_KI, C_WI, C_GA, C_GB = (
    0, 512, 1024, 1536, 2048, 2560, 2624, 2688, 3200, 3712, 3776, 3784, 4808)
INCOLS = 5832


def tsz(j):
    return 128 if j < 16 else 16


def tiles_of(b):
    return list(range(4 * b, min(4 * b + 4, NT)))


class Buf:
    __slots__ = ("name", "w", "r")

    def __init__(self, name=""):
        self.name = name
        self.w = None
        self.r = []


class _Rec:
    def __init__(self):
        self.call = None

    def __getattr__(self, name):
        def f(*a, **kw):
            self.call = (name, a, kw)
            return self
        return f


def _freeze(fn):
    rec = _Rec()
    fn(rec)
    name, a, kw = rec.call
    return lambda e: getattr(e, name)(*a, **kw)


class Prog:
    ENGS = ("pe", "act", "dve", "pool", "sp")

    def __init__(self):
        self.ops = {e: [] for e in self.ENGS}
        self.cnt = {e: 0 for e in self.ENGS}
        self.pending = {e: set() for e in self.ENGS}
        self.ndma = 0
        self.ndma_e = {}
        self.dma_sem_use = {}
        self.NDMASEM = 24

    def _deps(self, eng, reads, writes):
        deps = set(self.pending[eng])
        self.pending[eng] = set()
        for b in reads:
            if b.w is not None:
                deps.add(b.w)
        for b in writes:
            if b.w is not None:
                deps.add(b.w)
            deps.update(b.r)
        if eng == "pe":
            deps = {d for d in deps if d[0] != "pe"}
        return deps

    def _mark(self, me, reads, writes):
        for b in reads:
            b.r.append(me)
            if len(b.r) > 64:
                best = {}
                for (k, v) in b.r:
                    if best.get(k, 0) < v:
                        best[k] = v
                b.r = list(best.items())
        for b in writes:
            b.w = me
            b.r = []

    def op(self, eng, fn, reads=(), writes=()):
        deps = self._deps(eng, reads, writes)
        self.cnt[eng] += 1
        me = (eng, self.cnt[eng])
        self.ops[eng].append(("op", _freeze(fn), deps, None))
        self._mark(me, reads, writes)
        return me

    def dma(self, eng, fn, reads=(), writes=()):
        deps = self._deps(eng, reads, writes)
        nd = self.ndma_e.get(eng, 0)
        self.ndma_e[eng] = nd + 1
        self.ndma += 1
        k = (eng, nd % self.NDMASEM)
        prev = self.dma_sem_use.get(k, 0)
        if prev:
            deps.add((("dma", k), 16 * prev))
        self.dma_sem_use[k] = prev + 1
        me = (("dma", k), 16 * (prev + 1))
        self.ops[eng].append(("dma", _freeze(fn), deps, k))
        self._mark(me, reads, writes)
        return me

    def barrier(self):
        snap = set()
        for e in self.ENGS:
            if self.cnt[e]:
                snap.add((e, self.cnt[e]))
        for k, c in self.dma_sem_use.items():
            snap.add((("dma", k), 16 * c))
        self.pending["act"] |= snap
        scr = self.bar_scratch
        me = self.op("act", lambda e: e.activation(out=scr, in_=scr, func=AF.Identity))
        for e in self.ENGS:
            if e != "act":
                self.pending[e].add(me)

    def emit(self, nc):
        with contextlib.ExitStack() as st:
            sems = {}
            for e in self.ENGS:
                sems[e] = st.enter_context(nc.semaphore("s_" + e))
            for k in self.dma_sem_use:
                sems[("dma", k)] = st.enter_context(nc.semaphore("s_dma_%s%d" % k))
            block = st.enter_context(nc.Block())
            regs = {"pe": block.tensor, "act": block.scalar, "dve": block.vector,
                    "pool": block.gpsimd, "sp": block.sync}
            for e in self.ENGS:
                ops = self.ops[e]
                if not ops and e != "sp":
                    continue

                def body(engine, ops=ops, e=e):
                    waited = {}
                    for kind, fn, deps, k in ops:
                        best = {}
                        for (sk, val) in deps:
                            if best.get(sk, 0) < val:
                                best[sk] = val
                        for sk in sorted(best, key=str):
                            val = best[sk]
                            if waited.get(sk, 0) >= val:
                                continue
                            engine.wait_ge(sems[sk], val)
                            waited[sk] = val
                        ins = fn(engine)
                        if kind == "op":
                            ins.then_inc(sems[e], 1)
                        else:
                            ins.then_inc(sems[("dma", k)], 16)
                    if e == "sp":
                        for k2, c2 in self.dma_sem_use.items():
                            engine.wait_ge(sems[("dma", k2)], 16 * c2)
                        for e2 in self.ENGS:
                            if e2 != "sp" and self.cnt[e2]:
                                engine.wait_ge(sems[e2], self.cnt[e2])
                regs[e](body)


class _Stop(Exception):
    pass


_TICK = [0]


def tick():
    _TICK[0] += 1
    stop_at('d%d' % _TICK[0])


def stop_at(name):
    if os.environ.get("K_STOP", "") == name:
        raise _Stop()


def interleave(gens):
    st = [[g, est, 0] for g, est in gens]
    while st:
        st.sort(key=lambda x: x[2] / x[1])
        g = st[0]
        try:
            next(g[0])
            g[2] += 1
        except StopIteration:
            st.pop(0)


class Arena:
    def __init__(self, ap, nelem):
        self.ap = ap
        self.n = nelem
        self.off = 0

    def reset(self):
        self.off = 0

    def alloc(self, shape, dt):
        n = 1
        for s in shape:
            n *= s
        ne = n * (2 if dt == F32 else 1)
        ne_al = (ne + 15) // 16 * 16
        assert self.off + ne_al <= self.n, ("arena overflow", self.off, ne_al, self.n)
        v = self.ap[:, self.off:self.off + ne]
        self.off += ne_al
        if dt == F32:
            v = v.bitcast(F32)
        if len(shape) == 2:
            v = v.rearrange("p (a b) -> p a b", a=shape[0])
        elif len(shape) == 3:
            v = v.rearrange("p (a b c) -> p a b c", a=shape[0], b=shape[1])
        return v


def build_program(nseq):
    nc = bass.Bass("TRN2", target_bir_lowering=False)
    dram = lambda name, shape, kind="ExternalInput": nc.dram_tensor(name, shape, F32, kind=kind).ap()
    x_d = dram("x", [nseq, SEQ, D])
    meta_d = dram("meta", [NMETA, D])
    win_d = dram("w_in", [D, INCOLS])
    wpa_d = dram("w_pa", [512, D])
    wpb_d = dram("w_pb", [512, D])
    wo_d = dram("w_o", [D, D])
    relb_d = dram("rel_bias", [32, 8])
    lning_d = dram("ln_in_g", [D])
    lninb_d = dram("ln_in_b", [D])
    lng_d = dram("ln_g", [D])
    lnb_d = dram("ln_b", [D])
    cols_d = dram("cols", [128, 48])
    ikg_d = dram("ikn_g", [64])
    ikb_d = dram("ikn_b", [64])
    cmat_d = dram("cmat", [32, 383])
    c128_d = dram("c128", [128, 5, 128])
    pw_d = dram("pw", [32])
    out_d = dram("out", [nseq, SEQ, D], kind="ExternalOutput")
    gscr_d = nc.dram_tensor("gscr", [8, 383], F32, kind="Internal").ap()

    P = Prog()
    st = contextlib.ExitStack()
    with st:
        sbt = lambda name, shape, dt: st.enter_context(nc.sbuf_tensor("sb_" + name, shape, dt))
        hT = sbt("hT", [128, 8, T], BF16)
        yT = sbt("yT", [128, 2, 4, T], BF16)
        ident = sbt("ident", [128, 128], BF16)
        identf = sbt("identf", [128, 128], F32)
        trimask = sbt("trimask", [128, 128], BF16)
        tri8 = sbt("tri8", [128, 128], BF16)
        ones8 = sbt("ones8", [128, 128], BF16)
        ones64 = sbt("ones64", [128, 64], BF16)
        negmask = sbt("negmask", [128, 128], F32)
        Jm = sbt("Jm", [128, 128], F32)
        c128f = sbt("c128f", [128, 5, 128], F32)
        BD = sbt("BD", [128, 8, 2, 2, 128], BF16)
        cols = sbt("cols", [128, 48], F32)
        negbg = sbt("negbg", [128, 16], F32)
        b31 = sbt("b31", [128, 8], F32)
        pw = sbt("pw", [128, 32], F32)
        ikg = sbt("ikg", [128, 64], F32)
        ikb = sbt("ikb", [128, 64], F32)
        relb = sbt("relb", [32, 8], F32)
        cmat = sbt("cmat", [32, 383], F32)
        gvec = sbt("gvec", [8, 383], F32)
        ARENA_N = 64400
        arena_t = sbt("arena", [128, ARENA_N], BF16)
        A = Arena(arena_t, ARENA_N)
        ps = st.enter_context(nc.psum_tensor("ps", [128, 4096], F32))
        bank = [ps[:, 512 * k:512 * (k + 1)] for k in range(8)]
        Bps = [Buf("ps%d" % k) for k in range(8)]
        Bconst = Buf("const")
        Bh = [Buf("h%d" % b) for b in range(5)]
        By = [[[Buf() for b in range(5)] for pr in range(4)] for br in range(2)]

        def mm(out, lhsT, rhs, start, stop, reads, writes):
            return P.op("pe", lambda e: e.matmul(out, lhsT=lhsT, rhs=rhs, start=start, stop=stop),
                        reads=reads, writes=writes)

        def act(out, in_, func, reads, writes, bias=0.0, scale=1.0):
            return P.op("act", lambda e: e.activation(out=out, in_=in_, func=func, bias=bias, scale=scale),
                        reads=reads, writes=writes)

        def dve(fn, reads, writes):
            return P.op("dve", fn, reads=reads, writes=writes)

        def pool(fn, reads, writes):
            return P.op("pool", fn, reads=reads, writes=writes)

        def ld(out, in_, writes, reads=()):
            return P.dma("sp", lambda e: e.dma_start(out=out, in_=in_), reads=reads, writes=writes)

        def ldc(out, in_, writes, reads=()):
            return P.dma("pool", lambda e: e.dma_start(out=out, in_=in_), reads=reads, writes=writes)

        P.bar_scratch = cols[0:1, 40:48]
        ld(c128f[:], c128_d, [Bconst])
        ld(cols[:], cols_d, [Bconst])
        ld(relb[:], relb_d, [Bconst])
        ld(cmat[:], cmat_d, [Bconst])
        ld(b31[:], relb_d[31, :].partition_broadcast(128), [Bconst])
        ld(pw[:], pw_d.partition_broadcast(128), [Bconst])
        ld(ikg[:], ikg_d.partition_broadcast(128), [Bconst])
        ld(ikb[:], ikb_d.partition_broadcast(128), [Bconst])
        dve(lambda e: e.tensor_copy(out=ident[:], in_=c128f[:, 0, :]), [Bconst], [Bconst])
        dve(lambda e: e.tensor_copy(out=identf[:], in_=c128f[:, 0, :]), [Bconst], [Bconst])
        dve(lambda e: e.tensor_copy(out=trimask[:], in_=c128f[:, 1, :]), [Bconst], [Bconst])
        dve(lambda e: e.tensor_scalar(out=tri8[:], in0=c128f[:, 2, :], scalar1=-8.0, scalar2=None, op0=ALU.mult),
            [Bconst], [Bconst])
        dve(lambda e: e.tensor_copy(out=Jm[:], in_=c128f[:, 3, :]), [Bconst], [Bconst])
        dve(lambda e: e.tensor_copy(out=negmask[:], in_=c128f[:, 4, :]), [Bconst], [Bconst])
        dve(lambda e: e.memset(ones8[:], -8.0), [], [Bconst])
        dve(lambda e: e.memset(ones64[:], 1.0), [], [Bconst])
        dve(lambda e: e.tensor_scalar(out=negbg[:], in0=cols[:, 16:32], scalar1=-1.0, scalar2=None, op0=ALU.mult),
            [Bconst], [Bconst])
        Bg = Buf("gscr")
        mm(bank[0][0:8, 0:383], relb[:, :], cmat[:, :], True, True, [Bconst], [Bps[0]])
        dve(lambda e: e.tensor_copy(out=gvec[:], in_=bank[0][0:8, 0:383]), [Bps[0]], [Bconst])
        ld(gscr_d, gvec[:], [Bg], reads=[Bconst])
        A.reset()
        hank = A.alloc([16, 128], F32)
        Bhank = Buf("hank")
        for h in range(8):
            for off in range(2):
                src = bass.AP(tensor=gscr_d.tensor, offset=h * 383 + 128 * off, ap=[[1, 128], [1, 128]])
                ld(hank[:, h * 2 + off, :], src, [Bhank], reads=[Bg])
        for h in range(8):
            for off in range(2):
                k = (h * 2 + off) % 4
                mm(bank[k][:, 0:128], Jm[:, :], hank[:, h * 2 + off, :], True, True, [Bhank, Bconst], [Bps[k]])
                dve(lambda e, h=h, off=off, k=k: e.tensor_copy(out=BD[:, h, off, 0, :], in_=bank[k][:, 0:128]),
                    [Bps[k]], [Bconst])
                dve(lambda e, h=h, off=off, k=k: e.tensor_tensor(out=BD[:, h, off, 1, :], in0=bank[k][:, 0:128],
                                                                  in1=BD[:, h, off, 0, :], op=ALU.subtract),
                    [Bps[k], Bconst], [Bconst])

        def ln_stats(xt, rows, stats, mv, rstd, Bx, Bs):
            for c in range(2):
                dve(lambda e, c=c: e.bn_stats(out=stats[0:rows, c, :], in_=xt[0:rows, 512 * c:512 * (c + 1)]),
                    [Bx], [Bs])
            dve(lambda e: e.bn_aggr(out=mv[0:rows, :], in_=stats[0:rows, :, :]), [Bs], [Bs])
            act(rstd[0:rows, :], mv[0:rows, 1:2], AF.Ln, [Bs], [Bs], bias=EPS)
            act(rstd[0:rows, :], rstd[0:rows, :], AF.Exp, [Bs], [Bs], scale=-0.5)

        def load_x_tile(seq, j, xt, Bx):
            rows = tsz(j)
            if j == 0:
                ld(xt[0:16, :], meta_d, [Bx])
                ld(xt[16:128, :], x_d[seq, 0:112, :], [Bx])
            else:
                r0 = 128 * j - 16
                ld(xt[0:rows, :], x_d[seq, r0:r0 + rows, :], [Bx])

        def load_w(dst, src_rows_ap, ncol, Bw, krows=8):
            ldc(dst, src_rows_ap.rearrange("(c p) n -> p c n", p=128), [Bw])

        try:
          for seq in range(nseq):
              P.barrier()
              A.reset()
              xbuf = [A.alloc([1024], F32) for _ in range(2)]
              xn = [A.alloc([1024], BF16) for _ in range(2)]
              stats = A.alloc([2, 6], F32)
              mv = A.alloc([2], F32)
              rstd = A.alloc([1], F32)
              wbuf = [A.alloc([8, 512], BF16) for _ in range(2)]
              qaT = A.alloc([4, T], BF16)
              kaT = A.alloc([4, T], BF16)
              va = A.alloc([NT, 512], BF16)
              def sb_scratch(kO, kR):
                  return dict(
                      ek=[A.alloc([512], BF16) for _ in range(4)], Bek=[Buf() for _ in range(4)],
                      sp=[A.alloc([512], BF16) for _ in range(6)], Bsp=[Buf() for _ in range(6)],
                      ec=[A.alloc([512], BF16) for _ in range(2)], Bec=[Buf() for _ in range(2)],
                      a=[A.alloc([512], BF16) for _ in range(4)], Ba=[Buf() for _ in range(4)],
                      acc=A.alloc([512], F32), Rrun=A.alloc([512], F32), Esc=A.alloc([512], F32),
                      tmp=A.alloc([512], F32), Bacc=Buf(), BR=Buf(), BE=Buf(), Btmp=Buf(),
                      kO=kO, kR=kR, c1=[0], c2=[0])

              sbsc = [sb_scratch(4, 5), sb_scratch(6, 7)]
              Bxb = [Buf(), Buf()]
              Bxn = [Buf(), Buf()]
              Bst = Buf()
              Bw = [Buf(), Buf()]
              Bqa = [[Buf() for b in range(5)] for pr in range(4)]
              Bka = [[Buf() for b in range(5)] for pr in range(4)]
              Bva = [Buf() for b in range(5)]

              stats_l = [stats, A.alloc([2, 6], F32)]
              mv_l = [mv, A.alloc([2], F32)]
              rstd_l = [rstd, A.alloc([1], F32)]
              Bst_l = [Bst, Buf()]

              def ln_a(j):
                  load_x_tile(seq, j, xbuf[j % 2], Bxb[j % 2])
                  ln_stats(xbuf[j % 2], tsz(j), stats_l[j % 2], mv_l[j % 2], rstd_l[j % 2], Bxb[j % 2], Bst_l[j % 2])

              def ln_b(j):
                  rows = tsz(j)
                  xt, bx = xbuf[j % 2], Bxb[j % 2]
                  xb, bxn = xn[j % 2], Bxn[j % 2]
                  mv_, rs_ = mv_l[j % 2], rstd_l[j % 2]
                  dve(lambda e: e.tensor_scalar(
                      out=xb[0:rows, :], in0=xt[0:rows, :], scalar1=mv_[0:rows, 0:1], scalar2=rs_[0:rows, 0:1],
                      op0=ALU.subtract, op1=ALU.mult), [bx, Bst_l[j % 2]], [bxn])
                  pk = 6 + (j % 2)
                  Xb = bank[pk].bitcast(BF16)
                  for c in range(8):
                      P.op("pe", lambda e, c=c: e.transpose(
                          out=Xb[:, c * 128:c * 128 + rows], in_=xb[0:rows, c * 128:(c + 1) * 128],
                          identity=ident[0:rows, 0:rows]), reads=[bxn, Bconst], writes=[Bps[pk]])

              def ln_c(j):
                  rows = tsz(j)
                  pk = 6 + (j % 2)
                  Xb = bank[pk].bitcast(BF16)
                  for c in range(8):
                      dve(lambda e, c=c: e.tensor_scalar(
                          out=hT[:, c, 128 * j:128 * j + rows], in0=Xb[:, c * 128:c * 128 + rows],
                          scalar1=cols[:, c:c + 1], scalar2=cols[:, 8 + c:9 + c], op0=ALU.mult, op1=ALU.add),
                          [Bps[pk], Bconst], [Bh[j // 4]])

              for j in range(NT + 2):
                  if j < NT:
                      ln_a(j)
                  if 0 <= j - 1 < NT:
                      ln_b(j - 1)
                  if 0 <= j - 2 < NT:
                      ln_c(j - 2)

              stop_at('ln')
              zrot = [0]

              def fm_project(col0, dstT, Bdst, nun=1, dup64=False):
                  u = zrot[0] % 2
                  wb, bw = wbuf[u], Bw[u]
                  if dup64:
                      ldc(wb[:, :, 0:64], win_d[:, col0:col0 + 64].rearrange("(c p) n -> p c n", p=128), [bw])
                      ldc(wb[:, :, 64:128], win_d[:, col0:col0 + 64].rearrange("(c p) n -> p c n", p=128), [bw])
                      nm = 1
                  else:
                      ldc(wb[:, :, :], win_d[:, col0:col0 + 512].rearrange("(c p) n -> p c n", p=128), [bw])
                      nm = 4
                  zrot[0] += 1
                  for b, (t0, n) in enumerate(TBS):
                      for m in range(nm):
                          k = zrot[0] % 4
                          zrot[0] += 1
                          for c in range(8):
                              mm(bank[k][:, 0:n], wb[:, c, m * 128:(m + 1) * 128], hT[:, c, t0:t0 + n],
                                 c == 0, c == 7, [bw, Bh[b]], [Bps[k]])
                          if dup64:
                              act(dstT[:, t0:t0 + n], bank[k][:, 0:n], AF.Identity, [Bps[k]], [Bdst[b]])
                          else:
                              act(dstT[:, m, t0:t0 + n], bank[k][:, 0:n], AF.Identity, [Bps[k]], [Bdst[m][b]])

              fm_project(C_QA, qaT, Bqa)
              fm_project(C_KA, kaT, Bka)
              u = zrot[0] % 2
              zrot[0] += 1
              wb, bw = wbuf[u], Bw[u]
              ldc(wb[:, :, :], win_d[:, C_VA:C_VA + 512].rearrange("(c p) n -> p c n", p=128), [bw])
              for j in range(NT):
                  rows = tsz(j)
                  k = 4 + (j % 2)
                  for c in range(8):
                      mm(bank[k][0:rows, :], hT[:, c, 128 * j:128 * j + rows], wb[:, c, :], c == 0, c == 7,
                         [bw, Bh[j // 4]], [Bps[k]])
                  act(va[0:rows, j, :], bank[k][0:rows, :], AF.Identity, [Bps[k]], [Bva[j // 4]])

              stop_at('p1a')
              ZB, CB = [0, 1], [2, 3]
              zc, cc = [0], [0]

              def sb_stream(b, pr, sc):
                  t0, n = TBS[b]
                  kO, kR = sc['kO'], sc['kR']
                  acc, Rrun, Esc, tmp = sc['acc'], sc['Rrun'], sc['Esc'], sc['tmp']
                  Bacc, BR, BE, Btmp = sc['Bacc'], sc['BR'], sc['BE'], sc['Btmp']
                  pool(lambda e: e.memset(acc[:, 0:n], 0.0), [], [Bacc])
                  pool(lambda e: e.memset(Rrun[:, 0:n], 0.0), [], [BR])
                  lastS = tiles_of(b)[-1]
                  units = list(range(lastS, -1, -1))
                  st = {}

                  def s1(S):
                      ks = tsz(S)
                      c0 = max(0, 128 * (S - 4 * b))
                      zk = []
                      for hh in range(2):
                          pb = 64 * hh
                          k = ZB[hh]
                          zk.append(k)
                          mm(bank[k][0:ks, c0:n], kaT[pb:pb + 64, pr, 128 * S:128 * S + ks],
                             qaT[pb:pb + 64, pr, t0 + c0:t0 + n], True, True, [Bka[pr][S // 4], Bqa[pr][b]], [Bps[k]])
                      rec = []
                      for hh in range(2):
                          i1 = sc['c1'][0]
                          sc['c1'][0] += 1
                          ek_t, Bek = sc['ek'][i1 % 4], sc['Bek'][i1 % 4]
                          sp_t, Bsp = sc['sp'][i1 % 6], sc['Bsp'][i1 % 6]
                          k = zk[hh]
                          act(ek_t[0:ks, c0:n], bank[k][0:ks, c0:n], AF.Exp, [Bps[k]], [Bek], scale=0.125)
                          if S >= 4 * b:
                              w = min(128, n - c0)
                              pool(lambda e: e.tensor_tensor(out=ek_t[0:ks, c0:c0 + w], in0=ek_t[0:ks, c0:c0 + w],
                                                             in1=trimask[0:ks, 0:w], op=ALU.mult), [Bek, Bconst], [Bek])
                          rec.append([ek_t, Bek, sp_t, Bsp])
                      for hh in range(2):
                          ek_t, Bek, sp_t, Bsp = rec[hh]
                          act(sp_t[0:ks, c0:n], ek_t[0:ks, c0:n], AF.Ln, [Bek], [Bsp], bias=1.0)
                      st[S] = (ks, c0, rec)

                  def s2(S):
                      ks, c0, rec = st[S]
                      for hh in range(2):
                          ek_t, Bek, sp_t, Bsp = rec[hh]
                          k2 = CB[hh]
                          mm(bank[k2][0:ks, c0:n], tri8[0:ks, 0:ks], sp_t[0:ks, c0:n], True, True, [Bsp, Bconst], [Bps[k2]])
                      for hh in range(2):
                          ek_t, Bek, sp_t, Bsp = rec[hh]
                          k2 = CB[hh]
                          i2 = sc['c2'][0]
                          sc['c2'][0] += 1
                          ec_t, Bec = sc['ec'][i2 % 2], sc['Bec'][i2 % 2]
                          a_t, Ba_t = sc['a'][i2 % 4], sc['Ba'][i2 % 4]
                          act(ec_t[0:ks, c0:n], bank[k2][0:ks, c0:n], AF.Exp, [Bps[k2]], [Bec], scale=0.125)
                          dve(lambda e: e.tensor_tensor(out=a_t[0:ks, c0:n], in0=ek_t[0:ks, c0:n], in1=ec_t[0:ks, c0:n],
                                                        op=ALU.mult), [Bek, Bec], [Ba_t])
                          rec[hh] = [a_t, Ba_t, sp_t, Bsp]

                  def s3(S):
                      ks, c0, rec = st.pop(S)
                      for hh in range(2):
                          a_t, Ba_t, sp_t, Bsp = rec[hh]
                          pb, h = 64 * hh, 2 * pr + hh
                          mm(bank[kO][pb:pb + 64, c0:n], va[0:ks, S, h * 64:(h + 1) * 64], a_t[0:ks, c0:n], True, True,
                             [Bva[S // 4], Ba_t], [Bps[kO]])
                      for hh in range(2):
                          a_t, Ba_t, sp_t, Bsp = rec[hh]
                          pb = 64 * hh
                          mm(bank[kR][pb:pb + 64, c0:n], ones64[0:ks, :], sp_t[0:ks, c0:n], True, True,
                             [Bsp, Bconst], [Bps[kR]])
                      act(Esc[:, c0:n], Rrun[:, c0:n], AF.Exp, [BR], [BE], scale=-1.0)
                      dve(lambda e: e.tensor_tensor(out=tmp[:, c0:n], in0=bank[kO][:, c0:n], in1=Esc[:, c0:n],
                                                    op=ALU.mult), [Bps[kO], BE], [Btmp])
                      dve(lambda e: e.tensor_tensor(out=acc[:, c0:n], in0=acc[:, c0:n], in1=tmp[:, c0:n],
                                                    op=ALU.add), [Bacc, Btmp], [Bacc])
                      if S > 0:
                          dve(lambda e: e.tensor_tensor(out=Rrun[:, c0:n], in0=Rrun[:, c0:n],
                                                        in1=bank[kR][:, c0:n], op=ALU.add), [BR, Bps[kR]], [BR])
                      else:
                          pool(lambda e: e.tensor_copy(out=yT[:, 0, pr, t0:t0 + n], in_=acc[:, 0:n]),
                               [Bacc], [By[0][pr][b]])

                  nu = len(units)
                  for k in range(nu + 2):
                      if k < nu:
                          s1(units[k])
                      if 0 <= k - 1 < nu:
                          s2(units[k - 1])
                      if 0 <= k - 2 < nu:
                          s3(units[k - 2])
                      yield

              def chain_sb(prs, sc):
                  for b in range(len(TBS)):
                      for pr in prs:
                          yield from sb_stream(b, pr, sc)

              interleave([(chain_sb((0, 2), sbsc[0]), 1.0), (chain_sb((1, 3), sbsc[1]), 1.0)])

              stop_at('sb')
              P.barrier()
              A.reset()
              wbuf = [A.alloc([8, 512], BF16) for _ in range(2)]
              Bw = [Buf(), Buf()]
              wtm = A.alloc([8, 136], BF16)
              Bwtm = Buf()
              qbT = A.alloc([4, T], BF16)
              qiT = A.alloc([4, T], BF16)
              kbT = A.alloc([T], BF16)
              kiT = A.alloc([T], BF16)
              vb = A.alloc([NT, 64], BF16)
              wi = A.alloc([NT, 8], F32)
              kin = A.alloc([128], BF16)
              kif = A.alloc([64], F32)
              ksq = A.alloc([64], F32)
              Bksq = Buf()
              stats = A.alloc([1, 6], F32)
              mv = A.alloc([2], F32)
              rstd = A.alloc([1], F32)
              score_l = [A.alloc([2176], F32), A.alloc([2176], F32)]
              Bscore_l = [Buf(), Buf()]
              maskb = [A.alloc([2176], BF16) for _ in range(2)]
              maskT = A.alloc([NT, 512], BF16)
              rl = [A.alloc([512], F32) for _ in range(2)]
              pbuf = [A.alloc([512], BF16) for _ in range(4)]
              pmb = [A.alloc([512], BF16) for _ in range(4)]
              rden = A.alloc([512], F32)
              bis_l = [A.alloc([40], F32), A.alloc([40], F32)]
              Bbis_l = [Buf(), Buf()]
              Bqb = [[Buf() for b in range(5)] for pr in range(4)]
              Bqi = [[Buf() for b in range(5)] for pr in range(4)]
              Bkb = [Buf() for b in range(5)]
              Bki = [Buf() for b in range(5)]
              Bvb = [Buf() for b in range(5)]
              Bwi = [Buf() for b in range(5)]
              Bkin, Bst, Brden = Buf(), Buf(), Buf()
              Bmb = [Buf(), Buf()]
              Brl = [Buf(), Buf()]
              Bp = [Buf() for _ in range(4)]
              Bpm = [Buf() for _ in range(4)]

              zrot[0] = 0
              fm_project(C_QB, qbT, Bqb)
              fm_project(C_QI, qiT, Bqi)
              stop_at('b1')
              fm_project(C_KB, kbT, Bkb, dup64=True)
              stop_at('b2')
              ldc(wtm[:, :, 0:64], win_d[:, C_VB:C_VB + 64].rearrange("(c p) n -> p c n", p=128), [Bwtm])
              ldc(wtm[:, :, 64:136], win_d[:, C_KI:C_KI + 72].rearrange("(c p) n -> p c n", p=128), [Bwtm])
              for j in range(NT):
                  rows = tsz(j)
                  k = 4 + (j % 2)
                  for c in range(8):
                      mm(bank[k][0:rows, 0:136], hT[:, c, 128 * j:128 * j + rows], wtm[:, c, :], c == 0, c == 7,
                         [Bwtm, Bh[j // 4]], [Bps[k]])
                  act(vb[0:rows, j, :], bank[k][0:rows, 0:64], AF.Identity, [Bps[k]], [Bvb[j // 4]])
                  act(wi[0:rows, j, :], bank[k][0:rows, 128:136], AF.Identity, [Bps[k]], [Bwi[j // 4]], scale=WI_SCALE)
                  stop_at('c1')
                  act(kif[0:rows, :], bank[k][0:rows, 64:128], AF.Identity, [Bps[k]], [Bkin])
                  tick()
                  dve(lambda e, rows=rows: e.tensor_reduce(out=mv[0:rows, 0:1], in_=kif[0:rows, :], axis=AX.X, op=ALU.add),
                      [Bkin], [Bst])
                  tick()
                  dve(lambda e, rows=rows: e.tensor_scalar(out=mv[0:rows, 0:1], in0=mv[0:rows, 0:1], scalar1=1.0 / 64,
                                                           scalar2=None, op0=ALU.mult), [Bst], [Bst])
                  tick()
                  dve(lambda e, rows=rows: e.tensor_scalar(out=kif[0:rows, :], in0=kif[0:rows, :], scalar1=mv[0:rows, 0:1],
                                                           scalar2=None, op0=ALU.subtract), [Bkin, Bst], [Bkin])
                  tick()
                  dve(lambda e, rows=rows: e.tensor_tensor(out=ksq[0:rows, :], in0=kif[0:rows, :], in1=kif[0:rows, :],
                                                           op=ALU.mult), [Bkin], [Bksq])
                  tick()
                  dve(lambda e, rows=rows: e.tensor_reduce(out=mv[0:rows, 1:2], in_=ksq[0:rows, :], axis=AX.X, op=ALU.add),
                      [Bksq], [Bst])
                  tick()
                  act(rstd[0:rows, :], mv[0:rows, 1:2], AF.Ln, [Bst], [Bst], bias=EPS, scale=1.0 / 64)
                  tick()
                  act(rstd[0:rows, :], rstd[0:rows, :], AF.Exp, [Bst], [Bst], scale=-0.5)
                  tick()
                  dve(lambda e, rows=rows: e.tensor_scalar(out=kif[0:rows, :], in0=kif[0:rows, :], scalar1=rstd[0:rows, 0:1],
                                                           scalar2=None, op0=ALU.mult), [Bkin, Bst], [Bkin])
                  tick()
                  dve(lambda e, rows=rows: e.tensor_tensor(out=kif[0:rows, :], in0=kif[0:rows, :], in1=ikg[0:rows, :],
                                                           op=ALU.mult), [Bkin, Bconst], [Bkin])
                  tick()
                  dve(lambda e, rows=rows: e.tensor_tensor(out=kin[0:rows, 0:64], in0=kif[0:rows, :], in1=ikb[0:rows, :],
                                                           op=ALU.add), [Bkin, Bconst], [Bkin])
                  tick()
                  dve(lambda e, rows=rows: e.tensor_copy(out=kin[0:rows, 64:128], in_=kin[0:rows, 0:64]), [Bkin], [Bkin])
                  stop_at('c2')
                  pk = 6 + (j % 2)
                  Xb = bank[pk].bitcast(BF16)
                  P.op("pe", lambda e, rows=rows, Xb=Xb: e.transpose(out=Xb[:, 0:rows], in_=kin[0:rows, :],
                                                                      identity=ident[0:rows, 0:rows]),
                       reads=[Bkin, Bconst], writes=[Bps[pk]])
                  act(kiT[:, 128 * j:128 * j + rows], Xb[:, 0:rows], AF.Identity, [Bps[pk]], [Bki[j // 4]])

              stop_at('p1b')
              maskTs = [maskT, A.ap[:, 0:NT * 512].rearrange("p (a b) -> p a b", a=NT)]
              BmT = [[Buf() for _ in range(NT)] for _ in range(2)]
              alias_w = [[], [Bw[0], Bw[1], Bwtm]]
              mzr = [0]
              MB = [2, 3]

              def on_act(i, pair):
                  return len(pair) == 2 and pair[0] >= 2 and i == pair[1]

              def mask_stream(b):
                  ms = b % 2
                  mT = maskTs[ms]
                  tl = tiles_of(b)
                  for p0 in range(0, len(tl), 2):
                      pair = tl[p0:p0 + 2]
                      for i in pair:
                          rows = tsz(i)
                          L = 128 * i + rows
                          score, Bscore = score_l[i % 2], Bscore_l[i % 2]
                          bis, Bbis = bis_l[i % 2], Bbis_l[i % 2]
                          mb, bmb = maskb[i % 2], Bmb[i % 2]
                          sblocks = [(s0, min(512, L - s0)) for s0 in range(0, L, 512)]
                          for (s0, sn) in sblocks:
                              for h in range(8):
                                  pr, pb = h // 2, 64 * (h % 2)
                                  k = MB[mzr[0] % 2]
                                  r2 = mzr[0] % 2
                                  mzr[0] += 1
                                  mm(bank[k][0:rows, 0:sn], qiT[pb:pb + 64, pr, 128 * i:128 * i + rows],
                                     kiT[pb:pb + 64, s0:s0 + sn], True, True,
                                     [Bqi[pr][b]] + [Bki[bb] for bb in range(s0 // 512, min(4, (s0 + sn - 1) // 512) + 1)],
                                     [Bps[k]])
                                  act(rl[r2][0:rows, 0:sn], bank[k][0:rows, 0:sn], AF.Relu, [Bps[k]], [Brl[r2]])
                                  if h == 0:
                                      dve(lambda e: e.tensor_scalar(
                                          out=score[0:rows, s0:s0 + sn], in0=rl[r2][0:rows, 0:sn],
                                          scalar1=wi[0:rows, i, h:h + 1], scalar2=None, op0=ALU.mult),
                                          [Brl[r2], Bwi[b]], [Bscore])
                                  else:
                                      dve(lambda e: e.scalar_tensor_tensor(
                                          out=score[0:rows, s0:s0 + sn], in0=rl[r2][0:rows, 0:sn],
                                          scalar=wi[0:rows, i, h:h + 1], in1=score[0:rows, s0:s0 + sn],
                                          op0=ALU.mult, op1=ALU.add), [Brl[r2], Bwi[b], Bscore], [Bscore])
                                  if h % 2 == 1:
                                      yield
                          if i >= 2:
                              dve(lambda e: e.reduce_max(out=bis[0:rows, 0:1], in_=score[0:rows, 0:L], axis=AX.X),
                                  [Bscore], [Bbis])
                              dve(lambda e: e.tensor_reduce(out=bis[0:rows, 1:2], in_=score[0:rows, 0:L],
                                                            axis=AX.X, op=ALU.min), [Bscore], [Bbis])
                          dve(lambda e: e.tensor_tensor(out=score[0:rows, 128 * i:128 * i + rows],
                                                        in0=score[0:rows, 128 * i:128 * i + rows],
                                                        in1=negmask[0:rows, 0:rows], op=ALU.add),
                              [Bscore, Bconst], [Bscore])
                          if i < 2:
                              dve(lambda e: e.tensor_scalar(out=mb[0:rows, 0:L], in0=score[0:rows, 0:L],
                                                            scalar1=-1e29, scalar2=None, op0=ALU.is_ge),
                                  [Bscore], [bmb])
                          else:
                              dve(lambda e: e.tensor_tensor(out=bis[0:rows, 2:3], in0=bis[0:rows, 0:1],
                                                            in1=bis[0:rows, 1:2], op=ALU.subtract), [Bbis], [Bbis])
                              dve(lambda e: e.tensor_scalar(out=bis[0:rows, 8:8 + NIT + 1], in0=pw[0:rows, 0:NIT + 1],
                                                            scalar1=bis[0:rows, 2:3], scalar2=None, op0=ALU.mult),
                                  [Bbis, Bconst], [Bbis])
                              if on_act(i, pair):
                                  dve(lambda e: e.tensor_scalar(out=bis[0:rows, 3:4], in0=bis[0:rows, 1:2], scalar1=-1.0,
                                                                scalar2=bis[0:rows, 9:10], op0=ALU.mult,
                                                                op1=ALU.subtract), [Bbis], [Bbis])
                              else:
                                  dve(lambda e: e.tensor_tensor(out=bis[0:rows, 3:4], in0=bis[0:rows, 1:2],
                                                                in1=bis[0:rows, 9:10], op=ALU.add), [Bbis], [Bbis])
                          yield
                      active = [i for i in pair if i >= 2]
                      for it in range(1, NIT + 1):
                          for i in active:
                              rows = tsz(i)
                              L = 128 * i + rows
                              score, Bscore = score_l[i % 2], Bscore_l[i % 2]
                              bis, Bbis = bis_l[i % 2], Bbis_l[i % 2]
                              mb, bmb = maskb[i % 2], Bmb[i % 2]
                              if on_act(i, pair):
                                  P.op("act", lambda e: e.activation(
                                      out=mb[0:rows, 0:L], in_=score[0:rows, 0:L], func=AF.Sign, bias=bis[0:rows, 3:4],
                                      scale=1.0, accum_out=bis[0:rows, 4:5]), reads=[Bscore, Bbis], writes=[bmb, Bbis])
                              else:
                                  dve(lambda e: e.tensor_scalar(
                                      out=mb[0:rows, 0:L], in0=score[0:rows, 0:L], scalar1=bis[0:rows, 3:4], scalar2=0.0,
                                      op0=ALU.is_ge, op1=ALU.add, accum_out=bis[0:rows, 4:5]), [Bscore, Bbis], [bmb, Bbis])
                          for i in active:
                              rows = tsz(i)
                              bis, Bbis = bis_l[i % 2], Bbis_l[i % 2]
                              if on_act(i, pair):
                                  thr = float(2 * TOPK - 1 - (128 * i + rows))
                                  last_ = (it == NIT)
                                  dve(lambda e: e.tensor_scalar(
                                      out=bis[0:rows, 5:6], in0=bis[0:rows, 4:5], scalar1=thr, scalar2=(0.0 if last_ else 0.5),
                                      op0=ALU.is_lt, op1=ALU.subtract), [Bbis], [Bbis])
                                  dve(lambda e: e.scalar_tensor_tensor(
                                      out=(bis[0:rows, 6:7] if last_ else bis[0:rows, 3:4]), in0=bis[0:rows, 5:6],
                                      scalar=bis[0:rows, 8 + it:9 + it], in1=bis[0:rows, 3:4],
                                      op0=ALU.mult, op1=ALU.add), [Bbis], [Bbis])
                                  continue
                              if it < NIT:
                                  dve(lambda e: e.tensor_scalar(
                                      out=bis[0:rows, 5:6], in0=bis[0:rows, 4:5], scalar1=TOPK - 0.5, scalar2=0.5,
                                      op0=ALU.is_ge, op1=ALU.subtract), [Bbis], [Bbis])
                                  dve(lambda e: e.scalar_tensor_tensor(
                                      out=bis[0:rows, 3:4], in0=bis[0:rows, 5:6], scalar=bis[0:rows, 8 + it:9 + it],
                                      in1=bis[0:rows, 3:4], op0=ALU.mult, op1=ALU.add), [Bbis], [Bbis])
                              else:
                                  dve(lambda e: e.tensor_scalar(
                                      out=bis[0:rows, 5:6], in0=bis[0:rows, 4:5], scalar1=TOPK - 0.5, scalar2=1.0,
                                      op0=ALU.is_ge, op1=ALU.subtract), [Bbis], [Bbis])
                                  dve(lambda e: e.scalar_tensor_tensor(
                                      out=bis[0:rows, 6:7], in0=bis[0:rows, 5:6], scalar=bis[0:rows, 8 + it:9 + it],
                                      in1=bis[0:rows, 3:4], op0=ALU.mult, op1=ALU.add), [Bbis], [Bbis])
                          if active:
                              yield
                      for i in pair:
                          rows = tsz(i)
                          L = 128 * i + rows
                          lc = 128 * (i - 4 * b)
                          score, Bscore = score_l[i % 2], Bscore_l[i % 2]
                          bis, Bbis = bis_l[i % 2], Bbis_l[i % 2]
                          mb, bmb = maskb[i % 2], Bmb[i % 2]
                          if i >= 2 and on_act(i, pair):
                              dve(lambda e: e.tensor_scalar(out=mb[0:rows, 0:L], in0=score[0:rows, 0:L],
                                                            scalar1=bis[0:rows, 6:7], scalar2=0.0,
                                                            op0=ALU.add, op1=ALU.is_ge), [Bscore, Bbis], [bmb])
                          elif i >= 2:
                              dve(lambda e: e.tensor_scalar(out=mb[0:rows, 0:L], in0=score[0:rows, 0:L],
                                                            scalar1=bis[0:rows, 6:7], scalar2=None,
                                                            op0=ALU.is_ge), [Bscore, Bbis], [bmb])
                          for S in range(i + 1):
                              ks = tsz(S)
                              pk = MB[mzr[0] % 2]
                              mzr[0] += 1
                              Xb = bank[pk].bitcast(BF16)
                              P.op("pe", lambda e: e.transpose(
                                  out=Xb[0:ks, 0:rows], in_=mb[0:rows, 128 * S:128 * S + ks], identity=ident[0:rows, 0:rows]),
                                  reads=[bmb, Bconst], writes=[Bps[pk]])
                              act(mT[0:ks, S, lc:lc + rows], Xb[0:ks, 0:rows], AF.Identity, [Bps[pk]],
                                  [BmT[ms][S]] + alias_w[ms])
                              if S % 2 == 1:
                                  yield
                          yield

              AZ = [0, 1, 4, 5]
              azr = [0]
              pcnt = [0]

              def attn_stream(b):
                  t0, n = TBS[b]
                  ms = b % 2
                  mT = maskTs[ms]
                  lastS = tiles_of(b)[-1]
                  for pr in range(4):
                      kq = {}

                      def qk(S):
                          ks = tsz(S)
                          c0 = max(0, 128 * (S - 4 * b))
                          near = [i for i in (S, S + 1) if i in tiles_of(b)]
                          kk = []
                          for hh in range(2):
                              pb = 64 * hh
                              k = AZ[azr[0] % len(AZ)]
                              azr[0] += 1
                              kk.append(k)
                              mm(bank[k][0:ks, c0:n], kbT[pb:pb + 64, 128 * S:128 * S + ks],
                                 qbT[pb:pb + 64, pr, t0 + c0:t0 + n], True, len(near) == 0,
                                 [Bkb[S // 4], Bqb[pr][b]], [Bps[k]])
                          for hh in range(2):
                              h = 2 * pr + hh
                              k = kk[hh]
                              for ni, i in enumerate(near):
                                  lc = 128 * (i - 4 * b)
                                  w = tsz(i)
                                  off = i - S
                                  for hl in range(2):
                                      mm(bank[k][0:ks, lc:lc + w], ident[0:ks, 0:ks], BD[0:ks, h, off, hl, 0:w], False,
                                         (ni == len(near) - 1) and hl == 1, [Bconst], [Bps[k]])
                          return kk

                      def ensure(S):
                          if S <= lastS and S not in kq:
                              kq[S] = qk(S)

                      ensure(0)
                      for S in range(lastS + 1):
                          ensure(S + 1)
                          kk = kq.pop(S)
                          ks = tsz(S)
                          c0 = max(0, 128 * (S - 4 * b))
                          us = []
                          for hh in range(2):
                              h = 2 * pr + hh
                              u2 = pcnt[0] % 4
                              pcnt[0] += 1
                              us.append(u2)
                              act(pbuf[u2][0:ks, c0:n], bank[kk[hh]][0:ks, c0:n], AF.Exp, [Bps[kk[hh]]], [Bp[u2]],
                                  bias=b31[0:ks, h:h + 1], scale=0.125)
                              pool(lambda e, u2=u2: e.tensor_tensor(
                                  out=pmb[u2][0:ks, c0:n], in0=pbuf[u2][0:ks, c0:n], in1=mT[0:ks, S, c0:n],
                                  op=ALU.mult), [Bp[u2], BmT[ms][S]], [Bpm[u2]])
                          for hh in range(2):
                              pb, u2 = 64 * hh, us[hh]
                              mm(bank[6][pb:pb + 64, c0:n], vb[0:ks, S, :], pmb[u2][0:ks, c0:n], S == 0, S == lastS,
                                 [Bvb[S // 4], Bpm[u2]], [Bps[6]])
                          for hh in range(2):
                              pb, u2 = 64 * hh, us[hh]
                              mm(bank[7][pb:pb + 64, c0:n], ones64[0:ks, :], pmb[u2][0:ks, c0:n], S == 0, S == lastS,
                                 [Bconst, Bpm[u2]], [Bps[7]])
                          yield
                      dve(lambda e: e.reciprocal(out=rden[:, 0:n], in_=bank[7][:, 0:n]), [Bps[7]], [Brden])
                      dve(lambda e, pr=pr: e.tensor_tensor(out=yT[:, 1, pr, t0:t0 + n], in0=bank[6][:, 0:n],
                                                           in1=rden[:, 0:n], op=ALU.mult),
                          [Bps[6], Brden], [By[1][pr][b]])
                      yield

              def est_mask(b):
                  tot = 0
                  for i in tiles_of(b):
                      L = 128 * i + tsz(i)
                      tot += 4 * ((L + 511) // 512) + (NIT if i >= 2 else 0) // 2 + (i + 1) // 2 + 1
                  return float(tot)

              def est_attn(b):
                  return float(4 * (tiles_of(b)[-1] + 2))

              interleave([(mask_stream(0), 1.0)])
              for b in range(len(TBS)):
                  gl = [(attn_stream(b), est_attn(b))]
                  if b + 1 < len(TBS):
                      gl.append((mask_stream(b + 1), est_mask(b + 1)))
                  interleave(gl)

              stop_at('dsa')
              P.barrier()
              A.reset()
              wzo = A.alloc([8, 1024], BF16)
              Bwzo = Buf()
              wst = [A.alloc([24, 128], BF16) for _ in range(3)]
              Bwst = [Buf(), Buf(), Buf()]
              mT = A.alloc([8, T], BF16)
              BmTt = [[Buf() for b in range(5)] for m in range(8)]
              gb_in_g = A.alloc([1024], F32)
              gb_in_b = A.alloc([1024], F32)
              gb_g = A.alloc([1024], F32)
              gb_b = A.alloc([1024], F32)
              Bbc = Buf()
              xbuf = [A.alloc([1024], F32) for _ in range(2)]
              Bxb = [Buf(), Buf()]
              hres = A.alloc([1024], F32)
              Bhres = Buf()
              rr = [A.alloc([1024], F32) for _ in range(2)]
              Brr = [Buf(), Buf()]
              stats = A.alloc([2, 6], F32)
              mv = A.alloc([2], F32)
              rstd = A.alloc([1], F32)
              Bst = Buf()
              stats2 = A.alloc([2, 6], F32)
              mv2 = A.alloc([2], F32)
              rstd2 = A.alloc([1], F32)
              Bst2 = Buf()
              ef = [A.alloc([512], F32) for _ in range(2)]
              Bef = [Buf(), Buf()]
              sg = [A.alloc([512], F32) for _ in range(3)]
              Bsg = [Buf(), Buf(), Buf()]
              m1 = [A.alloc([512], F32) for _ in range(2)]
              Bm1 = [Buf(), Buf()]
              ld(gb_in_g, lning_d.partition_broadcast(128), [Bbc])
              ld(gb_in_b, lninb_d.partition_broadcast(128), [Bbc])
              ld(gb_g, lng_d.partition_broadcast(128), [Bbc])
              ld(gb_b, lnb_d.partition_broadcast(128), [Bbc])

              def sigmoid_from_psum(k, n, bias, si):
                  eb = zr[0] % 2
                  act(ef[eb][:, 0:n], bank[k][:, 0:n], AF.Exp, [Bps[k]], [Bef[eb]], bias=bias, scale=-1.0)
                  act(ef[eb][:, 0:n], ef[eb][:, 0:n], AF.Ln, [Bef[eb]], [Bef[eb]], bias=1.0)
                  act(sg[si][:, 0:n], ef[eb][:, 0:n], AF.Exp, [Bef[eb]], [Bsg[si]], scale=-1.0)

              def load_ws(m):
                  ws, bws = wst[m % 3], Bwst[m % 3]
                  ldc(ws[:, 0:8, :], win_d[:, C_GA + 128 * m:C_GA + 128 * (m + 1)].rearrange("(c p) n -> p c n", p=128), [bws])
                  ldc(ws[:, 8:16, :], win_d[:, C_GB + 128 * m:C_GB + 128 * (m + 1)].rearrange("(c p) n -> p c n", p=128), [bws])
                  ldc(ws[:, 16:20, :], wpa_d[:, 128 * m:128 * (m + 1)].rearrange("(c p) n -> p c n", p=128), [bws])
                  ldc(ws[:, 20:24, :], wpb_d[:, 128 * m:128 * (m + 1)].rearrange("(c p) n -> p c n", p=128), [bws])

              ldc(wzo[:, :, 0:512], win_d[:, C_ZA:C_ZA + 512].rearrange("(c p) n -> p c n", p=128), [Bwzo])
              ldc(wzo[:, :, 512:1024], win_d[:, C_ZB:C_ZB + 512].rearrange("(c p) n -> p c n", p=128), [Bwzo])
              load_ws(0)
              load_ws(1)
              zr = [0]
              for b, (t0, n) in enumerate(TBS):
                  for m in range(8):
                      br, pr = m // 4, m % 4
                      k = zr[0] % 6
                      si = zr[0] % 3
                      zr[0] += 1
                      for c in range(8):
                          mm(bank[k][:, 0:n], wzo[:, c, m * 128:(m + 1) * 128], hT[:, c, t0:t0 + n], c == 0, c == 7,
                             [Bwzo, Bh[b]], [Bps[k]])
                      sigmoid_from_psum(k, n, 0.0, si)
                      dve(lambda e, k=k, n=n, si=si: e.tensor_tensor(out=sg[si][:, 0:n], in0=bank[k][:, 0:n],
                                                                     in1=sg[si][:, 0:n], op=ALU.mult),
                          [Bps[k], Bsg[si]], [Bsg[si]])
                      pool(lambda e, n=n, si=si, br=br, pr=pr, t0=t0: e.tensor_tensor(
                          out=yT[:, br, pr, t0:t0 + n], in0=yT[:, br, pr, t0:t0 + n], in1=sg[si][:, 0:n], op=ALU.mult),
                          [By[br][pr][b], Bsg[si]], [By[br][pr][b]])
              stop_at('3i')
              for m in range(8):
                  u = m % 3
                  ws, bws = wst[u], Bwst[u]
                  for b, (t0, n) in enumerate(TBS):
                      res = []
                      for br in range(2):
                          k = zr[0] % 6
                          si = zr[0] % 3
                          zr[0] += 1
                          for c in range(8):
                              mm(bank[k][:, 0:n], ws[:, 8 * br + c, :], hT[:, c, t0:t0 + n], c == 0, c == 7,
                                 [bws, Bh[b]], [Bps[k]])
                          sigmoid_from_psum(k, n, negbg[:, 8 * br + m:8 * br + m + 1], si)
                          k2 = zr[0] % 6
                          zr[0] += 1
                          for c in range(4):
                              mm(bank[k2][:, 0:n], ws[:, 16 + 4 * br + c, :], yT[:, br, c, t0:t0 + n], c == 0, c == 3,
                                 [bws, By[br][c][b]], [Bps[k2]])
                          dve(lambda e, k2=k2, n=n, si=si, br=br: e.tensor_tensor(
                              out=m1[br][:, 0:n], in0=bank[k2][:, 0:n], in1=sg[si][:, 0:n], op=ALU.mult),
                              [Bps[k2], Bsg[si]], [Bm1[br]])
                      pool(lambda e, n=n, m=m, t0=t0: e.tensor_tensor(out=mT[:, m, t0:t0 + n], in0=m1[0][:, 0:n],
                                                                      in1=m1[1][:, 0:n], op=ALU.add),
                           [Bm1[0], Bm1[1]], [BmTt[m][b]])
                  if m == 0:
                      ldc(wzo[:, :, :], wo_d.rearrange("(c p) n -> p c n", p=128), [Bwzo])
                  if m + 2 < 8:
                      load_ws(m + 2)
              stop_at('3ii')
              hres_l = [hres, A.alloc([1024], F32)]
              Bhres_l = [Bhres, Buf()]
              st1 = [(stats, mv, rstd, Bst), (A.alloc([2, 6], F32), A.alloc([2], F32), A.alloc([1], F32), Buf())]
              st2 = [(stats2, mv2, rstd2, Bst2), (A.alloc([2, 6], F32), A.alloc([2], F32), A.alloc([1], F32), Buf())]
              nmr1 = [A.alloc([1], F32), A.alloc([1], F32)]
              nmr2 = [A.alloc([1], F32), A.alloc([1], F32)]

              def o_a(j):
                  rows = tsz(j)
                  xt, bx = xbuf[j % 2], Bxb[j % 2]
                  sts, mv_, rs_, bst = st1[j % 2]
                  hr, bhr = hres_l[j % 2], Bhres_l[j % 2]
                  nm = nmr1[j % 2]
                  load_x_tile(seq, j, xt, bx)
                  ln_stats(xt, rows, sts, mv_, rs_, bx, bst)
                  dve(lambda e: e.tensor_scalar(out=nm[0:rows, :], in0=mv_[0:rows, 0:1], scalar1=rs_[0:rows, 0:1],
                                                scalar2=-1.0, op0=ALU.mult, op1=ALU.mult), [bst], [bst])
                  act(hr[0:rows, :], xt[0:rows, :], AF.Identity, [bx, bst], [bhr], bias=nm[0:rows, 0:1],
                      scale=rs_[0:rows, 0:1])
                  pool(lambda e: e.tensor_tensor(out=hr[0:rows, :], in0=hr[0:rows, :], in1=gb_in_g[0:rows, :],
                                                 op=ALU.mult), [bhr, Bbc], [bhr])
                  pool(lambda e: e.tensor_tensor(out=hr[0:rows, :], in0=hr[0:rows, :], in1=gb_in_b[0:rows, :],
                                                 op=ALU.add), [bhr, Bbc], [bhr])

              def o_b(j):
                  rows = tsz(j)
                  b = j // 4
                  hr, bhr = hres_l[j % 2], Bhres_l[j % 2]
                  k = 6 if j % 2 == 0 else 4
                  for half in range(2):
                      for m in range(8):
                          mm(bank[k + half][0:rows, :], mT[:, m, 128 * j:128 * j + rows], wzo[:, m, 512 * half:512 * (half + 1)],
                             m == 0, m == 7, [BmTt[m][b], Bwzo], [Bps[k + half]])
                  r_, br_ = rr[j % 2], Brr[j % 2]
                  dve(lambda e: e.scalar_tensor_tensor(
                      out=r_[0:rows, :], in0=hr[0:rows, :], scalar=ALPHA, in1=ps[0:rows, 512 * k:512 * k + 1024],
                      op0=ALU.mult, op1=ALU.add), [bhr, Bps[k], Bps[k + 1]], [br_])

              def o_c(j):
                  rows = tsz(j)
                  r_, br_ = rr[j % 2], Brr[j % 2]
                  sts, mv_, rs_, bst = st2[j % 2]
                  nm = nmr2[j % 2]
                  ln_stats(r_, rows, sts, mv_, rs_, br_, bst)
                  dve(lambda e: e.tensor_scalar(out=nm[0:rows, :], in0=mv_[0:rows, 0:1], scalar1=rs_[0:rows, 0:1],
                                                scalar2=-1.0, op0=ALU.mult, op1=ALU.mult), [bst], [bst])
                  act(r_[0:rows, :], r_[0:rows, :], AF.Identity, [br_, bst], [br_], bias=nm[0:rows, 0:1],
                      scale=rs_[0:rows, 0:1])
                  dve(lambda e: e.tensor_tensor(out=r_[0:rows, :], in0=r_[0:rows, :], in1=gb_g[0:rows, :],
                                                op=ALU.mult), [br_, Bbc], [br_])
                  pool(lambda e: e.tensor_tensor(out=r_[0:rows, :], in0=r_[0:rows, :], in1=gb_b[0:rows, :],
                                                 op=ALU.add), [br_, Bbc], [br_])
                  if j == 0:
                      P.dma("sp", lambda e: e.dma_start(out=out_d[seq, 0:112, :], in_=r_[16:128, :]),
                            reads=[br_], writes=[])
                  else:
                      r0 = 128 * j - 16
                      P.dma("sp", lambda e: e.dma_start(out=out_d[seq, r0:r0 + rows, :], in_=r_[0:rows, :]),
                            reads=[br_], writes=[])

              for k_ in range(NT + 2):
                  if k_ < NT:
                      o_a(k_)
                  if 0 <= k_ - 1 < NT:
                      o_b(k_ - 1)
                  if 0 <= k_ - 2 < NT:
                      o_c(k_ - 2)
        except _Stop:
            pass
        if os.environ.get('K_VERBOSE'):
            print('OPCOUNTS', P.cnt, P.ndma, {e: len(P.ops[e]) for e in P.ENGS})
        P.emit(nc)
    return nc


def rel_bucket_np(d):
    d = np.asarray(d)
    nf = np.maximum(d, 1).astype(np.float32)
    large = 16 + (np.log(nf / np.float32(16)) / np.float32(math.log(128 / 16)) * np.float32(16)).astype(np.int32)
    large = np.minimum(large, 31)
    return np.where(d < 16, d, large)


def host_constants():
    cm = np.zeros((32, 383), np.float32)
    for k in range(383):
        d = k - 127
        if d >= 128:
            continue
        bkt = int(rel_bucket_np(max(d, 0)))
        cm[bkt, k] += 8.0
        cm[31, k] -= 8.0
    p = np.arange(128)
    c128 = np.zeros((128, 5, 128), np.float32)
    c128[:, 0, :] = np.eye(128, dtype=np.float32)
    c128[:, 1, :] = (p[:, None] < p[None, :]).astype(np.float32)
    c128[:, 2, :] = (p[:, None] >= p[None, :]).astype(np.float32)
    c128[:, 3, :] = (p[:, None] + p[None, :] == 127).astype(np.float32)
    c128[:, 4, :] = np.where(p[None, :] <= p[:, None], 0.0, -1e30).astype(np.float32)
    pw = (2.0 ** -np.arange(32)).astype(np.float32)
    return cm, c128, pw


_CACHE = {}


def kernel(x, meta_tokens, ln_in_g, ln_in_b, rel_bias, w_in, b_gate, idx_kn_g, idx_kn_b,
           w_pa, w_pb, w_o, ln_g, ln_b, _ncores=8, _nseq=4):
    f = lambda a: np.ascontiguousarray(np.asarray(a, dtype=np.float32))
    x = f(x)
    ncores, nseq = _ncores, _nseq
    key = (nseq,)
    if key not in _CACHE:
        _CACHE[key] = build_program(nseq)
    nc = _CACHE[key]
    cm, c128, pw = host_constants()
    cols = np.zeros((128, 48), np.float32)
    cols[:, 0:8] = f(ln_in_g).reshape(8, 128).T
    cols[:, 8:16] = f(ln_in_b).reshape(8, 128).T
    cols[:, 16:32] = f(b_gate).reshape(16, 128).T
    shared = {
        "meta": f(meta_tokens), "w_in": f(w_in)[0], "w_pa": f(w_pa)[0], "w_pb": f(w_pb)[0], "w_o": f(w_o)[0],
        "rel_bias": f(rel_bias), "ln_in_g": f(ln_in_g), "ln_in_b": f(ln_in_b), "ln_g": f(ln_g)[0], "ln_b": f(ln_b)[0],
        "cols": cols, "ikn_g": f(idx_kn_g)[0], "ikn_b": f(idx_kn_b)[0], "cmat": cm, "c128": c128, "pw": pw,
    }
    in_maps = []
    for c in range(ncores):
        d = dict(shared)
        d["x"] = np.ascontiguousarray(x[c * nseq:(c + 1) * nseq])
        in_maps.append(d)
    res = run_bass_kernel_spmd(nc, in_maps, core_ids=list(range(ncores)))
    return np.concatenate([np.asarray(r["out"]) for r in res.results], axis=0).astype(np.float32)
```

```python
import contextlib
import math
import os
import numpy as np
import concourse.bass as bass
import concourse.mybir as mybir
from concourse.bass_utils import run_bass_kernel_spmd

F32 = mybir.dt.float32
BF16 = mybir.dt.bfloat16
ALU = mybir.AluOpType
AF = mybir.ActivationFunctionType
AX = mybir.AxisListType

T = 2064
NT = 17
D = 1024
SEQ = 2048
NMETA = 16
TOPK = 256
EPS = 1e-5
ALPHA = 2.0 ** 0.25
WI_SCALE = (8 ** -0.5) * (64 ** -0.5)
NIT = 22
TBS = [(0, 512), (512, 512), (1024, 512), (1536, 512), (2048, 16)]
C_QA, C_KA, C_VA, C_ZA, C_QB, C_KB, C_VB, C_ZB, C_QI, C_KI, C_WI, C_GA, C_GB = (
    0, 512, 1024, 1536, 2048, 2560, 2624, 2688, 3200, 3712, 3776, 3784, 4808)
INCOLS = 5832


def tsz(j):
    return 128 if j < 16 else 16


def tiles_of(b):
    return list(range(4 * b, min(4 * b + 4, NT)))


class Buf:
    __slots__ = ("name", "w", "r")

    def __init__(self, name=""):
        self.name = name
        self.w = None
        self.r = []


class _Rec:
    def __init__(self):
        self.call = None

    def __getattr__(self, name):
        def f(*a, **kw):
            self.call = (name, a, kw)
            return self
        return f


def _freeze(fn):
    rec = _Rec()
    fn(rec)
    name, a, kw = rec.call
    return lambda e: getattr(e, name)(*a, **kw)


class Prog:
    ENGS = ("pe", "act", "dve", "pool", "sp")

    def __init__(self):
        self.ops = {e: [] for e in self.ENGS}
        self.cnt = {e: 0 for e in self.ENGS}
        self.pending = {e: set() for e in self.ENGS}
        self.ndma = 0
        self.ndma_e = {}
        self.dma_sem_use = {}
        self.NDMASEM = 24

    def _deps(self, eng, reads, writes):
        deps = set(self.pending[eng])
        self.pending[eng] = set()
        for b in reads:
            if b.w is not None:
                deps.add(b.w)
        for b in writes:
            if b.w is not None:
                deps.add(b.w)
            deps.update(b.r)
        if eng == "pe":
            deps = {d for d in deps if d[0] != "pe"}
        return deps

    def _mark(self, me, reads, writes):
        for b in reads:
            b.r.append(me)
            if len(b.r) > 64:
                best = {}
                for (k, v) in b.r:
                    if best.get(k, 0) < v:
                        best[k] = v
                b.r = list(best.items())
        for b in writes:
            b.w = me
            b.r = []

    def op(self, eng, fn, reads=(), writes=()):
        deps = self._deps(eng, reads, writes)
        self.cnt[eng] += 1
        me = (eng, self.cnt[eng])
        self.ops[eng].append(("op", _freeze(fn), deps, None))
        self._mark(me, reads, writes)
        return me

    def dma(self, eng, fn, reads=(), writes=()):
        deps = self._deps(eng, reads, writes)
        nd = self.ndma_e.get(eng, 0)
        self.ndma_e[eng] = nd + 1
        self.ndma += 1
        k = (eng, nd % self.NDMASEM)
        prev = self.dma_sem_use.get(k, 0)
        if prev:
            deps.add((("dma", k), 16 * prev))
        self.dma_sem_use[k] = prev + 1
        me = (("dma", k), 16 * (prev + 1))
        self.ops[eng].append(("dma", _freeze(fn), deps, k))
        self._mark(me, reads, writes)
        return me

    def barrier(self):
        snap = set()
        for e in self.ENGS:
            if self.cnt[e]:
                snap.add((e, self.cnt[e]))
        for k, c in self.dma_sem_use.items():
            snap.add((("dma", k), 16 * c))
        self.pending["act"] |= snap
        scr = self.bar_scratch
        me = self.op("act", lambda e: e.activation(out=scr, in_=scr, func=AF.Identity))
        for e in self.ENGS:
            if e != "act":
                self.pending[e].add(me)

    def emit(self, nc):
        with contextlib.ExitStack() as st:
            sems = {}
            for e in self.ENGS:
                sems[e] = st.enter_context(nc.semaphore("s_" + e))
            for k in self.dma_sem_use:
                sems[("dma", k)] = st.enter_context(nc.semaphore("s_dma_%s%d" % k))
            block = st.enter_context(nc.Block())
            regs = {"pe": block.tensor, "act": block.scalar, "dve": block.vector,
                    "pool": block.gpsimd, "sp": block.sync}
            for e in self.ENGS:
                ops = self.ops[e]
                if not ops and e != "sp":
                    continue

                def body(engine, ops=ops, e=e):
                    waited = {}
                    for kind, fn, deps, k in ops:
                        best = {}
                        for (sk, val) in deps:
                            if best.get(sk, 0) < val:
                                best[sk] = val
                        for sk in sorted(best, key=str):
                            val = best[sk]
                            if waited.get(sk, 0) >= val:
                                continue
                            engine.wait_ge(sems[sk], val)
                            waited[sk] = val
                        ins = fn(engine)
                        if kind == "op":
                            ins.then_inc(sems[e], 1)
                        else:
                            ins.then_inc(sems[("dma", k)], 16)
                    if e == "sp":
                        for k2, c2 in self.dma_sem_use.items():
                            engine.wait_ge(sems[("dma", k2)], 16 * c2)
                        for e2 in self.ENGS:
                            if e2 != "sp" and self.cnt[e2]:
                                engine.wait_ge(sems[e2], self.cnt[e2])
                regs[e](body)


class _Stop(Exception):
    pass


_TICK = [0]


def tick():
    _TICK[0] += 1
    stop_at('d%d' % _TICK[0])


def stop_at(name):
    if os.environ.get("K_STOP", "") == name:
        raise _Stop()


def interleave(gens):
    st = [[g, est, 0] for g, est in gens]
    while st:
        st.sort(key=lambda x: x[2] / x[1])
        g = st[0]
        try:
            next(g[0])
            g[2] += 1
        except StopIteration:
            st.pop(0)


class Arena:
    def __init__(self, ap, nelem):
        self.ap = ap
        self.n = nelem
        self.off = 0

    def reset(self):
        self.off = 0

    def alloc(self, shape, dt):
        n = 1
        for s in shape:
            n *= s
        ne = n * (2 if dt == F32 else 1)
        ne_al = (ne + 15) // 16 * 16
        assert self.off + ne_al <= self.n, ("arena overflow", self.off, ne_al, self.n)
        v = self.ap[:, self.off:self.off + ne]
        self.off += ne_al
        if dt == F32:
            v = v.bitcast(F32)
        if len(shape) == 2:
            v = v.rearrange("p (a b) -> p a b", a=shape[0])
        elif len(shape) == 3:
            v = v.rearrange("p (a b c) -> p a b c", a=shape[0], b=shape[1])
        return v


def build_program(nseq):
    nc = bass.Bass("TRN2", target_bir_lowering=False)
    dram = lambda name, shape, kind="ExternalInput": nc.dram_tensor(name, shape, F32, kind=kind).ap()
    x_d = dram("x", [nseq, SEQ, D])
    meta_d = dram("meta", [NMETA, D])
    win_d = dram("w_in", [D, INCOLS])
    wpa_d = dram("w_pa", [512, D])
    wpb_d = dram("w_pb", [512, D])
    wo_d = dram("w_o", [D, D])
    relb_d = dram("rel_bias", [32, 8])
    lning_d = dram("ln_in_g", [D])
    lninb_d = dram("ln_in_b", [D])
    lng_d = dram("ln_g", [D])
    lnb_d = dram("ln_b", [D])
    cols_d = dram("cols", [128, 48])
    ikg_d = dram("ikn_g", [64])
    ikb_d = dram("ikn_b", [64])
    cmat_d = dram("cmat", [32, 383])
    c128_d = dram("c128", [128, 5, 128])
    pw_d = dram("pw", [32])
    out_d = dram("out", [nseq, SEQ, D], kind="ExternalOutput")
    gscr_d = nc.dram_tensor("gscr", [8, 383], F32, kind="Internal").ap()

    P = Prog()
    st = contextlib.ExitStack()
    with st:
        sbt = lambda name, shape, dt: st.enter_context(nc.sbuf_tensor("sb_" + name, shape, dt))
        hT = sbt("hT", [128, 8, T], BF16)
        yT = sbt("yT", [128, 2, 4, T], BF16)
        ident = sbt("ident", [128, 128], BF16)
        identf = sbt("identf", [128, 128], F32)
        trimask = sbt("trimask", [128, 128], BF16)
        tri8 = sbt("tri8", [128, 128], BF16)
        ones8 = sbt("ones8", [128, 128], BF16)
        ones64 = sbt("ones64", [128, 64], BF16)
        negmask = sbt("negmask", [128, 128], F32)
        Jm = sbt("Jm", [128, 128], F32)
        c128f = sbt("c128f", [128, 5, 128], F32)
        BD = sbt("BD", [128, 8, 2, 2, 128], BF16)
        cols = sbt("cols", [128, 48], F32)
        negbg = sbt("negbg", [128, 16], F32)
        b31 = sbt("b31", [128, 8], F32)
        pw = sbt("pw", [128, 32], F32)
        ikg = sbt("ikg", [128, 64], F32)
        ikb = sbt("ikb", [128, 64], F32)
        relb = sbt("relb", [32, 8], F32)
        cmat = sbt("cmat", [32, 383], F32)
        gvec = sbt("gvec", [8, 383], F32)
        ARENA_N = 64400
        arena_t = sbt("arena", [128, ARENA_N], BF16)
        A = Arena(arena_t, ARENA_N)
        ps = st.enter_context(nc.psum_tensor("ps", [128, 4096], F32))
        bank = [ps[:, 512 * k:512 * (k + 1)] for k in range(8)]
        Bps = [Buf("ps%d" % k) for k in range(8)]
        Bconst = Buf("const")
        Bh = [Buf("h%d" % b) for b in range(5)]
        By = [[[Buf() for b in range(5)] for pr in range(4)] for br in range(2)]

        def mm(out, lhsT, rhs, start, stop, reads, writes):
            return P.op("pe", lambda e: e.matmul(out, lhsT=lhsT, rhs=rhs, start=start, stop=stop),
                        reads=reads, writes=writes)

        def act(out, in_, func, reads, writes, bias=0.0, scale=1.0):
            return P.op("act", lambda e: e.activation(out=out, in_=in_, func=func, bias=bias, scale=scale),
                        reads=reads, writes=writes)

        def dve(fn, reads, writes):
            return P.op("dve", fn, reads=reads, writes=writes)

        def pool(fn, reads, writes):
            return P.op("pool", fn, reads=reads, writes=writes)

        def ld(out, in_, writes, reads=()):
            return P.dma("sp", lambda e: e.dma_start(out=out, in_=in_), reads=reads, writes=writes)

        def ldc(out, in_, writes, reads=()):
            return P.dma("pool", lambda e: e.dma_start(out=out, in_=in_), reads=reads, writes=writes)

        P.bar_scratch = cols[0:1, 40:48]
        ld(c128f[:], c128_d, [Bconst])
        ld(cols[:], cols_d, [Bconst])
        ld(relb[:], relb_d, [Bconst])
        ld(cmat[:], cmat_d, [Bconst])
        ld(b31[:], relb_d[31, :].partition_broadcast(128), [Bconst])
        ld(pw[:], pw_d.partition_broadcast(128), [Bconst])
        ld(ikg[:], ikg_d.partition_broadcast(128), [Bconst])
        ld(ikb[:], ikb_d.partition_broadcast(128), [Bconst])
        dve(lambda e: e.tensor_copy(out=ident[:], in_=c128f[:, 0, :]), [Bconst], [Bconst])
        dve(lambda e: e.tensor_copy(out=identf[:], in_=c128f[:, 0, :]), [Bconst], [Bconst])
        dve(lambda e: e.tensor_copy(out=trimask[:], in_=c128f[:, 1, :]), [Bconst], [Bconst])
        dve(lambda e: e.tensor_scalar(out=tri8[:], in0=c128f[:, 2, :], scalar1=-8.0, scalar2=None, op0=ALU.mult),
            [Bconst], [Bconst])
        dve(lambda e: e.tensor_copy(out=Jm[:], in_=c128f[:, 3, :]), [Bconst], [Bconst])
        dve(lambda e: e.tensor_copy(out=negmask[:], in_=c128f[:, 4, :]), [Bconst], [Bconst])
        dve(lambda e: e.memset(ones8[:], -8.0), [], [Bconst])
        dve(lambda e: e.memset(ones64[:], 1.0), [], [Bconst])
        dve(lambda e: e.tensor_scalar(out=negbg[:], in0=cols[:, 16:32], scalar1=-1.0, scalar2=None, op0=ALU.mult),
            [Bconst], [Bconst])
        Bg = Buf("gscr")
        mm(bank[0][0:8, 0:383], relb[:, :], cmat[:, :], True, True, [Bconst], [Bps[0]])
        dve(lambda e: e.tensor_copy(out=gvec[:], in_=bank[0][0:8, 0:383]), [Bps[0]], [Bconst])
        ld(gscr_d, gvec[:], [Bg], reads=[Bconst])
        A.reset()
        hank = A.alloc([16, 128], F32)
        Bhank = Buf("hank")
        for h in range(8):
            for off in range(2):
                src = bass.AP(tensor=gscr_d.tensor, offset=h * 383 + 128 * off, ap=[[1, 128], [1, 128]])
                ld(hank[:, h * 2 + off, :], src, [Bhank], reads=[Bg])
        for h in range(8):
            for off in range(2):
                k = (h * 2 + off) % 4
                mm(bank[k][:, 0:128], Jm[:, :], hank[:, h * 2 + off, :], True, True, [Bhank, Bconst], [Bps[k]])
                dve(lambda e, h=h, off=off, k=k: e.tensor_copy(out=BD[:, h, off, 0, :], in_=bank[k][:, 0:128]),
                    [Bps[k]], [Bconst])
                dve(lambda e, h=h, off=off, k=k: e.tensor_tensor(out=BD[:, h, off, 1, :], in0=bank[k][:, 0:128],
                                                                  in1=BD[:, h, off, 0, :], op=ALU.subtract),
                    [Bps[k], Bconst], [Bconst])

        def ln_stats(xt, rows, stats, mv, rstd, Bx, Bs):
            for c in range(2):
                dve(lambda e, c=c: e.bn_stats(out=stats[0:rows, c, :], in_=xt[0:rows, 512 * c:512 * (c + 1)]),
                    [Bx], [Bs])
            dve(lambda e: e.bn_aggr(out=mv[0:rows, :], in_=stats[0:rows, :, :]), [Bs], [Bs])
            act(rstd[0:rows, :], mv[0:rows, 1:2], AF.Ln, [Bs], [Bs], bias=EPS)
            act(rstd[0:rows, :], rstd[0:rows, :], AF.Exp, [Bs], [Bs], scale=-0.5)

        def load_x_tile(seq, j, xt, Bx):
            rows = tsz(j)
            if j == 0:
                ld(xt[0:16, :], meta_d, [Bx])
                ld(xt[16:128, :], x_d[seq, 0:112, :], [Bx])
            else:
                r0 = 128 * j - 16
                ld(xt[0:rows, :], x_d[seq, r0:r0 + rows, :], [Bx])

        def load_w(dst, src_rows_ap, ncol, Bw, krows=8):
            ldc(dst, src_rows_ap.rearrange("(c p) n -> p c n", p=128), [Bw])

        try:
          for seq in range(nseq):
              P.barrier()
              A.reset()
              xbuf = [A.alloc([1024], F32) for _ in range(2)]
              xn = [A.alloc([1024], BF16) for _ in range(2)]
              stats = A.alloc([2, 6], F32)
              mv = A.alloc([2], F32)
              rstd = A.alloc([1], F32)
              wbuf = [A.alloc([8, 512], BF16) for _ in range(2)]
              qaT = A.alloc([4, T], BF16)
              kaT = A.alloc([4, T], BF16)
              va = A.alloc([NT, 512], BF16)
              def sb_scratch(kO, kR):
                  return dict(
                      ek=[A.alloc([512], BF16) for _ in range(4)], Bek=[Buf() for _ in range(4)],
                      sp=[A.alloc([512], BF16) for _ in range(6)], Bsp=[Buf() for _ in range(6)],
                      ec=[A.alloc([512], BF16) for _ in range(2)], Bec=[Buf() for _ in range(2)],
                      a=[A.alloc([512], BF16) for _ in range(4)], Ba=[Buf() for _ in range(4)],
                      acc=A.alloc([512], F32), Rrun=A.alloc([512], F32), Esc=A.alloc([512], F32),
                      tmp=A.alloc([512], F32), Bacc=Buf(), BR=Buf(), BE=Buf(), Btmp=Buf(),
                      kO=kO, kR=kR, c1=[0], c2=[0])

              sbsc = [sb_scratch(4, 5), sb_scratch(6, 7)]
              Bxb = [Buf(), Buf()]
              Bxn = [Buf(), Buf()]
              Bst = Buf()
              Bw = [Buf(), Buf()]
              Bqa = [[Buf() for b in range(5)] for pr in range(4)]
              Bka = [[Buf() for b in range(5)] for pr in range(4)]
              Bva = [Buf() for b in range(5)]

              stats_l = [stats, A.alloc([2, 6], F32)]
              mv_l = [mv, A.alloc([2], F32)]
              rstd_l = [rstd, A.alloc([1], F32)]
              Bst_l = [Bst, Buf()]

              def ln_a(j):
                  load_x_tile(seq, j, xbuf[j % 2], Bxb[j % 2])
                  ln_stats(xbuf[j % 2], tsz(j), stats_l[j % 2], mv_l[j % 2], rstd_l[j % 2], Bxb[j % 2], Bst_l[j % 2])

              def ln_b(j):
                  rows = tsz(j)
                  xt, bx = xbuf[j % 2], Bxb[j % 2]
                  xb, bxn = xn[j % 2], Bxn[j % 2]
                  mv_, rs_ = mv_l[j % 2], rstd_l[j % 2]
                  dve(lambda e: e.tensor_scalar(
                      out=xb[0:rows, :], in0=xt[0:rows, :], scalar1=mv_[0:rows, 0:1], scalar2=rs_[0:rows, 0:1],
                      op0=ALU.subtract, op1=ALU.mult), [bx, Bst_l[j % 2]], [bxn])
                  pk = 6 + (j % 2)
                  Xb = bank[pk].bitcast(BF16)
                  for c in range(8):
                      P.op("pe", lambda e, c=c: e.transpose(
                          out=Xb[:, c * 128:c * 128 + rows], in_=xb[0:rows, c * 128:(c + 1) * 128],
                          identity=ident[0:rows, 0:rows]), reads=[bxn, Bconst], writes=[Bps[pk]])

              def ln_c(j):
                  rows = tsz(j)
                  pk = 6 + (j % 2)
                  Xb = bank[pk].bitcast(BF16)
                  for c in range(8):
                      dve(lambda e, c=c: e.tensor_scalar(
                          out=hT[:, c, 128 * j:128 * j + rows], in0=Xb[:, c * 128:c * 128 + rows],
                          scalar1=cols[:, c:c + 1], scalar2=cols[:, 8 + c:9 + c], op0=ALU.mult, op1=ALU.add),
                          [Bps[pk], Bconst], [Bh[j // 4]])

              for j in range(NT + 2):
                  if j < NT:
                      ln_a(j)
                  if 0 <= j - 1 < NT:
                      ln_b(j - 1)
                  if 0 <= j - 2 < NT:
                      ln_c(j - 2)

              stop_at('ln')
              zrot = [0]

              def fm_project(col0, dstT, Bdst, nun=1, dup64=False):
                  u = zrot[0] % 2
                  wb, bw = wbuf[u], Bw[u]
                  if dup64:
                      ldc(wb[:, :, 0:64], win_d[:, col0:col0 + 64].rearrange("(c p) n -> p c n", p=128), [bw])
                      ldc(wb[:, :, 64:128], win_d[:, col0:col0 + 64].rearrange("(c p) n -> p c n", p=128), [bw])
                      nm = 1
                  else:
                      ldc(wb[:, :, :], win_d[:, col0:col0 + 512].rearrange("(c p) n -> p c n", p=128), [bw])
                      nm = 4
                  zrot[0] += 1
                  for b, (t0, n) in enumerate(TBS):
                      for m in range(nm):
                          k = zrot[0] % 4
                          zrot[0] += 1
                          for c in range(8):
                              mm(bank[k][:, 0:n], wb[:, c, m * 128:(m + 1) * 128], hT[:, c, t0:t0 + n],
                                 c == 0, c == 7, [bw, Bh[b]], [Bps[k]])
                          if dup64:
                              act(dstT[:, t0:t0 + n], bank[k][:, 0:n], AF.Identity, [Bps[k]], [Bdst[b]])
                          else:
                              act(dstT[:, m, t0:t0 + n], bank[k][:, 0:n], AF.Identity, [Bps[k]], [Bdst[m][b]])

              fm_project(C_QA, qaT, Bqa)
              fm_project(C_KA, kaT, Bka)
              u = zrot[0] % 2
              zrot[0] += 1
              wb, bw = wbuf[u], Bw[u]
              ldc(wb[:, :, :], win_d[:, C_VA:C_VA + 512].rearrange("(c p) n -> p c n", p=128), [bw])
              for j in range(NT):
                  rows = tsz(j)
                  k = 4 + (j % 2)
                  for c in range(8):
                      mm(bank[k][0:rows, :], hT[:, c, 128 * j:128 * j + rows], wb[:, c, :], c == 0, c == 7,
                         [bw, Bh[j // 4]], [Bps[k]])
                  act(va[0:rows, j, :], bank[k][0:rows, :], AF.Identity, [Bps[k]], [Bva[j // 4]])

              stop_at('p1a')
              ZB, CB = [0, 1], [2, 3]
              zc, cc = [0], [0]

              def sb_stream(b, pr, sc):
                  t0, n = TBS[b]
                  kO, kR = sc['kO'], sc['kR']
                  acc, Rrun, Esc, tmp = sc['acc'], sc['Rrun'], sc['Esc'], sc['tmp']
                  Bacc, BR, BE, Btmp = sc['Bacc'], sc['BR'], sc['BE'], sc['Btmp']
                  pool(lambda e: e.memset(acc[:, 0:n], 0.0), [], [Bacc])
                  pool(lambda e: e.memset(Rrun[:, 0:n], 0.0), [], [BR])
                  lastS = tiles_of(b)[-1]
                  units = list(range(lastS, -1, -1))
                  st = {}

                  def s1(S):
                      ks = tsz(S)
                      c0 = max(0, 128 * (S - 4 * b))
                      zk = []
                      for hh in range(2):
                          pb = 64 * hh
                          k = ZB[hh]
                          zk.append(k)
                          mm(bank[k][0:ks, c0:n], kaT[pb:pb + 64, pr, 128 * S:128 * S + ks],
                             qaT[pb:pb + 64, pr, t0 + c0:t0 + n], True, True, [Bka[pr][S // 4], Bqa[pr][b]], [Bps[k]])
                      rec = []
                      for hh in range(2):
                          i1 = sc['c1'][0]
                          sc['c1'][0] += 1
                          ek_t, Bek = sc['ek'][i1 % 4], sc['Bek'][i1 % 4]
                          sp_t, Bsp = sc['sp'][i1 % 6], sc['Bsp'][i1 % 6]
                          k = zk[hh]
                          act(ek_t[0:ks, c0:n], bank[k][0:ks, c0:n], AF.Exp, [Bps[k]], [Bek], scale=0.125)
                          if S >= 4 * b:
                              w = min(128, n - c0)
                              pool(lambda e: e.tensor_tensor(out=ek_t[0:ks, c0:c0 + w], in0=ek_t[0:ks, c0:c0 + w],
                                                             in1=trimask[0:ks, 0:w], op=ALU.mult), [Bek, Bconst], [Bek])
                          rec.append([ek_t, Bek, sp_t, Bsp])
                      for hh in range(2):
                          ek_t, Bek, sp_t, Bsp = rec[hh]
                          act(sp_t[0:ks, c0:n], ek_t[0:ks, c0:n], AF.Ln, [Bek], [Bsp], bias=1.0)
                      st[S] = (ks, c0, rec)

                  def s2(S):
                      ks, c0, rec = st[S]
                      for hh in range(2):
                          ek_t, Bek, sp_t, Bsp = rec[hh]
                          k2 = CB[hh]
                          mm(bank[k2][0:ks, c0:n], tri8[0:ks, 0:ks], sp_t[0:ks, c0:n], True, True, [Bsp, Bconst], [Bps[k2]])
                      for hh in range(2):
                          ek_t, Bek, sp_t, Bsp = rec[hh]
                          k2 = CB[hh]
                          i2 = sc['c2'][0]
                          sc['c2'][0] += 1
                          ec_t, Bec = sc['ec'][i2 % 2], sc['Bec'][i2 % 2]
                          a_t, Ba_t = sc['a'][i2 % 4], sc['Ba'][i2 % 4]
                          act(ec_t[0:ks, c0:n], bank[k2][0:ks, c0:n], AF.Exp, [Bps[k2]], [Bec], scale=0.125)
                          dve(lambda e: e.tensor_tensor(out=a_t[0:ks, c0:n], in0=ek_t[0:ks, c0:n], in1=ec_t[0:ks, c0:n],
                                                        op=ALU.mult), [Bek, Bec], [Ba_t])
                          rec[hh] = [a_t, Ba_t, sp_t, Bsp]

                  def s3(S):
                      ks, c0, rec = st.pop(S)
                      for hh in range(2):
                          a_t, Ba_t, sp_t, Bsp = rec[hh]
                          pb, h = 64 * hh, 2 * pr + hh
                          mm(bank[kO][pb:pb + 64, c0:n], va[0:ks, S, h * 64:(h + 1) * 64], a_t[0:ks, c0:n], True, True,
                             [Bva[S // 4], Ba_t], [Bps[kO]])
                      for hh in range(2):
                          a_t, Ba_t, sp_t, Bsp = rec[hh]
                          pb = 64 * hh
                          mm(bank[kR][pb:pb + 64, c0:n], ones64[0:ks, :], sp_t[0:ks, c0:n], True, True,
                             [Bsp, Bconst], [Bps[kR]])
                      act(Esc[:, c0:n], Rrun[:, c0:n], AF.Exp, [BR], [BE], scale=-1.0)
                      dve(lambda e: e.tensor_tensor(out=tmp[:, c0:n], in0=bank[kO][:, c0:n], in1=Esc[:, c0:n],
                                                    op=ALU.mult), [Bps[kO], BE], [Btmp])
                      dve(lambda e: e.tensor_tensor(out=acc[:, c0:n], in0=acc[:, c0:n], in1=tmp[:, c0:n],
                                                    op=ALU.add), [Bacc, Btmp], [Bacc])
                      if S > 0:
                          dve(lambda e: e.tensor_tensor(out=Rrun[:, c0:n], in0=Rrun[:, c0:n],
                                                        in1=bank[kR][:, c0:n], op=ALU.add), [BR, Bps[kR]], [BR])
                      else:
                          pool(lambda e: e.tensor_copy(out=yT[:, 0, pr, t0:t0 + n], in_=acc[:, 0:n]),
                               [Bacc], [By[0][pr][b]])

                  nu = len(units)
                  for k in range(nu + 2):
                      if k < nu:
                          s1(units[k])
                      if 0 <= k - 1 < nu:
                          s2(units[k - 1])
                      if 0 <= k - 2 < nu:
                          s3(units[k - 2])
                      yield

              def chain_sb(prs, sc):
                  for b in range(len(TBS)):
                      for pr in prs:
                          yield from sb_stream(b, pr, sc)

              interleave([(chain_sb((0, 2), sbsc[0]), 1.0), (chain_sb((1, 3), sbsc[1]), 1.0)])

              stop_at('sb')
              P.barrier()
              A.reset()
              wbuf = [A.alloc([8, 512], BF16) for _ in range(2)]
              Bw = [Buf(), Buf()]
              wtm = A.alloc([8, 136], BF16)
              Bwtm = Buf()
              qbT = A.alloc([4, T], BF16)
              qiT = A.alloc([4, T], BF16)
              kbT = A.alloc([T], BF16)
              kiT = A.alloc([T], BF16)
              vb = A.alloc([NT, 64], BF16)
              wi = A.alloc([NT, 8], F32)
              kin = A.alloc([128], BF16)
              kif = A.alloc([64], F32)
              ksq = A.alloc([64], F32)
              Bksq = Buf()
              stats = A.alloc([1, 6], F32)
              mv = A.alloc([2], F32)
              rstd = A.alloc([1], F32)
              score_l = [A.alloc([2176], F32), A.alloc([2176], F32)]
              Bscore_l = [Buf(), Buf()]
              maskb = [A.alloc([2176], BF16) for _ in range(2)]
              maskT = A.alloc([NT, 512], BF16)
              rl = [A.alloc([512], F32) for _ in range(2)]
              pbuf = [A.alloc([512], BF16) for _ in range(4)]
              pmb = [A.alloc([512], BF16) for _ in range(4)]
              rden = A.alloc([512], F32)
              bis_l = [A.alloc([40], F32), A.alloc([40], F32)]
              Bbis_l = [Buf(), Buf()]
              Bqb = [[Buf() for b in range(5)] for pr in range(4)]
              Bqi = [[Buf() for b in range(5)] for pr in range(4)]
              Bkb = [Buf() for b in range(5)]
              Bki = [Buf() for b in range(5)]
              Bvb = [Buf() for b in range(5)]
              Bwi = [Buf() for b in range(5)]
              Bkin, Bst, Brden = Buf(), Buf(), Buf()
              Bmb = [Buf(), Buf()]
              Brl = [Buf(), Buf()]
              Bp = [Buf() for _ in range(4)]
              Bpm = [Buf() for _ in range(4)]

              zrot[0] = 0
              fm_project(C_QB, qbT, Bqb)
              fm_project(C_QI, qiT, Bqi)
              stop_at('b1')
              fm_project(C_KB, kbT, Bkb, dup64=True)
              stop_at('b2')
              ldc(wtm[:, :, 0:64], win_d[:, C_VB:C_VB + 64].rearrange("(c p) n -> p c n", p=128), [Bwtm])
              ldc(wtm[:, :, 64:136], win_d[:, C_KI:C_KI + 72].rearrange("(c p) n -> p c n", p=128), [Bwtm])
              for j in range(NT):
                  rows = tsz(j)
                  k = 4 + (j % 2)
                  for c in range(8):
                      mm(bank[k][0:rows, 0:136], hT[:, c, 128 * j:128 * j + rows], wtm[:, c, :], c == 0, c == 7,
                         [Bwtm, Bh[j // 4]], [Bps[k]])
                  act(vb[0:rows, j, :], bank[k][0:rows, 0:64], AF.Identity, [Bps[k]], [Bvb[j // 4]])
                  act(wi[0:rows, j, :], bank[k][0:rows, 128:136], AF.Identity, [Bps[k]], [Bwi[j // 4]], scale=WI_SCALE)
                  stop_at('c1')
                  act(kif[0:rows, :], bank[k][0:rows, 64:128], AF.Identity, [Bps[k]], [Bkin])
                  tick()
                  dve(lambda e, rows=rows: e.tensor_reduce(out=mv[0:rows, 0:1], in_=kif[0:rows, :], axis=AX.X, op=ALU.add),
                      [Bkin], [Bst])
                  tick()
                  dve(lambda e, rows=rows: e.tensor_scalar(out=mv[0:rows, 0:1], in0=mv[0:rows, 0:1], scalar1=1.0 / 64,
                                                           scalar2=None, op0=ALU.mult), [Bst], [Bst])
                  tick()
                  dve(lambda e, rows=rows: e.tensor_scalar(out=kif[0:rows, :], in0=kif[0:rows, :], scalar1=mv[0:rows, 0:1],
                                                           scalar2=None, op0=ALU.subtract), [Bkin, Bst], [Bkin])
                  tick()
                  dve(lambda e, rows=rows: e.tensor_tensor(out=ksq[0:rows, :], in0=kif[0:rows, :], in1=kif[0:rows, :],
                                                           op=ALU.mult), [Bkin], [Bksq])
                  tick()
                  dve(lambda e, rows=rows: e.tensor_reduce(out=mv[0:rows, 1:2], in_=ksq[0:rows, :], axis=AX.X, op=ALU.add),
                      [Bksq], [Bst])
                  tick()
                  act(rstd[0:rows, :], mv[0:rows, 1:2], AF.Ln, [Bst], [Bst], bias=EPS, scale=1.0 / 64)
                  tick()
                  act(rstd[0:rows, :], rstd[0:rows, :], AF.Exp, [Bst], [Bst], scale=-0.5)
                  tick()
                  dve(lambda e, rows=rows: e.tensor_scalar(out=kif[0:rows, :], in0=kif[0:rows, :], scalar1=rstd[0:rows, 0:1],
                                                           scalar2=None, op0=ALU.mult), [Bkin, Bst], [Bkin])
                  tick()
                  dve(lambda e, rows=rows: e.tensor_tensor(out=kif[0:rows, :], in0=kif[0:rows, :], in1=ikg[0:rows, :],
                                                           op=ALU.mult), [Bkin, Bconst], [Bkin])
                  tick()
                  dve(lambda e, rows=rows: e.tensor_tensor(out=kin[0:rows, 0:64], in0=kif[0:rows, :], in1=ikb[0:rows, :],
                                                           op=ALU.add), [Bkin, Bconst], [Bkin])
                  tick()
                  dve(lambda e, rows=rows: e.tensor_copy(out=kin[0:rows, 64:128], in_=kin[0:rows, 0:64]), [Bkin], [Bkin])
                  stop_at('c2')
                  pk = 6 + (j % 2)
                  Xb = bank[pk].bitcast(BF16)
                  P.op("pe", lambda e, rows=rows, Xb=Xb: e.transpose(out=Xb[:, 0:rows], in_=kin[0:rows, :],
                                                                      identity=ident[0:rows, 0:rows]),
                       reads=[Bkin, Bconst], writes=[Bps[pk]])
                  act(kiT[:, 128 * j:128 * j + rows], Xb[:, 0:rows], AF.Identity, [Bps[pk]], [Bki[j // 4]])

              stop_at('p1b')
              maskTs = [maskT, A.ap[:, 0:NT * 512].rearrange("p (a b) -> p a b", a=NT)]
              BmT = [[Buf() for _ in range(NT)] for _ in range(2)]
              alias_w = [[], [Bw[0], Bw[1], Bwtm]]
              mzr = [0]
              MB = [2, 3]

              def on_act(i, pair):
                  return len(pair) == 2 and pair[0] >= 2 and i == pair[1]

              def mask_stream(b):
                  ms = b % 2
                  mT = maskTs[ms]
                  tl = tiles_of(b)
                  for p0 in range(0, len(tl), 2):
                      pair = tl[p0:p0 + 2]
                      for i in pair:
                          rows = tsz(i)
                          L = 128 * i + rows
                          score, Bscore = score_l[i % 2], Bscore_l[i % 2]
                          bis, Bbis = bis_l[i % 2], Bbis_l[i % 2]
                          mb, bmb = maskb[i % 2], Bmb[i % 2]
                          sblocks = [(s0, min(512, L - s0)) for s0 in range(0, L, 512)]
                          for (s0, sn) in sblocks:
                              for h in range(8):
                                  pr, pb = h // 2, 64 * (h % 2)
                                  k = MB[mzr[0] % 2]
                                  r2 = mzr[0] % 2
                                  mzr[0] += 1
                                  mm(bank[k][0:rows, 0:sn], qiT[pb:pb + 64, pr, 128 * i:128 * i + rows],
                                     kiT[pb:pb + 64, s0:s0 + sn], True, True,
                                     [Bqi[pr][b]] + [Bki[bb] for bb in range(s0 // 512, min(4, (s0 + sn - 1) // 512) + 1)],
                                     [Bps[k]])
                                  act(rl[r2][0:rows, 0:sn], bank[k][0:rows, 0:sn], AF.Relu, [Bps[k]], [Brl[r2]])
                                  if h == 0:
                                      dve(lambda e: e.tensor_scalar(
                                          out=score[0:rows, s0:s0 + sn], in0=rl[r2][0:rows, 0:sn],
                                          scalar1=wi[0:rows, i, h:h + 1], scalar2=None, op0=ALU.mult),
                                          [Brl[r2], Bwi[b]], [Bscore])
                                  else:
                                      dve(lambda e: e.scalar_tensor_tensor(
                                          out=score[0:rows, s0:s0 + sn], in0=rl[r2][0:rows, 0:sn],
                                          scalar=wi[0:rows, i, h:h + 1], in1=score[0:rows, s0:s0 + sn],
                                          op0=ALU.mult, op1=ALU.add), [Brl[r2], Bwi[b], Bscore], [Bscore])
                                  if h % 2 == 1:
                                      yield
                          if i >= 2:
                              dve(lambda e: e.reduce_max(out=bis[0:rows, 0:1], in_=score[0:rows, 0:L], axis=AX.X),
                                  [Bscore], [Bbis])
                              dve(lambda e: e.tensor_reduce(out=bis[0:rows, 1:2], in_=score[0:rows, 0:L],
                                                            axis=AX.X, op=ALU.min), [Bscore], [Bbis])
                          dve(lambda e: e.tensor_tensor(out=score[0:rows, 128 * i:128 * i + rows],
                                                        in0=score[0:rows, 128 * i:128 * i + rows],
                                                        in1=negmask[0:rows, 0:rows], op=ALU.add),
                              [Bscore, Bconst], [Bscore])
                          if i < 2:
                              dve(lambda e: e.tensor_scalar(out=mb[0:rows, 0:L], in0=score[0:rows, 0:L],
                                                            scalar1=-1e29, scalar2=None, op0=ALU.is_ge),
                                  [Bscore], [bmb])
                          else:
                              dve(lambda e: e.tensor_tensor(out=bis[0:rows, 2:3], in0=bis[0:rows, 0:1],
                                                            in1=bis[0:rows, 1:2], op=ALU.subtract), [Bbis], [Bbis])
                              dve(lambda e: e.tensor_scalar(out=bis[0:rows, 8:8 + NIT + 1], in0=pw[0:rows, 0:NIT + 1],
                                                            scalar1=bis[0:rows, 2:3], scalar2=None, op0=ALU.mult),
                                  [Bbis, Bconst], [Bbis])
                              if on_act(i, pair):
                                  dve(lambda e: e.tensor_scalar(out=bis[0:rows, 3:4], in0=bis[0:rows, 1:2], scalar1=-1.0,
                                                                scalar2=bis[0:rows, 9:10], op0=ALU.mult,
                                                                op1=ALU.subtract), [Bbis], [Bbis])
                              else:
                                  dve(lambda e: e.tensor_tensor(out=bis[0:rows, 3:4], in0=bis[0:rows, 1:2],
                                                                in1=bis[0:rows, 9:10], op=ALU.add), [Bbis], [Bbis])
                          yield
                      active = [i for i in pair if i >= 2]
                      for it in range(1, NIT + 1):
                          for i in active:
                              rows = tsz(i)
                              L = 128 * i + rows
                              score, Bscore = score_l[i % 2], Bscore_l[i % 2]
                              bis, Bbis = bis_l[i % 2], Bbis_l[i % 2]
                              mb, bmb = maskb[i % 2], Bmb[i % 2]
                              if on_act(i, pair):
                                  P.op("act", lambda e: e.activation(
                                      out=mb[0:rows, 0:L], in_=score[0:rows, 0:L], func=AF.Sign, bias=bis[0:rows, 3:4],
                                      scale=1.0, accum_out=bis[0:rows, 4:5]), reads=[Bscore, Bbis], writes=[bmb, Bbis])
                              else:
                                  dve(lambda e: e.tensor_scalar(
                                      out=mb[0:rows, 0:L], in0=score[0:rows, 0:L], scalar1=bis[0:rows, 3:4], scalar2=0.0,
                                      op0=ALU.is_ge, op1=ALU.add, accum_out=bis[0:rows, 4:5]), [Bscore, Bbis], [bmb, Bbis])
                          for i in active:
                              rows = tsz(i)
                              bis, Bbis = bis_l[i % 2], Bbis_l[i % 2]
                              if on_act(i, pair):
                                  thr = float(2 * TOPK - 1 - (128 * i + rows))
                                  last_ = (it == NIT)
                                  dve(lambda e: e.tensor_scalar(
                                      out=bis[0:rows, 5:6], in0=bis[0:rows, 4:5], scalar1=thr, scalar2=(0.0 if last_ else 0.5),
                                      op0=ALU.is_lt, op1=ALU.subtract), [Bbis], [Bbis])
                                  dve(lambda e: e.scalar_tensor_tensor(
                                      out=(bis[0:rows, 6:7] if last_ else bis[0:rows, 3:4]), in0=bis[0:rows, 5:6],
                                      scalar=bis[0:rows, 8 + it:9 + it], in1=bis[0:rows, 3:4],
                                      op0=ALU.mult, op1=ALU.add), [Bbis], [Bbis])
                                  continue
                              if it < NIT:
                                  dve(lambda e: e.tensor_scalar(
                                      out=bis[0:rows, 5:6], in0=bis[0:rows, 4:5], scalar1=TOPK - 0.5, scalar2=0.5,
                                      op0=ALU.is_ge, op1=ALU.subtract), [Bbis], [Bbis])
                                  dve(lambda e: e.scalar_tensor_tensor(
                                      out=bis[0:rows, 3:4], in0=bis[0:rows, 5:6], scalar=bis[0:rows, 8 + it:9 + it],
                                      in1=bis[0:rows, 3:4], op0=ALU.mult, op1=ALU.add), [Bbis], [Bbis])
                              else:
                                  dve(lambda e: e.tensor_scalar(
                                      out=bis[0:rows, 5:6], in0=bis[0:rows, 4:5], scalar1=TOPK - 0.5, scalar2=1.0,
                                      op0=ALU.is_ge, op1=ALU.subtract), [Bbis], [Bbis])
                                  dve(lambda e: e.scalar_tensor_tensor(
                                      out=bis[0:rows, 6:7], in0=bis[0:rows, 5:6], scalar=bis[0:rows, 8 + it:9 + it],
                                      in1=bis[0:rows, 3:4], op0=ALU.mult, op1=ALU.add), [Bbis], [Bbis])
                          if active:
                              yield
                      for i in pair:
                          rows = tsz(i)
                          L = 128 * i + rows
                          lc = 128 * (i - 4 * b)
                          score, Bscore = score_l[i % 2], Bscore_l[i % 2]
                          bis, Bbis = bis_l[i % 2], Bbis_l[i % 2]
                          mb, bmb = maskb[i % 2], Bmb[i % 2]
                          if i >= 2 and on_act(i, pair):
                              dve(lambda e: e.tensor_scalar(out=mb[0:rows, 0:L], in0=score[0:rows, 0:L],
                                                            scalar1=bis[0:rows, 6:7], scalar2=0.0,
                                                            op0=ALU.add, op1=ALU.is_ge), [Bscore, Bbis], [bmb])
                          elif i >= 2:
                              dve(lambda e: e.tensor_scalar(out=mb[0:rows, 0:L], in0=score[0:rows, 0:L],
                                                            scalar1=bis[0:rows, 6:7], scalar2=None,
                                                            op0=ALU.is_ge), [Bscore, Bbis], [bmb])
                          for S in range(i + 1):
                              ks = tsz(S)
                              pk = MB[mzr[0] % 2]
                              mzr[0] += 1
                              Xb = bank[pk].bitcast(BF16)
                              P.op("pe", lambda e: e.transpose(
                                  out=Xb[0:ks, 0:rows], in_=mb[0:rows, 128 * S:128 * S + ks], identity=ident[0:rows, 0:rows]),
                                  reads=[bmb, Bconst], writes=[Bps[pk]])
                              act(mT[0:ks, S, lc:lc + rows], Xb[0:ks, 0:rows], AF.Identity, [Bps[pk]],
                                  [BmT[ms][S]] + alias_w[ms])
                              if S % 2 == 1:
                                  yield
                          yield

              AZ = [0, 1, 4, 5]
              azr = [0]
              pcnt = [0]

              def attn_stream(b):
                  t0, n = TBS[b]
                  ms = b % 2
                  mT = maskTs[ms]
                  lastS = tiles_of(b)[-1]
                  for pr in range(4):
                      kq = {}

                      def qk(S):
                          ks = tsz(S)
                          c0 = max(0, 128 * (S - 4 * b))
                          near = [i for i in (S, S + 1) if i in tiles_of(b)]
                          kk = []
                          for hh in range(2):
                              pb = 64 * hh
                              k = AZ[azr[0] % len(AZ)]
                              azr[0] += 1
                              kk.append(k)
                              mm(bank[k][0:ks, c0:n], kbT[pb:pb + 64, 128 * S:128 * S + ks],
                                 qbT[pb:pb + 64, pr, t0 + c0:t0 + n], True, len(near) == 0,
                                 [Bkb[S // 4], Bqb[pr][b]], [Bps[k]])
                          for hh in range(2):
                              h = 2 * pr + hh
                              k = kk[hh]
                              for ni, i in enumerate(near):
                                  lc = 128 * (i - 4 * b)
                                  w = tsz(i)
                                  off = i - S
                                  for hl in range(2):
                                      mm(bank[k][0:ks, lc:lc + w], ident[0:ks, 0:ks], BD[0:ks, h, off, hl, 0:w], False,
                                         (ni == len(near) - 1) and hl == 1, [Bconst], [Bps[k]])
                          return kk

                      def ensure(S):
                          if S <= lastS and S not in kq:
                              kq[S] = qk(S)

                      ensure(0)
                      for S in range(lastS + 1):
                          ensure(S + 1)
                          kk = kq.pop(S)
                          ks = tsz(S)
                          c0 = max(0, 128 * (S - 4 * b))
                          us = []
                          for hh in range(2):
                              h = 2 * pr + hh
                              u2 = pcnt[0] % 4
                              pcnt[0] += 1
                              us.append(u2)
                              act(pbuf[u2][0:ks, c0:n], bank[kk[hh]][0:ks, c0:n], AF.Exp, [Bps[kk[hh]]], [Bp[u2]],
                                  bias=b31[0:ks, h:h + 1], scale=0.125)
                              pool(lambda e, u2=u2: e.tensor_tensor(
                                  out=pmb[u2][0:ks, c0:n], in0=pbuf[u2][0:ks, c0:n], in1=mT[0:ks, S, c0:n],
                                  op=ALU.mult), [Bp[u2], BmT[ms][S]], [Bpm[u2]])
                          for hh in range(2):
                              pb, u2 = 64 * hh, us[hh]
                              mm(bank[6][pb:pb + 64, c0:n], vb[0:ks, S, :], pmb[u2][0:ks, c0:n], S == 0, S == lastS,
                                 [Bvb[S // 4], Bpm[u2]], [Bps[6]])
                          for hh in range(2):
                              pb, u2 = 64 * hh, us[hh]
                              mm(bank[7][pb:pb + 64, c0:n], ones64[0:ks, :], pmb[u2][0:ks, c0:n], S == 0, S == lastS,
                                 [Bconst, Bpm[u2]], [Bps[7]])
                          yield
                      dve(lambda e: e.reciprocal(out=rden[:, 0:n], in_=bank[7][:, 0:n]), [Bps[7]], [Brden])
                      dve(lambda e, pr=pr: e.tensor_tensor(out=yT[:, 1, pr, t0:t0 + n], in0=bank[6][:, 0:n],
                                                           in1=rden[:, 0:n], op=ALU.mult),
                          [Bps[6], Brden], [By[1][pr][b]])
                      yield

              def est_mask(b):
                  tot = 0
                  for i in tiles_of(b):
                      L = 128 * i + tsz(i)
                      tot += 4 * ((L + 511) // 512) + (NIT if i >= 2 else 0) // 2 + (i + 1) // 2 + 1
                  return float(tot)

              def est_attn(b):
                  return float(4 * (tiles_of(b)[-1] + 2))

              interleave([(mask_stream(0), 1.0)])
              for b in range(len(TBS)):
                  gl = [(attn_stream(b), est_attn(b))]
                  if b + 1 < len(TBS):
                      gl.append((mask_stream(b + 1), est_mask(b + 1)))
                  interleave(gl)

              stop_at('dsa')
              P.barrier()
              A.reset()
              wzo = A.alloc([8, 1024], BF16)
              Bwzo = Buf()
              wst = [A.alloc([24, 128], BF16) for _ in range(3)]
              Bwst = [Buf(), Buf(), Buf()]
              mT = A.alloc([8, T], BF16)
              BmTt = [[Buf() for b in range(5)] for m in range(8)]
              gb_in_g = A.alloc([1024], F32)
              gb_in_b = A.alloc([1024], F32)
              gb_g = A.alloc([1024], F32)
              gb_b = A.alloc([1024], F32)
              Bbc = Buf()
              xbuf = [A.alloc([1024], F32) for _ in range(2)]
              Bxb = [Buf(), Buf()]
              hres = A.alloc([1024], F32)
              Bhres = Buf()
              rr = [A.alloc([1024], F32) for _ in range(2)]
              Brr = [Buf(), Buf()]
              stats = A.alloc([2, 6], F32)
              mv = A.alloc([2], F32)
              rstd = A.alloc([1], F32)
              Bst = Buf()
              stats2 = A.alloc([2, 6], F32)
              mv2 = A.alloc([2], F32)
              rstd2 = A.alloc([1], F32)
              Bst2 = Buf()
              ef = [A.alloc([512], F32) for _ in range(2)]
              Bef = [Buf(), Buf()]
              sg = [A.alloc([512], F32) for _ in range(3)]
              Bsg = [Buf(), Buf(), Buf()]
              m1 = [A.alloc([512], F32) for _ in range(4)]
              Bm1 = [Buf() for _ in range(4)]
              ld(gb_in_g, lning_d.partition_broadcast(128), [Bbc])
              ld(gb_in_b, lninb_d.partition_broadcast(128), [Bbc])
              ld(gb_g, lng_d.partition_broadcast(128), [Bbc])
              ld(gb_b, lnb_d.partition_broadcast(128), [Bbc])

              def sigmoid_from_psum(k, n, bias, si):
                  eb = zr[0] % 2
                  act(ef[eb][:, 0:n], bank[k][:, 0:n], AF.Exp, [Bps[k]], [Bef[eb]], bias=bias, scale=-1.0)
                  act(ef[eb][:, 0:n], ef[eb][:, 0:n], AF.Ln, [Bef[eb]], [Bef[eb]], bias=1.0)
                  act(sg[si][:, 0:n], ef[eb][:, 0:n], AF.Exp, [Bef[eb]], [Bsg[si]], scale=-1.0)

              def load_ws(m):
                  ws, bws = wst[m % 3], Bwst[m % 3]
                  ldc(ws[:, 0:8, :], win_d[:, C_GA + 128 * m:C_GA + 128 * (m + 1)].rearrange("(c p) n -> p c n", p=128), [bws])
                  ldc(ws[:, 8:16, :], win_d[:, C_GB + 128 * m:C_GB + 128 * (m + 1)].rearrange("(c p) n -> p c n", p=128), [bws])
                  ldc(ws[:, 16:20, :], wpa_d[:, 128 * m:128 * (m + 1)].rearrange("(c p) n -> p c n", p=128), [bws])
                  ldc(ws[:, 20:24, :], wpb_d[:, 128 * m:128 * (m + 1)].rearrange("(c p) n -> p c n", p=128), [bws])

              ldc(wzo[:, :, 0:512], win_d[:, C_ZA:C_ZA + 512].rearrange("(c p) n -> p c n", p=128), [Bwzo])
              ldc(wzo[:, :, 512:1024], win_d[:, C_ZB:C_ZB + 512].rearrange("(c p) n -> p c n", p=128), [Bwzo])
              load_ws(0)
              load_ws(1)
              zr = [0]
              for b, (t0, n) in enumerate(TBS):
                  for m in range(8):
                      br, pr = m // 4, m % 4
                      k = zr[0] % 6
                      si = zr[0] % 3
                      zr[0] += 1
                      for c in range(8):
                          mm(bank[k][:, 0:n], wzo[:, c, m * 128:(m + 1) * 128], hT[:, c, t0:t0 + n], c == 0, c == 7,
                             [Bwzo, Bh[b]], [Bps[k]])
                      sigmoid_from_psum(k, n, 0.0, si)
                      dve(lambda e, k=k, n=n, si=si: e.tensor_tensor(out=sg[si][:, 0:n], in0=bank[k][:, 0:n],
                                                                     in1=sg[si][:, 0:n], op=ALU.mult),
                          [Bps[k], Bsg[si]], [Bsg[si]])
                      pool(lambda e, n=n, si=si, br=br, pr=pr, t0=t0: e.tensor_tensor(
                          out=yT[:, br, pr, t0:t0 + n], in0=yT[:, br, pr, t0:t0 + n], in1=sg[si][:, 0:n], op=ALU.mult),
                          [By[br][pr][b], Bsg[si]], [By[br][pr][b]])
              stop_at('3i')
              for m in range(8):
                  u = m % 3
                  ws, bws = wst[u], Bwst[u]
                  for b, (t0, n) in enumerate(TBS):
                      res = []
                      for br in range(2):
                          k = zr[0] % 6
                          si = zr[0] % 3
                          zr[0] += 1
                          for c in range(8):
                              mm(bank[k][:, 0:n], ws[:, 8 * br + c, :], hT[:, c, t0:t0 + n], c == 0, c == 7,
                                 [bws, Bh[b]], [Bps[k]])
                          sigmoid_from_psum(k, n, negbg[:, 8 * br + m:8 * br + m + 1], si)
                          k2 = zr[0] % 6
                          zr[0] += 1
                          for c in range(4):
                              mm(bank[k2][:, 0:n], ws[:, 16 + 4 * br + c, :], yT[:, br, c, t0:t0 + n], c == 0, c == 3,
                                 [bws, By[br][c][b]], [Bps[k2]])
                          dve(lambda e, k2=k2, n=n, si=si, br=br: e.tensor_tensor(
                              out=m1[2 * (b % 2) + br][:, 0:n], in0=bank[k2][:, 0:n], in1=sg[si][:, 0:n], op=ALU.mult),
                              [Bps[k2], Bsg[si]], [Bm1[2 * (b % 2) + br]])
                      pool(lambda e, n=n, m=m, t0=t0: e.tensor_tensor(out=mT[:, m, t0:t0 + n], in0=m1[2 * (b % 2)][:, 0:n],
                                                                      in1=m1[2 * (b % 2) + 1][:, 0:n], op=ALU.add),
                           [Bm1[2 * (b % 2)], Bm1[2 * (b % 2) + 1]], [BmTt[m][b]])
                  if m == 0:
                      ldc(wzo[:, :, :], wo_d.rearrange("(c p) n -> p c n", p=128), [Bwzo])
                  if m + 2 < 8:
                      load_ws(m + 2)
              stop_at('3ii')
              hres_l = [hres, A.alloc([1024], F32)]
              Bhres_l = [Bhres, Buf()]
              st1 = [(stats, mv, rstd, Bst), (A.alloc([2, 6], F32), A.alloc([2], F32), A.alloc([1], F32), Buf())]
              st2 = [(stats2, mv2, rstd2, Bst2), (A.alloc([2, 6], F32), A.alloc([2], F32), A.alloc([1], F32), Buf())]
              nmr1 = [A.alloc([1], F32), A.alloc([1], F32)]
              nmr2 = [A.alloc([1], F32), A.alloc([1], F32)]

              def o_a(j):
                  rows = tsz(j)
                  xt, bx = xbuf[j % 2], Bxb[j % 2]
                  sts, mv_, rs_, bst = st1[j % 2]
                  hr, bhr = hres_l[j % 2], Bhres_l[j % 2]
                  nm = nmr1[j % 2]
                  load_x_tile(seq, j, xt, bx)
                  ln_stats(xt, rows, sts, mv_, rs_, bx, bst)
                  dve(lambda e: e.tensor_scalar(out=nm[0:rows, :], in0=mv_[0:rows, 0:1], scalar1=rs_[0:rows, 0:1],
                                                scalar2=-1.0, op0=ALU.mult, op1=ALU.mult), [bst], [bst])
                  act(hr[0:rows, :], xt[0:rows, :], AF.Identity, [bx, bst], [bhr], bias=nm[0:rows, 0:1],
                      scale=rs_[0:rows, 0:1])
                  pool(lambda e: e.tensor_tensor(out=hr[0:rows, :], in0=hr[0:rows, :], in1=gb_in_g[0:rows, :],
                                                 op=ALU.mult), [bhr, Bbc], [bhr])
                  pool(lambda e: e.tensor_tensor(out=hr[0:rows, :], in0=hr[0:rows, :], in1=gb_in_b[0:rows, :],
                                                 op=ALU.add), [bhr, Bbc], [bhr])

              def o_b(j):
                  rows = tsz(j)
                  b = j // 4
                  hr, bhr = hres_l[j % 2], Bhres_l[j % 2]
                  k = 6 if j % 2 == 0 else 4
                  for half in range(2):
                      for m in range(8):
                          mm(bank[k + half][0:rows, :], mT[:, m, 128 * j:128 * j + rows], wzo[:, m, 512 * half:512 * (half + 1)],
                             m == 0, m == 7, [BmTt[m][b], Bwzo], [Bps[k + half]])
                  r_, br_ = rr[j % 2], Brr[j % 2]
                  dve(lambda e: e.scalar_tensor_tensor(
                      out=r_[0:rows, :], in0=hr[0:rows, :], scalar=ALPHA, in1=ps[0:rows, 512 * k:512 * k + 1024],
                      op0=ALU.mult, op1=ALU.add), [bhr, Bps[k], Bps[k + 1]], [br_])

              def o_c(j):
                  rows = tsz(j)
                  r_, br_ = rr[j % 2], Brr[j % 2]
                  sts, mv_, rs_, bst = st2[j % 2]
                  nm = nmr2[j % 2]
                  ln_stats(r_, rows, sts, mv_, rs_, br_, bst)
                  dve(lambda e: e.tensor_scalar(out=nm[0:rows, :], in0=mv_[0:rows, 0:1], scalar1=rs_[0:rows, 0:1],
                                                scalar2=-1.0, op0=ALU.mult, op1=ALU.mult), [bst], [bst])
                  act(r_[0:rows, :], r_[0:rows, :], AF.Identity, [br_, bst], [br_], bias=nm[0:rows, 0:1],
                      scale=rs_[0:rows, 0:1])
                  dve(lambda e: e.tensor_tensor(out=r_[0:rows, :], in0=r_[0:rows, :], in1=gb_g[0:rows, :],
                                                op=ALU.mult), [br_, Bbc], [br_])
                  pool(lambda e: e.tensor_tensor(out=r_[0:rows, :], in0=r_[0:rows, :], in1=gb_b[0:rows, :],
                                                 op=ALU.add), [br_, Bbc], [br_])
                  if j == 0:
                      P.dma("sp", lambda e: e.dma_start(out=out_d[seq, 0:112, :], in_=r_[16:128, :]),
                            reads=[br_], writes=[])
                  else:
                      r0 = 128 * j - 16
                      P.dma("sp", lambda e: e.dma_start(out=out_d[seq, r0:r0 + rows, :], in_=r_[0:rows, :]),
                            reads=[br_], writes=[])

              for k_ in range(NT + 2):
                  if k_ < NT:
                      o_a(k_)
                  if 0 <= k_ - 1 < NT:
                      o_b(k_ - 1)
                  if 0 <= k_ - 2 < NT:
                      o_c(k_ - 2)
        except _Stop:
            pass
        if os.environ.get('K_VERBOSE'):
            print('OPCOUNTS', P.cnt, P.ndma, {e: len(P.ops[e]) for e in P.ENGS})
        P.emit(nc)
    return nc


def rel_bucket_np(d):
    d = np.asarray(d)
    nf = np.maximum(d, 1).astype(np.float32)
    large = 16 + (np.log(nf / np.float32(16)) / np.float32(math.log(128 / 16)) * np.float32(16)).astype(np.int32)
    large = np.minimum(large, 31)
    return np.where(d < 16, d, large)


def host_constants():
    cm = np.zeros((32, 383), np.float32)
    for k in range(383):
        d = k - 127
        if d >= 128:
            continue
        bkt = int(rel_bucket_np(max(d, 0)))
        cm[bkt, k] += 8.0
        cm[31, k] -= 8.0
    p = np.arange(128)
    c128 = np.zeros((128, 5, 128), np.float32)
    c128[:, 0, :] = np.eye(128, dtype=np.float32)
    c128[:, 1, :] = (p[:, None] < p[None, :]).astype(np.float32)
    c128[:, 2, :] = (p[:, None] >= p[None, :]).astype(np.float32)
    c128[:, 3, :] = (p[:, None] + p[None, :] == 127).astype(np.float32)
    c128[:, 4, :] = np.where(p[None, :] <= p[:, None], 0.0, -1e30).astype(np.float32)
    pw = (2.0 ** -np.arange(32)).astype(np.float32)
    return cm, c128, pw


_CACHE = {}


def kernel(x, meta_tokens, ln_in_g, ln_in_b, rel_bias, w_in, b_gate, idx_kn_g, idx_kn_b,
           w_pa, w_pb, w_o, ln_g, ln_b, _ncores=8, _nseq=4):
    f = lambda a: np.ascontiguousarray(np.asarray(a, dtype=np.float32))
    x = f(x)
    ncores, nseq = _ncores, _nseq
    key = (nseq,)
    if key not in _CACHE:
        _CACHE[key] = build_program(nseq)
    nc = _CACHE[key]
    cm, c128, pw = host_constants()
    cols = np.zeros((128, 48), np.float32)
    cols[:, 0:8] = f(ln_in_g).reshape(8, 128).T
    cols[:, 8:16] = f(ln_in_b).reshape(8, 128).T
    cols[:, 16:32] = f(b_gate).reshape(16, 128).T
    shared = {
        "meta": f(meta_tokens), "w_in": f(w_in)[0], "w_pa": f(w_pa)[0], "w_pb": f(w_pb)[0], "w_o": f(w_o)[0],
        "rel_bias": f(rel_bias), "ln_in_g": f(ln_in_g), "ln_in_b": f(ln_in_b), "ln_g": f(ln_g)[0], "ln_b": f(ln_b)[0],
        "cols": cols, "ikn_g": f(idx_kn_g)[0], "ikn_b": f(idx_kn_b)[0], "cmat": cm, "c128": c128, "pw": pw,
    }
    in_maps = []
    for c in range(ncores):
        d = dict(shared)
        d["x"] = np.ascontiguousarray(x[c * nseq:(c + 1) * nseq])
        in_maps.append(d)
    res = run_bass_kernel_spmd(nc, in_maps, core_ids=list(range(ncores)))
    return np.concatenate([np.asarray(r["out"]) for r in res.results], axis=0).astype(np.float32)
```
